# Optimizing a Trainium2 kernel written in Bass

```python
import math
import jax, jax.numpy as jnp
from jax import lax
import numpy as np

D_MODEL = 1024
BATCH = 8
SEQ = 4096
DEPTH = 2

N_META = 16
EPS = 1e-6
D_FF = 4 * D_MODEL

S5_WIDTH = D_MODEL // 4
S5_GROUP = 16
S5_GROUPS = S5_WIDTH // S5_GROUP
S5_STATE = 64

SSD_WIDTH = D_MODEL // 2
SSD_HEAD_DIM = 64
SSD_HEADS = SSD_WIDTH // SSD_HEAD_DIM
SSD_GROUPS = 2
SSD_HEADS_PER_GROUP = SSD_HEADS // SSD_GROUPS
SSD_STATE = 128
SSD_CONV = 5
SSD_CHUNK = 128
SSD_CONV_CH = SSD_WIDTH + 2 * SSD_GROUPS * SSD_STATE

RWKV_WIDTH = D_MODEL // 4
RWKV_HEAD = 64
RWKV_HEADS = RWKV_WIDTH // RWKV_HEAD
RWKV_DECAY_LORA = 64
RWKV_AAA_LORA = 64
RWKV_GATE_LORA = 128
RWKV_LN_EPS = 64e-5

N_BRANCH = 3
IN_SPLITS = (S5_WIDTH, SSD_WIDTH, SSD_CONV_CH, SSD_HEADS, 4 * RWKV_WIDTH, N_BRANCH * D_MODEL)
N_IN = sum(IN_SPLITS)

kernel_name = 'hybrid_s5_ssd_rwkv7_bidir_encoder'


def rms_norm(x, w):
    xf = x.astype(jnp.float32)
    y = xf * lax.rsqrt(jnp.mean(xf * xf, -1, keepdims=True) + EPS)
    return (y * w).astype(x.dtype)


def _complex_affine(e1, e2):
    ar1, ai1, br1, bi1 = e1
    ar2, ai2, br2, bi2 = e2
    return (ar2 * ar1 - ai2 * ai1,
            ar2 * ai1 + ai2 * ar1,
            ar2 * br1 - ai2 * bi1 + br2,
            ar2 * bi1 + ai2 * br1 + bi2)


def s5_mixer(u, lam_re, lam_im, log_step, b_re, b_im, c_re, c_im, d_skip, glu_w, glu_b):
    bsz, length, _ = u.shape
    ug = u.reshape(bsz, length, S5_GROUPS, S5_GROUP)
    bu_re = jnp.einsum('gnh,blgh->blgn', b_re, ug)
    bu_im = jnp.einsum('gnh,blgh->blgn', b_im, ug)
    h_re = 0.0
    h_im = 0.0
    for direction in range(2):
        lr, li = lam_re[direction], lam_im[direction]
        step = jnp.exp(log_step[direction])[:, None]
        mag = jnp.exp(lr * step)
        ab_re, ab_im = mag * jnp.cos(li * step), mag * jnp.sin(li * step)
        den = lr * lr + li * li
        co_re = ((ab_re - 1.0) * lr + ab_im * li) / den
        co_im = (ab_im * lr - (ab_re - 1.0) * li) / den
        e_re = co_re * bu_re - co_im * bu_im
        e_im = co_re * bu_im + co_im * bu_re
        shape = (1, length) + ab_re.shape
        elems = (jnp.broadcast_to(ab_re, shape), jnp.broadcast_to(ab_im, shape), e_re, e_im)
        _, _, s_re, s_im = lax.associative_scan(_complex_affine, elems, reverse=(direction == 1), axis=1)
        h_re = h_re + s_re
        h_im = h_im + s_im
    y = jnp.einsum('ghn,blgn->blgh', c_re, h_re) - jnp.einsum('ghn,blgn->blgh', c_im, h_im)
    y = y.reshape(bsz, length, S5_WIDTH) + d_skip * u
    zz = jax.nn.gelu(y) @ glu_w + glu_b
    return zz[..., :S5_WIDTH] * jax.nn.sigmoid(zz[..., S5_WIDTH:])


def _segsum_exp(a_cs):
    q = a_cs.shape[2]
    diff = a_cs[:, :, :, None] - a_cs[:, :, None, :]
    mask = jnp.tril(jnp.ones((q, q), bool))[None, None, :, :, None, None]
    return jnp.exp(jnp.where(mask, diff, -jnp.inf))


def ssd_chunked(xdt, a_step, bmat, cmat):
    bsz, t = xdt.shape[:2]
    nc = t // SSD_CHUNK
    chunk = lambda z: z.reshape((bsz, nc, SSD_CHUNK) + z.shape[2:])
    xdt, a_step, bmat, cmat = chunk(xdt), chunk(a_step), chunk(bmat), chunk(cmat)
    a_cs = jnp.cumsum(a_step, axis=2)
    cb = jnp.einsum('bclgn,bcsgn->bclsg', cmat, bmat)
    m = cb[..., None] * _segsum_exp(a_cs)
    y_diag = jnp.einsum('bclsgj,bcsgjp->bclgjp', m, xdt)
    decay_states = jnp.exp(a_cs[:, :, -1:] - a_cs)
    states = jnp.einsum('bcsgn,bcsgjp->bcgjpn', bmat, xdt * decay_states[..., None])
    chunk_decay = jnp.exp(a_cs[:, :, -1])

    def step(h, inp):
        s, dec = inp
        return h * dec[..., None, None] + s, h

    h0 = jnp.zeros_like(states[:, 0])
    _, h_in = lax.scan(step, h0, (jnp.moveaxis(states, 1, 0), jnp.moveaxis(chunk_decay, 1, 0)))
    h_in = jnp.moveaxis(h_in, 0, 1)
    y_off = jnp.einsum('bclgn,bcgjpn->bclgjp', cmat, h_in) * jnp.exp(a_cs)[..., None]
    y = y_diag + y_off
    return y.reshape((bsz, t) + y.shape[3:])


def ssd_mixer(z, xbc, dt_raw, conv_w, conv_b, a_log, dt_bias, d_skip, norm_w):
    bsz, length, _ = xbc.shape
    pad = SSD_CONV // 2
    xbc = lax.conv_general_dilated(xbc, conv_w[:, None, :], window_strides=(1,), padding=[(pad, pad)],
                                   dimension_numbers=('NWC', 'WIO', 'NWC'),
                                   feature_group_count=SSD_CONV_CH) + conv_b
    xbc = jax.nn.silu(xbc)
    xs, bmat, cmat = jnp.split(xbc, [SSD_WIDTH, SSD_WIDTH + SSD_GROUPS * SSD_STATE], axis=-1)
    xh = xs.reshape(bsz, length, SSD_GROUPS, SSD_HEADS_PER_GROUP, SSD_HEAD_DIM)
    bmat = bmat.reshape(bsz, length, SSD_GROUPS, SSD_STATE)
    cmat = cmat.reshape(bsz, length, SSD_GROUPS, SSD_STATE)
    front = SSD_CHUNK - N_META
    padt = lambda t: jnp.pad(t, ((0, 0), (front, 0)) + ((0, 0),) * (t.ndim - 2))
    xh_p, b_p, c_p = padt(xh), padt(bmat), padt(cmat)
    y = 0.0
    for direction in range(2):
        dt = jax.nn.softplus(dt_raw + dt_bias[direction]).reshape(bsz, length, SSD_GROUPS, SSD_HEADS_PER_GROUP)
        dt_p = padt(dt)
        a_step = dt_p * (-jnp.exp(a_log[direction])).reshape(SSD_GROUPS, SSD_HEADS_PER_GROUP)
        xdt = xh_p * dt_p[..., None]
        if direction == 0:
            y = y + ssd_chunked(xdt, a_step, b_p, c_p)
        else:
            yb = ssd_chunked(jnp.flip(xdt, 1), jnp.flip(a_step, 1), jnp.flip(b_p, 1), jnp.flip(c_p, 1))
            y = y + jnp.flip(yb, 1)
    y = y[:, front:] + d_skip.reshape(SSD_GROUPS, SSD_HEADS_PER_GROUP)[..., None] * xh
    y = y.reshape(bsz, length, SSD_WIDTH) * jax.nn.silu(z)
    return rms_norm(y, norm_w)


def centred_shift(x):
    prev = jnp.pad(x[:, :-1], ((0, 0), (1, 0), (0, 0)))
    nxt = jnp.pad(x[:, 1:], ((0, 0), (0, 1), (0, 0)))
    return 0.5 * (prev + nxt) - x


def wkv7_scan(r, w, k, v, a, b, reverse):
    bsz, _, heads, n = r.shape
    xs = tuple(jnp.moveaxis(t, 1, 0) for t in (r, w, k, v, a, b))

    def step(state, inp):
        rt, wt, kt, vt, at, bt = inp
        sa = jnp.einsum('bhvk,bhk->bhv', state, at)
        state = state * wt[:, :, None, :] + sa[..., None] * bt[:, :, None, :] + vt[..., None] * kt[:, :, None, :]
        return state, jnp.einsum('bhvk,bhk->bhv', state, rt)

    s0 = jnp.zeros((bsz, heads, n, n), r.dtype)
    _, ys = lax.scan(step, s0, xs, reverse=reverse)
    return jnp.moveaxis(ys, 0, 1)


def rwkv7_mixer(r, k, v, xc, mu_rkv, mu_wag, w0, w1, w2, a0, a1, a2, g1, g2, k_k, k_a, r_k, ln_w, ln_b):
    bsz, length, _ = r.shape
    heads = lambda t: t.reshape(bsz, length, RWKV_HEADS, RWKV_HEAD)
    r = r + centred_shift(r) * mu_rkv[0]
    k = k + centred_shift(k) * mu_rkv[1]
    v = v + centred_shift(v) * mu_rkv[2]
    dxc = centred_shift(xc)
    xw = xc + dxc * mu_wag[0]
    xa = xc + dxc * mu_wag[1]
    xg = xc + dxc * mu_wag[2]
    g = jax.nn.sigmoid(xg @ g1) @ g2
    kk = heads(k * k_k)
    kk = kk * lax.rsqrt(jnp.sum(jnp.square(kk), -1, keepdims=True) + 1e-12)
    rh, vh = heads(r), heads(v)
    y = 0.0
    for direction in range(2):
        w_log = -jax.nn.softplus(-(w0[direction] + jnp.tanh(xw @ w1[direction]) @ w2[direction])) - 0.5
        decay = jnp.exp(-jnp.exp(w_log))
        a = jax.nn.sigmoid(a0[direction] + (xa @ a1[direction]) @ a2[direction])
        kd = k * (1.0 + (a - 1.0) * k_a)
        y = y + wkv7_scan(rh, heads(decay), heads(kd), vh, -kk, kk * heads(a), reverse=(direction == 1))
    yf = y.astype(jnp.float32)
    mean = jnp.mean(yf, -1, keepdims=True)
    var = jnp.mean(jnp.square(yf - mean), -1, keepdims=True)
    yn = ((yf - mean) * lax.rsqrt(var + RWKV_LN_EPS)).astype(y.dtype).reshape(bsz, length, RWKV_WIDTH)
    yn = yn * ln_w + ln_b
    bonus = jnp.sum(rh * heads(k) * r_k, -1, keepdims=True) * vh
    return (yn + bonus.reshape(bsz, length, RWKV_WIDTH)) * g


def setup_inputs(seed: int = 0) -> dict:
    key = jax.random.key(seed)
    ks = iter(jax.random.split(key, 64))

    def nrm(shape, scale):
        return scale * jax.random.normal(next(ks), shape, jnp.float32)

    def near(shape, centre, spread):
        return centre + spread * jax.random.normal(next(ks), shape, jnp.float32)

    def unif(shape, lo, hi):
        return jax.random.uniform(next(ks), shape, jnp.float32, lo, hi)

    x = nrm((BATCH, SEQ, D_MODEL), 1.0)
    meta_tokens = nrm((N_META, D_MODEL), 1.0)
    final_norm_w = near((D_MODEL,), 1.0, 0.02)
    mix_norm_w = near((DEPTH, D_MODEL), 1.0, 0.02)
    w_in = nrm((DEPTH, D_MODEL, N_IN), D_MODEL ** -0.5)
    s5_lambda_re = near((DEPTH, 2, S5_GROUPS, S5_STATE), -0.5, 0.01)
    s5_lambda_im = jnp.pi * jnp.arange(S5_STATE, dtype=jnp.float32) + nrm((DEPTH, 2, S5_GROUPS, S5_STATE), 0.01)
    s5_log_step = unif((DEPTH, 2, S5_GROUPS), math.log(1e-3), math.log(1e-1))
    s5_b_re = nrm((DEPTH, S5_GROUPS, S5_STATE, S5_GROUP), (2 * S5_GROUP) ** -0.5)
    s5_b_im = nrm((DEPTH, S5_GROUPS, S5_STATE, S5_GROUP), (2 * S5_GROUP) ** -0.5)
    s5_c_re = nrm((DEPTH, S5_GROUPS, S5_GROUP, S5_STATE), S5_STATE ** -0.5)
    s5_c_im = nrm((DEPTH, S5_GROUPS, S5_GROUP, S5_STATE), S5_STATE ** -0.5)
    s5_d = nrm((DEPTH, S5_WIDTH), 1.0)
    s5_glu_w = nrm((DEPTH, S5_WIDTH, 2 * S5_WIDTH), S5_WIDTH ** -0.5)
    s5_glu_b = nrm((DEPTH, 2 * S5_WIDTH), 0.01)
    ssd_conv_w = nrm((DEPTH, SSD_CONV, SSD_CONV_CH), SSD_CONV ** -0.5)
    ssd_conv_b = nrm((DEPTH, SSD_CONV_CH), 0.01)
    ssd_a_log = jnp.log(unif((DEPTH, 2, SSD_HEADS), 1.0, 16.0))
    dt0 = jnp.exp(unif((DEPTH, 2, SSD_HEADS), math.log(1e-3), math.log(1e-1)))
    ssd_dt_bias = dt0 + jnp.log(-jnp.expm1(-dt0))
    ssd_d = near((DEPTH, SSD_HEADS), 1.0, 0.1)
    ssd_norm_w = near((DEPTH, SSD_WIDTH), 1.0, 0.02)
    rwkv_mu_rkv = unif((DEPTH, 3, RWKV_WIDTH), 0.0, 1.0)
    rwkv_mu_wag = unif((DEPTH, 3, RWKV_WIDTH), 0.0, 1.0)
    rwkv_w0 = jnp.linspace(-6.0, -1.0, RWKV_WIDTH, dtype=jnp.float32) + nrm((DEPTH, 2, RWKV_WIDTH), 0.1)
    rwkv_w1 = nrm((DEPTH, 2, RWKV_WIDTH, RWKV_DECAY_LORA), RWKV_WIDTH ** -0.5)
    rwkv_w2 = nrm((DEPTH, 2, RWKV_DECAY_LORA, RWKV_WIDTH), 0.1 * RWKV_DECAY_LORA ** -0.5)
    rwkv_a0 = nrm((DEPTH, 2, RWKV_WIDTH), 0.1)
    rwkv_a1 = nrm((DEPTH, 2, RWKV_WIDTH, RWKV_AAA_LORA), RWKV_WIDTH ** -0.5)
    rwkv_a2 = nrm((DEPTH, 2, RWKV_AAA_LORA, RWKV_WIDTH), 0.1 * RWKV_AAA_LORA ** -0.5)
    rwkv_g1 = nrm((DEPTH, RWKV_WIDTH, RWKV_GATE_LORA), RWKV_WIDTH ** -0.5)
    rwkv_g2 = nrm((DEPTH, RWKV_GATE_LORA, RWKV_WIDTH), RWKV_GATE_LORA ** -0.5)
    rwkv_k_k = near((DEPTH, RWKV_WIDTH), 0.85, 0.02)
    rwkv_k_a = near((DEPTH, RWKV_WIDTH), 1.0, 0.02)
    rwkv_r_k = near((DEPTH, RWKV_HEADS, RWKV_HEAD), -0.04, 0.02)
    rwkv_ln_w = near((DEPTH, RWKV_WIDTH), 1.0, 0.02)
    rwkv_ln_b = nrm((DEPTH, RWKV_WIDTH), 0.01)
    proj_a = nrm((DEPTH, S5_WIDTH, D_MODEL), S5_WIDTH ** -0.5)
    proj_b = nrm((DEPTH, SSD_WIDTH, D_MODEL), SSD_WIDTH ** -0.5)
    proj_c = nrm((DEPTH, RWKV_WIDTH, D_MODEL), RWKV_WIDTH ** -0.5)
    w_out = nrm((DEPTH, D_MODEL, D_MODEL), D_MODEL ** -0.5)
    mlp_norm_w = near((DEPTH, D_MODEL), 1.0, 0.02)
    mlp_w1 = nrm((DEPTH, D_MODEL, D_FF), D_MODEL ** -0.5)
    mlp_w2 = nrm((DEPTH, D_FF, D_MODEL), D_FF ** -0.5)
    return {'x': x, 'meta_tokens': meta_tokens, 'final_norm_w': final_norm_w,
            'mix_norm_w': mix_norm_w, 'w_in': w_in,
            's5_lambda_re': s5_lambda_re, 's5_lambda_im': s5_lambda_im, 's5_log_step': s5_log_step,
            's5_b_re': s5_b_re, 's5_b_im': s5_b_im, 's5_c_re': s5_c_re, 's5_c_im': s5_c_im,
            's5_d': s5_d, 's5_glu_w': s5_glu_w, 's5_glu_b': s5_glu_b,
            'ssd_conv_w': ssd_conv_w, 'ssd_conv_b': ssd_conv_b, 'ssd_a_log': ssd_a_log,
            'ssd_dt_bias': ssd_dt_bias, 'ssd_d': ssd_d, 'ssd_norm_w': ssd_norm_w,
            'rwkv_mu_rkv': rwkv_mu_rkv, 'rwkv_mu_wag': rwkv_mu_wag,
            'rwkv_w0': rwkv_w0, 'rwkv_w1': rwkv_w1, 'rwkv_w2': rwkv_w2,
            'rwkv_a0': rwkv_a0, 'rwkv_a1': rwkv_a1, 'rwkv_a2': rwkv_a2,
            'rwkv_g1': rwkv_g1, 'rwkv_g2': rwkv_g2, 'rwkv_k_k': rwkv_k_k, 'rwkv_k_a': rwkv_k_a,
            'rwkv_r_k': rwkv_r_k, 'rwkv_ln_w': rwkv_ln_w, 'rwkv_ln_b': rwkv_ln_b,
            'proj_a': proj_a, 'proj_b': proj_b, 'proj_c': proj_c, 'w_out': w_out,
            'mlp_norm_w': mlp_norm_w, 'mlp_w1': mlp_w1, 'mlp_w2': mlp_w2}


def reference(x, meta_tokens, final_norm_w, mix_norm_w, w_in,
              s5_lambda_re, s5_lambda_im, s5_log_step, s5_b_re, s5_b_im, s5_c_re, s5_c_im,
              s5_d, s5_glu_w, s5_glu_b,
              ssd_conv_w, ssd_conv_b, ssd_a_log, ssd_dt_bias, ssd_d, ssd_norm_w,
              rwkv_mu_rkv, rwkv_mu_wag, rwkv_w0, rwkv_w1, rwkv_w2, rwkv_a0, rwkv_a1, rwkv_a2,
              rwkv_g1, rwkv_g2, rwkv_k_k, rwkv_k_a, rwkv_r_k, rwkv_ln_w, rwkv_ln_b,
              proj_a, proj_b, proj_c, w_out, mlp_norm_w, mlp_w1, mlp_w2):
    bsz = x.shape[0]
    meta = jnp.broadcast_to(meta_tokens[None].astype(x.dtype), (bsz, N_META, D_MODEL))
    h = jnp.concatenate([meta, x], axis=1)
    length = h.shape[1]
    offsets = [int(o) for o in np.cumsum(IN_SPLITS)[:-1]]
    for i in range(DEPTH):
        hn = rms_norm(h, mix_norm_w[i])
        proj = hn @ w_in[i]
        u_a, z_b, xbc_b, dt_b, rkvx_c, gates = jnp.split(proj, offsets, axis=-1)
        r_c, k_c, v_c, x_c = jnp.split(rkvx_c, 4, axis=-1)
        y_a = s5_mixer(u_a, s5_lambda_re[i], s5_lambda_im[i], s5_log_step[i], s5_b_re[i], s5_b_im[i],
                       s5_c_re[i], s5_c_im[i], s5_d[i], s5_glu_w[i], s5_glu_b[i])
        y_b = ssd_mixer(z_b, xbc_b, dt_b, ssd_conv_w[i], ssd_conv_b[i], ssd_a_log[i], ssd_dt_bias[i],
                        ssd_d[i], ssd_norm_w[i])
        y_c = rwkv7_mixer(r_c, k_c, v_c, x_c, rwkv_mu_rkv[i], rwkv_mu_wag[i], rwkv_w0[i], rwkv_w1[i],
                          rwkv_w2[i], rwkv_a0[i], rwkv_a1[i], rwkv_a2[i], rwkv_g1[i], rwkv_g2[i],
                          rwkv_k_k[i], rwkv_k_a[i], rwkv_r_k[i], rwkv_ln_w[i], rwkv_ln_b[i])
        gt = jax.nn.sigmoid(gates.reshape(bsz, length, N_BRANCH, D_MODEL))
        merged = (gt[:, :, 0] * (y_a @ proj_a[i]) + gt[:, :, 1] * (y_b @ proj_b[i])
                  + gt[:, :, 2] * (y_c @ proj_c[i]))
        h = h + merged @ w_out[i]
        hn = rms_norm(h, mlp_norm_w[i])
        h = h + jnp.square(jax.nn.relu(hn @ mlp_w1[i])) @ mlp_w2[i]
    return rms_norm(h, final_norm_w)[:, N_META:]
```

```python
import numpy as np
from contextlib import ExitStack
import concourse.bass as bass
import concourse.mybir as mybir
from concourse.bass_utils import run_bass_kernel_spmd

F32 = mybir.dt.float32
BF16 = mybir.dt.bfloat16
AF = mybir.ActivationFunctionType
ALU = mybir.AluOpType

D = 1024
SEQ = 4096
NMETA = 16
L = SEQ + NMETA
DEPTH = 2
DFF = 4096
NIN = 5896
EPS = 1e-6
NDS = 48

OFF_U, OFF_Z, OFF_XBC, OFF_DT, OFF_RKVX, OFF_G = 0, 256, 768, 1792, 1800, 2824

BLOCKS = [(i * 512, 512) for i in range(8)] + [(4096, 16)]


class Dep:
    __slots__ = ("w", "r")

    def __init__(self):
        self.w = None
        self.r = []


class FW:
    def __init__(self, nc, es):
        self.nc = nc
        self.engs = dict(pe=nc.tensor, act=nc.scalar, dve=nc.vector, pool=nc.gpsimd, sp=nc.sync)
        self.sem = {k: es.enter_context(nc.semaphore("s_" + k)) for k in self.engs}
        self.cnt = {k: 0 for k in self.engs}
        self.seen = {k: {} for k in self.engs}
        self.dsem = [es.enter_context(nc.semaphore("d%d" % i)) for i in range(NDS)]
        self.dval = [0] * NDS
        self.dnext = 0
        self.dnext2 = 0
        self.nins = 0

    def _wait(self, eng, ev):
        key, val = ev
        if self.seen[eng].get(key, 0) >= val:
            return
        self.seen[eng][key] = val
        sem = self.sem[key[1]] if key[0] == "e" else self.dsem[key[1]]
        self.engs[eng].wait_ge(sem, val)

    def _deps(self, eng, reads, writes):
        me = ("e", eng)
        for d in reads:
            if d.w is not None:
                self._wait(eng, d.w)
        for d in writes:
            if d.w is not None and d.w[0] != me:
                self._wait(eng, d.w)
            for r in d.r:
                if r[0] != me:
                    self._wait(eng, r)

    def _post(self, ev, reads, writes):
        for d in writes:
            d.w = ev
            d.r = []
        for d in reads:
            d.r = [r for r in d.r if r[0] != ev[0]] + [ev]

    def op(self, eng, fn, reads=(), writes=()):
        self._deps(eng, reads, writes)
        ins = fn(self.engs[eng])
        self.cnt[eng] += 1
        self.nins += 1
        ins.then_inc(self.sem[eng], 1)
        self._post((("e", eng), self.cnt[eng]), reads, writes)

    def dma(self, q, out, in_, reads=(), writes=(), slow=False):
        self._deps(q, reads, writes)
        half = NDS // 2
        if q == "pool":
            i = half + self.dnext2
            self.dnext2 = (self.dnext2 + 1) % half
        else:
            i = self.dnext
            self.dnext = (self.dnext + 1) % half
        if self.dval[i] > 0:
            self._wait(q, (("d", i), self.dval[i]))
        self.dval[i] += 16
        self.nins += 1
        if slow:
            self.engs[q].dma_start(out=out, in_=in_, allow_slow_non_contiguous=True).then_inc(self.dsem[i], 16)
        else:
            self.engs[q].dma_start(out=out, in_=in_).then_inc(self.dsem[i], 16)
        self._post((("d", i), self.dval[i]), reads, writes)

    def barrier(self):
        for e in self.engs:
            for e2 in self.engs:
                if e2 != e and self.cnt[e2] > 0:
                    self._wait(e, (("e", e2), self.cnt[e2]))
            for i in range(NDS):
                if self.dval[i] > 0:
                    self._wait(e, (("d", i), self.dval[i]))


def col_tiles(lo, hi):
    out = []
    c = lo
    while c < hi:
        m = min(128, hi - c)
        out.append((c, m))
        c += m
    return out


IN_TILES = (col_tiles(OFF_U, OFF_Z) + col_tiles(OFF_Z, OFF_XBC) + col_tiles(OFF_XBC, OFF_DT)
            + col_tiles(OFF_DT, OFF_RKVX) + col_tiles(OFF_RKVX, OFF_G) + col_tiles(OFF_G, NIN))


class Builder:
    def __init__(self, cfg):
        self.cfg = cfg
        nc = self.nc = bass.Bass("TRN2", target_bir_lowering=False)
        self.I = {}
        self.es = ExitStack()

    def inp(self, name, shape):
        t = self.nc.dram_tensor(name, list(shape), F32, kind="ExternalInput").ap()
        self.I[name] = t
        return t

    def scratch(self, name, shape, dt=F32):
        kind = "ExternalOutput" if name in self.cfg.get("dump", ()) else "Internal"
        return self.nc.dram_tensor(name, list(shape), dt, kind=kind).ap()

    def sb(self, es, name, shape, dt=F32):
        self.uid = getattr(self, "uid", 0) + 1
        return es.enter_context(self.nc.sbuf_tensor("%s_%d" % (name, self.uid), list(shape), dt))

    def build(self):
        nc = self.nc
        cfg = self.cfg
        with self.es as es:
            fw = self.fw = FW(nc, es)
            I = self.I
            x = self.inp("x", (SEQ, D))
            for name, shape in WEIGHT_SHAPES:
                self.inp(name, shape)
            self.inp("c_ident", (128, 128))
            self.inp("c_iota", (128, 512))
            self.inp("c_triu", (128, 128))
            self.inp("c_padm", (128, 8))
            self.inp("c_blk", (128, 128))
            self.inp("c_trilT_s", (128, 128))
            self.inp("c_mneg", (128, 128))
            self.inp("c_tril_s", (128, 128))
            self.inp("c_tril_i", (128, 128))
            out = nc.dram_tensor("out", [SEQ, D], F32, kind="ExternalOutput").ap()
            self.hT = self.scratch("hT", (D, L))
            self.projT = self.scratch("projT", (NIN, L))
            self.yaT = self.scratch("yaT", (256, L))
            self.ybT = self.scratch("ybT", (512, L))
            self.ycT = self.scratch("ycT", (256, L))
            self.dep_hT = Dep()
            self.dep_proj = Dep()
            self.dep_ya, self.dep_yb, self.dep_yc = Dep(), Dep(), Dep()

            self.ident = self.sb(es, "ident", [128, 128])
            self.ones_bf = self.sb(es, "ones_bf", [128, 128], BF16)
            self.d_const = Dep()
            fw.dma("sp", self.ident[:], I["c_ident"][:, :], writes=[self.d_const])
            fw.op("dve", lambda e: e.memset(self.ones_bf[:], 1.0), writes=[self.d_const])
            self.one_t = self.sb(es, "one_t", [128, 1])
            fw.op("dve", lambda e: e.memset(self.one_t[:], 1.0), writes=[self.d_const])
            self.ps = [es.enter_context(nc.psum_tensor("ps%d" % i, [128, 512], F32)) for i in range(8)]
            self.dps = [Dep() for _ in range(8)]
            self.psn = 0

            self.phase0(x)
            nlayers = cfg.get("layers", DEPTH)
            for li in range(nlayers):
                self.phase1(li)
                if cfg.get("fake_mix", False):
                    self.fake_mix()
                else:
                    self.mixers(li)
                if cfg.get("stop_after_mix", False):
                    break
                self.phase3a(li)
                self.phase3b(li)
            if not cfg.get("stop_after_mix", False):
                self.phase_final(out)
            fw.barrier()
        return nc

    def next_ps(self):
        i = self.psn
        self.psn = (i + 1) % 8
        return self.ps[i], self.dps[i]

    def phase0(self, x):
        fw = self.fw
        I = self.I
        with ExitStack() as es:
            xin = [self.sb(es, "p0_x%d" % i, [128, D]) for i in range(2)]
            dxin = [Dep(), Dep()]
            ho = [self.sb(es, "p0_h%d" % i, [128, 8, 128]) for i in range(2)]
            dho = [Dep(), Dep()]
            ntile = (L + 127) // 128
            for ti in range(ntile):
                t0 = ti * 128
                w = min(128, L - t0)
                xi, dx = xin[ti % 2], dxin[ti % 2]
                if ti == 0:
                    fw.dma("sp", xi[0:NMETA, :], I["meta_tokens"][:, :], writes=[dx])
                    fw.dma("sp", xi[NMETA:128, :], x[0:128 - NMETA, :], writes=[dx])
                else:
                    fw.dma("sp", xi[0:w, :], x[t0 - NMETA:t0 - NMETA + w, :], writes=[dx])
                h, dh = ho[ti % 2], dho[ti % 2]
                for kt in range(8):
                    ps, dp = self.next_ps()
                    fw.op("pe", lambda e: e.transpose(out=ps[:, 0:w], in_=xi[0:w, kt * 128:(kt + 1) * 128],
                                                      identity=self.ident[0:w, 0:w]),
                          reads=[dx, self.d_const], writes=[dp])
                    eng = "act" if kt % 2 == 0 else "dve"
                    if eng == "act":
                        fw.op("act", lambda e: e.copy(out=h[:, kt, 0:w], in_=ps[:, 0:w]), reads=[dp], writes=[dh])
                    else:
                        fw.op("dve", lambda e: e.tensor_copy(out=h[:, kt, 0:w], in_=ps[:, 0:w]), reads=[dp], writes=[dh])
                fw.dma("pool", self.hT.rearrange("(kt p) t -> p kt t", p=128)[:, :, t0:t0 + w], h[:, :, 0:w],
                       reads=[dh], writes=[self.dep_hT])
        fw.barrier()

    def load_weight_bf(self, es, name, w_ap, K, N, scale_ap=None, chunk=512):
        fw = self.fw
        kt_n = K // 128
        wbf = self.sb(es, name, [128, kt_n, N], BF16)
        dw = Dep()
        with ExitStack() as es2:
            stg = [self.sb(es2, name + "_stg%d" % i, [128, chunk]) for i in range(3)]
            dstg = [Dep() for _ in range(3)]
            sc = None
            if scale_ap is not None:
                sc = self.sb(es2, name + "_sc", [128, kt_n])
                dsc = Dep()
                fw.dma("sp", sc[:], scale_ap.rearrange("(kt p) -> p kt", p=128), writes=[dsc], slow=True)
            n = 0
            for kt in range(kt_n):
                for c0 in range(0, N, chunk):
                    cw = min(chunk, N - c0)
                    s, ds = stg[n % 3], dstg[n % 3]
                    fw.dma("sp", s[:, 0:cw], w_ap[kt * 128:(kt + 1) * 128, c0:c0 + cw], writes=[ds])
                    eng = "dve" if n % 2 == 0 else "pool"
                    if sc is not None:
                        fw.op(eng, lambda e: e.tensor_scalar(out=wbf[:, kt, c0:c0 + cw], in0=s[:, 0:cw],
                                                             scalar1=sc[:, kt:kt + 1], scalar2=None, op0=ALU.mult),
                              reads=[ds, dsc], writes=[dw])
                    else:
                        fw.op(eng, lambda e: e.tensor_copy(out=wbf[:, kt, c0:c0 + cw], in_=s[:, 0:cw]),
                              reads=[ds], writes=[dw])
                    n += 1
            fw.barrier()
        return wbf, dw

    def rmsnorm_block(self, h, dh, hn, dhn, sq, dsq, rstd, drstd, W):
        fw = self.fw
        for kt in range(8):
            fw.op("act", lambda e: e.activation(out=sq[:, kt, 0:W], in_=h[:, kt, 0:W], func=AF.Square),
                  reads=[dh], writes=[dsq])
        ps, dp = self.next_ps()
        for kt in range(8):
            fw.op("pe", lambda e: e.matmul(ps[:, 0:W], lhsT=self.ones_bf[:, :], rhs=sq[:, kt, 0:W],
                                           start=(kt == 0), stop=(kt == 7)),
                  reads=[dsq, self.d_const], writes=[dp])
        fw.op("act", lambda e: e.activation(out=rstd[:, 0:W], in_=ps[:, 0:W], func=AF.Sqrt, bias=self.eps_t[:, 0:1],
                                            scale=1.0 / D),
              reads=[dp, self.d_const], writes=[drstd])
        fw.op("dve", lambda e: e.reciprocal(out=rstd[:, 0:W], in_=rstd[:, 0:W]), reads=[drstd], writes=[drstd])
        for kt in range(8):
            eng = "dve" if kt % 2 == 0 else "pool"
            fw.op(eng, lambda e: e.tensor_tensor(out=hn[:, kt, 0:W], in0=h[:, kt, 0:W], in1=rstd[:, 0:W], op=ALU.mult),
                  reads=[dh, drstd], writes=[dhn])

    def ensure_eps(self, es):
        self.eps_t = self.sb(es, "eps_t", [128, 1])
        self.fw.op("dve", lambda e: e.memset(self.eps_t[:], EPS), writes=[self.d_const])

    def phase1(self, li):
        fw = self.fw
        I = self.I
        with ExitStack() as es:
            self.ensure_eps(es)
            wbf, dw = self.load_weight_bf(es, "p1_w", I["w_in"][li], D, NIN, scale_ap=I["mix_norm_w"][li])
            h = [self.sb(es, "p1_h%d" % i, [128, 8, 512]) for i in range(2)]
            dh = [Dep(), Dep()]
            sq = self.sb(es, "p1_sq", [128, 8, 512], BF16)
            dsq = Dep()
            rstd = self.sb(es, "p1_rstd", [128, 512])
            drstd = Dep()
            hn = [self.sb(es, "p1_hn%d" % i, [128, 8, 512], BF16) for i in range(2)]
            dhn = [Dep(), Dep()]
            stg = [self.sb(es, "p1_o%d" % i, [128, 512]) for i in range(4)]
            dstg = [Dep() for _ in range(4)]
            hTv = self.hT.rearrange("(kt p) t -> p kt t", p=128)
            ns = 0
            for bi, (t0, W) in enumerate(BLOCKS):
                hb, dhb = h[bi % 2], dh[bi % 2]
                fw.dma("sp", hb[:, :, 0:W], hTv[:, :, t0:t0 + W], reads=[self.dep_hT], writes=[dhb])
                hnb, dhnb = hn[bi % 2], dhn[bi % 2]
                self.rmsnorm_block(hb, dhb, hnb, dhnb, sq, dsq, rstd, drstd, W)
                for (c0, M) in IN_TILES:
                    ps, dp = self.next_ps()
                    for kt in range(8):
                        fw.op("pe", lambda e: e.matmul(ps[0:M, 0:W], lhsT=wbf[:, kt, c0:c0 + M], rhs=hnb[:, kt, 0:W],
                                                       start=(kt == 0), stop=(kt == 7)),
                              reads=[dhnb, dw], writes=[dp])
                    s, ds = stg[ns % 4], dstg[ns % 4]
                    if ns % 2 == 0:
                        fw.op("act", lambda e: e.copy(out=s[0:M, 0:W], in_=ps[0:M, 0:W]), reads=[dp], writes=[ds])
                    else:
                        fw.op("dve", lambda e: e.tensor_copy(out=s[0:M, 0:W], in_=ps[0:M, 0:W]), reads=[dp], writes=[ds])
                    fw.dma("pool", self.projT[c0:c0 + M, t0:t0 + W], s[0:M, 0:W], reads=[ds], writes=[self.dep_proj])
                    ns += 1
        fw.barrier()

    def fake_mix(self):
        fw = self.fw
        with ExitStack() as es:
            t = self.sb(es, "fm_t", [128, L])
            dt_ = Dep()
            for (dst, ddst, src0, n) in ((self.yaT, self.dep_ya, OFF_U, 2), (self.ybT, self.dep_yb, OFF_Z, 4),
                                         (self.ycT, self.dep_yc, OFF_RKVX, 2)):
                for j in range(n):
                    fw.dma("sp", t[:, :], self.projT[src0 + j * 128:src0 + (j + 1) * 128, :], reads=[self.dep_proj],
                           writes=[dt_])
                    fw.dma("sp", dst[j * 128:(j + 1) * 128, :], t[:, :], reads=[dt_], writes=[ddst])
        fw.barrier()

    def mixers(self, li):
        which = self.cfg.get("mix", ("s5", "ssd", "rwkv"))
        if "s5" in which:
            self.mix_s5(li)
        if "ssd" in which:
            self.mix_ssd(li)
        if "rwkv" in which:
            self.mix_rwkv(li)

    def phase3a(self, li):
        fw = self.fw
        I = self.I
        W = 512
        with ExitStack() as es:
            pa, dpa = self.load_weight_bf(es, "p3_pa", I["proj_a"][li], 256, D)
            pb, dpb = self.load_weight_bf(es, "p3_pb", I["proj_b"][li], 512, D)
            pc, dpc = self.load_weight_bf(es, "p3_pc", I["proj_c"][li], 256, D)
            wo, dwo = self.load_weight_bf(es, "p3_wo", I["w_out"][li], D, D)
            ystg = [self.sb(es, "p3_ys%d" % i, [128, 8, W]) for i in range(2)]
            dystg = [Dep(), Dep()]
            ybf = [self.sb(es, "p3_yb%d" % i, [128, 8, W], BF16) for i in range(2)]
            dybf = [Dep(), Dep()]
            g = [self.sb(es, "p3_g%d" % i, [128, 3, W]) for i in range(2)]
            dg = [Dep(), Dep()]
            mrg = self.sb(es, "p3_m", [128, 8, W], BF16)
            dmrg = Dep()
            tmp = [self.sb(es, "p3_t%d" % i, [128, W]) for i in range(2)]
            dtmp = [Dep(), Dep()]
            h = [self.sb(es, "p3_h%d" % i, [128, 8, W]) for i in range(2)]
            dh = [Dep(), Dep()]
            hTv = self.hT.rearrange("(kt p) t -> p kt t", p=128)
            gv = self.projT[OFF_G:NIN, :].rearrange("(b kt p) t -> p b kt t", p=128, b=3)
            ng = 0
            for bi, (t0, Wb) in enumerate(BLOCKS):
                ys, dys = ystg[bi % 2], dystg[bi % 2]
                yb, dyb = ybf[bi % 2], dybf[bi % 2]
                hb, dhb = h[bi % 2], dh[bi % 2]
                fw.dma("sp", ys[:, 0:2, 0:Wb], self.yaT.rearrange("(kt p) t -> p kt t", p=128)[:, :, t0:t0 + Wb],
                       reads=[self.dep_ya], writes=[dys])
                fw.dma("sp", ys[:, 2:6, 0:Wb], self.ybT.rearrange("(kt p) t -> p kt t", p=128)[:, :, t0:t0 + Wb],
                       reads=[self.dep_yb], writes=[dys])
                fw.dma("sp", ys[:, 6:8, 0:Wb], self.ycT.rearrange("(kt p) t -> p kt t", p=128)[:, :, t0:t0 + Wb],
                       reads=[self.dep_yc], writes=[dys])
                fw.dma("sp", hb[:, :, 0:Wb], hTv[:, :, t0:t0 + Wb], reads=[self.dep_hT], writes=[dhb])
                for kt in range(8):
                    eng = "dve" if kt % 2 == 0 else "pool"
                    fw.op(eng, lambda e: e.tensor_copy(out=yb[:, kt, 0:Wb], in_=ys[:, kt, 0:Wb]), reads=[dys], writes=[dyb])
                for dtile in range(8):
                    gg, dgg = g[ng % 2], dg[ng % 2]
                    ng += 1
                    fw.dma("sp", gg[:, :, 0:Wb], gv[:, :, dtile, t0:t0 + Wb], reads=[self.dep_proj], writes=[dgg])
                    fw.op("act", lambda e: e.activation(out=gg[:, :, 0:Wb], in_=gg[:, :, 0:Wb], func=AF.Sigmoid),
                          reads=[dgg], writes=[dgg])
                    tm, dtm = tmp[dtile % 2], dtmp[dtile % 2]
                    for br, (wt, dwt, k0, nk) in enumerate(((pa, dpa, 0, 2), (pb, dpb, 2, 4), (pc, dpc, 6, 2))):
                        ps, dp = self.next_ps()
                        for k in range(nk):
                            fw.op("pe", lambda e: e.matmul(ps[:, 0:Wb], lhsT=wt[:, k, dtile * 128:(dtile + 1) * 128],
                                                           rhs=yb[:, k0 + k, 0:Wb], start=(k == 0), stop=(k == nk - 1)),
                                  reads=[dyb, dwt], writes=[dp])
                        if br == 0:
                            fw.op("dve", lambda e: e.tensor_tensor(out=tm[:, 0:Wb], in0=ps[:, 0:Wb], in1=gg[:, 0, 0:Wb],
                                                                   op=ALU.mult), reads=[dp, dgg], writes=[dtm])
                        else:
                            fw.op("dve", lambda e: e.tensor_tensor(out=gg[:, br, 0:Wb], in0=ps[:, 0:Wb],
                                                                   in1=gg[:, br, 0:Wb], op=ALU.mult),
                                  reads=[dp, dgg], writes=[dgg])
                            if br == 1:
                                fw.op("dve", lambda e: e.tensor_tensor(out=tm[:, 0:Wb], in0=tm[:, 0:Wb],
                                                                       in1=gg[:, 1, 0:Wb], op=ALU.add),
                                      reads=[dtm, dgg], writes=[dtm])
                            else:
                                fw.op("dve", lambda e: e.tensor_tensor(out=mrg[:, dtile, 0:Wb], in0=tm[:, 0:Wb],
                                                                       in1=gg[:, 2, 0:Wb], op=ALU.add),
                                      reads=[dtm, dgg], writes=[dmrg])
                for dtile in range(8):
                    ps, dp = self.next_ps()
                    for k in range(8):
                        fw.op("pe", lambda e: e.matmul(ps[:, 0:Wb], lhsT=wo[:, k, dtile * 128:(dtile + 1) * 128],
                                                       rhs=mrg[:, k, 0:Wb], start=(k == 0), stop=(k == 7)),
                              reads=[dmrg, dwo], writes=[dp])
                    fw.op("dve", lambda e: e.tensor_tensor(out=hb[:, dtile, 0:Wb], in0=ps[:, 0:Wb], in1=hb[:, dtile, 0:Wb],
                                                           op=ALU.add), reads=[dp, dhb], writes=[dhb])
                fw.dma("pool", hTv[:, :, t0:t0 + Wb], hb[:, :, 0:Wb], reads=[dhb], writes=[self.dep_hT])
        fw.barrier()

    def phase3b(self, li):
        fw = self.fw
        I = self.I
        W = 256
        blocks = []
        for (t0, Wb) in BLOCKS:
            for s in range(0, Wb, W):
                blocks.append((t0 + s, min(W, Wb - s)))
        with ExitStack() as es:
            self.ensure_eps(es)
            w1, dw1 = self.load_weight_bf(es, "p4_w1", I["mlp_w1"][li], D, DFF, scale_ap=I["mlp_norm_w"][li])
            w2, dw2 = self.load_weight_bf(es, "p4_w2", I["mlp_w2"][li], DFF, D)
            h = [self.sb(es, "p4_h%d" % i, [128, 8, W]) for i in range(2)]
            dh = [Dep(), Dep()]
            sq = self.sb(es, "p4_sq", [128, 8, W], BF16)
            dsq = Dep()
            rstd = self.sb(es, "p4_rstd", [128, W])
            drstd = Dep()
            hn = self.sb(es, "p4_hn", [128, 8, W], BF16)
            dhn = Dep()
            act = self.sb(es, "p4_act", [128, 32, W], BF16)
            dact = Dep()
            rl = [self.sb(es, "p4_rl%d" % i, [128, W]) for i in range(2)]
            drl = [Dep(), Dep()]
            hTv = self.hT.rearrange("(kt p) t -> p kt t", p=128)
            for bi, (t0, Wb) in enumerate(blocks):
                hb, dhb = h[bi % 2], dh[bi % 2]
                fw.dma("sp", hb[:, :, 0:Wb], hTv[:, :, t0:t0 + Wb], reads=[self.dep_hT], writes=[dhb])
                self.rmsnorm_block(hb, dhb, hn, dhn, sq, dsq, rstd, drstd, Wb)
                for f in range(32):
                    ps, dp = self.next_ps()
                    for k in range(8):
                        fw.op("pe", lambda e: e.matmul(ps[:, 0:Wb], lhsT=w1[:, k, f * 128:(f + 1) * 128],
                                                       rhs=hn[:, k, 0:Wb], start=(k == 0), stop=(k == 7)),
                              reads=[dhn, dw1], writes=[dp])
                    r, dr = rl[f % 2], drl[f % 2]
                    fw.op("act", lambda e: e.activation(out=r[:, 0:Wb], in_=ps[:, 0:Wb], func=AF.Relu),
                          reads=[dp], writes=[dr])
                    eng = "dve" if f % 2 == 0 else "pool"
                    fw.op(eng, lambda e: e.tensor_tensor(out=act[:, f, 0:Wb], in0=r[:, 0:Wb], in1=r[:, 0:Wb], op=ALU.mult),
                          reads=[dr], writes=[dact])
                for dtile in range(8):
                    ps, dp = self.next_ps()
                    for f in range(32):
                        fw.op("pe", lambda e: e.matmul(ps[:, 0:Wb], lhsT=w2[:, f, dtile * 128:(dtile + 1) * 128],
                                                       rhs=act[:, f, 0:Wb], start=(f == 0), stop=(f == 31)),
                              reads=[dact, dw2], writes=[dp])
                    fw.op("dve", lambda e: e.tensor_tensor(out=hb[:, dtile, 0:Wb], in0=ps[:, 0:Wb], in1=hb[:, dtile, 0:Wb],
                                                           op=ALU.add), reads=[dp, dhb], writes=[dhb])
                fw.dma("pool", hTv[:, :, t0:t0 + Wb], hb[:, :, 0:Wb], reads=[dhb], writes=[self.dep_hT])
        fw.barrier()

    def phase_final(self, out):
        fw = self.fw
        I = self.I
        with ExitStack() as es:
            self.ensure_eps(es)
            fnw = self.sb(es, "pf_w", [128, 8])
            dfnw = Dep()
            fw.dma("sp", fnw[:], I["final_norm_w"].rearrange("(kt p) -> p kt", p=128), writes=[dfnw], slow=True)
            h = [self.sb(es, "pf_h%d" % i, [128, 8, 512]) for i in range(2)]
            dh = [Dep(), Dep()]
            sq = self.sb(es, "pf_sq", [128, 8, 512], BF16)
            dsq = Dep()
            rstd = self.sb(es, "pf_rstd", [128, 512])
            drstd = Dep()
            o = [self.sb(es, "pf_o%d" % i, [128, D]) for i in range(2)]
            do = [Dep(), Dep()]
            dout = Dep()
            hTv = self.hT.rearrange("(kt p) t -> p kt t", p=128)
            no = 0
            for bi in range(8):
                t0 = NMETA + bi * 512
                W = 512
                hb, dhb = h[bi % 2], dh[bi % 2]
                fw.dma("sp", hb[:, :, 0:W], hTv[:, :, t0:t0 + W], reads=[self.dep_hT], writes=[dhb])
                for kt in range(8):
                    fw.op("act", lambda e: e.activation(out=sq[:, kt, 0:W], in_=hb[:, kt, 0:W], func=AF.Square),
                          reads=[dhb], writes=[dsq])
                ps, dp = self.next_ps()
                for kt in range(8):
                    fw.op("pe", lambda e: e.matmul(ps[:, 0:W], lhsT=self.ones_bf[:, :], rhs=sq[:, kt, 0:W],
                                                   start=(kt == 0), stop=(kt == 7)),
                          reads=[dsq, self.d_const], writes=[dp])
                fw.op("act", lambda e: e.activation(out=rstd[:, 0:W], in_=ps[:, 0:W], func=AF.Sqrt,
                                                    bias=self.eps_t[:, 0:1], scale=1.0 / D),
                      reads=[dp, self.d_const], writes=[drstd])
                fw.op("dve", lambda e: e.reciprocal(out=rstd[:, 0:W], in_=rstd[:, 0:W]), reads=[drstd], writes=[drstd])
                for kt in range(8):
                    fw.op("dve", lambda e: e.scalar_tensor_tensor(out=hb[:, kt, 0:W], in0=hb[:, kt, 0:W],
                                                                  scalar=fnw[:, kt:kt + 1], in1=rstd[:, 0:W],
                                                                  op0=ALU.mult, op1=ALU.mult),
                          reads=[dhb, drstd, dfnw], writes=[dhb])
                for tt in range(4):
                    ob, dob = o[no % 2], do[no % 2]
                    no += 1
                    for kt in range(8):
                        ps, dp = self.next_ps()
                        fw.op("pe", lambda e: e.transpose(out=ps[:, 0:128], in_=hb[:, kt, tt * 128:(tt + 1) * 128],
                                                          identity=self.ident[:, :]),
                              reads=[dhb, self.d_const], writes=[dp])
                        if kt % 2 == 0:
                            fw.op("act", lambda e: e.copy(out=ob[:, kt * 128:(kt + 1) * 128], in_=ps[:, 0:128]),
                                  reads=[dp], writes=[dob])
                        else:
                            fw.op("dve", lambda e: e.tensor_copy(out=ob[:, kt * 128:(kt + 1) * 128], in_=ps[:, 0:128]),
                                  reads=[dp], writes=[dob])
                    r0 = bi * 512 + tt * 128
                    fw.dma("pool", out[r0:r0 + 128, :], ob[:, :], reads=[dob], writes=[dout])
        fw.barrier()


WEIGHT_SHAPES = [
    ("meta_tokens", (16, 1024)), ("final_norm_w", (1024,)), ("mix_norm_w", (2, 1024)), ("w_in", (2, 1024, 5896)),
    ("s5_lambda_re", (2, 2, 16, 64)), ("s5_lambda_im", (2, 2, 16, 64)), ("s5_log_step", (2, 2, 16)),
    ("s5_b_re", (2, 16, 64, 16)), ("s5_b_im", (2, 16, 64, 16)), ("s5_c_re", (2, 16, 16, 64)),
    ("s5_c_im", (2, 16, 16, 64)), ("s5_d", (2, 256)), ("s5_glu_w", (2, 256, 512)), ("s5_glu_b", (2, 512)),
    ("ssd_conv_w", (2, 5, 1024)), ("ssd_conv_b", (2, 1024)), ("ssd_a_log", (2, 2, 8)), ("ssd_dt_bias", (2, 2, 8)),
    ("ssd_d", (2, 8)), ("ssd_norm_w", (2, 512)), ("rwkv_mu_rkv", (2, 3, 256)), ("rwkv_mu_wag", (2, 3, 256)),
    ("rwkv_w0", (2, 2, 256)), ("rwkv_w1", (2, 2, 256, 64)), ("rwkv_w2", (2, 2, 64, 256)), ("rwkv_a0", (2, 2, 256)),
    ("rwkv_a1", (2, 2, 256, 64)), ("rwkv_a2", (2, 2, 64, 256)), ("rwkv_g1", (2, 256, 128)), ("rwkv_g2", (2, 128, 256)),
    ("rwkv_k_k", (2, 256)), ("rwkv_k_a", (2, 256)), ("rwkv_r_k", (2, 4, 64)), ("rwkv_ln_w", (2, 256)),
    ("rwkv_ln_b", (2, 256)), ("proj_a", (2, 256, 1024)), ("proj_b", (2, 512, 1024)), ("proj_c", (2, 256, 1024)),
    ("w_out", (2, 1024, 1024)), ("mlp_norm_w", (2, 1024)), ("mlp_w1", (2, 1024, 4096)), ("mlp_w2", (2, 4096, 1024)),
]


def host_consts():
    return {"c_ident": np.eye(128, dtype=np.float32),
            "c_iota": np.ascontiguousarray(np.broadcast_to(np.arange(512, dtype=np.float32), (128, 512))),
            "c_triu": np.triu(np.ones((128, 128), np.float32)),
            "c_blk": np.kron(np.eye(2, dtype=np.float32), np.ones((64, 64), np.float32)),
            "c_trilT_s": np.tril(np.ones((128, 128), np.float32), -1),
            "c_padm": np.ascontiguousarray(np.broadcast_to((np.arange(128) < 16).astype(np.float32)[:, None], (128, 8))),
            "c_mneg": np.where(np.triu(np.ones((128, 128), bool)), 0.0, -30000.0).astype(np.float32),
            "c_tril_s": np.triu(np.ones((128, 128), np.float32), 1),
            "c_tril_i": np.triu(np.ones((128, 128), np.float32), 0)}


def run(inputs, cfg, ncores=8):
    b = Builder(cfg)
    nc = b.build()
    consts = host_consts()
    in_maps = []
    for c in range(ncores):
        m = {"x": np.ascontiguousarray(inputs["x"][c], dtype=np.float32)}
        for name, _ in WEIGHT_SHAPES:
            m[name] = np.ascontiguousarray(inputs[name], dtype=np.float32)
        m.update(consts)
        in_maps.append(m)
    res = run_bass_kernel_spmd(nc, in_maps, core_ids=list(range(ncores)))
    return res, b


def kernel(**inputs):
    res, _ = run(inputs, {})
    return np.stack([np.asarray(res.results[c]["out"]) for c in range(8)], axis=0).astype(np.float32)


PI = float(np.pi)
S5W = 256


def _mix_s5(self, li):
    fw = self.fw
    I = self.I
    nc = self.nc
    with ExitStack() as es:
        lr = self.sb(es, "s5_lr", [128, 16])
        lim = self.sb(es, "s5_li", [128, 16])
        dpar = Dep()
        fw.dma("sp", lr[:], I["s5_lambda_re"][li].rearrange("d (q gp) n -> (gp n) (d q)", gp=2), writes=[dpar], slow=True)
        fw.dma("sp", lim[:], I["s5_lambda_im"][li].rearrange("d (q gp) n -> (gp n) (d q)", gp=2), writes=[dpar], slow=True)
        stepb = self.sb(es, "s5_stepb", [128, 2, 8, 2])
        fw.dma("sp", stepb[:], I["s5_log_step"][li].rearrange("d (q gp) -> d q gp", gp=2).partition_broadcast(128),
               writes=[dpar], slow=True)
        step = self.sb(es, "s5_step", [128, 16])
        fw.op("act", lambda e: e.activation(out=step[0:64, :].rearrange("p (d q) -> p d q", d=2), in_=stepb[0:64, :, :, 0],
                                            func=AF.Exp), reads=[dpar], writes=[dpar])
        fw.op("act", lambda e: e.activation(out=step[64:128, :].rearrange("p (d q) -> p d q", d=2),
                                            in_=stepb[64:128, :, :, 1], func=AF.Exp), reads=[dpar], writes=[dpar])
        th = self.sb(es, "s5_th", [128, 16])
        rho = self.sb(es, "s5_rho", [128, 16])
        fw.op("dve", lambda e: e.tensor_tensor(out=th[:], in0=lim[:], in1=step[:], op=ALU.mult), reads=[dpar], writes=[dpar])
        fw.op("dve", lambda e: e.tensor_tensor(out=rho[:], in0=lr[:], in1=step[:], op=ALU.mult), reads=[dpar], writes=[dpar])
        fw.op("act", lambda e: e.activation(out=rho[:], in_=rho[:], func=AF.Exp), reads=[dpar], writes=[dpar])

        NT = S5W + 1
        tc = self.sb(es, "s5_tc", [128, 16, NT])
        ts = self.sb(es, "s5_ts", [128, 16, NT])
        dtab = Dep()
        with ExitStack() as es2:
            iot = self.sb(es2, "s5_iota", [128, NT])
            fw.dma("sp", iot[:], I["c_iota"][:, 0:NT], writes=[dtab])
            ph = self.sb(es2, "s5_ph", [128, 16, NT])
            ki = self.sb(es2, "s5_ki", [128, 16, NT], mybir.dt.int32)
            kf = self.sb(es2, "s5_kf", [128, 16, NT])
            for j in range(16):
                fw.op("dve", lambda e: e.tensor_scalar(out=ph[:, j, :], in0=iot[:], scalar1=th[:, j:j + 1], scalar2=None,
                                                       op0=ALU.mult), reads=[dpar, dtab], writes=[dtab])
            fw.op("dve", lambda e: e.tensor_scalar(out=ki[:], in0=ph[:], scalar1=1.0 / (2 * PI), scalar2=None, op0=ALU.mult),
                  reads=[dtab], writes=[dtab])
            fw.op("dve", lambda e: e.tensor_copy(out=kf[:], in_=ki[:]), reads=[dtab], writes=[dtab])
            fw.op("dve", lambda e: e.scalar_tensor_tensor(out=ph[:], in0=kf[:], scalar=-2 * PI, in1=ph[:], op0=ALU.mult,
                                                          op1=ALU.add), reads=[dtab], writes=[dtab])

            def wrap(t):
                fw.op("dve", lambda e: e.tensor_scalar(out=kf[:], in0=t[:], scalar1=PI, scalar2=-2 * PI, op0=ALU.is_gt,
                                                       op1=ALU.mult), reads=[dtab], writes=[dtab])
                fw.op("dve", lambda e: e.tensor_tensor(out=t[:], in0=t[:], in1=kf[:], op=ALU.add), reads=[dtab], writes=[dtab])
                fw.op("dve", lambda e: e.tensor_scalar(out=kf[:], in0=t[:], scalar1=-PI, scalar2=2 * PI, op0=ALU.is_lt,
                                                       op1=ALU.mult), reads=[dtab], writes=[dtab])
                fw.op("dve", lambda e: e.tensor_tensor(out=t[:], in0=t[:], in1=kf[:], op=ALU.add), reads=[dtab], writes=[dtab])

            wrap(ph)
            fw.op("act", lambda e: e.activation(out=ts[:], in_=ph[:], func=AF.Sin), reads=[dtab], writes=[dtab])
            fw.op("dve", lambda e: e.tensor_scalar(out=ph[:], in0=ph[:], scalar1=PI / 2, scalar2=None, op0=ALU.add),
                  reads=[dtab], writes=[dtab])
            wrap(ph)
            fw.op("act", lambda e: e.activation(out=tc[:], in_=ph[:], func=AF.Sin), reads=[dtab], writes=[dtab])
            fw.barrier()
        nsW = self.sb(es, "s5_nsW", [128, 16])
        fw.op("dve", lambda e: e.tensor_scalar(out=nsW[:], in0=ts[:, :, S5W], scalar1=-1.0, scalar2=None, op0=ALU.mult),
              reads=[dtab], writes=[dpar])
        nsB = self.sb(es, "s5_nsB", [128, 16])
        abr = self.sb(es, "s5_abr", [128, 16])
        abi = self.sb(es, "s5_abi", [128, 16])
        fw.op("dve", lambda e: e.tensor_tensor(out=abr[:], in0=rho[:], in1=tc[:, :, 1], op=ALU.mult), reads=[dpar, dtab], writes=[dpar])
        fw.op("dve", lambda e: e.tensor_tensor(out=abi[:], in0=rho[:], in1=ts[:, :, 1], op=ALU.mult), reads=[dpar, dtab], writes=[dpar])
        den = self.sb(es, "s5_den", [128, 16])
        t1 = self.sb(es, "s5_t1", [128, 16])
        t2 = self.sb(es, "s5_t2", [128, 16])
        cor = self.sb(es, "s5_cor", [128, 16])
        coi = self.sb(es, "s5_coi", [128, 16])
        V = lambda fn: fw.op("dve", fn, reads=[dpar], writes=[dpar])
        V(lambda e: e.tensor_tensor(out=den[:], in0=lr[:], in1=lr[:], op=ALU.mult))
        V(lambda e: e.tensor_tensor(out=t1[:], in0=lim[:], in1=lim[:], op=ALU.mult))
        V(lambda e: e.tensor_tensor(out=den[:], in0=den[:], in1=t1[:], op=ALU.add))
        V(lambda e: e.reciprocal(out=den[:], in_=den[:]))
        V(lambda e: e.tensor_scalar(out=abr[:], in0=abr[:], scalar1=-1.0, scalar2=None, op0=ALU.add))
        V(lambda e: e.tensor_tensor(out=t1[:], in0=abr[:], in1=lr[:], op=ALU.mult))
        V(lambda e: e.tensor_tensor(out=t2[:], in0=abi[:], in1=lim[:], op=ALU.mult))
        V(lambda e: e.tensor_tensor(out=t1[:], in0=t1[:], in1=t2[:], op=ALU.add))
        V(lambda e: e.tensor_tensor(out=cor[:], in0=t1[:], in1=den[:], op=ALU.mult))
        V(lambda e: e.tensor_tensor(out=t1[:], in0=abi[:], in1=lr[:], op=ALU.mult))
        V(lambda e: e.tensor_tensor(out=t2[:], in0=abr[:], in1=lim[:], op=ALU.mult))
        V(lambda e: e.tensor_tensor(out=t1[:], in0=t1[:], in1=t2[:], op=ALU.subtract))
        V(lambda e: e.tensor_tensor(out=coi[:], in0=t1[:], in1=den[:], op=ALU.mult))

        LB = self.sb(es, "s5_LB", [128, 2, 8, 2, 128], BF16)
        LC = self.sb(es, "s5_LC", [128, 8, 2, 128], BF16)
        dLB = Dep()
        fw.op("dve", lambda e: e.memset(LC[:], 0.0), writes=[dLB])
        with ExitStack() as es2:
            Xr = self.sb(es2, "s5_Xr", [128, 8, 128])
            Xi = self.sb(es2, "s5_Xi", [128, 8, 128])
            dX = Dep()
            fw.op("dve", lambda e: e.memset(Xr[:], 0.0), writes=[dX])
            fw.op("dve", lambda e: e.memset(Xi[:], 0.0), writes=[dX])
            for (X, nm) in ((Xr, "s5_b_re"), (Xi, "s5_b_im")):
                for q in range(8):
                    r = q % 4
                    fw.dma("sp", X[0:64, q, 32 * r:32 * r + 16], I[nm][li, 2 * q], writes=[dX])
                    fw.dma("sp", X[64:128, q, 32 * r + 16:32 * r + 32], I[nm][li, 2 * q + 1], writes=[dX])
            Xc = self.sb(es2, "s5_Xc", [128, 2, 8, 2, 128])
            tmpx = self.sb(es2, "s5_tmpx", [128, 8, 128])
            for d in range(2):
                cr = cor[:, d * 8:(d + 1) * 8].unsqueeze(2).to_broadcast([128, 8, 128])
                ci = coi[:, d * 8:(d + 1) * 8].unsqueeze(2).to_broadcast([128, 8, 128])
                fw.op("dve", lambda e: e.tensor_tensor(out=Xc[:, d, :, 0, :], in0=Xr[:], in1=cr, op=ALU.mult), reads=[dX, dpar], writes=[dX])
                fw.op("dve", lambda e: e.tensor_tensor(out=tmpx[:], in0=Xi[:], in1=ci, op=ALU.mult), reads=[dX, dpar], writes=[dX])
                fw.op("dve", lambda e: e.tensor_tensor(out=Xc[:, d, :, 0, :], in0=Xc[:, d, :, 0, :], in1=tmpx[:], op=ALU.subtract), reads=[dX], writes=[dX])
                fw.op("dve", lambda e: e.tensor_tensor(out=Xc[:, d, :, 1, :], in0=Xi[:], in1=cr, op=ALU.mult), reads=[dX, dpar], writes=[dX])
                fw.op("dve", lambda e: e.tensor_tensor(out=tmpx[:], in0=Xr[:], in1=ci, op=ALU.mult), reads=[dX, dpar], writes=[dX])
                fw.op("dve", lambda e: e.tensor_tensor(out=Xc[:, d, :, 1, :], in0=Xc[:, d, :, 1, :], in1=tmpx[:], op=ALU.add), reads=[dX], writes=[dX])
            for d in range(2):
                for q in range(8):
                    for ri in range(2):
                        r = q % 4
                        ps, dp = self.next_ps()
                        fw.op("pe", lambda e: e.transpose(out=ps[:, 0:128], in_=Xc[:, d, q, ri, :],
                                                          identity=self.ident[:, :]), reads=[dX, self.d_const], writes=[dp])
                        fw.op("act", lambda e: e.copy(out=LB[:, d, q, ri, :], in_=ps[:, 0:128]),
                              reads=[dp], writes=[dLB])
            Yr = self.sb(es2, "s5_Yr", [32, 8, 128])
            Yi = self.sb(es2, "s5_Yi", [32, 8, 128])
            dY = Dep()
            fw.op("dve", lambda e: e.memset(Yr[:], 0.0), writes=[dY])
            fw.op("dve", lambda e: e.memset(Yi[:], 0.0), writes=[dY])
            for (Y, nm) in ((Yr, "s5_c_re"), (Yi, "s5_c_im")):
                src = I[nm][li].rearrange("(q gp) h n -> gp h q n", gp=2)
                fw.dma("sp", Y[0:16, :, 0:64], src[0], writes=[dY])
                fw.dma("sp", Y[16:32, :, 64:128], src[1], writes=[dY])
            for q in range(8):
                for ri, Y in enumerate((Yr, Yi)):
                    ps, dp = self.next_ps()
                    fw.op("pe", lambda e: e.transpose(out=ps[:, 0:32], in_=Y[:, q, :], identity=self.ident[0:32, 0:32]),
                          reads=[dY, self.d_const], writes=[dp])
                    if ri == 0:
                        fw.op("act", lambda e: e.copy(out=LC[:, q, 0, 32 * (q % 4):32 * (q % 4) + 32], in_=ps[:, 0:32]), reads=[dp], writes=[dLB])
                    else:
                        fw.op("act", lambda e: e.mul(out=LC[:, q, 1, 32 * (q % 4):32 * (q % 4) + 32], in_=ps[:, 0:32], mul=-1.0), reads=[dp], writes=[dLB])
            fw.barrier()

        ubf = self.sb(es, "s5_ubf", [128, 2, L], BF16)
        urv = self.sb(es, "s5_urv", [128, 2, L], BF16)
        yacc = self.sb(es, "s5_yacc", [128, 2, L])
        du = Dep()
        dyacc = Dep()
        with ExitStack() as es2:
            uf = self.sb(es2, "s5_uf", [128, 2, L])
            fw.dma("sp", uf[:], self.projT[OFF_U:OFF_U + 256, :].rearrange("(kt p) t -> p kt t", p=128),
                   reads=[self.dep_proj], writes=[du])
            for kt in range(2):
                fw.op("dve", lambda e: e.tensor_copy(out=ubf[:, kt, :], in_=uf[:, kt, :]), reads=[du], writes=[du])
                fw.op("pool", lambda e: e.tensor_copy(out=urv[:, kt, ::-1], in_=uf[:, kt, :]), reads=[du], writes=[du])
            fw.barrier()

        blocks = [(i * S5W, S5W) for i in range(L // S5W)]
        if L % S5W:
            blocks.append((L - L % S5W, L % S5W))
        NB = 3
        tmp = [[self.sb(es, "s5_w%d_%d" % (i, k), [128, S5W]) for k in range(6)] for i in range(NB)]
        dtmp = [[Dep() for k in range(6)] for i in range(NB)]
        hb = [[self.sb(es, "s5_h%d_%d" % (i, k), [128, S5W], BF16) for k in range(2)] for i in range(NB)]
        dhb = [[Dep() for k in range(2)] for i in range(NB)]
        init = [[self.sb(es, "s5_in%d_%d" % (i, k), [128, 1]) for k in range(3)] for i in range(2)]
        dinit = [Dep(), Dep()]
        it = 0
        for d in range(2):
            usrc = ubf if d == 0 else urv
            for q in range(8):
                j = d * 8 + q
                r = q % 4
                kt = q // 4
                rho_b = rho[:, j:j + 1]
                prev = None
                for bi, (t0, W) in enumerate(blocks):
                    T, dT = tmp[it % NB], dtmp[it % NB]
                    H, dH = hb[it % NB], dhb[it % NB]
                    it += 1
                    pre, dpre = self.next_ps()
                    pim, dpim = self.next_ps()
                    fw.op("pe", lambda e: e.matmul(pre[:, 0:W], lhsT=LB[:, d, q, 0, :],
                                                   rhs=usrc[:, kt, t0:t0 + W], start=True, stop=True),
                          reads=[dLB, du], writes=[dpre])
                    fw.op("pe", lambda e: e.matmul(pim[:, 0:W], lhsT=LB[:, d, q, 1, :],
                                                   rhs=usrc[:, kt, t0:t0 + W], start=True, stop=True),
                          reads=[dLB, du], writes=[dpim])
                    c_, s_ = tc[:, j, 0:W], ts[:, j, 0:W]
                    fw.op("dve", lambda e: e.tensor_tensor(out=T[0][:, 0:W], in0=pre[:, 0:W], in1=c_, op=ALU.mult), reads=[dpre, dtab], writes=[dT[0]])
                    fw.op("dve", lambda e: e.tensor_tensor(out=T[1][:, 0:W], in0=pim[:, 0:W], in1=s_, op=ALU.mult), reads=[dpim, dtab], writes=[dT[1]])
                    fw.op("dve", lambda e: e.tensor_tensor(out=T[2][:, 0:W], in0=pim[:, 0:W], in1=c_, op=ALU.mult), reads=[dpim, dtab], writes=[dT[2]])
                    fw.op("dve", lambda e: e.tensor_tensor(out=T[3][:, 0:W], in0=pre[:, 0:W], in1=s_, op=ALU.mult), reads=[dpre, dtab], writes=[dT[3]])
                    fw.op("pool", lambda e: e.tensor_tensor(out=T[0][:, 0:W], in0=T[0][:, 0:W], in1=T[1][:, 0:W], op=ALU.add), reads=[dT[0], dT[1]], writes=[dT[0]])
                    fw.op("pool", lambda e: e.tensor_tensor(out=T[2][:, 0:W], in0=T[2][:, 0:W], in1=T[3][:, 0:W], op=ALU.subtract), reads=[dT[2], dT[3]], writes=[dT[2]])
                    ini, dini = init[bi % 2], dinit[bi % 2]
                    if bi == 0:
                        i_re, i_im = 0.0, 0.0
                        rd = []
                    else:
                        pT, pdT, pW, pini, pdini = prev
                        cW, sW = tc[:, j, pW:pW + 1], ts[:, j, pW:pW + 1]
                        fw.op("dve", lambda e: e.tensor_scalar(out=ini[2][:], in0=pT[4][:, pW - 1:pW], scalar1=cW, scalar2=None, op0=ALU.mult), reads=[pdT[4], dtab], writes=[dini])
                        fw.op("dve", lambda e: e.scalar_tensor_tensor(out=ini[2][:], in0=pT[5][:, pW - 1:pW], scalar=sW, in1=ini[2][:], op0=ALU.mult, op1=ALU.subtract), reads=[pdT[5], dini, dtab], writes=[dini])
                        fw.op("dve", lambda e: e.tensor_scalar(out=ini[0][:], in0=ini[2][:], scalar1=-1.0, scalar2=None, op0=ALU.mult), reads=[dini], writes=[dini])
                        fw.op("dve", lambda e: e.tensor_scalar(out=ini[2][:], in0=pT[4][:, pW - 1:pW], scalar1=sW, scalar2=None, op0=ALU.mult), reads=[pdT[4], dtab], writes=[dini])
                        fw.op("dve", lambda e: e.scalar_tensor_tensor(out=ini[1][:], in0=pT[5][:, pW - 1:pW], scalar=cW, in1=ini[2][:], op0=ALU.mult, op1=ALU.add), reads=[pdT[5], dini, dtab], writes=[dini])
                        i_re, i_im = ini[0][:, 0:1], ini[1][:, 0:1]
                        rd = [dini]
                    fw.op("dve", lambda e: e.tensor_tensor_scan(out=T[4][:, 0:W], data0=rho_b.to_broadcast([128, W]), data1=T[0][:, 0:W], initial=i_re, op0=ALU.mult, op1=ALU.add),
                          reads=[dT[0], dpar] + rd, writes=[dT[4]])
                    fw.op("dve", lambda e: e.tensor_tensor_scan(out=T[5][:, 0:W], data0=rho_b.to_broadcast([128, W]), data1=T[2][:, 0:W], initial=i_im, op0=ALU.mult, op1=ALU.add),
                          reads=[dT[2], dpar] + rd, writes=[dT[5]])
                    prev = (T, dT, W, ini, dini)
                    fw.op("pool", lambda e: e.tensor_tensor(out=T[0][:, 0:W], in0=T[4][:, 0:W], in1=c_, op=ALU.mult), reads=[dT[4], dtab], writes=[dT[0]])
                    fw.op("pool", lambda e: e.tensor_tensor(out=T[1][:, 0:W], in0=T[5][:, 0:W], in1=s_, op=ALU.mult), reads=[dT[5], dtab], writes=[dT[1]])
                    fw.op("pool", lambda e: e.tensor_tensor(out=H[0][:, 0:W], in0=T[0][:, 0:W], in1=T[1][:, 0:W], op=ALU.subtract), reads=[dT[0], dT[1]], writes=[dH[0]])
                    fw.op("dve", lambda e: e.tensor_tensor(out=T[2][:, 0:W], in0=T[4][:, 0:W], in1=s_, op=ALU.mult), reads=[dT[4], dtab], writes=[dT[2]])
                    fw.op("dve", lambda e: e.tensor_tensor(out=T[3][:, 0:W], in0=T[5][:, 0:W], in1=c_, op=ALU.mult), reads=[dT[5], dtab], writes=[dT[3]])
                    fw.op("pool", lambda e: e.tensor_tensor(out=H[1][:, 0:W], in0=T[2][:, 0:W], in1=T[3][:, 0:W], op=ALU.add), reads=[dT[2], dT[3]], writes=[dH[1]])
                    py, dpy = self.next_ps()
                    fw.op("pe", lambda e: e.matmul(py[:, 0:W], lhsT=LC[:, q, 0, :], rhs=H[0][:, 0:W], start=True, stop=False), reads=[dLB, dH[0]], writes=[dpy])
                    fw.op("pe", lambda e: e.matmul(py[:, 0:W], lhsT=LC[:, q, 1, :], rhs=H[1][:, 0:W], start=False, stop=True), reads=[dLB, dH[1]], writes=[dpy])
                    if d == 0 and r == 0:
                        fw.op("act", lambda e: e.copy(out=yacc[:, kt, t0:t0 + W], in_=py[:, 0:W]), reads=[dpy], writes=[dyacc])
                    elif d == 0:
                        ya = yacc[:, kt, t0:t0 + W]
                        fw.op("dve", lambda e: e.tensor_tensor(out=ya, in0=py[:, 0:W], in1=ya, op=ALU.add), reads=[dpy, dyacc], writes=[dyacc])
                    else:
                        lo = L - (t0 + W)
                        ya = yacc[:, kt, lo:lo + W]
                        fw.op("dve", lambda e: e.tensor_tensor(out=ya[:, ::-1], in0=py[:, 0:W], in1=ya[:, ::-1], op=ALU.add), reads=[dpy, dyacc], writes=[dyacc])
        fw.barrier()
        self._s5_post(li, es, yacc, dyacc)


def _s5_post(self, li, es_outer, yacc, dyacc):
    fw = self.fw
    I = self.I
    with ExitStack() as es:
        gw, dgw = self.load_weight_bf(es, "s5_gw", I["s5_glu_w"][li], 256, 512)
        dsk = self.sb(es, "s5_dsk", [128, 2])
        gb = self.sb(es, "s5_gb", [128, 4])
        dpp = Dep()
        fw.dma("sp", dsk[:], I["s5_d"][li].rearrange("(kt p) -> p kt", p=128), writes=[dpp], slow=True)
        fw.dma("sp", gb[:], I["s5_glu_b"][li].rearrange("(kt p) -> p kt", p=128), writes=[dpp], slow=True)
        W = 512
        uf = [self.sb(es, "s5p_u%d" % i, [128, 2, W]) for i in range(2)]
        duf = [Dep(), Dep()]
        t1 = self.sb(es, "s5p_t1", [128, 2, W])
        t2 = self.sb(es, "s5p_t2", [128, 2, W])
        dt1 = Dep()
        gl = [self.sb(es, "s5p_gl%d" % i, [128, 2, W], BF16) for i in range(2)]
        dgl = [Dep(), Dep()]
        sg = [self.sb(es, "s5p_sg%d" % i, [128, W]) for i in range(2)]
        dsg = [Dep(), Dep()]
        o = [self.sb(es, "s5p_o%d" % i, [128, 2, W]) for i in range(2)]
        do = [Dep(), Dep()]
        uv = self.projT[OFF_U:OFF_U + 256, :].rearrange("(kt p) t -> p kt t", p=128)
        yv = self.yaT.rearrange("(kt p) t -> p kt t", p=128)
        for bi, (t0, Wb) in enumerate(BLOCKS):
            u, du = uf[bi % 2], duf[bi % 2]
            g, dg = gl[bi % 2], dgl[bi % 2]
            ob, dob = o[bi % 2], do[bi % 2]
            fw.dma("sp", u[:, :, 0:Wb], uv[:, :, t0:t0 + Wb], reads=[self.dep_proj], writes=[du])
            for kt in range(2):
                fw.op("dve", lambda e: e.scalar_tensor_tensor(out=t1[:, kt, 0:Wb], in0=u[:, kt, 0:Wb], scalar=dsk[:, kt:kt + 1], in1=yacc[:, kt, t0:t0 + Wb], op0=ALU.mult, op1=ALU.add),
                      reads=[du, dpp, dyacc], writes=[dt1])
                fw.op("pool", lambda e: e.tensor_tensor(out=t2[:, kt, 0:Wb], in0=t1[:, kt, 0:Wb], in1=t1[:, kt, 0:Wb], op=ALU.mult), reads=[dt1], writes=[dt1])
                fw.op("dve", lambda e: e.tensor_scalar(out=t2[:, kt, 0:Wb], in0=t2[:, kt, 0:Wb], scalar1=0.044715, scalar2=1.0, op0=ALU.mult, op1=ALU.add), reads=[dt1], writes=[dt1])
                fw.op("pool", lambda e: e.tensor_tensor(out=t2[:, kt, 0:Wb], in0=t2[:, kt, 0:Wb], in1=t1[:, kt, 0:Wb], op=ALU.mult), reads=[dt1], writes=[dt1])
                fw.op("act", lambda e: e.activation(out=t2[:, kt, 0:Wb], in_=t2[:, kt, 0:Wb], func=AF.Sigmoid, scale=1.5957691216), reads=[dt1], writes=[dt1])
                fw.op("dve", lambda e: e.tensor_tensor(out=g[:, kt, 0:Wb], in0=t2[:, kt, 0:Wb], in1=t1[:, kt, 0:Wb], op=ALU.mult), reads=[dt1], writes=[dg])
            for c in range(2):
                plo, dplo = self.next_ps()
                phi, dphi = self.next_ps()
                for k in range(2):
                    fw.op("pe", lambda e: e.matmul(plo[:, 0:Wb], lhsT=gw[:, k, c * 128:(c + 1) * 128], rhs=g[:, k, 0:Wb], start=(k == 0), stop=(k == 1)), reads=[dgw, dg], writes=[dplo])
                for k in range(2):
                    fw.op("pe", lambda e: e.matmul(phi[:, 0:Wb], lhsT=gw[:, k, 256 + c * 128:256 + (c + 1) * 128], rhs=g[:, k, 0:Wb], start=(k == 0), stop=(k == 1)), reads=[dgw, dg], writes=[dphi])
                s, ds = sg[c], dsg[c]
                fw.op("act", lambda e: e.activation(out=s[:, 0:Wb], in_=phi[:, 0:Wb], func=AF.Sigmoid, bias=gb[:, 2 + c:3 + c]), reads=[dphi, dpp], writes=[ds])
                fw.op("dve", lambda e: e.scalar_tensor_tensor(out=ob[:, c, 0:Wb], in0=plo[:, 0:Wb], scalar=gb[:, c:c + 1], in1=s[:, 0:Wb], op0=ALU.add, op1=ALU.mult), reads=[dplo, ds, dpp], writes=[dob])
            fw.dma("pool", yv[:, :, t0:t0 + Wb], ob[:, :, 0:Wb], reads=[dob], writes=[self.dep_ya])
    fw.barrier()


Builder.mix_s5 = _mix_s5
Builder._s5_post = _s5_post


LP = 33 * 128
NCH = 33


def _mix_ssd(self, li):
    fw = self.fw
    I = self.I
    xcT = self.scratch_once("xcT", (1024, L))
    d_xc = self.dep_once("xcT")
    xbv = self.projT[OFF_XBC:OFF_XBC + 1024, :].rearrange("(j p) t -> p j t", p=128)
    xcv = xcT.rearrange("(j p) t -> p j t", p=128)
    with ExitStack() as es:
        cw = self.sb(es, "sd_cw", [128, 5, 8])
        cb = self.sb(es, "sd_cb", [128, 8])
        dcw = Dep()
        for k in range(5):
            fw.dma("sp", cw[:, k, :], I["ssd_conv_w"][li, k].rearrange("(j p) -> p j", p=128), writes=[dcw], slow=True)
        fw.dma("sp", cb[:], I["ssd_conv_b"][li].rearrange("(j p) -> p j", p=128), writes=[dcw], slow=True)
        xp = [self.sb(es, "sd_xp%d" % i, [128, L + 4]) for i in range(2)]
        dxp = [Dep(), Dep()]
        acc = [self.sb(es, "sd_acc%d" % i, [128, L]) for i in range(2)]
        dacc = [Dep(), Dep()]
        for i in range(2):
            fw.op("dve", lambda e: e.memset(xp[i][:, 0:2], 0.0), writes=[dxp[i]])
            fw.op("dve", lambda e: e.memset(xp[i][:, L + 2:L + 4], 0.0), writes=[dxp[i]])
        for j in range(8):
            x_, dx_ = xp[j % 2], dxp[j % 2]
            a_, da_ = acc[j % 2], dacc[j % 2]
            fw.dma("sp", x_[:, 2:L + 2], xbv[:, j, :], reads=[self.dep_proj], writes=[dx_])
            eng = "dve"
            fw.op(eng, lambda e: e.tensor_scalar(out=a_[:], in0=x_[:, 0:L], scalar1=cw[:, 0, j:j + 1], scalar2=cb[:, j:j + 1], op0=ALU.mult, op1=ALU.add),
                  reads=[dx_, dcw], writes=[da_])
            for k in range(1, 5):
                fw.op(eng, lambda e: e.scalar_tensor_tensor(out=a_[:], in0=x_[:, k:k + L], scalar=cw[:, k, j:j + 1], in1=a_[:], op0=ALU.mult, op1=ALU.add),
                      reads=[dx_, dcw, da_], writes=[da_])
            fw.op("act", lambda e: e.activation(out=a_[:], in_=a_[:], func=AF.Silu), reads=[da_], writes=[da_])
            fw.dma("pool", xcv[:, j, :], a_[:], reads=[da_], writes=[d_xc])
    fw.barrier()

    with ExitStack() as es:
        triu = self.sb(es, "sd_triu", [128, 128])
        mneg = self.sb(es, "sd_mneg", [128, 128])
        onesf = self.sb(es, "sd_onesf", [128, 128])
        negones = self.sb(es, "sd_negones", [128, 128])
        identb = self.sb(es, "sd_identb", [128, 128], BF16)
        dc = Dep()
        fw.dma("sp", triu[:], I["c_triu"][:, :], writes=[dc])
        fw.dma("sp", mneg[:], I["c_mneg"][:, :], writes=[dc])
        padm = self.sb(es, "sd_padm", [128, 8])
        fw.dma("sp", padm[:], I["c_padm"][:, :], writes=[dc])
        fw.op("dve", lambda e: e.memset(onesf[:], 1.0), writes=[dc])
        fw.op("dve", lambda e: e.memset(negones[:], -1.0), writes=[dc])
        fw.op("dve", lambda e: e.tensor_copy(out=identb[:], in_=self.ident[:]), reads=[self.d_const], writes=[dc])
        dtb = self.sb(es, "sd_dtb", [8, 2])
        nea = self.sb(es, "sd_nea", [8, 2])
        dpp = Dep()
        fw.dma("sp", dtb[:], I["ssd_dt_bias"][li].rearrange("d h -> h d"), writes=[dpp], slow=True)
        fw.dma("sp", nea[:], I["ssd_a_log"][li].rearrange("d h -> h d"), writes=[dpp], slow=True)
        fw.op("act", lambda e: e.activation(out=nea[:], in_=nea[:], func=AF.Exp), reads=[dpp], writes=[dpp])
        fw.op("dve", lambda e: e.tensor_scalar(out=nea[:], in0=nea[:], scalar1=-1.0, scalar2=None, op0=ALU.mult), reads=[dpp], writes=[dpp])
        dtok = self.sb(es, "sd_dtok", [128, NCH, 16])
        ddtok = Dep()
        xs = self.sb(es, "sd_xs", [128, 4, LP], BF16)
        Bm = self.sb(es, "sd_B", [128, 2, LP], BF16)
        Cm = self.sb(es, "sd_C", [128, 2, LP], BF16)
        dws = Dep()
        yacc = self.sb(es, "sd_yacc", [128, 4, L])
        dyacc = Dep()
        ST = self.sb(es, "sd_ST", [128, 8, 64])
        STb = self.sb(es, "sd_STb", [128, 8, 64], BF16)
        dST = [Dep() for _ in range(8)]
        dSTb = [Dep() for _ in range(8)]
        dbias = self.sb(es, "sd_dbias", [128, 2, 8])
        nea_bc = self.sb(es, "sd_neabc", [128, 2, 8])
        fw.dma("sp", dbias[:], I["ssd_dt_bias"][li].partition_broadcast(128), writes=[dpp], slow=True)
        fw.dma("sp", nea_bc[:], I["ssd_a_log"][li].partition_broadcast(128), writes=[dpp], slow=True)
        fw.op("act", lambda e: e.activation(out=nea_bc[:], in_=nea_bc[:], func=AF.Exp), reads=[dpp], writes=[dpp])
        fw.op("dve", lambda e: e.tensor_scalar(out=nea_bc[:], in0=nea_bc[:], scalar1=-1.0, scalar2=None, op0=ALU.mult), reads=[dpp], writes=[dpp])

        for d in range(2):
          with ExitStack() as es3:
            stg = [self.sb(es3, "sd_stg%d" % i, [128, L]) for i in range(2)]
            dstg = [Dep(), Dep()]
            raw = self.sb(es3, "sd_raw", [8, L])
            rawd = self.sb(es3, "sd_rawd", [8, LP])
            draw = Dep()
            for j in range(8):
                s_, ds_ = stg[j % 2], dstg[j % 2]
                fw.dma("sp", s_[:], xcv[:, j, :], reads=[d_xc], writes=[ds_])
                dst = xs[:, j, :] if j < 4 else (Bm[:, j - 4, :] if j < 6 else Cm[:, j - 6, :])
                eng = "dve" if j % 2 == 0 else "pool"
                fw.op(eng, lambda e: e.memset(dst[:, L:LP], 0.0), writes=[dws])
                if d == 0:
                    fw.op(eng, lambda e: e.tensor_copy(out=dst[:, 0:L], in_=s_[:]), reads=[ds_], writes=[dws])
                else:
                    fw.op(eng, lambda e: e.tensor_copy(out=dst[:, 0:L][:, ::-1], in_=s_[:]), reads=[ds_], writes=[dws])
            fw.dma("sp", raw[:], self.projT[OFF_DT:OFF_DT + 8, :], reads=[self.dep_proj], writes=[draw])
            fw.op("dve", lambda e: e.memset(rawd[:, L:LP], 0.0), writes=[draw])
            if d == 0:
                fw.op("dve", lambda e: e.tensor_copy(out=rawd[:, 0:L], in_=raw[:]), reads=[draw], writes=[draw])
            else:
                fw.op("dve", lambda e: e.tensor_copy(out=rawd[:, 0:L][:, ::-1], in_=raw[:]), reads=[draw], writes=[draw])
            for c in range(NCH):
                ps, dp = self.next_ps()
                fw.op("pe", lambda e: e.transpose(out=ps[:, 0:8], in_=rawd[:, c * 128:(c + 1) * 128], identity=self.ident[0:8, 0:8]), reads=[draw, self.d_const], writes=[dp])
                fw.op("dve", lambda e: e.tensor_tensor(out=dtok[:, c, 0:8], in0=ps[:, 0:8], in1=dbias[:, d, :], op=ALU.add), reads=[dp, dpp], writes=[ddtok])
            fw.op("act", lambda e: e.activation(out=dtok[:, :, 0:8], in_=dtok[:, :, 0:8], func=AF.Exp), reads=[ddtok], writes=[ddtok])
            fw.op("act", lambda e: e.activation(out=dtok[:, :, 0:8], in_=dtok[:, :, 0:8], func=AF.Ln, bias=self.one_t[:, 0:1]), reads=[ddtok, self.d_const], writes=[ddtok])
            fw.op("dve", lambda e: e.tensor_tensor(out=dtok[:, NCH - 1, 0:8], in0=dtok[:, NCH - 1, 0:8], in1=padm[:, :], op=ALU.mult), reads=[ddtok, dc], writes=[ddtok])
            fw.op("dve", lambda e: e.tensor_tensor(out=dtok[:, :, 8:16], in0=dtok[:, :, 0:8], in1=nea_bc[:, d, :].unsqueeze(1).to_broadcast([128, NCH, 8]), op=ALU.mult), reads=[ddtok, dpp], writes=[ddtok])
            fw.barrier()
          with ExitStack() as es4:
            NBF = 2
            xtk = [self.sb(es4, "sd_xtk%d" % i, [128, 512]) for i in range(NBF)]
            dxtk = [Dep() for _ in range(NBF)]
            btok = [self.sb(es4, "sd_btok%d" % i, [128, 256], BF16) for i in range(NBF)]
            dbtok = [Dep() for _ in range(NBF)]
            cbt = [self.sb(es4, "sd_cbt%d" % i, [128, 2, 128]) for i in range(NBF)]
            dcbt = [Dep() for _ in range(NBF)]
            sm = [self.sb(es4, "sd_sm%d" % i, [128, 4, 8]) for i in range(NBF)]
            dsm = [Dep() for _ in range(NBF)]
            NH = 3
            atri = [self.sb(es4, "sd_atri%d" % i, [128, 128]) for i in range(NH)]
            datri = [Dep() for _ in range(NH)]
            DT = [self.sb(es4, "sd_DT%d" % i, [128, 128]) for i in range(NH)]
            dDT = [Dep() for _ in range(NH)]
            EE = [self.sb(es4, "sd_EE%d" % i, [128, 128]) for i in range(NH)]
            dEE = [Dep() for _ in range(NH)]
            MT = [self.sb(es4, "sd_MT%d" % i, [128, 128], BF16) for i in range(NH)]
            dMT = [Dep() for _ in range(NH)]
            CE = [self.sb(es4, "sd_CE%d" % i, [128, 128], BF16) for i in range(NH)]
            dCE = [Dep() for _ in range(NH)]
            xdt = [self.sb(es4, "sd_xdt%d" % i, [128, 2, 64], BF16) for i in range(NH)]
            dxdt = [Dep() for _ in range(NH)]
            for j in range(8):
                fw.op("dve", lambda e: e.memset(ST[:, j, :], 0.0), writes=[dST[j]])
                fw.op("pool", lambda e: e.memset(STb[:, j, :], 0.0), writes=[dSTb[j]])
            ih = 0
            for c in range(NCH):
                t0 = c * 128
                Wv = min(128, L - t0)
                k_ = c % NBF
                px, dpx = self.next_ps()
                for j in range(4):
                    fw.op("pe", lambda e: e.matmul(px[:, j * 128:(j + 1) * 128], lhsT=xs[:, j, t0:t0 + 128], rhs=identb[:, :], start=True, stop=True), reads=[dws, dc], writes=[dpx])
                xtok, dxtok = xtk[k_], dxtk[k_]
                fw.op("act", lambda e: e.copy(out=xtok[:, :], in_=px[:, 0:512]), reads=[dpx], writes=[dxtok])
                pb, dpb = self.next_ps()
                for g in range(2):
                    fw.op("pe", lambda e: e.matmul(pb[:, g * 128:(g + 1) * 128], lhsT=Bm[:, g, t0:t0 + 128], rhs=identb[:, :], start=True, stop=True), reads=[dws, dc], writes=[dpb])
                fw.op("act", lambda e: e.copy(out=btok[k_][:, :], in_=pb[:, 0:256]), reads=[dpb], writes=[dbtok[k_]])
                pc, dpc = self.next_ps()
                fw.op("pe", lambda e: e.matmul(pc[:, 0:8], lhsT=triu[:, :], rhs=dtok[:, c, 8:16], start=True, stop=True), reads=[dc, ddtok], writes=[dpc])
                fw.op("pe", lambda e: e.matmul(pc[:, 8:16], lhsT=onesf[:, :], rhs=dtok[:, c, 8:16], start=True, stop=True), reads=[dc, ddtok], writes=[dpc])
                S_, dS_ = sm[k_], dsm[k_]
                fw.op("act", lambda e: e.copy(out=S_[:, 0, :], in_=pc[:, 0:8]), reads=[dpc], writes=[dS_])
                fw.op("dve", lambda e: e.tensor_tensor(out=S_[:, 1, :], in0=pc[:, 8:16], in1=S_[:, 0, :], op=ALU.subtract), reads=[dpc, dS_], writes=[dS_])
                fw.op("act", lambda e: e.activation(out=S_[:, 1, :], in_=S_[:, 1, :], func=AF.Exp), reads=[dS_], writes=[dS_])
                fw.op("dve", lambda e: e.tensor_tensor(out=S_[:, 2, :], in0=S_[:, 1, :], in1=dtok[:, c, 0:8], op=ALU.mult), reads=[dS_, ddtok], writes=[dS_])
                fw.op("act", lambda e: e.activation(out=S_[:, 3, :], in_=pc[:, 8:16], func=AF.Exp), reads=[dpc], writes=[dS_])
                for g in range(2):
                    pcb, dpcb = self.next_ps()
                    fw.op("pe", lambda e: e.matmul(pcb[:, 0:128], lhsT=Bm[:, g, t0:t0 + 128], rhs=Cm[:, g, t0:t0 + 128], start=True, stop=True), reads=[dws], writes=[dpcb])
                    fw.op("act", lambda e: e.copy(out=cbt[k_][:, g, :], in_=pcb[:, 0:128]), reads=[dpcb], writes=[dcbt[k_]])
                for jp in range(4):
                    py, dpy = self.next_ps()
                    for jj in range(2):
                        j = jp * 2 + jj
                        g = j // 4
                        h_ = ih % NH
                        ih += 1
                        fw.op("dve", lambda e: e.tensor_scalar(out=atri[h_][:], in0=triu[:], scalar1=dtok[:, c, 8 + j:9 + j], scalar2=None, op0=ALU.mult), reads=[dc, ddtok], writes=[datri[h_]])
                        pD, dpD = self.next_ps()
                        fw.op("pe", lambda e: e.matmul(pD[:, 0:128], lhsT=onesf[:, :], rhs=atri[h_][:, :], start=True, stop=False), reads=[dc, datri[h_]], writes=[dpD])
                        fw.op("pe", lambda e: e.matmul(pD[:, 0:128], lhsT=atri[h_][:, :], rhs=negones[:, :], start=False, stop=False), reads=[dc, datri[h_]], writes=[dpD])
                        fw.op("pe", lambda e: e.matmul(pD[:, 0:128], lhsT=self.ident[:, :], rhs=mneg[:, :], start=False, stop=True), reads=[dc, self.d_const], writes=[dpD])
                        fw.op("pe", lambda e: e.matmul(pD[:, 128:256], lhsT=onesf[:, :], rhs=atri[h_][:, :], start=True, stop=True), reads=[dc, datri[h_]], writes=[dpD])
                        fw.op("act", lambda e: e.activation(out=DT[h_][:], in_=pD[:, 0:128], func=AF.Exp), reads=[dpD], writes=[dDT[h_]])
                        fw.op("act", lambda e: e.activation(out=EE[h_][:], in_=pD[:, 128:256], func=AF.Exp), reads=[dpD], writes=[dEE[h_]])
                        fw.op("dve", lambda e: e.tensor_tensor(out=MT[h_][:], in0=cbt[k_][:, g, :], in1=DT[h_][:], op=ALU.mult), reads=[dcbt[k_], dDT[h_]], writes=[dMT[h_]])
                        fw.op("pool", lambda e: e.tensor_tensor(out=CE[h_][:], in0=Cm[:, g, t0:t0 + 128], in1=EE[h_][:], op=ALU.mult), reads=[dws, dEE[h_]], writes=[dCE[h_]])
                        fw.op("dve", lambda e: e.tensor_scalar(out=xdt[h_][:, 0, :], in0=xtok[:, j * 64:(j + 1) * 64], scalar1=dtok[:, c, j:j + 1], scalar2=None, op0=ALU.mult), reads=[dxtok, ddtok], writes=[dxdt[h_]])
                        fw.op("dve", lambda e: e.tensor_scalar(out=xdt[h_][:, 1, :], in0=xtok[:, j * 64:(j + 1) * 64], scalar1=S_[:, 2, j:j + 1], scalar2=None, op0=ALU.mult), reads=[dxtok, dS_], writes=[dxdt[h_]])
                        fw.op("pe", lambda e: e.matmul(py[jj * 64:(jj + 1) * 64, 0:128], lhsT=xdt[h_][:, 0, :], rhs=MT[h_][:, :], start=True, stop=False), reads=[dxdt[h_], dMT[h_]], writes=[dpy])
                        fw.op("pe", lambda e: e.matmul(py[jj * 64:(jj + 1) * 64, 0:128], lhsT=STb[:, j, :], rhs=CE[h_][:, :], start=False, stop=True), reads=[dSTb[j], dCE[h_]], writes=[dpy])
                        pS, dpS = self.next_ps()
                        fw.op("pe", lambda e: e.matmul(pS[:, 0:64], lhsT=btok[k_][:, g * 128:(g + 1) * 128], rhs=xdt[h_][:, 1, :], start=True, stop=True), reads=[dbtok[k_], dxdt[h_]], writes=[dpS])
                        fw.op("dve", lambda e: e.scalar_tensor_tensor(out=ST[:, j, :], in0=ST[:, j, :], scalar=S_[:, 3, j:j + 1], in1=pS[:, 0:64], op0=ALU.mult, op1=ALU.add), reads=[dST[j], dS_, dpS], writes=[dST[j]])
                        fw.op("act", lambda e: e.copy(out=STb[:, j, :], in_=ST[:, j, :]), reads=[dST[j]], writes=[dSTb[j]])
                    if d == 0:
                        fw.op("act", lambda e: e.copy(out=yacc[:, jp, t0:t0 + Wv], in_=py[:, 0:Wv]), reads=[dpy], writes=[dyacc])
                    else:
                        lo = L - (t0 + Wv)
                        ya = yacc[:, jp, lo:lo + Wv]
                        fw.op("dve", lambda e: e.tensor_tensor(out=ya[:, ::-1], in0=py[:, 0:Wv], in1=ya[:, ::-1], op=ALU.add), reads=[dpy, dyacc], writes=[dyacc])
            fw.barrier()
        fw.barrier()
        if "dbg_yacc" in self.cfg.get("dump", ()):
            dbg = self.scratch("dbg_yacc", (512, L))
            fw.dma("sp", dbg.rearrange("(j p) t -> p j t", p=128), yacc[:, :, :], reads=[dyacc], writes=[Dep()])
            dbg2 = self.scratch("dbg_dtok", (128, NCH * 16))
            fw.dma("sp", dbg2[:, :], dtok[:, :, :].rearrange("p c k -> p (c k)"), reads=[ddtok], writes=[Dep()])
        with ExitStack() as es2:
            self.ensure_eps(es2)
            dsk = self.sb(es2, "sd_dsk", [128, 4])
            nw = self.sb(es2, "sd_nw", [128, 4])
            dq = Dep()
            for j in range(8):
                fw.dma("sp", dsk[64 * (j % 2):64 * (j % 2) + 64, j // 2:j // 2 + 1], I["ssd_d"][li, j:j + 1].partition_broadcast(64), writes=[dq], slow=True)
            fw.dma("sp", nw[:], I["ssd_norm_w"][li].rearrange("(j p) -> p j", p=128), writes=[dq], slow=True)
            W = 512
            xb = [self.sb(es2, "sd4_x%d" % i, [128, 4, W]) for i in range(2)]
            zb = [self.sb(es2, "sd4_z%d" % i, [128, 4, W]) for i in range(2)]
            dxb = [Dep(), Dep()]
            yb = self.sb(es2, "sd4_y", [128, 4, W])
            sq = self.sb(es2, "sd4_sq", [128, 4, W], BF16)
            dyb = Dep()
            rstd = self.sb(es2, "sd4_r", [128, W])
            ob = [self.sb(es2, "sd4_o%d" % i, [128, 4, W]) for i in range(2)]
            dob = [Dep(), Dep()]
            zv = self.projT[OFF_Z:OFF_Z + 512, :].rearrange("(j p) t -> p j t", p=128)
            yv = self.ybT.rearrange("(j p) t -> p j t", p=128)
            for bi, (t0, Wb) in enumerate(BLOCKS):
                x_, z_, dxz = xb[bi % 2], zb[bi % 2], dxb[bi % 2]
                o_, do_ = ob[bi % 2], dob[bi % 2]
                fw.dma("sp", x_[:, :, 0:Wb], xcv[:, 0:4, t0:t0 + Wb], reads=[d_xc], writes=[dxz])
                fw.dma("sp", z_[:, :, 0:Wb], zv[:, :, t0:t0 + Wb], reads=[self.dep_proj], writes=[dxz])
                fw.op("act", lambda e: e.activation(out=z_[:, :, 0:Wb], in_=z_[:, :, 0:Wb], func=AF.Silu), reads=[dxz], writes=[dxz])
                for j in range(4):
                    fw.op("dve", lambda e: e.scalar_tensor_tensor(out=yb[:, j, 0:Wb], in0=x_[:, j, 0:Wb], scalar=dsk[:, j:j + 1], in1=yacc[:, j, t0:t0 + Wb], op0=ALU.mult, op1=ALU.add), reads=[dxz, dq, dyacc], writes=[dyb])
                    fw.op("pool", lambda e: e.tensor_tensor(out=yb[:, j, 0:Wb], in0=yb[:, j, 0:Wb], in1=z_[:, j, 0:Wb], op=ALU.mult), reads=[dyb, dxz], writes=[dyb])
                    fw.op("act", lambda e: e.activation(out=sq[:, j, 0:Wb], in_=yb[:, j, 0:Wb], func=AF.Square), reads=[dyb], writes=[dyb])
                ps, dp = self.next_ps()
                for j in range(4):
                    fw.op("pe", lambda e: e.matmul(ps[:, 0:Wb], lhsT=self.ones_bf[:, :], rhs=sq[:, j, 0:Wb], start=(j == 0), stop=(j == 3)), reads=[dyb, self.d_const], writes=[dp])
                fw.op("act", lambda e: e.activation(out=rstd[:, 0:Wb], in_=ps[:, 0:Wb], func=AF.Sqrt, bias=self.eps_t[:, 0:1], scale=1.0 / 512), reads=[dp, self.d_const], writes=[dyb])
                fw.op("dve", lambda e: e.reciprocal(out=rstd[:, 0:Wb], in_=rstd[:, 0:Wb]), reads=[dyb], writes=[dyb])
                for j in range(4):
                    fw.op("dve", lambda e: e.scalar_tensor_tensor(out=o_[:, j, 0:Wb], in0=yb[:, j, 0:Wb], scalar=nw[:, j:j + 1], in1=rstd[:, 0:Wb], op0=ALU.mult, op1=ALU.mult), reads=[dyb, dq], writes=[do_])
                fw.dma("pool", yv[:, :, t0:t0 + Wb], o_[:, :, 0:Wb], reads=[do_], writes=[self.dep_yb])
    fw.barrier()


def _scratch_once(self, name, shape):
    if not hasattr(self, "_sc"):
        self._sc = {}
        self._scd = {}
    if name not in self._sc:
        self._sc[name] = self.scratch(name, shape)
        self._scd[name] = Dep()
    return self._sc[name]


def _dep_once(self, name):
    return self._scd[name]


Builder.mix_ssd = _mix_ssd
Builder.scratch_once = _scratch_once
Builder.dep_once = _dep_once


RW_ARR = ("r", "v", "kkn", "g", "bonus", "lw0", "kd0", "b0", "lw1", "kd1", "b1")


def _mix_rwkv(self, li):
    fw = self.fw
    I = self.I
    SC = {n: self.scratch_once("rw_" + n, (256, L)) for n in RW_ARR}
    dSC = {n: self.dep_once("rw_" + n) for n in RW_ARR}
    scv = {n: SC[n].rearrange("(kt p) t -> p kt t", p=128) for n in RW_ARR}

    def vec2(es_, name, ap1d, dep):
        t = self.sb(es_, name, [128, 2])
        fw.dma("sp", t[:], ap1d.rearrange("(kt p) -> p kt", p=128), writes=[dep], slow=True)
        return t

    with ExitStack() as es:
        dpar = Dep()
        mu = [vec2(es, "rw_mu%d" % a, I["rwkv_mu_rkv"][li, a], dpar) for a in range(3)]
        muw = [vec2(es, "rw_muw%d" % a, I["rwkv_mu_wag"][li, a], dpar) for a in range(3)]
        w0 = [vec2(es, "rw_w0%d" % d, I["rwkv_w0"][li, d], dpar) for d in range(2)]
        a0 = [vec2(es, "rw_a0%d" % d, I["rwkv_a0"][li, d], dpar) for d in range(2)]
        k_k = vec2(es, "rw_kk", I["rwkv_k_k"][li], dpar)
        k_a = vec2(es, "rw_ka", I["rwkv_k_a"][li], dpar)
        r_k = vec2(es, "rw_rk", I["rwkv_r_k"][li].rearrange("h n -> (h n)"), dpar)
        tiny = self.sb(es, "rw_tiny", [128, 1])
        fw.op("dve", lambda e: e.memset(tiny[:], 1e-12), writes=[dpar])
        blk = self.sb(es, "rw_blk", [128, 128], BF16)
        with ExitStack() as es2:
            blkf = self.sb(es2, "rw_blkf", [128, 128])
            dblk = Dep()
            fw.dma("sp", blkf[:], I["c_blk"][:, :], writes=[dblk])
            fw.op("dve", lambda e: e.tensor_copy(out=blk[:], in_=blkf[:]), reads=[dblk], writes=[dpar])
            fw.barrier()
        w1 = [self.load_weight_bf(es, "rw_w1%d" % d, I["rwkv_w1"][li, d], 256, 64) for d in range(2)]
        a1 = [self.load_weight_bf(es, "rw_a1%d" % d, I["rwkv_a1"][li, d], 256, 64) for d in range(2)]
        g1 = self.load_weight_bf(es, "rw_g1", I["rwkv_g1"][li], 256, 128)
        g2 = self.load_weight_bf(es, "rw_g2", I["rwkv_g2"][li], 128, 256)

        def load64(name, ap):
            t = self.sb(es, name, [64, 256], BF16)
            dd = Dep()
            with ExitStack() as es2:
                tf = self.sb(es2, name + "f", [64, 256])
                fw.dma("sp", tf[:], ap, writes=[dd])
                fw.op("dve", lambda e: e.tensor_copy(out=t[:], in_=tf[:]), reads=[dd], writes=[dd])
                fw.barrier()
            return t, dd
        w2 = [load64("rw_w2%d" % d, I["rwkv_w2"][li, d]) for d in range(2)]
        a2 = [load64("rw_a2%d" % d, I["rwkv_a2"][li, d]) for d in range(2)]

        W = 512
        X = self.sb(es, "rw_X", [128, 4, 2, W + 2])
        dX = Dep()
        Q = self.sb(es, "rw_Q", [128, 3, 2, W])
        dQ = Dep()
        T1 = self.sb(es, "rw_T1", [128, 2, W])
        dT1 = Dep()
        XW = self.sb(es, "rw_XW", [128, 3, 2, W], BF16)
        dXW = Dep()
        Hh = self.sb(es, "rw_Hh", [128, W], BF16)
        dHh = Dep()
        AS = self.sb(es, "rw_AS", [128, 2, W])
        dAS = Dep()
        KK = self.sb(es, "rw_KK", [128, 2, W])
        dKK = Dep()
        SQ = self.sb(es, "rw_SQ", [128, 2, W], BF16)
        dSQ = Dep()
        RS = self.sb(es, "rw_RS", [128, W])
        dRS = Dep()
        O = {n: self.sb(es, "rw_O_" + n, [128, 2, W]) for n in ("g", "bonus", "lw", "kd", "b")}
        dO = {n: Dep() for n in O}
        rkv_src = [self.projT[OFF_RKVX + a * 256:OFF_RKVX + (a + 1) * 256, :].rearrange("(kt p) t -> p kt t", p=128) for a in range(4)]
        for bi, (t0, Wb) in enumerate(BLOCKS):
            lo = max(t0 - 1, 0)
            hi = min(t0 + Wb + 1, L)
            c0 = lo - (t0 - 1)
            if t0 == 0:
                fw.op("dve", lambda e: e.memset(X[:, :, :, 0:1], 0.0), writes=[dX])
            if t0 + Wb == L:
                fw.op("dve", lambda e: e.memset(X[:, :, :, Wb + 1:Wb + 2], 0.0), writes=[dX])
            for a in range(4):
                fw.dma("sp", X[:, a, :, c0:c0 + (hi - lo)], rkv_src[a][:, :, lo:hi], reads=[self.dep_proj], writes=[dX])
            for a in range(4):
                for kt in range(2):
                    ctr, lf, rt = X[:, a, kt, 1:Wb + 1], X[:, a, kt, 0:Wb], X[:, a, kt, 2:Wb + 2]
                    fw.op("pool", lambda e: e.tensor_tensor(out=T1[:, kt, 0:Wb], in0=lf, in1=rt, op=ALU.add), reads=[dX], writes=[dT1])
                    fw.op("dve", lambda e: e.scalar_tensor_tensor(out=T1[:, kt, 0:Wb], in0=T1[:, kt, 0:Wb], scalar=0.5, in1=ctr, op0=ALU.mult, op1=ALU.subtract), reads=[dT1, dX], writes=[dT1])
                    if a < 3:
                        fw.op("dve", lambda e: e.scalar_tensor_tensor(out=Q[:, a, kt, 0:Wb], in0=T1[:, kt, 0:Wb], scalar=mu[a][:, kt:kt + 1], in1=ctr, op0=ALU.mult, op1=ALU.add), reads=[dT1, dX, dpar], writes=[dQ])
                    else:
                        for i3 in range(3):
                            fw.op("dve", lambda e: e.scalar_tensor_tensor(out=XW[:, i3, kt, 0:Wb], in0=T1[:, kt, 0:Wb], scalar=muw[i3][:, kt:kt + 1], in1=ctr, op0=ALU.mult, op1=ALU.add), reads=[dT1, dX, dpar], writes=[dXW])
            fw.dma("pool", scv["r"][:, :, t0:t0 + Wb], Q[:, 0, :, 0:Wb], reads=[dQ], writes=[dSC["r"]])
            fw.dma("pool", scv["v"][:, :, t0:t0 + Wb], Q[:, 2, :, 0:Wb], reads=[dQ], writes=[dSC["v"]])
            ps, dp = self.next_ps()
            for kt in range(2):
                fw.op("pe", lambda e: e.matmul(ps[:, 0:Wb], lhsT=g1[0][:, kt, :], rhs=XW[:, 2, kt, 0:Wb], start=(kt == 0), stop=(kt == 1)), reads=[g1[1], dXW], writes=[dp])
            fw.op("act", lambda e: e.activation(out=Hh[:, 0:Wb], in_=ps[:, 0:Wb], func=AF.Sigmoid), reads=[dp], writes=[dHh])
            for ct in range(2):
                ps, dp = self.next_ps()
                fw.op("pe", lambda e: e.matmul(ps[:, 0:Wb], lhsT=g2[0][:, 0, ct * 128:(ct + 1) * 128], rhs=Hh[:, 0:Wb], start=True, stop=True), reads=[g2[1], dHh], writes=[dp])
                fw.op("act", lambda e: e.copy(out=O["g"][:, ct, 0:Wb], in_=ps[:, 0:Wb]), reads=[dp], writes=[dO["g"]])
            fw.dma("pool", scv["g"][:, :, t0:t0 + Wb], O["g"][:, :, 0:Wb], reads=[dO["g"]], writes=[dSC["g"]])
            for kt in range(2):
                fw.op("dve", lambda e: e.tensor_scalar(out=KK[:, kt, 0:Wb], in0=Q[:, 1, kt, 0:Wb], scalar1=k_k[:, kt:kt + 1], scalar2=None, op0=ALU.mult), reads=[dQ, dpar], writes=[dKK])
                fw.op("act", lambda e: e.activation(out=SQ[:, kt, 0:Wb], in_=KK[:, kt, 0:Wb], func=AF.Square), reads=[dKK], writes=[dSQ])
                ps, dp = self.next_ps()
                fw.op("pe", lambda e: e.matmul(ps[:, 0:Wb], lhsT=blk[:, :], rhs=SQ[:, kt, 0:Wb], start=True, stop=True), reads=[dSQ, dpar], writes=[dp])
                fw.op("act", lambda e: e.activation(out=RS[:, 0:Wb], in_=ps[:, 0:Wb], func=AF.Sqrt, bias=tiny[:, 0:1]), reads=[dp, dpar], writes=[dRS])
                fw.op("dve", lambda e: e.reciprocal(out=RS[:, 0:Wb], in_=RS[:, 0:Wb]), reads=[dRS], writes=[dRS])
                fw.op("dve", lambda e: e.tensor_tensor(out=KK[:, kt, 0:Wb], in0=KK[:, kt, 0:Wb], in1=RS[:, 0:Wb], op=ALU.mult), reads=[dKK, dRS], writes=[dKK])
            fw.dma("pool", scv["kkn"][:, :, t0:t0 + Wb], KK[:, :, 0:Wb], reads=[dKK], writes=[dSC["kkn"]])
            for kt in range(2):
                fw.op("pool", lambda e: e.tensor_tensor(out=T1[:, kt, 0:Wb], in0=Q[:, 0, kt, 0:Wb], in1=Q[:, 1, kt, 0:Wb], op=ALU.mult), reads=[dQ, dT1], writes=[dT1])
                fw.op("dve", lambda e: e.tensor_scalar(out=SQ[:, kt, 0:Wb], in0=T1[:, kt, 0:Wb], scalar1=r_k[:, kt:kt + 1], scalar2=None, op0=ALU.mult), reads=[dT1, dpar, dSQ], writes=[dSQ])
                ps, dp = self.next_ps()
                fw.op("pe", lambda e: e.matmul(ps[:, 0:Wb], lhsT=blk[:, :], rhs=SQ[:, kt, 0:Wb], start=True, stop=True), reads=[dSQ, dpar], writes=[dp])
                fw.op("dve", lambda e: e.tensor_tensor(out=O["bonus"][:, kt, 0:Wb], in0=ps[:, 0:Wb], in1=Q[:, 2, kt, 0:Wb], op=ALU.mult), reads=[dp, dQ], writes=[dO["bonus"]])
            fw.dma("pool", scv["bonus"][:, :, t0:t0 + Wb], O["bonus"][:, :, 0:Wb], reads=[dO["bonus"]], writes=[dSC["bonus"]])
            for d in range(2):
                ps, dp = self.next_ps()
                for kt in range(2):
                    fw.op("pe", lambda e: e.matmul(ps[0:64, 0:Wb], lhsT=w1[d][0][:, kt, :], rhs=XW[:, 0, kt, 0:Wb], start=(kt == 0), stop=(kt == 1)), reads=[w1[d][1], dXW], writes=[dp])
                fw.op("act", lambda e: e.activation(out=Hh[0:64, 0:Wb], in_=ps[0:64, 0:Wb], func=AF.Tanh), reads=[dp], writes=[dHh])
                for ct in range(2):
                    ps, dp = self.next_ps()
                    fw.op("pe", lambda e: e.matmul(ps[:, 0:Wb], lhsT=w2[d][0][:, ct * 128:(ct + 1) * 128], rhs=Hh[0:64, 0:Wb], start=True, stop=True), reads=[w2[d][1], dHh], writes=[dp])
                    fw.op("act", lambda e: e.activation(out=O["lw"][:, ct, 0:Wb], in_=ps[:, 0:Wb], func=AF.Sigmoid, bias=w0[d][:, ct:ct + 1]), reads=[dp, dpar], writes=[dO["lw"]])
                    fw.op("dve", lambda e: e.tensor_scalar(out=O["lw"][:, ct, 0:Wb], in0=O["lw"][:, ct, 0:Wb], scalar1=-0.6065306597126334, scalar2=None, op0=ALU.mult), reads=[dO["lw"]], writes=[dO["lw"]])
                fw.dma("pool", scv["lw%d" % d][:, :, t0:t0 + Wb], O["lw"][:, :, 0:Wb], reads=[dO["lw"]], writes=[dSC["lw%d" % d]])
                ps, dp = self.next_ps()
                for kt in range(2):
                    fw.op("pe", lambda e: e.matmul(ps[0:64, 0:Wb], lhsT=a1[d][0][:, kt, :], rhs=XW[:, 1, kt, 0:Wb], start=(kt == 0), stop=(kt == 1)), reads=[a1[d][1], dXW], writes=[dp])
                fw.op("act", lambda e: e.copy(out=Hh[0:64, 0:Wb], in_=ps[0:64, 0:Wb]), reads=[dp], writes=[dHh])
                for ct in range(2):
                    ps, dp = self.next_ps()
                    fw.op("pe", lambda e: e.matmul(ps[:, 0:Wb], lhsT=a2[d][0][:, ct * 128:(ct + 1) * 128], rhs=Hh[0:64, 0:Wb], start=True, stop=True), reads=[a2[d][1], dHh], writes=[dp])
                    fw.op("act", lambda e: e.activation(out=AS[:, ct, 0:Wb], in_=ps[:, 0:Wb], func=AF.Sigmoid, bias=a0[d][:, ct:ct + 1]), reads=[dp, dpar], writes=[dAS])
                for kt in range(2):
                    fw.op("dve", lambda e: e.tensor_scalar(out=O["kd"][:, kt, 0:Wb], in0=AS[:, kt, 0:Wb], scalar1=-1.0, scalar2=None, op0=ALU.add), reads=[dAS], writes=[dO["kd"]])
                    fw.op("dve", lambda e: e.tensor_scalar(out=O["kd"][:, kt, 0:Wb], in0=O["kd"][:, kt, 0:Wb], scalar1=k_a[:, kt:kt + 1], scalar2=1.0, op0=ALU.mult, op1=ALU.add), reads=[dO["kd"], dpar], writes=[dO["kd"]])
                    fw.op("pool", lambda e: e.tensor_tensor(out=O["kd"][:, kt, 0:Wb], in0=O["kd"][:, kt, 0:Wb], in1=Q[:, 1, kt, 0:Wb], op=ALU.mult), reads=[dO["kd"], dQ], writes=[dO["kd"]])
                    fw.op("pool", lambda e: e.tensor_tensor(out=O["b"][:, kt, 0:Wb], in0=KK[:, kt, 0:Wb], in1=AS[:, kt, 0:Wb], op=ALU.mult), reads=[dKK, dAS], writes=[dO["b"]])
                fw.dma("pool", scv["kd%d" % d][:, :, t0:t0 + Wb], O["kd"][:, :, 0:Wb], reads=[dO["kd"]], writes=[dSC["kd%d" % d]])
                fw.dma("pool", scv["b%d" % d][:, :, t0:t0 + Wb], O["b"][:, :, 0:Wb], reads=[dO["b"]], writes=[dSC["b%d" % d]])
    fw.barrier()
    self._rwkv_scan(li, SC, dSC, scv)


Builder.mix_rwkv = _mix_rwkv


def _rwkv_scan(self, li, SC, dSC, scv):
    fw = self.fw
    I = self.I
    with ExitStack() as es:
        yacc = self.sb(es, "rs_yacc", [128, 2, L])
        dyacc = Dep()
        tril_s = self.sb(es, "rs_tril_s", [128, 128])
        tril_i = self.sb(es, "rs_tril_i", [128, 128])
        trilT_s = self.sb(es, "rs_trilT_s", [128, 128])
        identb = self.sb(es, "rs_identb", [128, 128], BF16)
        dc = Dep()
        fw.dma("sp", tril_s[:], I["c_tril_s"][:, :], writes=[dc])
        fw.dma("sp", tril_i[:], I["c_tril_i"][:, :], writes=[dc])
        fw.dma("sp", trilT_s[:], I["c_trilT_s"][:, :], writes=[dc])
        fw.op("dve", lambda e: e.tensor_copy(out=identb[:], in_=self.ident[:]), reads=[self.d_const], writes=[dc])
        ones1 = self.sb(es, "rs_ones", [128, 128])
        fw.op("dve", lambda e: e.memset(ones1[:], 1.0), writes=[dc])
        for d in range(2):
            for kt in range(2):
                with ExitStack() as esw:
                    names = ("rt", "at", "kt", "bt", "kh", "bh", "vb")
                    A = {n: self.sb(esw, "rs_" + n, [128, LP], BF16) for n in names}
                    dA = Dep()
                    etot = self.sb(esw, "rs_etot", [128, NCH])
                    with ExitStack() as es3:
                        stg = [self.sb(es3, "rs_stg%d" % i, [128, L]) for i in range(2)]
                        dstg = [Dep(), Dep()]
                        cs = self.sb(es3, "rs_cs", [128, LP])
                        lwr = self.sb(es3, "rs_lwr", [128, LP])
                        E1 = self.sb(es3, "rs_E1", [128, LP])
                        E2 = self.sb(es3, "rs_E2", [128, LP])
                        dcs, dlw, dE1, dE2 = Dep(), Dep(), Dep(), Dep()
                        for n in names:
                            fw.op("pool", lambda e: e.memset(A[n][:, L:LP], 0.0), writes=[dA])

                        def ld(i, name):
                            fw.dma("sp", stg[i][:], scv[name][:, kt, :], reads=[dSC[name]], writes=[dstg[i]])
                            return stg[i][:, :] if d == 0 else stg[i][:, ::-1]

                        sv = ld(0, "lw%d" % d)
                        fw.op("dve", lambda e: e.memset(lwr[:, L:LP], 0.0), writes=[dlw])
                        fw.op("dve", lambda e: e.tensor_copy(out=lwr[:, 0:L], in_=sv), reads=[dstg[0]], writes=[dlw])
                        for c in range(NCH):
                            fw.op("dve", lambda e: e.tensor_tensor_scan(out=cs[:, c * 128:(c + 1) * 128], data0=ones1[:, :], data1=lwr[:, c * 128:(c + 1) * 128], initial=0.0, op0=ALU.mult, op1=ALU.add),
                                  reads=[dlw, dc], writes=[dcs])
                        fw.op("act", lambda e: e.activation(out=etot[:, :], in_=cs[:, 127::128], func=AF.Exp), reads=[dcs], writes=[dA])
                        fw.op("act", lambda e: e.activation(out=E1[:], in_=cs[:], func=AF.Exp), reads=[dcs], writes=[dE1])
                        sv = ld(1, "r")
                        fw.op("dve", lambda e: e.tensor_tensor(out=A["rt"][:, 0:L], in0=sv, in1=E1[:, 0:L], op=ALU.mult), reads=[dstg[1], dE1], writes=[dA])
                        fw.op("pool", lambda e: e.tensor_tensor(out=lwr[:], in0=cs[:], in1=lwr[:], op=ALU.subtract), reads=[dcs, dlw], writes=[dlw])
                        fw.op("act", lambda e: e.activation(out=E1[:], in_=lwr[:], func=AF.Exp), reads=[dlw, dE1], writes=[dE1])
                        sv = ld(0, "kkn")
                        fw.op("dve", lambda e: e.scalar_tensor_tensor(out=A["at"][:, 0:L], in0=sv, scalar=-1.0, in1=E1[:, 0:L], op0=ALU.mult, op1=ALU.mult), reads=[dstg[0], dE1], writes=[dA])
                        for c in range(NCH):
                            fw.op("dve", lambda e: e.tensor_scalar(out=lwr[:, c * 128:(c + 1) * 128], in0=cs[:, c * 128:(c + 1) * 128], scalar1=cs[:, c * 128 + 127:c * 128 + 128], scalar2=-1.0, op0=ALU.subtract, op1=ALU.mult),
                                  reads=[dcs, dlw], writes=[dlw])
                        fw.op("act", lambda e: e.activation(out=E1[:], in_=lwr[:], func=AF.Exp), reads=[dlw, dE1], writes=[dE1])
                        fw.op("act", lambda e: e.activation(out=E2[:], in_=cs[:], func=AF.Exp, scale=-1.0), reads=[dcs], writes=[dE2])
                        sv = ld(1, "kd%d" % d)
                        fw.op("dve", lambda e: e.tensor_tensor(out=A["kt"][:, 0:L], in0=sv, in1=E2[:, 0:L], op=ALU.mult), reads=[dstg[1], dE2], writes=[dA])
                        fw.op("pool", lambda e: e.tensor_tensor(out=A["kh"][:, 0:L], in0=sv, in1=E1[:, 0:L], op=ALU.mult), reads=[dstg[1], dE1], writes=[dA])
                        sv = ld(0, "b%d" % d)
                        fw.op("dve", lambda e: e.tensor_tensor(out=A["bt"][:, 0:L], in0=sv, in1=E2[:, 0:L], op=ALU.mult), reads=[dstg[0], dE2], writes=[dA])
                        fw.op("pool", lambda e: e.tensor_tensor(out=A["bh"][:, 0:L], in0=sv, in1=E1[:, 0:L], op=ALU.mult), reads=[dstg[0], dE1], writes=[dA])
                        sv = ld(1, "v")
                        fw.op("dve", lambda e: e.tensor_copy(out=A["vb"][:, 0:L], in_=sv), reads=[dstg[1]], writes=[dA])
                        fw.barrier()
                    with ExitStack() as es4:
                        S0 = self.sb(es4, "rs_S0", [128, 64])
                        S0b = self.sb(es4, "rs_S0b", [128, 64], BF16)
                        dS0 = [Dep(), Dep()]
                        dS0b = [Dep(), Dep()]
                        fw.op("dve", lambda e: e.memset(S0[:], 0.0), writes=dS0)
                        fw.op("dve", lambda e: e.memset(S0b[:], 0.0), writes=dS0b)
                        NHB = 2
                        def mk(nm, shape, dt=F32):
                            return [[self.sb(es4, "rs_%s_%d_%d" % (nm, hh, i), shape, dt) for i in range(NHB)] for hh in range(2)], [[Dep() for i in range(NHB)] for hh in range(2)]
                        TK, dTK = mk("TK", [128, 192], BF16)
                        Pm, dPm = mk("P", [128, 128])
                        PTm, dPTm = mk("PT", [128, 128])
                        XT, dXT = mk("XT", [128, 128])
                        XTb, dXTb = mk("XTb", [128, 128], BF16)
                        Ak, dAk = mk("Ak", [128, 3, 128], BF16)
                        P1b, dP1b = mk("P1b", [128, 64], BF16)
                        Zb, dZb = mk("Zb", [128, 64], BF16)
                        for c in range(NCH):
                            t0 = c * 128
                            Wv = min(128, L - t0)
                            C = slice(t0, t0 + 128)
                            for hh in range(2):
                                R = slice(64 * hh, 64 * hh + 64)
                                b_ = c % NHB
                                tk, dtk = TK[hh][b_], dTK[hh][b_]
                                ptk, dptk = self.next_ps()
                                for i3, n in enumerate(("vb", "kh", "bh")):
                                    fw.op("pe", lambda e: e.matmul(ptk[:, i3 * 64:(i3 + 1) * 64], lhsT=A[n][R, C], rhs=identb[R, 64 * hh:64 * hh + 64], start=True, stop=True), reads=[dA, dc], writes=[dptk])
                                fw.op("act", lambda e: e.copy(out=tk[:, :], in_=ptk[:, 0:192]), reads=[dptk], writes=[dtk])
                                pa, dpa = self.next_ps()
                                pb2, dpb2 = self.next_ps()
                                for i5, (l_, r_) in enumerate((("bt", "at"), ("at", "bt"), ("kt", "at"), ("kt", "rt"))):
                                    fw.op("pe", lambda e: e.matmul(pa[:, i5 * 128:(i5 + 1) * 128], lhsT=A[l_][R, C], rhs=A[r_][R, C], start=True, stop=True), reads=[dA], writes=[dpa])
                                fw.op("pe", lambda e: e.matmul(pb2[:, 0:128], lhsT=A["bt"][R, C], rhs=A["rt"][R, C], start=True, stop=True), reads=[dA], writes=[dpb2])
                                P_, dP_ = Pm[hh][b_], dPm[hh][b_]
                                PT_, dPT_ = PTm[hh][b_], dPTm[hh][b_]
                                X_, dX_ = XT[hh][b_], dXT[hh][b_]
                                ak, dak = Ak[hh][b_], dAk[hh][b_]
                                fw.op("dve", lambda e: e.tensor_tensor(out=PT_[:], in0=pa[:, 0:128], in1=tril_s[:], op=ALU.mult), reads=[dpa, dc], writes=[dPT_])
                                fw.op("dve", lambda e: e.tensor_tensor(out=P_[:], in0=pa[:, 128:256], in1=trilT_s[:], op=ALU.mult), reads=[dpa, dc], writes=[dP_])
                                fw.op("dve", lambda e: e.tensor_tensor(out=ak[:, 0, :], in0=pa[:, 256:384], in1=tril_s[:], op=ALU.mult), reads=[dpa, dc], writes=[dak])
                                fw.op("dve", lambda e: e.tensor_tensor(out=ak[:, 1, :], in0=pa[:, 384:512], in1=tril_i[:], op=ALU.mult), reads=[dpa, dc], writes=[dak])
                                fw.op("dve", lambda e: e.tensor_tensor(out=ak[:, 2, :], in0=pb2[:, 0:128], in1=tril_i[:], op=ALU.mult), reads=[dpb2, dc], writes=[dak])
                                fw.op("pool", lambda e: e.tensor_tensor(out=X_[:], in0=PT_[:], in1=self.ident[:], op=ALU.add), reads=[dPT_, self.d_const], writes=[dX_])
                                for s_ in range(6):
                                    pn, dpn = self.next_ps()
                                    fw.op("pe", lambda e: e.matmul(pn[:, 0:128], lhsT=PT_[:, :], rhs=P_[:, :], start=True, stop=True), reads=[dPT_, dP_], writes=[dpn])
                                    if s_ < 5:
                                        fw.op("pe", lambda e: e.matmul(pn[:, 128:256], lhsT=P_[:, :], rhs=PT_[:, :], start=True, stop=True), reads=[dPT_, dP_], writes=[dpn])
                                    fw.op("act", lambda e: e.copy(out=P_[:], in_=pn[:, 0:128]), reads=[dpn], writes=[dP_])
                                    if s_ < 5:
                                        fw.op("act", lambda e: e.copy(out=PT_[:], in_=pn[:, 128:256]), reads=[dpn], writes=[dPT_])
                                    px_, dpx_ = self.next_ps()
                                    fw.op("pe", lambda e: e.matmul(px_[:, 0:128], lhsT=P_[:, :], rhs=X_[:, :], start=True, stop=True), reads=[dP_, dX_], writes=[dpx_])
                                    fw.op("dve", lambda e: e.tensor_tensor(out=X_[:], in0=px_[:, 0:128], in1=X_[:], op=ALU.add), reads=[dpx_, dX_], writes=[dX_])
                                xb_, dxb_ = XTb[hh][b_], dXTb[hh][b_]
                                fw.op("act", lambda e: e.copy(out=xb_[:], in_=X_[:]), reads=[dX_], writes=[dxb_])
                                pp, dpp = self.next_ps()
                                fw.op("pe", lambda e: e.matmul(pp[:, 0:64], lhsT=A["at"][R, C], rhs=S0b[R, :], start=True, stop=False), reads=[dA, dS0b[hh]], writes=[dpp])
                                fw.op("pe", lambda e: e.matmul(pp[:, 0:64], lhsT=ak[:, 0, :], rhs=tk[:, 0:64], start=False, stop=True), reads=[dak, dtk], writes=[dpp])
                                p1, dp1 = P1b[hh][b_], dP1b[hh][b_]
                                fw.op("act", lambda e: e.copy(out=p1[:], in_=pp[:, 0:64]), reads=[dpp], writes=[dp1])
                                pz, dpz = self.next_ps()
                                fw.op("pe", lambda e: e.matmul(pz[:, 0:64], lhsT=xb_[:, :], rhs=p1[:, :], start=True, stop=True), reads=[dxb_, dp1], writes=[dpz])
                                z_, dz_ = Zb[hh][b_], dZb[hh][b_]
                                fw.op("dve", lambda e: e.tensor_copy(out=z_[:], in_=pz[:, 0:64]), reads=[dpz], writes=[dz_])
                                py, dpy = self.next_ps()
                                fw.op("pe", lambda e: e.matmul(py[R, 0:128], lhsT=S0b[R, :], rhs=A["rt"][R, C], start=True, stop=False), reads=[dA, dS0b[hh]], writes=[dpy])
                                fw.op("pe", lambda e: e.matmul(py[R, 0:128], lhsT=tk[:, 0:64], rhs=ak[:, 1, :], start=False, stop=False), reads=[dak, dtk], writes=[dpy])
                                fw.op("pe", lambda e: e.matmul(py[R, 0:128], lhsT=z_[:, :], rhs=ak[:, 2, :], start=False, stop=True), reads=[dak, dz_], writes=[dpy])
                                if d == 0:
                                    fw.op("act", lambda e: e.copy(out=yacc[R, kt, t0:t0 + Wv], in_=py[R, 0:Wv]), reads=[dpy], writes=[dyacc])
                                else:
                                    lo = L - (t0 + Wv)
                                    ya = yacc[R, kt, lo:lo + Wv]
                                    fw.op("dve", lambda e: e.tensor_tensor(out=ya[:, ::-1], in0=py[R, 0:Wv], in1=ya[:, ::-1], op=ALU.add), reads=[dpy, dyacc], writes=[dyacc])
                                pS, dpS = self.next_ps()
                                fw.op("pe", lambda e: e.matmul(pS[R, 0:64], lhsT=tk[:, 64:128], rhs=tk[:, 0:64], start=True, stop=False), reads=[dtk], writes=[dpS])
                                fw.op("pe", lambda e: e.matmul(pS[R, 0:64], lhsT=tk[:, 128:192], rhs=z_[:, :], start=False, stop=True), reads=[dtk, dz_], writes=[dpS])
                                fw.op("dve", lambda e: e.scalar_tensor_tensor(out=S0[R, :], in0=S0[R, :], scalar=etot[R, c:c + 1], in1=pS[R, 0:64], op0=ALU.mult, op1=ALU.add), reads=[dS0[hh], dA, dpS], writes=[dS0[hh]])
                                fw.op("act", lambda e: e.copy(out=S0b[R, :], in_=S0[R, :]), reads=[dS0[hh]], writes=[dS0b[hh]])
                        fw.barrier()
        fw.barrier()
        with ExitStack() as es5:
            dq = Dep()
            def vec2(name, ap1d):
                t = self.sb(es5, name, [128, 2])
                fw.dma("sp", t[:], ap1d.rearrange("(kt p) -> p kt", p=128), writes=[dq], slow=True)
                return t
            lnw = vec2("r3_lnw", I["rwkv_ln_w"][li])
            lnb = vec2("r3_lnb", I["rwkv_ln_b"][li])
            epsl = self.sb(es5, "r3_eps", [128, 1])
            fw.op("dve", lambda e: e.memset(epsl[:], 64e-5), writes=[dq])
            blk = self.sb(es5, "r3_blk", [128, 128], BF16)
            blkf = self.sb(es5, "r3_blkf", [128, 128])
            fw.dma("sp", blkf[:], I["c_blk"][:, :], writes=[dq])
            fw.op("dve", lambda e: e.tensor_copy(out=blk[:], in_=blkf[:]), reads=[dq], writes=[dq])
            W = 512
            yb = self.sb(es5, "r3_yb", [128, W], BF16)
            sq = self.sb(es5, "r3_sq", [128, W], BF16)
            mean = self.sb(es5, "r3_mean", [128, W])
            var = self.sb(es5, "r3_var", [128, W])
            yc = [self.sb(es5, "r3_yc%d" % i, [128, 2, W]) for i in range(2)]
            bg = [self.sb(es5, "r3_bg%d" % i, [128, 2, 2, W]) for i in range(2)]
            dt_ = Dep()
            dyc = [Dep(), Dep()]
            dbg = [Dep(), Dep()]
            yv = self.ycT.rearrange("(kt p) t -> p kt t", p=128)
            for bi, (t0, Wb) in enumerate(BLOCKS):
                y_, dy_ = yc[bi % 2], dyc[bi % 2]
                b_, db_ = bg[bi % 2], dbg[bi % 2]
                fw.dma("sp", b_[:, 0, :, 0:Wb], scv["bonus"][:, :, t0:t0 + Wb], reads=[dSC["bonus"]], writes=[db_])
                fw.dma("sp", b_[:, 1, :, 0:Wb], scv["g"][:, :, t0:t0 + Wb], reads=[dSC["g"]], writes=[db_])
                for kt in range(2):
                    ysl = yacc[:, kt, t0:t0 + Wb]
                    fw.op("act", lambda e: e.copy(out=yb[:, 0:Wb], in_=ysl), reads=[dyacc], writes=[dt_])
                    fw.op("pool", lambda e: e.tensor_tensor(out=sq[:, 0:Wb], in0=ysl, in1=ysl, op=ALU.mult), reads=[dyacc], writes=[dt_])
                    pm, dpm = self.next_ps()
                    pq, dpq = self.next_ps()
                    fw.op("pe", lambda e: e.matmul(pm[:, 0:Wb], lhsT=blk[:, :], rhs=yb[:, 0:Wb], start=True, stop=True), reads=[dq, dt_], writes=[dpm])
                    fw.op("pe", lambda e: e.matmul(pq[:, 0:Wb], lhsT=blk[:, :], rhs=sq[:, 0:Wb], start=True, stop=True), reads=[dq, dt_], writes=[dpq])
                    fw.op("act", lambda e: e.mul(out=mean[:, 0:Wb], in_=pm[:, 0:Wb], mul=1.0 / 64), reads=[dpm], writes=[dt_])
                    fw.op("dve", lambda e: e.tensor_tensor(out=var[:, 0:Wb], in0=mean[:, 0:Wb], in1=mean[:, 0:Wb], op=ALU.mult), reads=[dt_], writes=[dt_])
                    fw.op("dve", lambda e: e.scalar_tensor_tensor(out=var[:, 0:Wb], in0=pq[:, 0:Wb], scalar=1.0 / 64, in1=var[:, 0:Wb], op0=ALU.mult, op1=ALU.subtract), reads=[dpq, dt_], writes=[dt_])
                    fw.op("act", lambda e: e.activation(out=var[:, 0:Wb], in_=var[:, 0:Wb], func=AF.Sqrt, bias=epsl[:, 0:1]), reads=[dt_, dq], writes=[dt_])
                    fw.op("dve", lambda e: e.reciprocal(out=var[:, 0:Wb], in_=var[:, 0:Wb]), reads=[dt_], writes=[dt_])
                    fw.op("pool", lambda e: e.tensor_tensor(out=y_[:, kt, 0:Wb], in0=ysl, in1=mean[:, 0:Wb], op=ALU.subtract), reads=[dyacc, dt_], writes=[dy_])
                    fw.op("dve", lambda e: e.tensor_tensor(out=y_[:, kt, 0:Wb], in0=y_[:, kt, 0:Wb], in1=var[:, 0:Wb], op=ALU.mult), reads=[dy_, dt_], writes=[dy_])
                    fw.op("dve", lambda e: e.tensor_scalar(out=y_[:, kt, 0:Wb], in0=y_[:, kt, 0:Wb], scalar1=lnw[:, kt:kt + 1], scalar2=lnb[:, kt:kt + 1], op0=ALU.mult, op1=ALU.add), reads=[dy_, dq], writes=[dy_])
                    fw.op("pool", lambda e: e.tensor_tensor(out=y_[:, kt, 0:Wb], in0=y_[:, kt, 0:Wb], in1=b_[:, 0, kt, 0:Wb], op=ALU.add), reads=[dy_, db_], writes=[dy_])
                    fw.op("dve", lambda e: e.tensor_tensor(out=y_[:, kt, 0:Wb], in0=y_[:, kt, 0:Wb], in1=b_[:, 1, kt, 0:Wb], op=ALU.mult), reads=[dy_, db_], writes=[dy_])
                fw.dma("pool", yv[:, :, t0:t0 + Wb], y_[:, :, 0:Wb], reads=[dy_], writes=[self.dep_yc])
    fw.barrier()


Builder._rwkv_scan = _rwkv_scan
```

```python
import numpy as np
from contextlib import ExitStack
import concourse.bass as bass
import concourse.mybir as mybir
from concourse.bass_utils import run_bass_kernel_spmd

F32 = mybir.dt.float32
BF16 = mybir.dt.bfloat16
AF = mybir.ActivationFunctionType
ALU = mybir.AluOpType

D = 1024
SEQ = 4096
NMETA = 16
L = SEQ + NMETA
DEPTH = 2
DFF = 4096
NIN = 5896
EPS = 1e-6
NDS = 48

OFF_U, OFF_Z, OFF_XBC, OFF_DT, OFF_RKVX, OFF_G = 0, 256, 768, 1792, 1800, 2824

BLOCKS = [(i * 512, 512) for i in range(8)] + [(4096, 16)]


class Dep:
    __slots__ = ("w", "r", "x")

    def __init__(self, excl=False):
        self.w = None
        self.r = []
        self.x = excl


class FW:
    def __init__(self, nc, es):
        self.nc = nc
        self.engs = dict(pe=nc.tensor, act=nc.scalar, dve=nc.vector, pool=nc.gpsimd, sp=nc.sync)
        self.sem = {k: es.enter_context(nc.semaphore("s_" + k)) for k in self.engs}
        self.cnt = {k: 0 for k in self.engs}
        self.seen = {k: {} for k in self.engs}
        self.dsem = [es.enter_context(nc.semaphore("d%d" % i)) for i in range(NDS)]
        self.dval = [0] * NDS
        self.dnext = 0
        self.dnext2 = 0
        self.nins = 0

    def _wait(self, eng, ev):
        key, val = ev
        if self.seen[eng].get(key, 0) >= val:
            return
        self.seen[eng][key] = val
        sem = self.sem[key[1]] if key[0] == "e" else self.dsem[key[1]]
        self.engs[eng].wait_ge(sem, val)

    def _deps(self, eng, reads, writes):
        me = ("e", eng)
        for d in reads:
            if d.w is not None:
                self._wait(eng, d.w)
        for d in writes:
            if d.w is not None and d.w[0] != me:
                self._wait(eng, d.w)
            for r in d.r:
                if r[0] != me:
                    self._wait(eng, r)

    def _post(self, ev, reads, writes):
        for d in writes:
            d.w = ev
            d.r = []
        for d in reads:
            d.r = [r for r in d.r if r[0] != ev[0]] + [ev]

    def op(self, eng, fn, reads=(), writes=()):
        xs = [d for d in reads if d.x]
        if xs:
            writes = list(writes) + [d for d in xs if d not in writes]
            reads = [d for d in reads if not d.x]
        self._deps(eng, reads, writes)
        ins = fn(self.engs[eng])
        self.cnt[eng] += 1
        self.nins += 1
        ins.then_inc(self.sem[eng], 1)
        self._post((("e", eng), self.cnt[eng]), reads, writes)

    def dma(self, q, out, in_, reads=(), writes=(), slow=False):
        self._deps(q, reads, writes)
        half = NDS // 2
        if q == "pool":
            i = half + self.dnext2
            self.dnext2 = (self.dnext2 + 1) % half
        else:
            i = self.dnext
            self.dnext = (self.dnext + 1) % half
        if self.dval[i] > 0:
            self._wait(q, (("d", i), self.dval[i]))
        self.dval[i] += 16
        self.nins += 1
        if slow:
            self.engs[q].dma_start(out=out, in_=in_, allow_slow_non_contiguous=True).then_inc(self.dsem[i], 16)
        else:
            self.engs[q].dma_start(out=out, in_=in_).then_inc(self.dsem[i], 16)
        self._post((("d", i), self.dval[i]), reads, writes)

    def barrier(self):
        for e in self.engs:
            for e2 in self.engs:
                if self.cnt[e2] > 0:
                    self._wait(e, (("e", e2), self.cnt[e2]))
            for i in range(NDS):
                if self.dval[i] > 0:
                    self._wait(e, (("d", i), self.dval[i]))


def col_tiles(lo, hi):
    out = []
    c = lo
    while c < hi:
        m = min(128, hi - c)
        out.append((c, m))
        c += m
    return out


IN_TILES = (col_tiles(OFF_U, OFF_Z) + col_tiles(OFF_Z, OFF_XBC) + col_tiles(OFF_XBC, OFF_DT)
            + col_tiles(OFF_DT, OFF_RKVX) + col_tiles(OFF_RKVX, OFF_G) + col_tiles(OFF_G, NIN))


class Builder:
    def __init__(self, cfg):
        self.cfg = cfg
        nc = self.nc = bass.Bass("TRN2", target_bir_lowering=False)
        self.I = {}
        self.es = ExitStack()

    def inp(self, name, shape):
        t = self.nc.dram_tensor(name, list(shape), F32, kind="ExternalInput").ap()
        self.I[name] = t
        return t

    def scratch(self, name, shape, dt=F32):
        kind = "ExternalOutput" if name in self.cfg.get("dump", ()) else "Internal"
        return self.nc.dram_tensor(name, list(shape), dt, kind=kind).ap()

    def sb(self, es, name, shape, dt=F32):
        self.uid = getattr(self, "uid", 0) + 1
        return es.enter_context(self.nc.sbuf_tensor("%s_%d" % (name, self.uid), list(shape), dt))

    def build(self):
        nc = self.nc
        cfg = self.cfg
        with self.es as es:
            fw = self.fw = FW(nc, es)
            I = self.I
            x = self.inp("x", (SEQ, D))
            for name, shape in WEIGHT_SHAPES:
                self.inp(name, shape)
            self.inp("c_ident", (128, 128))
            self.inp("c_iota", (128, 512))
            self.inp("c_triu", (128, 128))
            self.inp("c_padm", (128, 8))
            self.inp("c_blk", (128, 128))
            self.inp("c_trilT_s", (128, 128))
            self.inp("c_mneg", (128, 128))
            self.inp("c_tril_s", (128, 128))
            self.inp("c_tril_i", (128, 128))
            out = nc.dram_tensor("out", [SEQ, D], F32, kind="ExternalOutput").ap()
            self.hT = self.scratch("hT", (D, L))
            self.projT = self.scratch("projT", (NIN, L))
            self.yaT = self.scratch("yaT", (256, L))
            self.ybT = self.scratch("ybT", (512, L))
            self.ycT = self.scratch("ycT", (256, L))
            self.dep_hT = Dep()
            self.dep_proj = Dep()
            self.dep_ya, self.dep_yb, self.dep_yc = Dep(), Dep(), Dep()

            self.ident = self.sb(es, "ident", [128, 128])
            self.ones_bf = self.sb(es, "ones_bf", [128, 128], BF16)
            self.d_const = Dep()
            fw.dma("sp", self.ident[:], I["c_ident"][:, :], writes=[self.d_const])
            fw.op("dve", lambda e: e.memset(self.ones_bf[:], 1.0), writes=[self.d_const])
            self.one_t = self.sb(es, "one_t", [128, 1])
            fw.op("dve", lambda e: e.memset(self.one_t[:], 1.0), writes=[self.d_const])
            self.ps = [es.enter_context(nc.psum_tensor("ps%d" % i, [128, 512], F32)) for i in range(8)]
            self.dps = [Dep(True) for _ in range(8)]
            self.psn = 0

            self.phase0(x)
            nlayers = cfg.get("layers", DEPTH)
            for li in range(nlayers):
                self.phase1(li)
                if cfg.get("fake_mix", False):
                    self.fake_mix()
                else:
                    self.mixers(li)
                if cfg.get("stop_after_mix", False):
                    break
                self.phase3a(li)
                self.phase3b(li)
            if not cfg.get("stop_after_mix", False):
                self.phase_final(out)
            fw.barrier()
        return nc

    def next_ps(self):
        i = self.psn
        self.psn = (i + 1) % 8
        return self.ps[i], self.dps[i]

    def phase0(self, x):
        fw = self.fw
        I = self.I
        with ExitStack() as es:
            xin = [self.sb(es, "p0_x%d" % i, [128, D]) for i in range(2)]
            dxin = [Dep(), Dep()]
            ho = [self.sb(es, "p0_h%d" % i, [128, 8, 128]) for i in range(2)]
            dho = [Dep(), Dep()]
            ntile = (L + 127) // 128
            for ti in range(ntile):
                t0 = ti * 128
                w = min(128, L - t0)
                xi, dx = xin[ti % 2], dxin[ti % 2]
                if ti == 0:
                    fw.dma("sp", xi[0:NMETA, :], I["meta_tokens"][:, :], writes=[dx])
                    fw.dma("sp", xi[NMETA:128, :], x[0:128 - NMETA, :], writes=[dx])
                else:
                    fw.dma("sp", xi[0:w, :], x[t0 - NMETA:t0 - NMETA + w, :], writes=[dx])
                h, dh = ho[ti % 2], dho[ti % 2]
                for kt in range(8):
                    ps, dp = self.next_ps()
                    fw.op("pe", lambda e: e.transpose(out=ps[:, 0:w], in_=xi[0:w, kt * 128:(kt + 1) * 128],
                                                      identity=self.ident[0:w, 0:w]),
                          reads=[dx, self.d_const], writes=[dp])
                    eng = "act" if kt % 2 == 0 else "dve"
                    if eng == "act":
                        fw.op("act", lambda e: e.copy(out=h[:, kt, 0:w], in_=ps[:, 0:w]), reads=[dp], writes=[dh])
                    else:
                        fw.op("dve", lambda e: e.tensor_copy(out=h[:, kt, 0:w], in_=ps[:, 0:w]), reads=[dp], writes=[dh])
                fw.dma("pool", self.hT.rearrange("(kt p) t -> p kt t", p=128)[:, :, t0:t0 + w], h[:, :, 0:w],
                       reads=[dh], writes=[self.dep_hT])
        fw.barrier()

    def load_weight_bf(self, es, name, w_ap, K, N, scale_ap=None, chunk=512):
        fw = self.fw
        kt_n = K // 128
        wbf = self.sb(es, name, [128, kt_n, N], BF16)
        dw = Dep()
        with ExitStack() as es2:
            stg = [self.sb(es2, name + "_stg%d" % i, [128, chunk]) for i in range(3)]
            dstg = [Dep() for _ in range(3)]
            sc = None
            if scale_ap is not None:
                sc = self.sb(es2, name + "_sc", [128, kt_n])
                dsc = Dep()
                fw.dma("sp", sc[:], scale_ap.rearrange("(kt p) -> p kt", p=128), writes=[dsc], slow=True)
            n = 0
            for kt in range(kt_n):
                for c0 in range(0, N, chunk):
                    cw = min(chunk, N - c0)
                    s, ds = stg[n % 3], dstg[n % 3]
                    fw.dma("sp", s[:, 0:cw], w_ap[kt * 128:(kt + 1) * 128, c0:c0 + cw], writes=[ds])
                    eng = "dve" if n % 2 == 0 else "pool"
                    if sc is not None:
                        fw.op(eng, lambda e: e.tensor_scalar(out=wbf[:, kt, c0:c0 + cw], in0=s[:, 0:cw],
                                                             scalar1=sc[:, kt:kt + 1], scalar2=None, op0=ALU.mult),
                              reads=[ds, dsc], writes=[dw])
                    else:
                        fw.op(eng, lambda e: e.tensor_copy(out=wbf[:, kt, c0:c0 + cw], in_=s[:, 0:cw]),
                              reads=[ds], writes=[dw])
                    n += 1
            fw.barrier()
        return wbf, dw

    def rmsnorm_block(self, h, dh, hn, dhn, sq, dsq, rstd, drstd, W):
        fw = self.fw
        for kt in range(8):
            fw.op("act", lambda e: e.activation(out=sq[:, kt, 0:W], in_=h[:, kt, 0:W], func=AF.Square),
                  reads=[dh], writes=[dsq])
        ps, dp = self.next_ps()
        for kt in range(8):
            fw.op("pe", lambda e: e.matmul(ps[:, 0:W], lhsT=self.ones_bf[:, :], rhs=sq[:, kt, 0:W],
                                           start=(kt == 0), stop=(kt == 7)),
                  reads=[dsq, self.d_const], writes=[dp])
        fw.op("act", lambda e: e.activation(out=rstd[:, 0:W], in_=ps[:, 0:W], func=AF.Sqrt, bias=self.eps_t[:, 0:1],
                                            scale=1.0 / D),
              reads=[dp, self.d_const], writes=[drstd])
        fw.op("dve", lambda e: e.reciprocal(out=rstd[:, 0:W], in_=rstd[:, 0:W]), reads=[drstd], writes=[drstd])
        for kt in range(8):
            eng = "dve" if kt % 2 == 0 else "pool"
            fw.op(eng, lambda e: e.tensor_tensor(out=hn[:, kt, 0:W], in0=h[:, kt, 0:W], in1=rstd[:, 0:W], op=ALU.mult),
                  reads=[dh, drstd], writes=[dhn])

    def ensure_eps(self, es):
        self.eps_t = self.sb(es, "eps_t", [128, 1])
        self.fw.op("dve", lambda e: e.memset(self.eps_t[:], EPS), writes=[self.d_const])

    def phase1(self, li):
        fw = self.fw
        I = self.I
        with ExitStack() as es:
            self.ensure_eps(es)
            wbf, dw = self.load_weight_bf(es, "p1_w", I["w_in"][li], D, NIN, scale_ap=I["mix_norm_w"][li])
            h = [self.sb(es, "p1_h%d" % i, [128, 8, 512]) for i in range(2)]
            dh = [Dep(), Dep()]
            sq = self.sb(es, "p1_sq", [128, 8, 512], BF16)
            dsq = Dep()
            rstd = self.sb(es, "p1_rstd", [128, 512])
            drstd = Dep()
            hn = [self.sb(es, "p1_hn%d" % i, [128, 8, 512], BF16) for i in range(2)]
            dhn = [Dep(), Dep()]
            stg = [self.sb(es, "p1_o%d" % i, [128, 512]) for i in range(4)]
            dstg = [Dep() for _ in range(4)]
            hTv = self.hT.rearrange("(kt p) t -> p kt t", p=128)
            ns = 0
            for bi, (t0, W) in enumerate(BLOCKS):
                hb, dhb = h[bi % 2], dh[bi % 2]
                fw.dma("sp", hb[:, :, 0:W], hTv[:, :, t0:t0 + W], reads=[self.dep_hT], writes=[dhb])
                hnb, dhnb = hn[bi % 2], dhn[bi % 2]
                self.rmsnorm_block(hb, dhb, hnb, dhnb, sq, dsq, rstd, drstd, W)
                for (c0, M) in IN_TILES:
                    ps, dp = self.next_ps()
                    for kt in range(8):
                        fw.op("pe", lambda e: e.matmul(ps[0:M, 0:W], lhsT=wbf[:, kt, c0:c0 + M], rhs=hnb[:, kt, 0:W],
                                                       start=(kt == 0), stop=(kt == 7)),
                              reads=[dhnb, dw], writes=[dp])
                    s, ds = stg[ns % 4], dstg[ns % 4]
                    if ns % 2 == 0:
                        fw.op("act", lambda e: e.copy(out=s[0:M, 0:W], in_=ps[0:M, 0:W]), reads=[dp], writes=[ds])
                    else:
                        fw.op("dve", lambda e: e.tensor_copy(out=s[0:M, 0:W], in_=ps[0:M, 0:W]), reads=[dp], writes=[ds])
                    fw.dma("pool", self.projT[c0:c0 + M, t0:t0 + W], s[0:M, 0:W], reads=[ds], writes=[self.dep_proj])
                    ns += 1
        fw.barrier()

    def fake_mix(self):
        fw = self.fw
        with ExitStack() as es:
            t = self.sb(es, "fm_t", [128, L])
            dt_ = Dep()
            for (dst, ddst, src0, n) in ((self.yaT, self.dep_ya, OFF_U, 2), (self.ybT, self.dep_yb, OFF_Z, 4),
                                         (self.ycT, self.dep_yc, OFF_RKVX, 2)):
                for j in range(n):
                    fw.dma("sp", t[:, :], self.projT[src0 + j * 128:src0 + (j + 1) * 128, :], reads=[self.dep_proj],
                           writes=[dt_])
                    fw.dma("sp", dst[j * 128:(j + 1) * 128, :], t[:, :], reads=[dt_], writes=[ddst])
        fw.barrier()

    def mixers(self, li):
        which = self.cfg.get("mix", ("s5", "ssd", "rwkv"))
        if "s5" in which:
            self.mix_s5(li)
        if "ssd" in which:
            self.mix_ssd(li)
        if "rwkv" in which:
            self.mix_rwkv(li)

    def phase3a(self, li):
        fw = self.fw
        I = self.I
        W = 512
        with ExitStack() as es:
            pa, dpa = self.load_weight_bf(es, "p3_pa", I["proj_a"][li], 256, D)
            pb, dpb = self.load_weight_bf(es, "p3_pb", I["proj_b"][li], 512, D)
            pc, dpc = self.load_weight_bf(es, "p3_pc", I["proj_c"][li], 256, D)
            wo, dwo = self.load_weight_bf(es, "p3_wo", I["w_out"][li], D, D)
            ystg = [self.sb(es, "p3_ys%d" % i, [128, 8, W]) for i in range(2)]
            dystg = [Dep(), Dep()]
            ybf = [self.sb(es, "p3_yb%d" % i, [128, 8, W], BF16) for i in range(2)]
            dybf = [Dep(), Dep()]
            g = [self.sb(es, "p3_g%d" % i, [128, 3, W]) for i in range(2)]
            dg = [Dep(), Dep()]
            mrg = self.sb(es, "p3_m", [128, 8, W], BF16)
            dmrg = Dep()
            tmp = [self.sb(es, "p3_t%d" % i, [128, W]) for i in range(2)]
            dtmp = [Dep(), Dep()]
            h = [self.sb(es, "p3_h%d" % i, [128, 8, W]) for i in range(2)]
            dh = [Dep(), Dep()]
            hTv = self.hT.rearrange("(kt p) t -> p kt t", p=128)
            gv = self.projT[OFF_G:NIN, :].rearrange("(b kt p) t -> p b kt t", p=128, b=3)
            ng = 0
            for bi, (t0, Wb) in enumerate(BLOCKS):
                ys, dys = ystg[bi % 2], dystg[bi % 2]
                yb, dyb = ybf[bi % 2], dybf[bi % 2]
                hb, dhb = h[bi % 2], dh[bi % 2]
                fw.dma("sp", ys[:, 0:2, 0:Wb], self.yaT.rearrange("(kt p) t -> p kt t", p=128)[:, :, t0:t0 + Wb],
                       reads=[self.dep_ya], writes=[dys])
                fw.dma("sp", ys[:, 2:6, 0:Wb], self.ybT.rearrange("(kt p) t -> p kt t", p=128)[:, :, t0:t0 + Wb],
                       reads=[self.dep_yb], writes=[dys])
                fw.dma("sp", ys[:, 6:8, 0:Wb], self.ycT.rearrange("(kt p) t -> p kt t", p=128)[:, :, t0:t0 + Wb],
                       reads=[self.dep_yc], writes=[dys])
                fw.dma("sp", hb[:, :, 0:Wb], hTv[:, :, t0:t0 + Wb], reads=[self.dep_hT], writes=[dhb])
                for kt in range(8):
                    eng = "dve" if kt % 2 == 0 else "pool"
                    fw.op(eng, lambda e: e.tensor_copy(out=yb[:, kt, 0:Wb], in_=ys[:, kt, 0:Wb]), reads=[dys], writes=[dyb])
                for dtile in range(8):
                    gg, dgg = g[ng % 2], dg[ng % 2]
                    ng += 1
                    fw.dma("sp", gg[:, :, 0:Wb], gv[:, :, dtile, t0:t0 + Wb], reads=[self.dep_proj], writes=[dgg])
                    fw.op("act", lambda e: e.activation(out=gg[:, :, 0:Wb], in_=gg[:, :, 0:Wb], func=AF.Sigmoid),
                          reads=[dgg], writes=[dgg])
                    tm, dtm = tmp[dtile % 2], dtmp[dtile % 2]
                    for br, (wt, dwt, k0, nk) in enumerate(((pa, dpa, 0, 2), (pb, dpb, 2, 4), (pc, dpc, 6, 2))):
                        ps, dp = self.next_ps()
                        for k in range(nk):
                            fw.op("pe", lambda e: e.matmul(ps[:, 0:Wb], lhsT=wt[:, k, dtile * 128:(dtile + 1) * 128],
                                                           rhs=yb[:, k0 + k, 0:Wb], start=(k == 0), stop=(k == nk - 1)),
                                  reads=[dyb, dwt], writes=[dp])
                        if br == 0:
                            fw.op("dve", lambda e: e.tensor_tensor(out=tm[:, 0:Wb], in0=ps[:, 0:Wb], in1=gg[:, 0, 0:Wb],
                                                                   op=ALU.mult), reads=[dp, dgg], writes=[dtm])
                        else:
                            fw.op("dve", lambda e: e.tensor_tensor(out=gg[:, br, 0:Wb], in0=ps[:, 0:Wb],
                                                                   in1=gg[:, br, 0:Wb], op=ALU.mult),
                                  reads=[dp, dgg], writes=[dgg])
                            if br == 1:
                                fw.op("dve", lambda e: e.tensor_tensor(out=tm[:, 0:Wb], in0=tm[:, 0:Wb],
                                                                       in1=gg[:, 1, 0:Wb], op=ALU.add),
                                      reads=[dtm, dgg], writes=[dtm])
                            else:
                                fw.op("dve", lambda e: e.tensor_tensor(out=mrg[:, dtile, 0:Wb], in0=tm[:, 0:Wb],
                                                                       in1=gg[:, 2, 0:Wb], op=ALU.add),
                                      reads=[dtm, dgg], writes=[dmrg])
                for dtile in range(8):
                    ps, dp = self.next_ps()
                    for k in range(8):
                        fw.op("pe", lambda e: e.matmul(ps[:, 0:Wb], lhsT=wo[:, k, dtile * 128:(dtile + 1) * 128],
                                                       rhs=mrg[:, k, 0:Wb], start=(k == 0), stop=(k == 7)),
                              reads=[dmrg, dwo], writes=[dp])
                    fw.op("dve", lambda e: e.tensor_tensor(out=hb[:, dtile, 0:Wb], in0=ps[:, 0:Wb], in1=hb[:, dtile, 0:Wb],
                                                           op=ALU.add), reads=[dp, dhb], writes=[dhb])
                fw.dma("pool", hTv[:, :, t0:t0 + Wb], hb[:, :, 0:Wb], reads=[dhb], writes=[self.dep_hT])
        fw.barrier()

    def phase3b(self, li):
        fw = self.fw
        I = self.I
        W = 256
        blocks = []
        for (t0, Wb) in BLOCKS:
            for s in range(0, Wb, W):
                blocks.append((t0 + s, min(W, Wb - s)))
        with ExitStack() as es:
            self.ensure_eps(es)
            w1, dw1 = self.load_weight_bf(es, "p4_w1", I["mlp_w1"][li], D, DFF, scale_ap=I["mlp_norm_w"][li])
            w2, dw2 = self.load_weight_bf(es, "p4_w2", I["mlp_w2"][li], DFF, D)
            h = [self.sb(es, "p4_h%d" % i, [128, 8, W]) for i in range(2)]
            dh = [Dep(), Dep()]
            sq = self.sb(es, "p4_sq", [128, 8, W], BF16)
            dsq = Dep()
            rstd = self.sb(es, "p4_rstd", [128, W])
            drstd = Dep()
            hn = self.sb(es, "p4_hn", [128, 8, W], BF16)
            dhn = Dep()
            act = self.sb(es, "p4_act", [128, 32, W], BF16)
            dact = Dep()
            rl = [self.sb(es, "p4_rl%d" % i, [128, W]) for i in range(2)]
            drl = [Dep(), Dep()]
            hTv = self.hT.rearrange("(kt p) t -> p kt t", p=128)
            for bi, (t0, Wb) in enumerate(blocks):
                hb, dhb = h[bi % 2], dh[bi % 2]
                fw.dma("sp", hb[:, :, 0:Wb], hTv[:, :, t0:t0 + Wb], reads=[self.dep_hT], writes=[dhb])
                self.rmsnorm_block(hb, dhb, hn, dhn, sq, dsq, rstd, drstd, Wb)
                for f in range(32):
                    ps, dp = self.next_ps()
                    for k in range(8):
                        fw.op("pe", lambda e: e.matmul(ps[:, 0:Wb], lhsT=w1[:, k, f * 128:(f + 1) * 128],
                                                       rhs=hn[:, k, 0:Wb], start=(k == 0), stop=(k == 7)),
                              reads=[dhn, dw1], writes=[dp])
                    r, dr = rl[f % 2], drl[f % 2]
                    fw.op("act", lambda e: e.activation(out=r[:, 0:Wb], in_=ps[:, 0:Wb], func=AF.Relu),
                          reads=[dp], writes=[dr])
                    eng = "dve" if f % 2 == 0 else "pool"
                    fw.op(eng, lambda e: e.tensor_tensor(out=act[:, f, 0:Wb], in0=r[:, 0:Wb], in1=r[:, 0:Wb], op=ALU.mult),
                          reads=[dr], writes=[dact])
                for dtile in range(8):
                    ps, dp = self.next_ps()
                    for f in range(32):
                        fw.op("pe", lambda e: e.matmul(ps[:, 0:Wb], lhsT=w2[:, f, dtile * 128:(dtile + 1) * 128],
                                                       rhs=act[:, f, 0:Wb], start=(f == 0), stop=(f == 31)),
                              reads=[dact, dw2], writes=[dp])
                    fw.op("dve", lambda e: e.tensor_tensor(out=hb[:, dtile, 0:Wb], in0=ps[:, 0:Wb], in1=hb[:, dtile, 0:Wb],
                                                           op=ALU.add), reads=[dp, dhb], writes=[dhb])
                fw.dma("pool", hTv[:, :, t0:t0 + Wb], hb[:, :, 0:Wb], reads=[dhb], writes=[self.dep_hT])
        fw.barrier()

    def phase_final(self, out):
        fw = self.fw
        I = self.I
        with ExitStack() as es:
            self.ensure_eps(es)
            fnw = self.sb(es, "pf_w", [128, 8])
            dfnw = Dep()
            fw.dma("sp", fnw[:], I["final_norm_w"].rearrange("(kt p) -> p kt", p=128), writes=[dfnw], slow=True)
            h = [self.sb(es, "pf_h%d" % i, [128, 8, 512]) for i in range(2)]
            dh = [Dep(), Dep()]
            sq = self.sb(es, "pf_sq", [128, 8, 512], BF16)
            dsq = Dep()
            rstd = self.sb(es, "pf_rstd", [128, 512])
            drstd = Dep()
            o = [self.sb(es, "pf_o%d" % i, [128, D]) for i in range(2)]
            do = [Dep(), Dep()]
            dout = Dep()
            hTv = self.hT.rearrange("(kt p) t -> p kt t", p=128)
            no = 0
            for bi in range(8):
                t0 = NMETA + bi * 512
                W = 512
                hb, dhb = h[bi % 2], dh[bi % 2]
                fw.dma("sp", hb[:, :, 0:W], hTv[:, :, t0:t0 + W], reads=[self.dep_hT], writes=[dhb])
                for kt in range(8):
                    fw.op("act", lambda e: e.activation(out=sq[:, kt, 0:W], in_=hb[:, kt, 0:W], func=AF.Square),
                          reads=[dhb], writes=[dsq])
                ps, dp = self.next_ps()
                for kt in range(8):
                    fw.op("pe", lambda e: e.matmul(ps[:, 0:W], lhsT=self.ones_bf[:, :], rhs=sq[:, kt, 0:W],
                                                   start=(kt == 0), stop=(kt == 7)),
                          reads=[dsq, self.d_const], writes=[dp])
                fw.op("act", lambda e: e.activation(out=rstd[:, 0:W], in_=ps[:, 0:W], func=AF.Sqrt,
                                                    bias=self.eps_t[:, 0:1], scale=1.0 / D),
                      reads=[dp, self.d_const], writes=[drstd])
                fw.op("dve", lambda e: e.reciprocal(out=rstd[:, 0:W], in_=rstd[:, 0:W]), reads=[drstd], writes=[drstd])
                for kt in range(8):
                    fw.op("dve", lambda e: e.scalar_tensor_tensor(out=hb[:, kt, 0:W], in0=hb[:, kt, 0:W],
                                                                  scalar=fnw[:, kt:kt + 1], in1=rstd[:, 0:W],
                                                                  op0=ALU.mult, op1=ALU.mult),
                          reads=[dhb, drstd, dfnw], writes=[dhb])
                for tt in range(4):
                    ob, dob = o[no % 2], do[no % 2]
                    no += 1
                    for kt in range(8):
                        ps, dp = self.next_ps()
                        fw.op("pe", lambda e: e.transpose(out=ps[:, 0:128], in_=hb[:, kt, tt * 128:(tt + 1) * 128],
                                                          identity=self.ident[:, :]),
                              reads=[dhb, self.d_const], writes=[dp])
                        if kt % 2 == 0:
                            fw.op("act", lambda e: e.copy(out=ob[:, kt * 128:(kt + 1) * 128], in_=ps[:, 0:128]),
                                  reads=[dp], writes=[dob])
                        else:
                            fw.op("dve", lambda e: e.tensor_copy(out=ob[:, kt * 128:(kt + 1) * 128], in_=ps[:, 0:128]),
                                  reads=[dp], writes=[dob])
                    r0 = bi * 512 + tt * 128
                    fw.dma("pool", out[r0:r0 + 128, :], ob[:, :], reads=[dob], writes=[dout])
        fw.barrier()


WEIGHT_SHAPES = [
    ("meta_tokens", (16, 1024)), ("final_norm_w", (1024,)), ("mix_norm_w", (2, 1024)), ("w_in", (2, 1024, 5896)),
    ("s5_lambda_re", (2, 2, 16, 64)), ("s5_lambda_im", (2, 2, 16, 64)), ("s5_log_step", (2, 2, 16)),
    ("s5_b_re", (2, 16, 64, 16)), ("s5_b_im", (2, 16, 64, 16)), ("s5_c_re", (2, 16, 16, 64)),
    ("s5_c_im", (2, 16, 16, 64)), ("s5_d", (2, 256)), ("s5_glu_w", (2, 256, 512)), ("s5_glu_b", (2, 512)),
    ("ssd_conv_w", (2, 5, 1024)), ("ssd_conv_b", (2, 1024)), ("ssd_a_log", (2, 2, 8)), ("ssd_dt_bias", (2, 2, 8)),
    ("ssd_d", (2, 8)), ("ssd_norm_w", (2, 512)), ("rwkv_mu_rkv", (2, 3, 256)), ("rwkv_mu_wag", (2, 3, 256)),
    ("rwkv_w0", (2, 2, 256)), ("rwkv_w1", (2, 2, 256, 64)), ("rwkv_w2", (2, 2, 64, 256)), ("rwkv_a0", (2, 2, 256)),
    ("rwkv_a1", (2, 2, 256, 64)), ("rwkv_a2", (2, 2, 64, 256)), ("rwkv_g1", (2, 256, 128)), ("rwkv_g2", (2, 128, 256)),
    ("rwkv_k_k", (2, 256)), ("rwkv_k_a", (2, 256)), ("rwkv_r_k", (2, 4, 64)), ("rwkv_ln_w", (2, 256)),
    ("rwkv_ln_b", (2, 256)), ("proj_a", (2, 256, 1024)), ("proj_b", (2, 512, 1024)), ("proj_c", (2, 256, 1024)),
    ("w_out", (2, 1024, 1024)), ("mlp_norm_w", (2, 1024)), ("mlp_w1", (2, 1024, 4096)), ("mlp_w2", (2, 4096, 1024)),
]


def host_consts():
    return {"c_ident": np.eye(128, dtype=np.float32),
            "c_iota": np.ascontiguousarray(np.broadcast_to(np.arange(512, dtype=np.float32), (128, 512))),
            "c_triu": np.triu(np.ones((128, 128), np.float32)),
            "c_blk": np.kron(np.eye(2, dtype=np.float32), np.ones((64, 64), np.float32)),
            "c_trilT_s": np.tril(np.ones((128, 128), np.float32), -1),
            "c_padm": np.ascontiguousarray(np.broadcast_to((np.arange(128) < 16).astype(np.float32)[:, None], (128, 8))),
            "c_mneg": np.where(np.triu(np.ones((128, 128), bool)), 0.0, -30000.0).astype(np.float32),
            "c_tril_s": np.triu(np.ones((128, 128), np.float32), 1),
            "c_tril_i": np.triu(np.ones((128, 128), np.float32), 0)}


def run(inputs, cfg, ncores=8):
    b = Builder(cfg)
    nc = b.build()
    consts = host_consts()
    in_maps = []
    for c in range(ncores):
        m = {"x": np.ascontiguousarray(inputs["x"][c], dtype=np.float32)}
        for name, _ in WEIGHT_SHAPES:
            m[name] = np.ascontiguousarray(inputs[name], dtype=np.float32)
        m.update(consts)
        in_maps.append(m)
    res = run_bass_kernel_spmd(nc, in_maps, core_ids=list(range(ncores)))
    return res, b


def kernel(**inputs):
    res, _ = run(inputs, {})
    return np.stack([np.asarray(res.results[c]["out"]) for c in range(8)], axis=0).astype(np.float32)


PI = float(np.pi)
S5W = 256


def _mix_s5(self, li):
    fw = self.fw
    I = self.I
    nc = self.nc
    with ExitStack() as es:
        lr = self.sb(es, "s5_lr", [128, 16])
        lim = self.sb(es, "s5_li", [128, 16])
        dpar = Dep()
        fw.dma("sp", lr[:], I["s5_lambda_re"][li].rearrange("d (q gp) n -> (gp n) (d q)", gp=2), writes=[dpar], slow=True)
        fw.dma("sp", lim[:], I["s5_lambda_im"][li].rearrange("d (q gp) n -> (gp n) (d q)", gp=2), writes=[dpar], slow=True)
        stepb = self.sb(es, "s5_stepb", [128, 2, 8, 2])
        fw.dma("sp", stepb[:], I["s5_log_step"][li].rearrange("d (q gp) -> d q gp", gp=2).partition_broadcast(128),
               writes=[dpar], slow=True)
        step = self.sb(es, "s5_step", [128, 16])
        fw.op("act", lambda e: e.activation(out=step[0:64, :].rearrange("p (d q) -> p d q", d=2), in_=stepb[0:64, :, :, 0],
                                            func=AF.Exp), reads=[dpar], writes=[dpar])
        fw.op("act", lambda e: e.activation(out=step[64:128, :].rearrange("p (d q) -> p d q", d=2),
                                            in_=stepb[64:128, :, :, 1], func=AF.Exp), reads=[dpar], writes=[dpar])
        th = self.sb(es, "s5_th", [128, 16])
        rho = self.sb(es, "s5_rho", [128, 16])
        fw.op("dve", lambda e: e.tensor_tensor(out=th[:], in0=lim[:], in1=step[:], op=ALU.mult), reads=[dpar], writes=[dpar])
        fw.op("dve", lambda e: e.tensor_tensor(out=rho[:], in0=lr[:], in1=step[:], op=ALU.mult), reads=[dpar], writes=[dpar])
        fw.op("act", lambda e: e.activation(out=rho[:], in_=rho[:], func=AF.Exp), reads=[dpar], writes=[dpar])

        NT = S5W + 1
        tc = self.sb(es, "s5_tc", [128, 16, NT])
        ts = self.sb(es, "s5_ts", [128, 16, NT])
        dtab = Dep()
        with ExitStack() as es2:
            iot = self.sb(es2, "s5_iota", [128, NT])
            fw.dma("sp", iot[:], I["c_iota"][:, 0:NT], writes=[dtab])
            ph = self.sb(es2, "s5_ph", [128, 16, NT])
            ki = self.sb(es2, "s5_ki", [128, 16, NT], mybir.dt.int32)
            kf = self.sb(es2, "s5_kf", [128, 16, NT])
            for j in range(16):
                fw.op("dve", lambda e: e.tensor_scalar(out=ph[:, j, :], in0=iot[:], scalar1=th[:, j:j + 1], scalar2=None,
                                                       op0=ALU.mult), reads=[dpar, dtab], writes=[dtab])
            fw.op("dve", lambda e: e.tensor_scalar(out=ki[:], in0=ph[:], scalar1=1.0 / (2 * PI), scalar2=None, op0=ALU.mult),
                  reads=[dtab], writes=[dtab])
            fw.op("dve", lambda e: e.tensor_copy(out=kf[:], in_=ki[:]), reads=[dtab], writes=[dtab])
            fw.op("dve", lambda e: e.scalar_tensor_tensor(out=ph[:], in0=kf[:], scalar=-2 * PI, in1=ph[:], op0=ALU.mult,
                                                          op1=ALU.add), reads=[dtab], writes=[dtab])

            def wrap(t):
                fw.op("dve", lambda e: e.tensor_scalar(out=kf[:], in0=t[:], scalar1=PI, scalar2=-2 * PI, op0=ALU.is_gt,
                                                       op1=ALU.mult), reads=[dtab], writes=[dtab])
                fw.op("dve", lambda e: e.tensor_tensor(out=t[:], in0=t[:], in1=kf[:], op=ALU.add), reads=[dtab], writes=[dtab])
                fw.op("dve", lambda e: e.tensor_scalar(out=kf[:], in0=t[:], scalar1=-PI, scalar2=2 * PI, op0=ALU.is_lt,
                                                       op1=ALU.mult), reads=[dtab], writes=[dtab])
                fw.op("dve", lambda e: e.tensor_tensor(out=t[:], in0=t[:], in1=kf[:], op=ALU.add), reads=[dtab], writes=[dtab])

            wrap(ph)
            fw.op("act", lambda e: e.activation(out=ts[:], in_=ph[:], func=AF.Sin), reads=[dtab], writes=[dtab])
            fw.op("dve", lambda e: e.tensor_scalar(out=ph[:], in0=ph[:], scalar1=PI / 2, scalar2=None, op0=ALU.add),
                  reads=[dtab], writes=[dtab])
            wrap(ph)
            fw.op("act", lambda e: e.activation(out=tc[:], in_=ph[:], func=AF.Sin), reads=[dtab], writes=[dtab])
            fw.barrier()
        nsW = self.sb(es, "s5_nsW", [128, 16])
        fw.op("dve", lambda e: e.tensor_scalar(out=nsW[:], in0=ts[:, :, S5W], scalar1=-1.0, scalar2=None, op0=ALU.mult),
              reads=[dtab], writes=[dpar])
        nsB = self.sb(es, "s5_nsB", [128, 16])
        abr = self.sb(es, "s5_abr", [128, 16])
        abi = self.sb(es, "s5_abi", [128, 16])
        fw.op("dve", lambda e: e.tensor_tensor(out=abr[:], in0=rho[:], in1=tc[:, :, 1], op=ALU.mult), reads=[dpar, dtab], writes=[dpar])
        fw.op("dve", lambda e: e.tensor_tensor(out=abi[:], in0=rho[:], in1=ts[:, :, 1], op=ALU.mult), reads=[dpar, dtab], writes=[dpar])
        den = self.sb(es, "s5_den", [128, 16])
        t1 = self.sb(es, "s5_t1", [128, 16])
        t2 = self.sb(es, "s5_t2", [128, 16])
        cor = self.sb(es, "s5_cor", [128, 16])
        coi = self.sb(es, "s5_coi", [128, 16])
        V = lambda fn: fw.op("dve", fn, reads=[dpar], writes=[dpar])
        V(lambda e: e.tensor_tensor(out=den[:], in0=lr[:], in1=lr[:], op=ALU.mult))
        V(lambda e: e.tensor_tensor(out=t1[:], in0=lim[:], in1=lim[:], op=ALU.mult))
        V(lambda e: e.tensor_tensor(out=den[:], in0=den[:], in1=t1[:], op=ALU.add))
        V(lambda e: e.reciprocal(out=den[:], in_=den[:]))
        V(lambda e: e.tensor_scalar(out=abr[:], in0=abr[:], scalar1=-1.0, scalar2=None, op0=ALU.add))
        V(lambda e: e.tensor_tensor(out=t1[:], in0=abr[:], in1=lr[:], op=ALU.mult))
        V(lambda e: e.tensor_tensor(out=t2[:], in0=abi[:], in1=lim[:], op=ALU.mult))
        V(lambda e: e.tensor_tensor(out=t1[:], in0=t1[:], in1=t2[:], op=ALU.add))
        V(lambda e: e.tensor_tensor(out=cor[:], in0=t1[:], in1=den[:], op=ALU.mult))
        V(lambda e: e.tensor_tensor(out=t1[:], in0=abi[:], in1=lr[:], op=ALU.mult))
        V(lambda e: e.tensor_tensor(out=t2[:], in0=abr[:], in1=lim[:], op=ALU.mult))
        V(lambda e: e.tensor_tensor(out=t1[:], in0=t1[:], in1=t2[:], op=ALU.subtract))
        V(lambda e: e.tensor_tensor(out=coi[:], in0=t1[:], in1=den[:], op=ALU.mult))

        LB = self.sb(es, "s5_LB", [128, 2, 8, 2, 128], BF16)
        LC = self.sb(es, "s5_LC", [128, 8, 2, 128], BF16)
        dLB = Dep()
        fw.op("dve", lambda e: e.memset(LC[:], 0.0), writes=[dLB])
        with ExitStack() as es2:
            Xr = self.sb(es2, "s5_Xr", [128, 8, 128])
            Xi = self.sb(es2, "s5_Xi", [128, 8, 128])
            dX = Dep()
            fw.op("dve", lambda e: e.memset(Xr[:], 0.0), writes=[dX])
            fw.op("dve", lambda e: e.memset(Xi[:], 0.0), writes=[dX])
            for (X, nm) in ((Xr, "s5_b_re"), (Xi, "s5_b_im")):
                for q in range(8):
                    r = q % 4
                    fw.dma("sp", X[0:64, q, 32 * r:32 * r + 16], I[nm][li, 2 * q], writes=[dX])
                    fw.dma("sp", X[64:128, q, 32 * r + 16:32 * r + 32], I[nm][li, 2 * q + 1], writes=[dX])
            Xc = self.sb(es2, "s5_Xc", [128, 2, 8, 2, 128])
            tmpx = self.sb(es2, "s5_tmpx", [128, 8, 128])
            for d in range(2):
                cr = cor[:, d * 8:(d + 1) * 8].unsqueeze(2).to_broadcast([128, 8, 128])
                ci = coi[:, d * 8:(d + 1) * 8].unsqueeze(2).to_broadcast([128, 8, 128])
                fw.op("dve", lambda e: e.tensor_tensor(out=Xc[:, d, :, 0, :], in0=Xr[:], in1=cr, op=ALU.mult), reads=[dX, dpar], writes=[dX])
                fw.op("dve", lambda e: e.tensor_tensor(out=tmpx[:], in0=Xi[:], in1=ci, op=ALU.mult), reads=[dX, dpar], writes=[dX])
                fw.op("dve", lambda e: e.tensor_tensor(out=Xc[:, d, :, 0, :], in0=Xc[:, d, :, 0, :], in1=tmpx[:], op=ALU.subtract), reads=[dX], writes=[dX])
                fw.op("dve", lambda e: e.tensor_tensor(out=Xc[:, d, :, 1, :], in0=Xi[:], in1=cr, op=ALU.mult), reads=[dX, dpar], writes=[dX])
                fw.op("dve", lambda e: e.tensor_tensor(out=tmpx[:], in0=Xr[:], in1=ci, op=ALU.mult), reads=[dX, dpar], writes=[dX])
                fw.op("dve", lambda e: e.tensor_tensor(out=Xc[:, d, :, 1, :], in0=Xc[:, d, :, 1, :], in1=tmpx[:], op=ALU.add), reads=[dX], writes=[dX])
            for d in range(2):
                for q in range(8):
                    for ri in range(2):
                        r = q % 4
                        ps, dp = self.next_ps()
                        fw.op("pe", lambda e: e.transpose(out=ps[:, 0:128], in_=Xc[:, d, q, ri, :],
                                                          identity=self.ident[:, :]), reads=[dX, self.d_const], writes=[dp])
                        fw.op("act", lambda e: e.copy(out=LB[:, d, q, ri, :], in_=ps[:, 0:128]),
                              reads=[dp], writes=[dLB])
            Yr = self.sb(es2, "s5_Yr", [32, 8, 128])
            Yi = self.sb(es2, "s5_Yi", [32, 8, 128])
            dY = Dep()
            fw.op("dve", lambda e: e.memset(Yr[:], 0.0), writes=[dY])
            fw.op("dve", lambda e: e.memset(Yi[:], 0.0), writes=[dY])
            for (Y, nm) in ((Yr, "s5_c_re"), (Yi, "s5_c_im")):
                src = I[nm][li].rearrange("(q gp) h n -> gp h q n", gp=2)
                fw.dma("sp", Y[0:16, :, 0:64], src[0], writes=[dY])
                fw.dma("sp", Y[16:32, :, 64:128], src[1], writes=[dY])
            for q in range(8):
                for ri, Y in enumerate((Yr, Yi)):
                    ps, dp = self.next_ps()
                    fw.op("pe", lambda e: e.transpose(out=ps[:, 0:32], in_=Y[:, q, :], identity=self.ident[0:32, 0:32]),
                          reads=[dY, self.d_const], writes=[dp])
                    if ri == 0:
                        fw.op("act", lambda e: e.copy(out=LC[:, q, 0, 32 * (q % 4):32 * (q % 4) + 32], in_=ps[:, 0:32]), reads=[dp], writes=[dLB])
                    else:
                        fw.op("act", lambda e: e.mul(out=LC[:, q, 1, 32 * (q % 4):32 * (q % 4) + 32], in_=ps[:, 0:32], mul=-1.0), reads=[dp], writes=[dLB])
            fw.barrier()

        ubf = self.sb(es, "s5_ubf", [128, 2, L], BF16)
        urv = self.sb(es, "s5_urv", [128, 2, L], BF16)
        yacc = self.sb(es, "s5_yacc", [128, 2, L])
        du = Dep()
        dyacc = Dep()
        with ExitStack() as es2:
            uf = self.sb(es2, "s5_uf", [128, 2, L])
            fw.dma("sp", uf[:], self.projT[OFF_U:OFF_U + 256, :].rearrange("(kt p) t -> p kt t", p=128),
                   reads=[self.dep_proj], writes=[du])
            for kt in range(2):
                fw.op("dve", lambda e: e.tensor_copy(out=ubf[:, kt, :], in_=uf[:, kt, :]), reads=[du], writes=[du])
                fw.op("pool", lambda e: e.tensor_copy(out=urv[:, kt, ::-1], in_=uf[:, kt, :]), reads=[du], writes=[du])
            fw.barrier()

        blocks = [(i * S5W, S5W) for i in range(L // S5W)]
        if L % S5W:
            blocks.append((L - L % S5W, L % S5W))
        NB = 3
        tmp = [[self.sb(es, "s5_w%d_%d" % (i, k), [128, S5W]) for k in range(6)] for i in range(NB)]
        dtmp = [[Dep() for k in range(6)] for i in range(NB)]
        hb = [[self.sb(es, "s5_h%d_%d" % (i, k), [128, S5W], BF16) for k in range(2)] for i in range(NB)]
        dhb = [[Dep() for k in range(2)] for i in range(NB)]
        init = [[self.sb(es, "s5_in%d_%d" % (i, k), [128, 1]) for k in range(3)] for i in range(2)]
        dinit = [Dep(), Dep()]
        it = 0
        for d in range(2):
            usrc = ubf if d == 0 else urv
            for q in range(8):
                j = d * 8 + q
                r = q % 4
                kt = q // 4
                rho_b = rho[:, j:j + 1]
                prev = None
                for bi, (t0, W) in enumerate(blocks):
                    T, dT = tmp[it % NB], dtmp[it % NB]
                    H, dH = hb[it % NB], dhb[it % NB]
                    it += 1
                    pre, dpre = self.next_ps()
                    pim, dpim = self.next_ps()
                    fw.op("pe", lambda e: e.matmul(pre[:, 0:W], lhsT=LB[:, d, q, 0, :],
                                                   rhs=usrc[:, kt, t0:t0 + W], start=True, stop=True),
                          reads=[dLB, du], writes=[dpre])
                    fw.op("pe", lambda e: e.matmul(pim[:, 0:W], lhsT=LB[:, d, q, 1, :],
                                                   rhs=usrc[:, kt, t0:t0 + W], start=True, stop=True),
                          reads=[dLB, du], writes=[dpim])
                    c_, s_ = tc[:, j, 0:W], ts[:, j, 0:W]
                    fw.op("dve", lambda e: e.tensor_tensor(out=T[0][:, 0:W], in0=pre[:, 0:W], in1=c_, op=ALU.mult), reads=[dpre, dtab], writes=[dT[0]])
                    fw.op("dve", lambda e: e.tensor_tensor(out=T[1][:, 0:W], in0=pim[:, 0:W], in1=s_, op=ALU.mult), reads=[dpim, dtab], writes=[dT[1]])
                    fw.op("dve", lambda e: e.tensor_tensor(out=T[2][:, 0:W], in0=pim[:, 0:W], in1=c_, op=ALU.mult), reads=[dpim, dtab], writes=[dT[2]])
                    fw.op("dve", lambda e: e.tensor_tensor(out=T[3][:, 0:W], in0=pre[:, 0:W], in1=s_, op=ALU.mult), reads=[dpre, dtab], writes=[dT[3]])
                    fw.op("pool", lambda e: e.tensor_tensor(out=T[0][:, 0:W], in0=T[0][:, 0:W], in1=T[1][:, 0:W], op=ALU.add), reads=[dT[0], dT[1]], writes=[dT[0]])
                    fw.op("pool", lambda e: e.tensor_tensor(out=T[2][:, 0:W], in0=T[2][:, 0:W], in1=T[3][:, 0:W], op=ALU.subtract), reads=[dT[2], dT[3]], writes=[dT[2]])
                    ini, dini = init[bi % 2], dinit[bi % 2]
                    if bi == 0:
                        i_re, i_im = 0.0, 0.0
                        rd = []
                    else:
                        pT, pdT, pW, pini, pdini = prev
                        cW, sW = tc[:, j, pW:pW + 1], ts[:, j, pW:pW + 1]
                        fw.op("dve", lambda e: e.tensor_scalar(out=ini[2][:], in0=pT[4][:, pW - 1:pW], scalar1=cW, scalar2=None, op0=ALU.mult), reads=[pdT[4], dtab], writes=[dini])
                        fw.op("dve", lambda e: e.scalar_tensor_tensor(out=ini[2][:], in0=pT[5][:, pW - 1:pW], scalar=sW, in1=ini[2][:], op0=ALU.mult, op1=ALU.subtract), reads=[pdT[5], dini, dtab], writes=[dini])
                        fw.op("dve", lambda e: e.tensor_scalar(out=ini[0][:], in0=ini[2][:], scalar1=-1.0, scalar2=None, op0=ALU.mult), reads=[dini], writes=[dini])
                        fw.op("dve", lambda e: e.tensor_scalar(out=ini[2][:], in0=pT[4][:, pW - 1:pW], scalar1=sW, scalar2=None, op0=ALU.mult), reads=[pdT[4], dtab], writes=[dini])
                        fw.op("dve", lambda e: e.scalar_tensor_tensor(out=ini[1][:], in0=pT[5][:, pW - 1:pW], scalar=cW, in1=ini[2][:], op0=ALU.mult, op1=ALU.add), reads=[pdT[5], dini, dtab], writes=[dini])
                        i_re, i_im = ini[0][:, 0:1], ini[1][:, 0:1]
                        rd = [dini]
                    fw.op("dve", lambda e: e.tensor_tensor_scan(out=T[4][:, 0:W], data0=rho_b.to_broadcast([128, W]), data1=T[0][:, 0:W], initial=i_re, op0=ALU.mult, op1=ALU.add),
                          reads=[dT[0], dpar] + rd, writes=[dT[4]])
                    fw.op("dve", lambda e: e.tensor_tensor_scan(out=T[5][:, 0:W], data0=rho_b.to_broadcast([128, W]), data1=T[2][:, 0:W], initial=i_im, op0=ALU.mult, op1=ALU.add),
                          reads=[dT[2], dpar] + rd, writes=[dT[5]])
                    prev = (T, dT, W, ini, dini)
                    fw.op("pool", lambda e: e.tensor_tensor(out=T[0][:, 0:W], in0=T[4][:, 0:W], in1=c_, op=ALU.mult), reads=[dT[4], dtab], writes=[dT[0]])
                    fw.op("pool", lambda e: e.tensor_tensor(out=T[1][:, 0:W], in0=T[5][:, 0:W], in1=s_, op=ALU.mult), reads=[dT[5], dtab], writes=[dT[1]])
                    fw.op("pool", lambda e: e.tensor_tensor(out=H[0][:, 0:W], in0=T[0][:, 0:W], in1=T[1][:, 0:W], op=ALU.subtract), reads=[dT[0], dT[1]], writes=[dH[0]])
                    fw.op("dve", lambda e: e.tensor_tensor(out=T[2][:, 0:W], in0=T[4][:, 0:W], in1=s_, op=ALU.mult), reads=[dT[4], dtab], writes=[dT[2]])
                    fw.op("dve", lambda e: e.tensor_tensor(out=T[3][:, 0:W], in0=T[5][:, 0:W], in1=c_, op=ALU.mult), reads=[dT[5], dtab], writes=[dT[3]])
                    fw.op("pool", lambda e: e.tensor_tensor(out=H[1][:, 0:W], in0=T[2][:, 0:W], in1=T[3][:, 0:W], op=ALU.add), reads=[dT[2], dT[3]], writes=[dH[1]])
                    py, dpy = self.next_ps()
                    fw.op("pe", lambda e: e.matmul(py[:, 0:W], lhsT=LC[:, q, 0, :], rhs=H[0][:, 0:W], start=True, stop=False), reads=[dLB, dH[0]], writes=[dpy])
                    fw.op("pe", lambda e: e.matmul(py[:, 0:W], lhsT=LC[:, q, 1, :], rhs=H[1][:, 0:W], start=False, stop=True), reads=[dLB, dH[1]], writes=[dpy])
                    if d == 0 and r == 0:
                        fw.op("act", lambda e: e.copy(out=yacc[:, kt, t0:t0 + W], in_=py[:, 0:W]), reads=[dpy], writes=[dyacc])
                    elif d == 0:
                        ya = yacc[:, kt, t0:t0 + W]
                        fw.op("dve", lambda e: e.tensor_tensor(out=ya, in0=py[:, 0:W], in1=ya, op=ALU.add), reads=[dpy, dyacc], writes=[dyacc])
                    else:
                        lo = L - (t0 + W)
                        ya = yacc[:, kt, lo:lo + W]
                        fw.op("dve", lambda e: e.tensor_tensor(out=ya[:, ::-1], in0=py[:, 0:W], in1=ya[:, ::-1], op=ALU.add), reads=[dpy, dyacc], writes=[dyacc])
        fw.barrier()
        self._s5_post(li, es, yacc, dyacc)


def _s5_post(self, li, es_outer, yacc, dyacc):
    fw = self.fw
    I = self.I
    with ExitStack() as es:
        gw, dgw = self.load_weight_bf(es, "s5_gw", I["s5_glu_w"][li], 256, 512)
        dsk = self.sb(es, "s5_dsk", [128, 2])
        gb = self.sb(es, "s5_gb", [128, 4])
        dpp = Dep()
        fw.dma("sp", dsk[:], I["s5_d"][li].rearrange("(kt p) -> p kt", p=128), writes=[dpp], slow=True)
        fw.dma("sp", gb[:], I["s5_glu_b"][li].rearrange("(kt p) -> p kt", p=128), writes=[dpp], slow=True)
        W = 512
        uf = [self.sb(es, "s5p_u%d" % i, [128, 2, W]) for i in range(2)]
        duf = [Dep(), Dep()]
        t1 = self.sb(es, "s5p_t1", [128, 2, W])
        t2 = self.sb(es, "s5p_t2", [128, 2, W])
        dt1 = Dep()
        gl = [self.sb(es, "s5p_gl%d" % i, [128, 2, W], BF16) for i in range(2)]
        dgl = [Dep(), Dep()]
        sg = [self.sb(es, "s5p_sg%d" % i, [128, W]) for i in range(2)]
        dsg = [Dep(), Dep()]
        o = [self.sb(es, "s5p_o%d" % i, [128, 2, W]) for i in range(2)]
        do = [Dep(), Dep()]
        uv = self.projT[OFF_U:OFF_U + 256, :].rearrange("(kt p) t -> p kt t", p=128)
        yv = self.yaT.rearrange("(kt p) t -> p kt t", p=128)
        for bi, (t0, Wb) in enumerate(BLOCKS):
            u, du = uf[bi % 2], duf[bi % 2]
            g, dg = gl[bi % 2], dgl[bi % 2]
            ob, dob = o[bi % 2], do[bi % 2]
            fw.dma("sp", u[:, :, 0:Wb], uv[:, :, t0:t0 + Wb], reads=[self.dep_proj], writes=[du])
            for kt in range(2):
                fw.op("dve", lambda e: e.scalar_tensor_tensor(out=t1[:, kt, 0:Wb], in0=u[:, kt, 0:Wb], scalar=dsk[:, kt:kt + 1], in1=yacc[:, kt, t0:t0 + Wb], op0=ALU.mult, op1=ALU.add),
                      reads=[du, dpp, dyacc], writes=[dt1])
                fw.op("pool", lambda e: e.tensor_tensor(out=t2[:, kt, 0:Wb], in0=t1[:, kt, 0:Wb], in1=t1[:, kt, 0:Wb], op=ALU.mult), reads=[dt1], writes=[dt1])
                fw.op("dve", lambda e: e.tensor_scalar(out=t2[:, kt, 0:Wb], in0=t2[:, kt, 0:Wb], scalar1=0.044715, scalar2=1.0, op0=ALU.mult, op1=ALU.add), reads=[dt1], writes=[dt1])
                fw.op("pool", lambda e: e.tensor_tensor(out=t2[:, kt, 0:Wb], in0=t2[:, kt, 0:Wb], in1=t1[:, kt, 0:Wb], op=ALU.mult), reads=[dt1], writes=[dt1])
                fw.op("act", lambda e: e.activation(out=t2[:, kt, 0:Wb], in_=t2[:, kt, 0:Wb], func=AF.Sigmoid, scale=1.5957691216), reads=[dt1], writes=[dt1])
                fw.op("dve", lambda e: e.tensor_tensor(out=g[:, kt, 0:Wb], in0=t2[:, kt, 0:Wb], in1=t1[:, kt, 0:Wb], op=ALU.mult), reads=[dt1], writes=[dg])
            for c in range(2):
                plo, dplo = self.next_ps()
                phi, dphi = self.next_ps()
                for k in range(2):
                    fw.op("pe", lambda e: e.matmul(plo[:, 0:Wb], lhsT=gw[:, k, c * 128:(c + 1) * 128], rhs=g[:, k, 0:Wb], start=(k == 0), stop=(k == 1)), reads=[dgw, dg], writes=[dplo])
                for k in range(2):
                    fw.op("pe", lambda e: e.matmul(phi[:, 0:Wb], lhsT=gw[:, k, 256 + c * 128:256 + (c + 1) * 128], rhs=g[:, k, 0:Wb], start=(k == 0), stop=(k == 1)), reads=[dgw, dg], writes=[dphi])
                s, ds = sg[c], dsg[c]
                fw.op("act", lambda e: e.activation(out=s[:, 0:Wb], in_=phi[:, 0:Wb], func=AF.Sigmoid, bias=gb[:, 2 + c:3 + c]), reads=[dphi, dpp], writes=[ds])
                fw.op("dve", lambda e: e.scalar_tensor_tensor(out=ob[:, c, 0:Wb], in0=plo[:, 0:Wb], scalar=gb[:, c:c + 1], in1=s[:, 0:Wb], op0=ALU.add, op1=ALU.mult), reads=[dplo, ds, dpp], writes=[dob])
            fw.dma("pool", yv[:, :, t0:t0 + Wb], ob[:, :, 0:Wb], reads=[dob], writes=[self.dep_ya])
    fw.barrier()


Builder.mix_s5 = _mix_s5
Builder._s5_post = _s5_post


LP = 33 * 128
NCH = 33


def _mix_ssd(self, li):
    fw = self.fw
    I = self.I
    xcT = self.scratch_once("xcT", (1024, L))
    d_xc = self.dep_once("xcT")
    xbv = self.projT[OFF_XBC:OFF_XBC + 1024, :].rearrange("(j p) t -> p j t", p=128)
    xcv = xcT.rearrange("(j p) t -> p j t", p=128)
    with ExitStack() as es:
        cw = self.sb(es, "sd_cw", [128, 5, 8])
        cb = self.sb(es, "sd_cb", [128, 8])
        dcw = Dep()
        for k in range(5):
            fw.dma("sp", cw[:, k, :], I["ssd_conv_w"][li, k].rearrange("(j p) -> p j", p=128), writes=[dcw], slow=True)
        fw.dma("sp", cb[:], I["ssd_conv_b"][li].rearrange("(j p) -> p j", p=128), writes=[dcw], slow=True)
        xp = [self.sb(es, "sd_xp%d" % i, [128, L + 4]) for i in range(2)]
        dxp = [Dep(), Dep()]
        acc = [self.sb(es, "sd_acc%d" % i, [128, L]) for i in range(2)]
        dacc = [Dep(), Dep()]
        for i in range(2):
            fw.op("dve", lambda e: e.memset(xp[i][:, 0:2], 0.0), writes=[dxp[i]])
            fw.op("dve", lambda e: e.memset(xp[i][:, L + 2:L + 4], 0.0), writes=[dxp[i]])
        for j in range(8):
            x_, dx_ = xp[j % 2], dxp[j % 2]
            a_, da_ = acc[j % 2], dacc[j % 2]
            fw.dma("sp", x_[:, 2:L + 2], xbv[:, j, :], reads=[self.dep_proj], writes=[dx_])
            eng = "dve"
            fw.op(eng, lambda e: e.tensor_scalar(out=a_[:], in0=x_[:, 0:L], scalar1=cw[:, 0, j:j + 1], scalar2=cb[:, j:j + 1], op0=ALU.mult, op1=ALU.add),
                  reads=[dx_, dcw], writes=[da_])
            for k in range(1, 5):
                fw.op(eng, lambda e: e.scalar_tensor_tensor(out=a_[:], in0=x_[:, k:k + L], scalar=cw[:, k, j:j + 1], in1=a_[:], op0=ALU.mult, op1=ALU.add),
                      reads=[dx_, dcw, da_], writes=[da_])
            fw.op("act", lambda e: e.activation(out=a_[:], in_=a_[:], func=AF.Silu), reads=[da_], writes=[da_])
            fw.dma("pool", xcv[:, j, :], a_[:], reads=[da_], writes=[d_xc])
    fw.barrier()

    with ExitStack() as es:
        triu = self.sb(es, "sd_triu", [128, 128])
        mneg = self.sb(es, "sd_mneg", [128, 128])
        onesf = self.sb(es, "sd_onesf", [128, 128])
        negones = self.sb(es, "sd_negones", [128, 128])
        identb = self.sb(es, "sd_identb", [128, 128], BF16)
        dc = Dep()
        fw.dma("sp", triu[:], I["c_triu"][:, :], writes=[dc])
        fw.dma("sp", mneg[:], I["c_mneg"][:, :], writes=[dc])
        padm = self.sb(es, "sd_padm", [128, 8])
        fw.dma("sp", padm[:], I["c_padm"][:, :], writes=[dc])
        fw.op("dve", lambda e: e.memset(onesf[:], 1.0), writes=[dc])
        fw.op("dve", lambda e: e.memset(negones[:], -1.0), writes=[dc])
        fw.op("dve", lambda e: e.tensor_copy(out=identb[:], in_=self.ident[:]), reads=[self.d_const], writes=[dc])
        dtb = self.sb(es, "sd_dtb", [8, 2])
        nea = self.sb(es, "sd_nea", [8, 2])
        dpp = Dep()
        fw.dma("sp", dtb[:], I["ssd_dt_bias"][li].rearrange("d h -> h d"), writes=[dpp], slow=True)
        fw.dma("sp", nea[:], I["ssd_a_log"][li].rearrange("d h -> h d"), writes=[dpp], slow=True)
        fw.op("act", lambda e: e.activation(out=nea[:], in_=nea[:], func=AF.Exp), reads=[dpp], writes=[dpp])
        fw.op("dve", lambda e: e.tensor_scalar(out=nea[:], in0=nea[:], scalar1=-1.0, scalar2=None, op0=ALU.mult), reads=[dpp], writes=[dpp])
        dtok = self.sb(es, "sd_dtok", [128, NCH, 16])
        ddtok = Dep()
        xs = self.sb(es, "sd_xs", [128, 4, LP], BF16)
        Bm = self.sb(es, "sd_B", [128, 2, LP], BF16)
        Cm = self.sb(es, "sd_C", [128, 2, LP], BF16)
        dws = Dep()
        yacc = self.sb(es, "sd_yacc", [128, 4, L])
        dyacc = Dep()
        ST = self.sb(es, "sd_ST", [128, 8, 64])
        STb = self.sb(es, "sd_STb", [128, 8, 64], BF16)
        dST = [Dep() for _ in range(8)]
        dSTb = [Dep() for _ in range(8)]
        dbias = self.sb(es, "sd_dbias", [128, 2, 8])
        nea_bc = self.sb(es, "sd_neabc", [128, 2, 8])
        fw.dma("sp", dbias[:], I["ssd_dt_bias"][li].partition_broadcast(128), writes=[dpp], slow=True)
        fw.dma("sp", nea_bc[:], I["ssd_a_log"][li].partition_broadcast(128), writes=[dpp], slow=True)
        fw.op("act", lambda e: e.activation(out=nea_bc[:], in_=nea_bc[:], func=AF.Exp), reads=[dpp], writes=[dpp])
        fw.op("dve", lambda e: e.tensor_scalar(out=nea_bc[:], in0=nea_bc[:], scalar1=-1.0, scalar2=None, op0=ALU.mult), reads=[dpp], writes=[dpp])

        for d in range(2):
          with ExitStack() as es3:
            stg = [self.sb(es3, "sd_stg%d" % i, [128, L]) for i in range(2)]
            dstg = [Dep(), Dep()]
            raw = self.sb(es3, "sd_raw", [8, L])
            rawd = self.sb(es3, "sd_rawd", [8, LP])
            draw = Dep()
            for j in range(8):
                s_, ds_ = stg[j % 2], dstg[j % 2]
                fw.dma("sp", s_[:], xcv[:, j, :], reads=[d_xc], writes=[ds_])
                dst = xs[:, j, :] if j < 4 else (Bm[:, j - 4, :] if j < 6 else Cm[:, j - 6, :])
                eng = "dve" if j % 2 == 0 else "pool"
                fw.op(eng, lambda e: e.memset(dst[:, L:LP], 0.0), writes=[dws])
                if d == 0:
                    fw.op(eng, lambda e: e.tensor_copy(out=dst[:, 0:L], in_=s_[:]), reads=[ds_], writes=[dws])
                else:
                    fw.op(eng, lambda e: e.tensor_copy(out=dst[:, 0:L][:, ::-1], in_=s_[:]), reads=[ds_], writes=[dws])
            fw.dma("sp", raw[:], self.projT[OFF_DT:OFF_DT + 8, :], reads=[self.dep_proj], writes=[draw])
            fw.op("dve", lambda e: e.memset(rawd[:, L:LP], 0.0), writes=[draw])
            if d == 0:
                fw.op("dve", lambda e: e.tensor_copy(out=rawd[:, 0:L], in_=raw[:]), reads=[draw], writes=[draw])
            else:
                fw.op("dve", lambda e: e.tensor_copy(out=rawd[:, 0:L][:, ::-1], in_=raw[:]), reads=[draw], writes=[draw])
            for c in range(NCH):
                ps, dp = self.next_ps()
                fw.op("pe", lambda e: e.transpose(out=ps[:, 0:8], in_=rawd[:, c * 128:(c + 1) * 128], identity=self.ident[0:8, 0:8]), reads=[draw, self.d_const], writes=[dp])
                fw.op("dve", lambda e: e.tensor_tensor(out=dtok[:, c, 0:8], in0=ps[:, 0:8], in1=dbias[:, d, :], op=ALU.add), reads=[dp, dpp], writes=[ddtok])
            fw.op("act", lambda e: e.activation(out=dtok[:, :, 0:8], in_=dtok[:, :, 0:8], func=AF.Exp), reads=[ddtok], writes=[ddtok])
            fw.op("act", lambda e: e.activation(out=dtok[:, :, 0:8], in_=dtok[:, :, 0:8], func=AF.Ln, bias=self.one_t[:, 0:1]), reads=[ddtok, self.d_const], writes=[ddtok])
            fw.op("dve", lambda e: e.tensor_tensor(out=dtok[:, NCH - 1, 0:8], in0=dtok[:, NCH - 1, 0:8], in1=padm[:, :], op=ALU.mult), reads=[ddtok, dc], writes=[ddtok])
            fw.op("dve", lambda e: e.tensor_tensor(out=dtok[:, :, 8:16], in0=dtok[:, :, 0:8], in1=nea_bc[:, d, :].unsqueeze(1).to_broadcast([128, NCH, 8]), op=ALU.mult), reads=[ddtok, dpp], writes=[ddtok])
            fw.barrier()
          with ExitStack() as es4:
            NBF = 2
            xtk = [self.sb(es4, "sd_xtk%d" % i, [128, 512]) for i in range(NBF)]
            dxtk = [Dep() for _ in range(NBF)]
            btok = [self.sb(es4, "sd_btok%d" % i, [128, 256], BF16) for i in range(NBF)]
            dbtok = [Dep() for _ in range(NBF)]
            cbt = [self.sb(es4, "sd_cbt%d" % i, [128, 2, 128]) for i in range(NBF)]
            dcbt = [Dep() for _ in range(NBF)]
            sm = [self.sb(es4, "sd_sm%d" % i, [128, 4, 8]) for i in range(NBF)]
            dsm = [Dep() for _ in range(NBF)]
            NH = 3
            atri = [self.sb(es4, "sd_atri%d" % i, [128, 128]) for i in range(NH)]
            datri = [Dep() for _ in range(NH)]
            DT = [self.sb(es4, "sd_DT%d" % i, [128, 128]) for i in range(NH)]
            dDT = [Dep() for _ in range(NH)]
            EE = [self.sb(es4, "sd_EE%d" % i, [128, 128]) for i in range(NH)]
            dEE = [Dep() for _ in range(NH)]
            MT = [self.sb(es4, "sd_MT%d" % i, [128, 128], BF16) for i in range(NH)]
            dMT = [Dep() for _ in range(NH)]
            CE = [self.sb(es4, "sd_CE%d" % i, [128, 128], BF16) for i in range(NH)]
            dCE = [Dep() for _ in range(NH)]
            xdt = [self.sb(es4, "sd_xdt%d" % i, [128, 2, 64], BF16) for i in range(NH)]
            dxdt = [Dep() for _ in range(NH)]
            for j in range(8):
                fw.op("dve", lambda e: e.memset(ST[:, j, :], 0.0), writes=[dST[j]])
                fw.op("pool", lambda e: e.memset(STb[:, j, :], 0.0), writes=[dSTb[j]])
            ih = 0
            for c in range(NCH):
                t0 = c * 128
                Wv = min(128, L - t0)
                k_ = c % NBF
                px, dpx = self.next_ps()
                for j in range(4):
                    fw.op("pe", lambda e: e.matmul(px[:, j * 128:(j + 1) * 128], lhsT=xs[:, j, t0:t0 + 128], rhs=identb[:, :], start=True, stop=True), reads=[dws, dc], writes=[dpx])
                xtok, dxtok = xtk[k_], dxtk[k_]
                fw.op("act", lambda e: e.copy(out=xtok[:, :], in_=px[:, 0:512]), reads=[dpx], writes=[dxtok])
                pb, dpb = self.next_ps()
                for g in range(2):
                    fw.op("pe", lambda e: e.matmul(pb[:, g * 128:(g + 1) * 128], lhsT=Bm[:, g, t0:t0 + 128], rhs=identb[:, :], start=True, stop=True), reads=[dws, dc], writes=[dpb])
                fw.op("act", lambda e: e.copy(out=btok[k_][:, :], in_=pb[:, 0:256]), reads=[dpb], writes=[dbtok[k_]])
                pc, dpc = self.next_ps()
                fw.op("pe", lambda e: e.matmul(pc[:, 0:8], lhsT=triu[:, :], rhs=dtok[:, c, 8:16], start=True, stop=True), reads=[dc, ddtok], writes=[dpc])
                fw.op("pe", lambda e: e.matmul(pc[:, 8:16], lhsT=onesf[:, :], rhs=dtok[:, c, 8:16], start=True, stop=True), reads=[dc, ddtok], writes=[dpc])
                S_, dS_ = sm[k_], dsm[k_]
                fw.op("act", lambda e: e.copy(out=S_[:, 0, :], in_=pc[:, 0:8]), reads=[dpc], writes=[dS_])
                fw.op("dve", lambda e: e.tensor_tensor(out=S_[:, 1, :], in0=pc[:, 8:16], in1=S_[:, 0, :], op=ALU.subtract), reads=[dpc, dS_], writes=[dS_])
                fw.op("act", lambda e: e.activation(out=S_[:, 1, :], in_=S_[:, 1, :], func=AF.Exp), reads=[dS_], writes=[dS_])
                fw.op("dve", lambda e: e.tensor_tensor(out=S_[:, 2, :], in0=S_[:, 1, :], in1=dtok[:, c, 0:8], op=ALU.mult), reads=[dS_, ddtok], writes=[dS_])
                fw.op("act", lambda e: e.activation(out=S_[:, 3, :], in_=pc[:, 8:16], func=AF.Exp), reads=[dpc], writes=[dS_])
                for g in range(2):
                    pcb, dpcb = self.next_ps()
                    fw.op("pe", lambda e: e.matmul(pcb[:, 0:128], lhsT=Bm[:, g, t0:t0 + 128], rhs=Cm[:, g, t0:t0 + 128], start=True, stop=True), reads=[dws], writes=[dpcb])
                    fw.op("act", lambda e: e.copy(out=cbt[k_][:, g, :], in_=pcb[:, 0:128]), reads=[dpcb], writes=[dcbt[k_]])
                for jp in range(4):
                    py, dpy = self.next_ps()
                    for jj in range(2):
                        j = jp * 2 + jj
                        g = j // 4
                        h_ = ih % NH
                        ih += 1
                        fw.op("dve", lambda e: e.tensor_scalar(out=atri[h_][:], in0=triu[:], scalar1=dtok[:, c, 8 + j:9 + j], scalar2=None, op0=ALU.mult), reads=[dc, ddtok], writes=[datri[h_]])
                        pD, dpD = self.next_ps()
                        fw.op("pe", lambda e: e.matmul(pD[:, 0:128], lhsT=onesf[:, :], rhs=atri[h_][:, :], start=True, stop=False), reads=[dc, datri[h_]], writes=[dpD])
                        fw.op("pe", lambda e: e.matmul(pD[:, 0:128], lhsT=atri[h_][:, :], rhs=negones[:, :], start=False, stop=False), reads=[dc, datri[h_]], writes=[dpD])
                        fw.op("pe", lambda e: e.matmul(pD[:, 0:128], lhsT=self.ident[:, :], rhs=mneg[:, :], start=False, stop=True), reads=[dc, self.d_const], writes=[dpD])
                        fw.op("pe", lambda e: e.matmul(pD[:, 128:256], lhsT=onesf[:, :], rhs=atri[h_][:, :], start=True, stop=True), reads=[dc, datri[h_]], writes=[dpD])
                        fw.op("act", lambda e: e.activation(out=DT[h_][:], in_=pD[:, 0:128], func=AF.Exp), reads=[dpD], writes=[dDT[h_]])
                        fw.op("act", lambda e: e.activation(out=EE[h_][:], in_=pD[:, 128:256], func=AF.Exp), reads=[dpD], writes=[dEE[h_]])
                        fw.op("dve", lambda e: e.tensor_tensor(out=MT[h_][:], in0=cbt[k_][:, g, :], in1=DT[h_][:], op=ALU.mult), reads=[dcbt[k_], dDT[h_]], writes=[dMT[h_]])
                        fw.op("pool", lambda e: e.tensor_tensor(out=CE[h_][:], in0=Cm[:, g, t0:t0 + 128], in1=EE[h_][:], op=ALU.mult), reads=[dws, dEE[h_]], writes=[dCE[h_]])
                        fw.op("dve", lambda e: e.tensor_scalar(out=xdt[h_][:, 0, :], in0=xtok[:, j * 64:(j + 1) * 64], scalar1=dtok[:, c, j:j + 1], scalar2=None, op0=ALU.mult), reads=[dxtok, ddtok], writes=[dxdt[h_]])
                        fw.op("dve", lambda e: e.tensor_scalar(out=xdt[h_][:, 1, :], in0=xtok[:, j * 64:(j + 1) * 64], scalar1=S_[:, 2, j:j + 1], scalar2=None, op0=ALU.mult), reads=[dxtok, dS_], writes=[dxdt[h_]])
                        fw.op("pe", lambda e: e.matmul(py[jj * 64:(jj + 1) * 64, 0:128], lhsT=xdt[h_][:, 0, :], rhs=MT[h_][:, :], start=True, stop=False), reads=[dxdt[h_], dMT[h_]], writes=[dpy])
                        fw.op("pe", lambda e: e.matmul(py[jj * 64:(jj + 1) * 64, 0:128], lhsT=STb[:, j, :], rhs=CE[h_][:, :], start=False, stop=True), reads=[dSTb[j], dCE[h_]], writes=[dpy])
                        pS, dpS = self.next_ps()
                        fw.op("pe", lambda e: e.matmul(pS[:, 0:64], lhsT=btok[k_][:, g * 128:(g + 1) * 128], rhs=xdt[h_][:, 1, :], start=True, stop=True), reads=[dbtok[k_], dxdt[h_]], writes=[dpS])
                        fw.op("dve", lambda e: e.scalar_tensor_tensor(out=ST[:, j, :], in0=ST[:, j, :], scalar=S_[:, 3, j:j + 1], in1=pS[:, 0:64], op0=ALU.mult, op1=ALU.add), reads=[dST[j], dS_, dpS], writes=[dST[j]])
                        fw.op("act", lambda e: e.copy(out=STb[:, j, :], in_=ST[:, j, :]), reads=[dST[j]], writes=[dSTb[j]])
                    if d == 0:
                        fw.op("act", lambda e: e.copy(out=yacc[:, jp, t0:t0 + Wv], in_=py[:, 0:Wv]), reads=[dpy], writes=[dyacc])
                    else:
                        lo = L - (t0 + Wv)
                        ya = yacc[:, jp, lo:lo + Wv]
                        fw.op("dve", lambda e: e.tensor_tensor(out=ya[:, ::-1], in0=py[:, 0:Wv], in1=ya[:, ::-1], op=ALU.add), reads=[dpy, dyacc], writes=[dyacc])
            fw.barrier()
        fw.barrier()
        if "dbg_yacc" in self.cfg.get("dump", ()):
            dbg = self.scratch("dbg_yacc", (512, L))
            fw.dma("sp", dbg.rearrange("(j p) t -> p j t", p=128), yacc[:, :, :], reads=[dyacc], writes=[Dep()])
            dbg2 = self.scratch("dbg_dtok", (128, NCH * 16))
            fw.dma("sp", dbg2[:, :], dtok[:, :, :].rearrange("p c k -> p (c k)"), reads=[ddtok], writes=[Dep()])
        with ExitStack() as es2:
            self.ensure_eps(es2)
            dsk = self.sb(es2, "sd_dsk", [128, 4])
            nw = self.sb(es2, "sd_nw", [128, 4])
            dq = Dep()
            for j in range(8):
                fw.dma("sp", dsk[64 * (j % 2):64 * (j % 2) + 64, j // 2:j // 2 + 1], I["ssd_d"][li, j:j + 1].partition_broadcast(64), writes=[dq], slow=True)
            fw.dma("sp", nw[:], I["ssd_norm_w"][li].rearrange("(j p) -> p j", p=128), writes=[dq], slow=True)
            W = 512
            xb = [self.sb(es2, "sd4_x%d" % i, [128, 4, W]) for i in range(2)]
            zb = [self.sb(es2, "sd4_z%d" % i, [128, 4, W]) for i in range(2)]
            dxb = [Dep(), Dep()]
            yb = self.sb(es2, "sd4_y", [128, 4, W])
            sq = self.sb(es2, "sd4_sq", [128, 4, W], BF16)
            dyb = Dep()
            rstd = self.sb(es2, "sd4_r", [128, W])
            ob = [self.sb(es2, "sd4_o%d" % i, [128, 4, W]) for i in range(2)]
            dob = [Dep(), Dep()]
            zv = self.projT[OFF_Z:OFF_Z + 512, :].rearrange("(j p) t -> p j t", p=128)
            yv = self.ybT.rearrange("(j p) t -> p j t", p=128)
            for bi, (t0, Wb) in enumerate(BLOCKS):
                x_, z_, dxz = xb[bi % 2], zb[bi % 2], dxb[bi % 2]
                o_, do_ = ob[bi % 2], dob[bi % 2]
                fw.dma("sp", x_[:, :, 0:Wb], xcv[:, 0:4, t0:t0 + Wb], reads=[d_xc], writes=[dxz])
                fw.dma("sp", z_[:, :, 0:Wb], zv[:, :, t0:t0 + Wb], reads=[self.dep_proj], writes=[dxz])
                fw.op("act", lambda e: e.activation(out=z_[:, :, 0:Wb], in_=z_[:, :, 0:Wb], func=AF.Silu), reads=[dxz], writes=[dxz])
                for j in range(4):
                    fw.op("dve", lambda e: e.scalar_tensor_tensor(out=yb[:, j, 0:Wb], in0=x_[:, j, 0:Wb], scalar=dsk[:, j:j + 1], in1=yacc[:, j, t0:t0 + Wb], op0=ALU.mult, op1=ALU.add), reads=[dxz, dq, dyacc], writes=[dyb])
                    fw.op("pool", lambda e: e.tensor_tensor(out=yb[:, j, 0:Wb], in0=yb[:, j, 0:Wb], in1=z_[:, j, 0:Wb], op=ALU.mult), reads=[dyb, dxz], writes=[dyb])
                    fw.op("act", lambda e: e.activation(out=sq[:, j, 0:Wb], in_=yb[:, j, 0:Wb], func=AF.Square), reads=[dyb], writes=[dyb])
                ps, dp = self.next_ps()
                for j in range(4):
                    fw.op("pe", lambda e: e.matmul(ps[:, 0:Wb], lhsT=self.ones_bf[:, :], rhs=sq[:, j, 0:Wb], start=(j == 0), stop=(j == 3)), reads=[dyb, self.d_const], writes=[dp])
                fw.op("act", lambda e: e.activation(out=rstd[:, 0:Wb], in_=ps[:, 0:Wb], func=AF.Sqrt, bias=self.eps_t[:, 0:1], scale=1.0 / 512), reads=[dp, self.d_const], writes=[dyb])
                fw.op("dve", lambda e: e.reciprocal(out=rstd[:, 0:Wb], in_=rstd[:, 0:Wb]), reads=[dyb], writes=[dyb])
                for j in range(4):
                    fw.op("dve", lambda e: e.scalar_tensor_tensor(out=o_[:, j, 0:Wb], in0=yb[:, j, 0:Wb], scalar=nw[:, j:j + 1], in1=rstd[:, 0:Wb], op0=ALU.mult, op1=ALU.mult), reads=[dyb, dq], writes=[do_])
                fw.dma("pool", yv[:, :, t0:t0 + Wb], o_[:, :, 0:Wb], reads=[do_], writes=[self.dep_yb])
    fw.barrier()


def _scratch_once(self, name, shape):
    if not hasattr(self, "_sc"):
        self._sc = {}
        self._scd = {}
    if name not in self._sc:
        self._sc[name] = self.scratch(name, shape)
        self._scd[name] = Dep()
    return self._sc[name]


def _dep_once(self, name):
    return self._scd[name]


Builder.mix_ssd = _mix_ssd
Builder.scratch_once = _scratch_once
Builder.dep_once = _dep_once


RW_ARR = ("r", "v", "kkn", "g", "bonus", "lw0", "kd0", "b0", "lw1", "kd1", "b1")


def _mix_rwkv(self, li):
    fw = self.fw
    I = self.I
    SC = {n: self.scratch_once("rw_" + n, (256, L)) for n in RW_ARR}
    dSC = {n: self.dep_once("rw_" + n) for n in RW_ARR}
    scv = {n: SC[n].rearrange("(kt p) t -> p kt t", p=128) for n in RW_ARR}

    def vec2(es_, name, ap1d, dep):
        t = self.sb(es_, name, [128, 2])
        fw.dma("sp", t[:], ap1d.rearrange("(kt p) -> p kt", p=128), writes=[dep], slow=True)
        return t

    with ExitStack() as es:
        dpar = Dep()
        mu = [vec2(es, "rw_mu%d" % a, I["rwkv_mu_rkv"][li, a], dpar) for a in range(3)]
        muw = [vec2(es, "rw_muw%d" % a, I["rwkv_mu_wag"][li, a], dpar) for a in range(3)]
        w0 = [vec2(es, "rw_w0%d" % d, I["rwkv_w0"][li, d], dpar) for d in range(2)]
        a0 = [vec2(es, "rw_a0%d" % d, I["rwkv_a0"][li, d], dpar) for d in range(2)]
        k_k = vec2(es, "rw_kk", I["rwkv_k_k"][li], dpar)
        k_a = vec2(es, "rw_ka", I["rwkv_k_a"][li], dpar)
        r_k = vec2(es, "rw_rk", I["rwkv_r_k"][li].rearrange("h n -> (h n)"), dpar)
        tiny = self.sb(es, "rw_tiny", [128, 1])
        fw.op("dve", lambda e: e.memset(tiny[:], 1e-12), writes=[dpar])
        blk = self.sb(es, "rw_blk", [128, 128], BF16)
        with ExitStack() as es2:
            blkf = self.sb(es2, "rw_blkf", [128, 128])
            dblk = Dep()
            fw.dma("sp", blkf[:], I["c_blk"][:, :], writes=[dblk])
            fw.op("dve", lambda e: e.tensor_copy(out=blk[:], in_=blkf[:]), reads=[dblk], writes=[dpar])
            fw.barrier()
        w1 = [self.load_weight_bf(es, "rw_w1%d" % d, I["rwkv_w1"][li, d], 256, 64) for d in range(2)]
        a1 = [self.load_weight_bf(es, "rw_a1%d" % d, I["rwkv_a1"][li, d], 256, 64) for d in range(2)]
        g1 = self.load_weight_bf(es, "rw_g1", I["rwkv_g1"][li], 256, 128)
        g2 = self.load_weight_bf(es, "rw_g2", I["rwkv_g2"][li], 128, 256)

        def load64(name, ap):
            t = self.sb(es, name, [64, 256], BF16)
            dd = Dep()
            with ExitStack() as es2:
                tf = self.sb(es2, name + "f", [64, 256])
                fw.dma("sp", tf[:], ap, writes=[dd])
                fw.op("dve", lambda e: e.tensor_copy(out=t[:], in_=tf[:]), reads=[dd], writes=[dd])
                fw.barrier()
            return t, dd
        w2 = [load64("rw_w2%d" % d, I["rwkv_w2"][li, d]) for d in range(2)]
        a2 = [load64("rw_a2%d" % d, I["rwkv_a2"][li, d]) for d in range(2)]

        W = 512
        X = self.sb(es, "rw_X", [128, 4, 2, W + 2])
        dX = Dep()
        Q = self.sb(es, "rw_Q", [128, 3, 2, W])
        dQ = Dep()
        T1 = self.sb(es, "rw_T1", [128, 2, W])
        dT1 = Dep()
        XW = self.sb(es, "rw_XW", [128, 3, 2, W], BF16)
        dXW = Dep()
        Hh = self.sb(es, "rw_Hh", [128, W], BF16)
        dHh = Dep()
        AS = self.sb(es, "rw_AS", [128, 2, W])
        dAS = Dep()
        KK = self.sb(es, "rw_KK", [128, 2, W])
        dKK = Dep()
        SQ = self.sb(es, "rw_SQ", [128, 2, W], BF16)
        dSQ = Dep()
        RS = self.sb(es, "rw_RS", [128, W])
        dRS = Dep()
        O = {n: self.sb(es, "rw_O_" + n, [128, 2, W]) for n in ("g", "bonus", "lw", "kd", "b")}
        dO = {n: Dep() for n in O}
        rkv_src = [self.projT[OFF_RKVX + a * 256:OFF_RKVX + (a + 1) * 256, :].rearrange("(kt p) t -> p kt t", p=128) for a in range(4)]
        for bi, (t0, Wb) in enumerate(BLOCKS):
            lo = max(t0 - 1, 0)
            hi = min(t0 + Wb + 1, L)
            c0 = lo - (t0 - 1)
            if t0 == 0:
                fw.op("dve", lambda e: e.memset(X[:, :, :, 0:1], 0.0), writes=[dX])
            if t0 + Wb == L:
                fw.op("dve", lambda e: e.memset(X[:, :, :, Wb + 1:Wb + 2], 0.0), writes=[dX])
            for a in range(4):
                fw.dma("sp", X[:, a, :, c0:c0 + (hi - lo)], rkv_src[a][:, :, lo:hi], reads=[self.dep_proj], writes=[dX])
            for a in range(4):
                for kt in range(2):
                    ctr, lf, rt = X[:, a, kt, 1:Wb + 1], X[:, a, kt, 0:Wb], X[:, a, kt, 2:Wb + 2]
                    fw.op("pool", lambda e: e.tensor_tensor(out=T1[:, kt, 0:Wb], in0=lf, in1=rt, op=ALU.add), reads=[dX], writes=[dT1])
                    fw.op("dve", lambda e: e.scalar_tensor_tensor(out=T1[:, kt, 0:Wb], in0=T1[:, kt, 0:Wb], scalar=0.5, in1=ctr, op0=ALU.mult, op1=ALU.subtract), reads=[dT1, dX], writes=[dT1])
                    if a < 3:
                        fw.op("dve", lambda e: e.scalar_tensor_tensor(out=Q[:, a, kt, 0:Wb], in0=T1[:, kt, 0:Wb], scalar=mu[a][:, kt:kt + 1], in1=ctr, op0=ALU.mult, op1=ALU.add), reads=[dT1, dX, dpar], writes=[dQ])
                    else:
                        for i3 in range(3):
                            fw.op("dve", lambda e: e.scalar_tensor_tensor(out=XW[:, i3, kt, 0:Wb], in0=T1[:, kt, 0:Wb], scalar=muw[i3][:, kt:kt + 1], in1=ctr, op0=ALU.mult, op1=ALU.add), reads=[dT1, dX, dpar], writes=[dXW])
            fw.dma("pool", scv["r"][:, :, t0:t0 + Wb], Q[:, 0, :, 0:Wb], reads=[dQ], writes=[dSC["r"]])
            fw.dma("pool", scv["v"][:, :, t0:t0 + Wb], Q[:, 2, :, 0:Wb], reads=[dQ], writes=[dSC["v"]])
            ps, dp = self.next_ps()
            for kt in range(2):
                fw.op("pe", lambda e: e.matmul(ps[:, 0:Wb], lhsT=g1[0][:, kt, :], rhs=XW[:, 2, kt, 0:Wb], start=(kt == 0), stop=(kt == 1)), reads=[g1[1], dXW], writes=[dp])
            fw.op("act", lambda e: e.activation(out=Hh[:, 0:Wb], in_=ps[:, 0:Wb], func=AF.Sigmoid), reads=[dp], writes=[dHh])
            for ct in range(2):
                ps, dp = self.next_ps()
                fw.op("pe", lambda e: e.matmul(ps[:, 0:Wb], lhsT=g2[0][:, 0, ct * 128:(ct + 1) * 128], rhs=Hh[:, 0:Wb], start=True, stop=True), reads=[g2[1], dHh], writes=[dp])
                fw.op("act", lambda e: e.copy(out=O["g"][:, ct, 0:Wb], in_=ps[:, 0:Wb]), reads=[dp], writes=[dO["g"]])
            fw.dma("pool", scv["g"][:, :, t0:t0 + Wb], O["g"][:, :, 0:Wb], reads=[dO["g"]], writes=[dSC["g"]])
            for kt in range(2):
                fw.op("dve", lambda e: e.tensor_scalar(out=KK[:, kt, 0:Wb], in0=Q[:, 1, kt, 0:Wb], scalar1=k_k[:, kt:kt + 1], scalar2=None, op0=ALU.mult), reads=[dQ, dpar], writes=[dKK])
                fw.op("act", lambda e: e.activation(out=SQ[:, kt, 0:Wb], in_=KK[:, kt, 0:Wb], func=AF.Square), reads=[dKK], writes=[dSQ])
                ps, dp = self.next_ps()
                fw.op("pe", lambda e: e.matmul(ps[:, 0:Wb], lhsT=blk[:, :], rhs=SQ[:, kt, 0:Wb], start=True, stop=True), reads=[dSQ, dpar], writes=[dp])
                fw.op("act", lambda e: e.activation(out=RS[:, 0:Wb], in_=ps[:, 0:Wb], func=AF.Sqrt, bias=tiny[:, 0:1]), reads=[dp, dpar], writes=[dRS])
                fw.op("dve", lambda e: e.reciprocal(out=RS[:, 0:Wb], in_=RS[:, 0:Wb]), reads=[dRS], writes=[dRS])
                fw.op("dve", lambda e: e.tensor_tensor(out=KK[:, kt, 0:Wb], in0=KK[:, kt, 0:Wb], in1=RS[:, 0:Wb], op=ALU.mult), reads=[dKK, dRS], writes=[dKK])
            fw.dma("pool", scv["kkn"][:, :, t0:t0 + Wb], KK[:, :, 0:Wb], reads=[dKK], writes=[dSC["kkn"]])
            for kt in range(2):
                fw.op("pool", lambda e: e.tensor_tensor(out=T1[:, kt, 0:Wb], in0=Q[:, 0, kt, 0:Wb], in1=Q[:, 1, kt, 0:Wb], op=ALU.mult), reads=[dQ, dT1], writes=[dT1])
                fw.op("dve", lambda e: e.tensor_scalar(out=SQ[:, kt, 0:Wb], in0=T1[:, kt, 0:Wb], scalar1=r_k[:, kt:kt + 1], scalar2=None, op0=ALU.mult), reads=[dT1, dpar, dSQ], writes=[dSQ])
                ps, dp = self.next_ps()
                fw.op("pe", lambda e: e.matmul(ps[:, 0:Wb], lhsT=blk[:, :], rhs=SQ[:, kt, 0:Wb], start=True, stop=True), reads=[dSQ, dpar], writes=[dp])
                fw.op("dve", lambda e: e.tensor_tensor(out=O["bonus"][:, kt, 0:Wb], in0=ps[:, 0:Wb], in1=Q[:, 2, kt, 0:Wb], op=ALU.mult), reads=[dp, dQ], writes=[dO["bonus"]])
            fw.dma("pool", scv["bonus"][:, :, t0:t0 + Wb], O["bonus"][:, :, 0:Wb], reads=[dO["bonus"]], writes=[dSC["bonus"]])
            for d in range(2):
                ps, dp = self.next_ps()
                for kt in range(2):
                    fw.op("pe", lambda e: e.matmul(ps[0:64, 0:Wb], lhsT=w1[d][0][:, kt, :], rhs=XW[:, 0, kt, 0:Wb], start=(kt == 0), stop=(kt == 1)), reads=[w1[d][1], dXW], writes=[dp])
                fw.op("act", lambda e: e.activation(out=Hh[0:64, 0:Wb], in_=ps[0:64, 0:Wb], func=AF.Tanh), reads=[dp], writes=[dHh])
                for ct in range(2):
                    ps, dp = self.next_ps()
                    fw.op("pe", lambda e: e.matmul(ps[:, 0:Wb], lhsT=w2[d][0][:, ct * 128:(ct + 1) * 128], rhs=Hh[0:64, 0:Wb], start=True, stop=True), reads=[w2[d][1], dHh], writes=[dp])
                    fw.op("act", lambda e: e.activation(out=O["lw"][:, ct, 0:Wb], in_=ps[:, 0:Wb], func=AF.Sigmoid, bias=w0[d][:, ct:ct + 1]), reads=[dp, dpar], writes=[dO["lw"]])
                    fw.op("dve", lambda e: e.tensor_scalar(out=O["lw"][:, ct, 0:Wb], in0=O["lw"][:, ct, 0:Wb], scalar1=-0.6065306597126334, scalar2=None, op0=ALU.mult), reads=[dO["lw"]], writes=[dO["lw"]])
                fw.dma("pool", scv["lw%d" % d][:, :, t0:t0 + Wb], O["lw"][:, :, 0:Wb], reads=[dO["lw"]], writes=[dSC["lw%d" % d]])
                ps, dp = self.next_ps()
                for kt in range(2):
                    fw.op("pe", lambda e: e.matmul(ps[0:64, 0:Wb], lhsT=a1[d][0][:, kt, :], rhs=XW[:, 1, kt, 0:Wb], start=(kt == 0), stop=(kt == 1)), reads=[a1[d][1], dXW], writes=[dp])
                fw.op("act", lambda e: e.copy(out=Hh[0:64, 0:Wb], in_=ps[0:64, 0:Wb]), reads=[dp], writes=[dHh])
                for ct in range(2):
                    ps, dp = self.next_ps()
                    fw.op("pe", lambda e: e.matmul(ps[:, 0:Wb], lhsT=a2[d][0][:, ct * 128:(ct + 1) * 128], rhs=Hh[0:64, 0:Wb], start=True, stop=True), reads=[a2[d][1], dHh], writes=[dp])
                    fw.op("act", lambda e: e.activation(out=AS[:, ct, 0:Wb], in_=ps[:, 0:Wb], func=AF.Sigmoid, bias=a0[d][:, ct:ct + 1]), reads=[dp, dpar], writes=[dAS])
                for kt in range(2):
                    fw.op("dve", lambda e: e.tensor_scalar(out=O["kd"][:, kt, 0:Wb], in0=AS[:, kt, 0:Wb], scalar1=-1.0, scalar2=None, op0=ALU.add), reads=[dAS], writes=[dO["kd"]])
                    fw.op("dve", lambda e: e.tensor_scalar(out=O["kd"][:, kt, 0:Wb], in0=O["kd"][:, kt, 0:Wb], scalar1=k_a[:, kt:kt + 1], scalar2=1.0, op0=ALU.mult, op1=ALU.add), reads=[dO["kd"], dpar], writes=[dO["kd"]])
                    fw.op("pool", lambda e: e.tensor_tensor(out=O["kd"][:, kt, 0:Wb], in0=O["kd"][:, kt, 0:Wb], in1=Q[:, 1, kt, 0:Wb], op=ALU.mult), reads=[dO["kd"], dQ], writes=[dO["kd"]])
                    fw.op("pool", lambda e: e.tensor_tensor(out=O["b"][:, kt, 0:Wb], in0=KK[:, kt, 0:Wb], in1=AS[:, kt, 0:Wb], op=ALU.mult), reads=[dKK, dAS], writes=[dO["b"]])
                fw.dma("pool", scv["kd%d" % d][:, :, t0:t0 + Wb], O["kd"][:, :, 0:Wb], reads=[dO["kd"]], writes=[dSC["kd%d" % d]])
                fw.dma("pool", scv["b%d" % d][:, :, t0:t0 + Wb], O["b"][:, :, 0:Wb], reads=[dO["b"]], writes=[dSC["b%d" % d]])
    fw.barrier()
    self._rwkv_scan(li, SC, dSC, scv)


Builder.mix_rwkv = _mix_rwkv


def _rwkv_scan(self, li, SC, dSC, scv):
    fw = self.fw
    I = self.I
    names = ("rt", "at", "kt", "bt", "kh", "bh", "vb")
    AD = {}
    dAD = {}
    for d in range(2):
        for kt in range(2):
            for n in names:
                AD[(d, kt, n)] = self.scratch_once("rwA_%d_%d_%s" % (d, kt, n), (128, LP)) if False else None
    if not hasattr(self, "_rwA"):
        self._rwA = {}
        for d in range(2):
            for kt in range(2):
                self._rwA[(d, kt)] = self.nc.dram_tensor("rwA_%d_%d" % (d, kt), [128, 7, LP], BF16, kind="Internal").ap()
        self._rwA_dep = {k: Dep() for k in self._rwA}
    with ExitStack() as es:
        yacc = self.sb(es, "rs_yacc", [128, 2, L])
        dyacc = Dep()
        tril_s = self.sb(es, "rs_tril_s", [128, 128])
        tril_i = self.sb(es, "rs_tril_i", [128, 128])
        trilT_s = self.sb(es, "rs_trilT_s", [128, 128])
        identb = self.sb(es, "rs_identb", [128, 128], BF16)
        dc = Dep()
        fw.dma("sp", tril_s[:], I["c_tril_s"][:, :], writes=[dc])
        fw.dma("sp", tril_i[:], I["c_tril_i"][:, :], writes=[dc])
        fw.dma("sp", trilT_s[:], I["c_trilT_s"][:, :], writes=[dc])
        fw.op("dve", lambda e: e.tensor_copy(out=identb[:], in_=self.ident[:]), reads=[self.d_const], writes=[dc])
        ones1 = self.sb(es, "rs_ones", [128, 128])
        fw.op("dve", lambda e: e.memset(ones1[:], 1.0), writes=[dc])
        etot = self.sb(es, "rs_etot", [128, 2, NCH])
        detot = Dep()
        for d in range(2):
            for kt in range(2):
                with ExitStack() as es3:
                    A = self.sb(es3, "rs_A", [128, 7, LP], BF16)
                    dA = Dep()
                    AI = {n: i for i, n in enumerate(names)}
                    stg = [self.sb(es3, "rs_stg%d" % i, [128, L]) for i in range(2)]
                    dstg = [Dep(), Dep()]
                    cs = self.sb(es3, "rs_cs", [128, LP])
                    lwr = self.sb(es3, "rs_lwr", [128, LP])
                    E1 = self.sb(es3, "rs_E1", [128, LP])
                    E2 = self.sb(es3, "rs_E2", [128, LP])
                    dcs, dlw, dE1, dE2 = Dep(), Dep(), Dep(), Dep()
                    fw.op("pool", lambda e: e.memset(A[:, :, L:LP], 0.0), writes=[dA])

                    def ld(i, name):
                        fw.dma("sp", stg[i][:], scv[name][:, kt, :], reads=[dSC[name]], writes=[dstg[i]])
                        return stg[i][:, :] if d == 0 else stg[i][:, ::-1]

                    sv = ld(0, "lw%d" % d)
                    fw.op("dve", lambda e: e.memset(lwr[:, L:LP], 0.0), writes=[dlw])
                    fw.op("dve", lambda e: e.tensor_copy(out=lwr[:, 0:L], in_=sv), reads=[dstg[0]], writes=[dlw])
                    for c in range(NCH):
                        fw.op("dve", lambda e: e.tensor_tensor_scan(out=cs[:, c * 128:(c + 1) * 128], data0=ones1[:, :], data1=lwr[:, c * 128:(c + 1) * 128], initial=0.0, op0=ALU.mult, op1=ALU.add),
                              reads=[dlw, dc], writes=[dcs])
                    fw.op("act", lambda e: e.activation(out=etot[:, kt, :], in_=cs[:, 127::128], func=AF.Exp), reads=[dcs], writes=[detot])
                    fw.op("act", lambda e: e.activation(out=E1[:], in_=cs[:], func=AF.Exp), reads=[dcs], writes=[dE1])
                    sv = ld(1, "r")
                    fw.op("dve", lambda e: e.tensor_tensor(out=A[:, AI["rt"], 0:L], in0=sv, in1=E1[:, 0:L], op=ALU.mult), reads=[dstg[1], dE1], writes=[dA])
                    fw.op("pool", lambda e: e.tensor_tensor(out=lwr[:], in0=cs[:], in1=lwr[:], op=ALU.subtract), reads=[dcs, dlw], writes=[dlw])
                    fw.op("act", lambda e: e.activation(out=E1[:], in_=lwr[:], func=AF.Exp), reads=[dlw, dE1], writes=[dE1])
                    sv = ld(0, "kkn")
                    fw.op("dve", lambda e: e.scalar_tensor_tensor(out=A[:, AI["at"], 0:L], in0=sv, scalar=-1.0, in1=E1[:, 0:L], op0=ALU.mult, op1=ALU.mult), reads=[dstg[0], dE1], writes=[dA])
                    for c in range(NCH):
                        fw.op("dve", lambda e: e.tensor_scalar(out=lwr[:, c * 128:(c + 1) * 128], in0=cs[:, c * 128:(c + 1) * 128], scalar1=cs[:, c * 128 + 127:c * 128 + 128], scalar2=-1.0, op0=ALU.subtract, op1=ALU.mult),
                              reads=[dcs, dlw], writes=[dlw])
                    fw.op("act", lambda e: e.activation(out=E1[:], in_=lwr[:], func=AF.Exp), reads=[dlw, dE1], writes=[dE1])
                    fw.op("act", lambda e: e.activation(out=E2[:], in_=cs[:], func=AF.Exp, scale=-1.0), reads=[dcs], writes=[dE2])
                    sv = ld(1, "kd%d" % d)
                    fw.op("dve", lambda e: e.tensor_tensor(out=A[:, AI["kt"], 0:L], in0=sv, in1=E2[:, 0:L], op=ALU.mult), reads=[dstg[1], dE2], writes=[dA])
                    fw.op("pool", lambda e: e.tensor_tensor(out=A[:, AI["kh"], 0:L], in0=sv, in1=E1[:, 0:L], op=ALU.mult), reads=[dstg[1], dE1], writes=[dA])
                    sv = ld(0, "b%d" % d)
                    fw.op("dve", lambda e: e.tensor_tensor(out=A[:, AI["bt"], 0:L], in0=sv, in1=E2[:, 0:L], op=ALU.mult), reads=[dstg[0], dE2], writes=[dA])
                    fw.op("pool", lambda e: e.tensor_tensor(out=A[:, AI["bh"], 0:L], in0=sv, in1=E1[:, 0:L], op=ALU.mult), reads=[dstg[0], dE1], writes=[dA])
                    sv = ld(1, "v")
                    fw.op("dve", lambda e: e.tensor_copy(out=A[:, AI["vb"], 0:L], in_=sv), reads=[dstg[1]], writes=[dA])
                    fw.dma("pool", self._rwA[(d, kt)][:, :, :], A[:, :, :], reads=[dA], writes=[self._rwA_dep[(d, kt)]])
                    fw.barrier()
            with ExitStack() as es4:
                AA = [self.sb(es4, "rs_AA%d" % kt, [128, 7, LP], BF16) for kt in range(2)]
                dAA = [Dep(), Dep()]
                for kt in range(2):
                    fw.dma("sp", AA[kt][:, :, :], self._rwA[(d, kt)][:, :, :], reads=[self._rwA_dep[(d, kt)]], writes=[dAA[kt]])
                AI = {n: i for i, n in enumerate(names)}
                S0 = self.sb(es4, "rs_S0", [128, 2, 64])
                S0b = self.sb(es4, "rs_S0b", [128, 2, 64], BF16)
                dS0 = [[Dep(), Dep()], [Dep(), Dep()]]
                dS0b = [[Dep(), Dep()], [Dep(), Dep()]]
                fw.op("dve", lambda e: e.memset(S0[:], 0.0), writes=[x for y in dS0 for x in y])
                fw.op("dve", lambda e: e.memset(S0b[:], 0.0), writes=[x for y in dS0b for x in y])
                NHB = 2
                chains = [(kt, hh) for kt in range(2) for hh in range(2)]

                def mk(nm, shape, dt=F32):
                    return ({ch: [self.sb(es4, "rs_%s_%d%d_%d" % (nm, ch[0], ch[1], i), shape, dt) for i in range(NHB)] for ch in chains},
                            {ch: [Dep() for i in range(NHB)] for ch in chains})
                TK, dTK = mk("TK", [128, 192], BF16)
                Pm, dPm = mk("P", [128, 128])
                PTm, dPTm = mk("PT", [128, 128])
                XT, dXT = mk("XT", [128, 128])
                XTb, dXTb = mk("XTb", [128, 128], BF16)
                Ak, dAk = mk("Ak", [128, 3, 128], BF16)
                P1b, dP1b = mk("P1b", [128, 64], BF16)
                Zb, dZb = mk("Zb", [128, 64], BF16)
                for c in range(NCH):
                    t0 = c * 128
                    Wv = min(128, L - t0)
                    C = slice(t0, t0 + 128)
                    b_ = c % NHB
                    for ch in chains:
                        kt, hh = ch
                        R = slice(64 * hh, 64 * hh + 64)
                        Aq = lambda n: AA[kt][R, AI[n], C]
                        tk, dtk = TK[ch][b_], dTK[ch][b_]
                        ptk, dptk = self.next_ps()
                        for i3, n in enumerate(("vb", "kh", "bh")):
                            fw.op("pe", lambda e: e.matmul(ptk[:, i3 * 64:(i3 + 1) * 64], lhsT=Aq(n), rhs=identb[R, 64 * hh:64 * hh + 64], start=True, stop=True), reads=[dAA[kt], dc], writes=[dptk])
                        fw.op("pe", lambda e: e.matmul(ptk[:, 256:384], lhsT=Aq("bt"), rhs=Aq("rt"), start=True, stop=True), reads=[dAA[kt]], writes=[dptk])
                        pa, dpa = self.next_ps()
                        for i5, (l_, r_) in enumerate((("bt", "at"), ("at", "bt"), ("kt", "at"), ("kt", "rt"))):
                            fw.op("pe", lambda e: e.matmul(pa[:, i5 * 128:(i5 + 1) * 128], lhsT=Aq(l_), rhs=Aq(r_), start=True, stop=True), reads=[dAA[kt]], writes=[dpa])
                        fw.op("act", lambda e: e.copy(out=tk[:, :], in_=ptk[:, 0:192]), reads=[dptk], writes=[dtk])
                        P_, dP_ = Pm[ch][b_], dPm[ch][b_]
                        PT_, dPT_ = PTm[ch][b_], dPTm[ch][b_]
                        X_, dX_ = XT[ch][b_], dXT[ch][b_]
                        ak, dak = Ak[ch][b_], dAk[ch][b_]
                        fw.op("dve", lambda e: e.tensor_tensor(out=PT_[:], in0=pa[:, 0:128], in1=tril_s[:], op=ALU.mult), reads=[dpa, dc], writes=[dPT_])
                        fw.op("dve", lambda e: e.tensor_tensor(out=P_[:], in0=pa[:, 128:256], in1=trilT_s[:], op=ALU.mult), reads=[dpa, dc], writes=[dP_])
                        fw.op("dve", lambda e: e.tensor_tensor(out=ak[:, 0, :], in0=pa[:, 256:384], in1=tril_s[:], op=ALU.mult), reads=[dpa, dc], writes=[dak])
                        fw.op("dve", lambda e: e.tensor_tensor(out=ak[:, 1, :], in0=pa[:, 384:512], in1=tril_i[:], op=ALU.mult), reads=[dpa, dc], writes=[dak])
                        fw.op("dve", lambda e: e.tensor_tensor(out=ak[:, 2, :], in0=ptk[:, 256:384], in1=tril_i[:], op=ALU.mult), reads=[dptk, dc], writes=[dak])
                        fw.op("pool", lambda e: e.tensor_tensor(out=X_[:], in0=PT_[:], in1=self.ident[:], op=ALU.add), reads=[dPT_, self.d_const], writes=[dX_])
                    for s_ in range(6):
                        pns = {}
                        for ch in chains:
                            P_, dP_ = Pm[ch][b_], dPm[ch][b_]
                            PT_, dPT_ = PTm[ch][b_], dPTm[ch][b_]
                            pn, dpn = self.next_ps()
                            pns[ch] = (pn, dpn)
                            fw.op("pe", lambda e: e.matmul(pn[:, 0:128], lhsT=PT_[:, :], rhs=P_[:, :], start=True, stop=True), reads=[dPT_, dP_], writes=[dpn])
                            if s_ < 5:
                                fw.op("pe", lambda e: e.matmul(pn[:, 128:256], lhsT=P_[:, :], rhs=PT_[:, :], start=True, stop=True), reads=[dPT_, dP_], writes=[dpn])
                        for ch in chains:
                            P_, dP_ = Pm[ch][b_], dPm[ch][b_]
                            PT_, dPT_ = PTm[ch][b_], dPTm[ch][b_]
                            pn, dpn = pns[ch]
                            fw.op("act", lambda e: e.copy(out=P_[:], in_=pn[:, 0:128]), reads=[dpn], writes=[dP_])
                            if s_ < 5:
                                fw.op("pool" if False else "act", lambda e: e.copy(out=PT_[:], in_=pn[:, 128:256]), reads=[dpn], writes=[dPT_])
                        pxs = {}
                        for ch in chains:
                            P_, dP_ = Pm[ch][b_], dPm[ch][b_]
                            X_, dX_ = XT[ch][b_], dXT[ch][b_]
                            px_, dpx_ = self.next_ps()
                            pxs[ch] = (px_, dpx_)
                            fw.op("pe", lambda e: e.matmul(px_[:, 0:128], lhsT=P_[:, :], rhs=X_[:, :], start=True, stop=True), reads=[dP_, dX_], writes=[dpx_])
                        for ch in chains:
                            X_, dX_ = XT[ch][b_], dXT[ch][b_]
                            px_, dpx_ = pxs[ch]
                            fw.op("dve", lambda e: e.tensor_tensor(out=X_[:], in0=px_[:, 0:128], in1=X_[:], op=ALU.add), reads=[dpx_, dX_], writes=[dX_])
                    pps = {}
                    for ch in chains:
                        kt, hh = ch
                        R = slice(64 * hh, 64 * hh + 64)
                        X_, dX_ = XT[ch][b_], dXT[ch][b_]
                        xb_, dxb_ = XTb[ch][b_], dXTb[ch][b_]
                        tk, dtk = TK[ch][b_], dTK[ch][b_]
                        ak, dak = Ak[ch][b_], dAk[ch][b_]
                        fw.op("act", lambda e: e.copy(out=xb_[:], in_=X_[:]), reads=[dX_], writes=[dxb_])
                        pp, dpp = self.next_ps()
                        pps[ch] = (pp, dpp)
                        fw.op("pe", lambda e: e.matmul(pp[:, 0:64], lhsT=AA[kt][R, AI["at"], C], rhs=S0b[R, kt, :], start=True, stop=False), reads=[dAA[kt], dS0b[kt][hh]], writes=[dpp])
                        fw.op("pe", lambda e: e.matmul(pp[:, 0:64], lhsT=ak[:, 0, :], rhs=tk[:, 0:64], start=False, stop=True), reads=[dak, dtk], writes=[dpp])
                    for ch in chains:
                        pp, dpp = pps[ch]
                        p1, dp1 = P1b[ch][b_], dP1b[ch][b_]
                        fw.op("act", lambda e: e.copy(out=p1[:], in_=pp[:, 0:64]), reads=[dpp], writes=[dp1])
                    pzs = {}
                    for ch in chains:
                        xb_, dxb_ = XTb[ch][b_], dXTb[ch][b_]
                        p1, dp1 = P1b[ch][b_], dP1b[ch][b_]
                        pz, dpz = self.next_ps()
                        pzs[ch] = (pz, dpz)
                        fw.op("pe", lambda e: e.matmul(pz[:, 0:64], lhsT=xb_[:, :], rhs=p1[:, :], start=True, stop=True), reads=[dxb_, dp1], writes=[dpz])
                    for ch in chains:
                        pz, dpz = pzs[ch]
                        z_, dz_ = Zb[ch][b_], dZb[ch][b_]
                        fw.op("dve", lambda e: e.tensor_copy(out=z_[:], in_=pz[:, 0:64]), reads=[dpz], writes=[dz_])
                    for ch in chains:
                        kt, hh = ch
                        R = slice(64 * hh, 64 * hh + 64)
                        tk, dtk = TK[ch][b_], dTK[ch][b_]
                        ak, dak = Ak[ch][b_], dAk[ch][b_]
                        z_, dz_ = Zb[ch][b_], dZb[ch][b_]
                        py, dpy = self.next_ps()
                        fw.op("pe", lambda e: e.matmul(py[R, 0:128], lhsT=S0b[R, kt, :], rhs=AA[kt][R, AI["rt"], C], start=True, stop=False), reads=[dAA[kt], dS0b[kt][hh]], writes=[dpy])
                        fw.op("pe", lambda e: e.matmul(py[R, 0:128], lhsT=tk[:, 0:64], rhs=ak[:, 1, :], start=False, stop=False), reads=[dak, dtk], writes=[dpy])
                        fw.op("pe", lambda e: e.matmul(py[R, 0:128], lhsT=z_[:, :], rhs=ak[:, 2, :], start=False, stop=True), reads=[dak, dz_], writes=[dpy])
                        pS, dpS = py, dpy
                        fw.op("pe", lambda e: e.matmul(pS[R, 256:320], lhsT=tk[:, 64:128], rhs=tk[:, 0:64], start=True, stop=False), reads=[dtk], writes=[dpS])
                        fw.op("pe", lambda e: e.matmul(pS[R, 256:320], lhsT=tk[:, 128:192], rhs=z_[:, :], start=False, stop=True), reads=[dtk, dz_], writes=[dpS])
                        if d == 0:
                            fw.op("act", lambda e: e.copy(out=yacc[R, kt, t0:t0 + Wv], in_=py[R, 0:Wv]), reads=[dpy], writes=[dyacc])
                        else:
                            lo = L - (t0 + Wv)
                            ya = yacc[R, kt, lo:lo + Wv]
                            fw.op("dve", lambda e: e.tensor_tensor(out=ya[:, ::-1], in0=py[R, 0:Wv], in1=ya[:, ::-1], op=ALU.add), reads=[dpy, dyacc], writes=[dyacc])
                        fw.op("dve", lambda e: e.scalar_tensor_tensor(out=S0[R, kt, :], in0=S0[R, kt, :], scalar=etot[R, kt, c:c + 1], in1=pS[R, 256:320], op0=ALU.mult, op1=ALU.add), reads=[dS0[kt][hh], detot, dpS], writes=[dS0[kt][hh]])
                        fw.op("act", lambda e: e.copy(out=S0b[R, kt, :], in_=S0[R, kt, :]), reads=[dS0[kt][hh]], writes=[dS0b[kt][hh]])
                fw.barrier()
        fw.barrier()
        with ExitStack() as es5:
            dq = Dep()
            def vec2(name, ap1d):
                t = self.sb(es5, name, [128, 2])
                fw.dma("sp", t[:], ap1d.rearrange("(kt p) -> p kt", p=128), writes=[dq], slow=True)
                return t
            lnw = vec2("r3_lnw", I["rwkv_ln_w"][li])
            lnb = vec2("r3_lnb", I["rwkv_ln_b"][li])
            epsl = self.sb(es5, "r3_eps", [128, 1])
            fw.op("dve", lambda e: e.memset(epsl[:], 64e-5), writes=[dq])
            blk = self.sb(es5, "r3_blk", [128, 128], BF16)
            blkf = self.sb(es5, "r3_blkf", [128, 128])
            fw.dma("sp", blkf[:], I["c_blk"][:, :], writes=[dq])
            fw.op("dve", lambda e: e.tensor_copy(out=blk[:], in_=blkf[:]), reads=[dq], writes=[dq])
            W = 512
            yb = self.sb(es5, "r3_yb", [128, W], BF16)
            sq = self.sb(es5, "r3_sq", [128, W], BF16)
            mean = self.sb(es5, "r3_mean", [128, W])
            var = self.sb(es5, "r3_var", [128, W])
            yc = [self.sb(es5, "r3_yc%d" % i, [128, 2, W]) for i in range(2)]
            bg = [self.sb(es5, "r3_bg%d" % i, [128, 2, 2, W]) for i in range(2)]
            dt_ = Dep()
            dyc = [Dep(), Dep()]
            dbg = [Dep(), Dep()]
            yv = self.ycT.rearrange("(kt p) t -> p kt t", p=128)
            for bi, (t0, Wb) in enumerate(BLOCKS):
                y_, dy_ = yc[bi % 2], dyc[bi % 2]
                b_, db_ = bg[bi % 2], dbg[bi % 2]
                fw.dma("sp", b_[:, 0, :, 0:Wb], scv["bonus"][:, :, t0:t0 + Wb], reads=[dSC["bonus"]], writes=[db_])
                fw.dma("sp", b_[:, 1, :, 0:Wb], scv["g"][:, :, t0:t0 + Wb], reads=[dSC["g"]], writes=[db_])
                for kt in range(2):
                    ysl = yacc[:, kt, t0:t0 + Wb]
                    fw.op("act", lambda e: e.copy(out=yb[:, 0:Wb], in_=ysl), reads=[dyacc], writes=[dt_])
                    fw.op("pool", lambda e: e.tensor_tensor(out=sq[:, 0:Wb], in0=ysl, in1=ysl, op=ALU.mult), reads=[dyacc], writes=[dt_])
                    pm, dpm = self.next_ps()
                    pq, dpq = self.next_ps()
                    fw.op("pe", lambda e: e.matmul(pm[:, 0:Wb], lhsT=blk[:, :], rhs=yb[:, 0:Wb], start=True, stop=True), reads=[dq, dt_], writes=[dpm])
                    fw.op("pe", lambda e: e.matmul(pq[:, 0:Wb], lhsT=blk[:, :], rhs=sq[:, 0:Wb], start=True, stop=True), reads=[dq, dt_], writes=[dpq])
                    fw.op("act", lambda e: e.mul(out=mean[:, 0:Wb], in_=pm[:, 0:Wb], mul=1.0 / 64), reads=[dpm], writes=[dt_])
                    fw.op("dve", lambda e: e.tensor_tensor(out=var[:, 0:Wb], in0=mean[:, 0:Wb], in1=mean[:, 0:Wb], op=ALU.mult), reads=[dt_], writes=[dt_])
                    fw.op("dve", lambda e: e.scalar_tensor_tensor(out=var[:, 0:Wb], in0=pq[:, 0:Wb], scalar=1.0 / 64, in1=var[:, 0:Wb], op0=ALU.mult, op1=ALU.subtract), reads=[dpq, dt_], writes=[dt_])
                    fw.op("act", lambda e: e.activation(out=var[:, 0:Wb], in_=var[:, 0:Wb], func=AF.Sqrt, bias=epsl[:, 0:1]), reads=[dt_, dq], writes=[dt_])
                    fw.op("dve", lambda e: e.reciprocal(out=var[:, 0:Wb], in_=var[:, 0:Wb]), reads=[dt_], writes=[dt_])
                    fw.op("pool", lambda e: e.tensor_tensor(out=y_[:, kt, 0:Wb], in0=ysl, in1=mean[:, 0:Wb], op=ALU.subtract), reads=[dyacc, dt_], writes=[dy_])
                    fw.op("dve", lambda e: e.tensor_tensor(out=y_[:, kt, 0:Wb], in0=y_[:, kt, 0:Wb], in1=var[:, 0:Wb], op=ALU.mult), reads=[dy_, dt_], writes=[dy_])
                    fw.op("dve", lambda e: e.tensor_scalar(out=y_[:, kt, 0:Wb], in0=y_[:, kt, 0:Wb], scalar1=lnw[:, kt:kt + 1], scalar2=lnb[:, kt:kt + 1], op0=ALU.mult, op1=ALU.add), reads=[dy_, dq], writes=[dy_])
                    fw.op("pool", lambda e: e.tensor_tensor(out=y_[:, kt, 0:Wb], in0=y_[:, kt, 0:Wb], in1=b_[:, 0, kt, 0:Wb], op=ALU.add), reads=[dy_, db_], writes=[dy_])
                    fw.op("dve", lambda e: e.tensor_tensor(out=y_[:, kt, 0:Wb], in0=y_[:, kt, 0:Wb], in1=b_[:, 1, kt, 0:Wb], op=ALU.mult), reads=[dy_, db_], writes=[dy_])
                fw.dma("pool", yv[:, :, t0:t0 + Wb], y_[:, :, 0:Wb], reads=[dy_], writes=[self.dep_yc])
    fw.barrier()


Builder._rwkv_scan = _rwkv_scan
```

```python
import numpy as np
from contextlib import ExitStack
import concourse.bass as bass
import concourse.mybir as mybir
from concourse.bass_utils import run_bass_kernel_spmd

F32 = mybir.dt.float32
BF16 = mybir.dt.bfloat16
AF = mybir.ActivationFunctionType
ALU = mybir.AluOpType

D = 1024
SEQ = 4096
NMETA = 16
L = SEQ + NMETA
DEPTH = 2
DFF = 4096
NIN = 5896
EPS = 1e-6
NDS = 48

OFF_U, OFF_Z, OFF_XBC, OFF_DT, OFF_RKVX, OFF_G = 0, 256, 768, 1792, 1800, 2824

BLOCKS = [(i * 512, 512) for i in range(8)] + [(4096, 16)]


class Dep:
    __slots__ = ("w", "r", "x")

    def __init__(self, excl=False):
        self.w = None
        self.r = []
        self.x = excl


class FW:
    def __init__(self, nc, es):
        self.nc = nc
        self.engs = dict(pe=nc.tensor, act=nc.scalar, dve=nc.vector, pool=nc.gpsimd, sp=nc.sync)
        self.sem = {k: es.enter_context(nc.semaphore("s_" + k)) for k in self.engs}
        self.cnt = {k: 0 for k in self.engs}
        self.seen = {k: {} for k in self.engs}
        self.dsem = [es.enter_context(nc.semaphore("d%d" % i)) for i in range(NDS)]
        self.dval = [0] * NDS
        self.dnext = 0
        self.dnext2 = 0
        self.nins = 0

    def _wait(self, eng, ev):
        key, val = ev
        if self.seen[eng].get(key, 0) >= val:
            return
        self.seen[eng][key] = val
        sem = self.sem[key[1]] if key[0] == "e" else self.dsem[key[1]]
        self.engs[eng].wait_ge(sem, val)

    def _deps(self, eng, reads, writes):
        me = ("e", eng)
        for d in reads:
            if d.w is not None:
                self._wait(eng, d.w)
        for d in writes:
            if d.w is not None and d.w[0] != me:
                self._wait(eng, d.w)
            for r in d.r:
                if r[0] != me:
                    self._wait(eng, r)

    def _post(self, ev, reads, writes):
        for d in writes:
            d.w = ev
            d.r = []
        for d in reads:
            d.r = [r for r in d.r if r[0] != ev[0]] + [ev]

    def op(self, eng, fn, reads=(), writes=()):
        xs = [d for d in reads if d.x]
        if xs:
            writes = list(writes) + [d for d in xs if d not in writes]
            reads = [d for d in reads if not d.x]
        self._deps(eng, reads, writes)
        ins = fn(self.engs[eng])
        self.cnt[eng] += 1
        self.nins += 1
        ins.then_inc(self.sem[eng], 1)
        self._post((("e", eng), self.cnt[eng]), reads, writes)

    def dma(self, q, out, in_, reads=(), writes=(), slow=False):
        self._deps(q, reads, writes)
        half = NDS // 2
        if q == "pool":
            i = half + self.dnext2
            self.dnext2 = (self.dnext2 + 1) % half
        else:
            i = self.dnext
            self.dnext = (self.dnext + 1) % half
        if self.dval[i] > 0:
            self._wait(q, (("d", i), self.dval[i]))
        self.dval[i] += 16
        self.nins += 1
        if slow:
            self.engs[q].dma_start(out=out, in_=in_, allow_slow_non_contiguous=True).then_inc(self.dsem[i], 16)
        else:
            self.engs[q].dma_start(out=out, in_=in_).then_inc(self.dsem[i], 16)
        self._post((("d", i), self.dval[i]), reads, writes)

    def barrier(self):
        for e in self.engs:
            for e2 in self.engs:
                if self.cnt[e2] > 0:
                    self._wait(e, (("e", e2), self.cnt[e2]))
            for i in range(NDS):
                if self.dval[i] > 0:
                    self._wait(e, (("d", i), self.dval[i]))


def col_tiles(lo, hi):
    out = []
    c = lo
    while c < hi:
        m = min(128, hi - c)
        out.append((c, m))
        c += m
    return out


IN_TILES = (col_tiles(OFF_U, OFF_Z) + col_tiles(OFF_Z, OFF_XBC) + col_tiles(OFF_XBC, OFF_DT)
            + col_tiles(OFF_DT, OFF_RKVX) + col_tiles(OFF_RKVX, OFF_G) + col_tiles(OFF_G, NIN))


class Builder:
    def __init__(self, cfg):
        self.cfg = cfg
        nc = self.nc = bass.Bass("TRN2", target_bir_lowering=False)
        self.I = {}
        self.es = ExitStack()

    def inp(self, name, shape):
        t = self.nc.dram_tensor(name, list(shape), F32, kind="ExternalInput").ap()
        self.I[name] = t
        return t

    def scratch(self, name, shape, dt=F32):
        kind = "ExternalOutput" if name in self.cfg.get("dump", ()) else "Internal"
        return self.nc.dram_tensor(name, list(shape), dt, kind=kind).ap()

    def sb(self, es, name, shape, dt=F32):
        self.uid = getattr(self, "uid", 0) + 1
        return es.enter_context(self.nc.sbuf_tensor("%s_%d" % (name, self.uid), list(shape), dt))

    def build(self):
        nc = self.nc
        cfg = self.cfg
        with self.es as es:
            fw = self.fw = FW(nc, es)
            I = self.I
            x = self.inp("x", (SEQ, D))
            for name, shape in WEIGHT_SHAPES:
                self.inp(name, shape)
            self.inp("c_ident", (128, 128))
            self.inp("c_iota", (128, 512))
            self.inp("c_triu", (128, 128))
            self.inp("c_padm", (128, 8))
            self.inp("c_blk", (128, 128))
            self.inp("c_trilT_s", (128, 128))
            self.inp("c_mneg", (128, 128))
            self.inp("c_tril_s", (128, 128))
            self.inp("c_tril_i", (128, 128))
            out = nc.dram_tensor("out", [SEQ, D], F32, kind="ExternalOutput").ap()
            self.hT = self.scratch("hT", (D, L))
            self.projT = self.scratch("projT", (NIN, L))
            self.yaT = self.scratch("yaT", (256, L))
            self.ybT = self.scratch("ybT", (512, L))
            self.ycT = self.scratch("ycT", (256, L))
            self.dep_hT = Dep()
            self.dep_proj = Dep()
            self.dep_ya, self.dep_yb, self.dep_yc = Dep(), Dep(), Dep()

            self.ident = self.sb(es, "ident", [128, 128])
            self.ones_bf = self.sb(es, "ones_bf", [128, 128], BF16)
            self.d_const = Dep()
            fw.dma("sp", self.ident[:], I["c_ident"][:, :], writes=[self.d_const])
            fw.op("dve", lambda e: e.memset(self.ones_bf[:], 1.0), writes=[self.d_const])
            self.one_t = self.sb(es, "one_t", [128, 1])
            fw.op("dve", lambda e: e.memset(self.one_t[:], 1.0), writes=[self.d_const])
            self.ps = [es.enter_context(nc.psum_tensor("ps%d" % i, [128, 512], F32)) for i in range(8)]
            self.dps = [Dep(True) for _ in range(8)]
            self.psn = 0

            only = cfg.get("only")
            if only is not None:
                for ph in only:
                    if ph == "p0":
                        self.phase0(x)
                    elif ph == "p1":
                        self.phase1(0)
                    elif ph == "3a":
                        self.phase3a(0)
                    elif ph == "3b":
                        self.phase3b(0)
                    elif ph == "pf":
                        self.phase_final(out)
                fw.barrier()
                return nc
            self.phase0(x)
            nlayers = cfg.get("layers", DEPTH)
            for li in range(nlayers):
                self.phase1(li)
                if cfg.get("fake_mix", False):
                    self.fake_mix()
                else:
                    self.mixers(li)
                if cfg.get("stop_after_mix", False):
                    break
                self.phase3a(li)
                self.phase3b(li)
            if not cfg.get("stop_after_mix", False):
                self.phase_final(out)
            fw.barrier()
        return nc

    def next_ps(self):
        i = self.psn
        self.psn = (i + 1) % 8
        return self.ps[i], self.dps[i]

    def phase0(self, x):
        fw = self.fw
        I = self.I
        with ExitStack() as es:
            xin = [self.sb(es, "p0_x%d" % i, [128, D]) for i in range(2)]
            dxin = [Dep(), Dep()]
            ho = [self.sb(es, "p0_h%d" % i, [128, 8, 128]) for i in range(2)]
            dho = [Dep(), Dep()]
            ntile = (L + 127) // 128
            for ti in range(ntile):
                t0 = ti * 128
                w = min(128, L - t0)
                xi, dx = xin[ti % 2], dxin[ti % 2]
                if ti == 0:
                    fw.dma("sp", xi[0:NMETA, :], I["meta_tokens"][:, :], writes=[dx])
                    fw.dma("sp", xi[NMETA:128, :], x[0:128 - NMETA, :], writes=[dx])
                else:
                    fw.dma("sp", xi[0:w, :], x[t0 - NMETA:t0 - NMETA + w, :], writes=[dx])
                h, dh = ho[ti % 2], dho[ti % 2]
                for kt in range(8):
                    ps, dp = self.next_ps()
                    fw.op("pe", lambda e: e.transpose(out=ps[:, 0:w], in_=xi[0:w, kt * 128:(kt + 1) * 128],
                                                      identity=self.ident[0:w, 0:w]),
                          reads=[dx, self.d_const], writes=[dp])
                    eng = "act" if kt % 2 == 0 else "dve"
                    if eng == "act":
                        fw.op("act", lambda e: e.copy(out=h[:, kt, 0:w], in_=ps[:, 0:w]), reads=[dp], writes=[dh])
                    else:
                        fw.op("dve", lambda e: e.tensor_copy(out=h[:, kt, 0:w], in_=ps[:, 0:w]), reads=[dp], writes=[dh])
                fw.dma("pool", self.hT.rearrange("(kt p) t -> p kt t", p=128)[:, :, t0:t0 + w], h[:, :, 0:w],
                       reads=[dh], writes=[self.dep_hT])
        fw.barrier()

    def load_weight_bf(self, es, name, w_ap, K, N, scale_ap=None, chunk=512):
        fw = self.fw
        kt_n = K // 128
        wbf = self.sb(es, name, [128, kt_n, N], BF16)
        dw = Dep()
        for kt in range(kt_n):
            fw.dma("pool", wbf[:, kt, :], w_ap[kt * 128:(kt + 1) * 128, :], writes=[dw])
        if scale_ap is not None:
            sc = self.sb(es, name + "_sc", [128, kt_n])
            dsc = Dep()
            fw.dma("sp", sc[:], scale_ap.rearrange("(kt p) -> p kt", p=128), writes=[dsc], slow=True)
            self.wcol = (sc, dsc)
        return wbf, dw

    def rmsnorm_block(self, h, dh, hn, dhn, sq, dsq, rstd, drstd, W):
        fw = self.fw
        for kt in range(8):
            fw.op("act", lambda e: e.activation(out=sq[:, kt, 0:W], in_=h[:, kt, 0:W], func=AF.Square),
                  reads=[dh], writes=[dsq])
        ps, dp = self.next_ps()
        for kt in range(8):
            fw.op("pe", lambda e: e.matmul(ps[:, 0:W], lhsT=self.ones_bf[:, :], rhs=sq[:, kt, 0:W],
                                           start=(kt == 0), stop=(kt == 7)),
                  reads=[dsq, self.d_const], writes=[dp])
        fw.op("act", lambda e: e.activation(out=rstd[:, 0:W], in_=ps[:, 0:W], func=AF.Sqrt, bias=self.eps_t[:, 0:1],
                                            scale=1.0 / D),
              reads=[dp, self.d_const], writes=[drstd])
        fw.op("dve", lambda e: e.reciprocal(out=rstd[:, 0:W], in_=rstd[:, 0:W]), reads=[drstd], writes=[drstd])
        sc, dsc = self.wcol
        for kt in range(8):
            fw.op("dve", lambda e: e.scalar_tensor_tensor(out=hn[:, kt, 0:W], in0=h[:, kt, 0:W], scalar=sc[:, kt:kt + 1],
                                                          in1=rstd[:, 0:W], op0=ALU.mult, op1=ALU.mult),
                  reads=[dh, drstd, dsc], writes=[dhn])

    def ensure_eps(self, es):
        self.eps_t = self.sb(es, "eps_t", [128, 1])
        self.fw.op("dve", lambda e: e.memset(self.eps_t[:], EPS), writes=[self.d_const])

    def phase1(self, li):
        fw = self.fw
        I = self.I
        with ExitStack() as es:
            self.ensure_eps(es)
            wbf, dw = self.load_weight_bf(es, "p1_w", I["w_in"][li], D, NIN, scale_ap=I["mix_norm_w"][li])
            h = [self.sb(es, "p1_h%d" % i, [128, 8, 512]) for i in range(2)]
            dh = [Dep(), Dep()]
            sq = self.sb(es, "p1_sq", [128, 8, 512], BF16)
            dsq = Dep()
            rstd = self.sb(es, "p1_rstd", [128, 512])
            drstd = Dep()
            hn = [self.sb(es, "p1_hn%d" % i, [128, 8, 512], BF16) for i in range(2)]
            dhn = [Dep(), Dep()]
            stg = [self.sb(es, "p1_o%d" % i, [128, 8, 512]) for i in range(2)]
            dstg = [Dep() for _ in range(2)]
            hTv = self.hT.rearrange("(kt p) t -> p kt t", p=128)
            batches = [(0, 6), (768, 8), (1792, None), (1800, 8)] + [(OFF_G + i * 1024, 8) for i in range(3)]
            ns = 0
            nb = 0
            for bi, (t0, W) in enumerate(BLOCKS):
                hb, dhb = h[bi % 2], dh[bi % 2]
                fw.dma("sp", hb[:, :, 0:W], hTv[:, :, t0:t0 + W], reads=[self.dep_hT], writes=[dhb])
                hnb, dhnb = hn[bi % 2], dhn[bi % 2]
                self.rmsnorm_block(hb, dhb, hnb, dhnb, sq, dsq, rstd, drstd, W)
                for (r0, nt) in batches:
                    s_, ds_ = stg[nb % 2], dstg[nb % 2]
                    nb += 1
                    tiles = [(r0, 8)] if nt is None else [(r0 + i * 128, 128) for i in range(nt)]
                    for ti, (c0, M) in enumerate(tiles):
                        ps, dp = self.next_ps()
                        for kt in range(8):
                            fw.op("pe", lambda e: e.matmul(ps[0:M, 0:W], lhsT=wbf[:, kt, c0:c0 + M], rhs=hnb[:, kt, 0:W],
                                                           start=(kt == 0), stop=(kt == 7)),
                                  reads=[dhnb, dw], writes=[dp])
                        if ns % 2 == 0:
                            fw.op("act", lambda e: e.copy(out=s_[0:M, ti, 0:W], in_=ps[0:M, 0:W]), reads=[dp], writes=[ds_])
                        else:
                            fw.op("dve", lambda e: e.tensor_copy(out=s_[0:M, ti, 0:W], in_=ps[0:M, 0:W]), reads=[dp], writes=[ds_])
                        ns += 1
                    if nt is None:
                        fw.dma("pool", self.projT[r0:r0 + 8, t0:t0 + W], s_[0:8, 0, 0:W], reads=[ds_], writes=[self.dep_proj])
                    else:
                        fw.dma("pool", self.projT[r0:r0 + nt * 128, t0:t0 + W].rearrange("(k p) t -> p k t", p=128),
                               s_[:, 0:nt, 0:W], reads=[ds_], writes=[self.dep_proj])
        fw.barrier()

    def fake_mix(self):
        fw = self.fw
        with ExitStack() as es:
            t = self.sb(es, "fm_t", [128, L])
            dt_ = Dep()
            for (dst, ddst, src0, n) in ((self.yaT, self.dep_ya, OFF_U, 2), (self.ybT, self.dep_yb, OFF_Z, 4),
                                         (self.ycT, self.dep_yc, OFF_RKVX, 2)):
                for j in range(n):
                    fw.dma("sp", t[:, :], self.projT[src0 + j * 128:src0 + (j + 1) * 128, :], reads=[self.dep_proj],
                           writes=[dt_])
                    fw.dma("sp", dst[j * 128:(j + 1) * 128, :], t[:, :], reads=[dt_], writes=[ddst])
        fw.barrier()

    def mixers(self, li):
        which = self.cfg.get("mix", ("s5", "ssd", "rwkv"))
        if "s5" in which:
            self.mix_s5(li)
        if "ssd" in which:
            self.mix_ssd(li)
        if "rwkv" in which:
            self.mix_rwkv(li)

    def phase3a(self, li):
        fw = self.fw
        I = self.I
        W = 512
        with ExitStack() as es:
            pa, dpa = self.load_weight_bf(es, "p3_pa", I["proj_a"][li], 256, D)
            pb, dpb = self.load_weight_bf(es, "p3_pb", I["proj_b"][li], 512, D)
            pc, dpc = self.load_weight_bf(es, "p3_pc", I["proj_c"][li], 256, D)
            wo, dwo = self.load_weight_bf(es, "p3_wo", I["w_out"][li], D, D)
            ystg = [self.sb(es, "p3_ys%d" % i, [128, 8, W]) for i in range(2)]
            dystg = [Dep(), Dep()]
            ybf = [self.sb(es, "p3_yb%d" % i, [128, 8, W], BF16) for i in range(2)]
            dybf = [Dep(), Dep()]
            g = [self.sb(es, "p3_g%d" % i, [128, 3, W]) for i in range(2)]
            dg = [Dep(), Dep()]
            mrg = self.sb(es, "p3_m", [128, 8, W], BF16)
            dmrg = Dep()
            tmp = [self.sb(es, "p3_t%d" % i, [128, W]) for i in range(2)]
            dtmp = [Dep(), Dep()]
            h = [self.sb(es, "p3_h%d" % i, [128, 8, W]) for i in range(2)]
            dh = [Dep(), Dep()]
            hTv = self.hT.rearrange("(kt p) t -> p kt t", p=128)
            gv = self.projT[OFF_G:NIN, :].rearrange("(b kt p) t -> p b kt t", p=128, b=3)
            ng = 0
            for bi, (t0, Wb) in enumerate(BLOCKS):
                ys, dys = ystg[bi % 2], dystg[bi % 2]
                yb, dyb = ybf[bi % 2], dybf[bi % 2]
                hb, dhb = h[bi % 2], dh[bi % 2]
                fw.dma("sp", ys[:, 0:2, 0:Wb], self.yaT.rearrange("(kt p) t -> p kt t", p=128)[:, :, t0:t0 + Wb],
                       reads=[self.dep_ya], writes=[dys])
                fw.dma("sp", ys[:, 2:6, 0:Wb], self.ybT.rearrange("(kt p) t -> p kt t", p=128)[:, :, t0:t0 + Wb],
                       reads=[self.dep_yb], writes=[dys])
                fw.dma("sp", ys[:, 6:8, 0:Wb], self.ycT.rearrange("(kt p) t -> p kt t", p=128)[:, :, t0:t0 + Wb],
                       reads=[self.dep_yc], writes=[dys])
                fw.dma("sp", hb[:, :, 0:Wb], hTv[:, :, t0:t0 + Wb], reads=[self.dep_hT], writes=[dhb])
                for kt in range(8):
                    eng = "dve" if kt % 2 == 0 else "pool"
                    fw.op(eng, lambda e: e.tensor_copy(out=yb[:, kt, 0:Wb], in_=ys[:, kt, 0:Wb]), reads=[dys], writes=[dyb])
                for dtile in range(8):
                    gg, dgg = g[ng % 2], dg[ng % 2]
                    ng += 1
                    fw.dma("sp", gg[:, :, 0:Wb], gv[:, :, dtile, t0:t0 + Wb], reads=[self.dep_proj], writes=[dgg])
                    fw.op("act", lambda e: e.activation(out=gg[:, :, 0:Wb], in_=gg[:, :, 0:Wb], func=AF.Sigmoid),
                          reads=[dgg], writes=[dgg])
                    tm, dtm = tmp[dtile % 2], dtmp[dtile % 2]
                    for br, (wt, dwt, k0, nk) in enumerate(((pa, dpa, 0, 2), (pb, dpb, 2, 4), (pc, dpc, 6, 2))):
                        ps, dp = self.next_ps()
                        for k in range(nk):
                            fw.op("pe", lambda e: e.matmul(ps[:, 0:Wb], lhsT=wt[:, k, dtile * 128:(dtile + 1) * 128],
                                                           rhs=yb[:, k0 + k, 0:Wb], start=(k == 0), stop=(k == nk - 1)),
                                  reads=[dyb, dwt], writes=[dp])
                        if br == 0:
                            fw.op("dve", lambda e: e.tensor_tensor(out=tm[:, 0:Wb], in0=ps[:, 0:Wb], in1=gg[:, 0, 0:Wb],
                                                                   op=ALU.mult), reads=[dp, dgg], writes=[dtm])
                        else:
                            fw.op("dve", lambda e: e.tensor_tensor(out=gg[:, br, 0:Wb], in0=ps[:, 0:Wb],
                                                                   in1=gg[:, br, 0:Wb], op=ALU.mult),
                                  reads=[dp, dgg], writes=[dgg])
                            if br == 1:
                                fw.op("dve", lambda e: e.tensor_tensor(out=tm[:, 0:Wb], in0=tm[:, 0:Wb],
                                                                       in1=gg[:, 1, 0:Wb], op=ALU.add),
                                      reads=[dtm, dgg], writes=[dtm])
                            else:
                                fw.op("dve", lambda e: e.tensor_tensor(out=mrg[:, dtile, 0:Wb], in0=tm[:, 0:Wb],
                                                                       in1=gg[:, 2, 0:Wb], op=ALU.add),
                                      reads=[dtm, dgg], writes=[dmrg])
                for dtile in range(8):
                    ps, dp = self.next_ps()
                    for k in range(8):
                        fw.op("pe", lambda e: e.matmul(ps[:, 0:Wb], lhsT=wo[:, k, dtile * 128:(dtile + 1) * 128],
                                                       rhs=mrg[:, k, 0:Wb], start=(k == 0), stop=(k == 7)),
                              reads=[dmrg, dwo], writes=[dp])
                    fw.op("dve", lambda e: e.tensor_tensor(out=hb[:, dtile, 0:Wb], in0=ps[:, 0:Wb], in1=hb[:, dtile, 0:Wb],
                                                           op=ALU.add), reads=[dp, dhb], writes=[dhb])
                fw.dma("pool", hTv[:, :, t0:t0 + Wb], hb[:, :, 0:Wb], reads=[dhb], writes=[self.dep_hT])
        fw.barrier()

    def phase3b(self, li):
        fw = self.fw
        I = self.I
        W = 384
        blocks = [(s, min(W, L - s)) for s in range(0, L, W)]
        with ExitStack() as es:
            self.ensure_eps(es)
            w1, dw1 = self.load_weight_bf(es, "p4_w1", I["mlp_w1"][li], D, DFF, scale_ap=I["mlp_norm_w"][li])
            w2, dw2 = self.load_weight_bf(es, "p4_w2", I["mlp_w2"][li], DFF, D)
            h = [self.sb(es, "p4_h%d" % i, [128, 8, W]) for i in range(2)]
            dh = [Dep(), Dep()]
            sq = self.sb(es, "p4_sq", [128, 8, W], BF16)
            dsq = Dep()
            rstd = self.sb(es, "p4_rstd", [128, W])
            drstd = Dep()
            hn = self.sb(es, "p4_hn", [128, 8, W], BF16)
            dhn = Dep()
            act = self.sb(es, "p4_act", [128, 32, W], BF16)
            dact = Dep()
            rl = [self.sb(es, "p4_rl%d" % i, [128, W]) for i in range(2)]
            drl = [Dep(), Dep()]
            hTv = self.hT.rearrange("(kt p) t -> p kt t", p=128)
            for bi, (t0, Wb) in enumerate(blocks):
                hb, dhb = h[bi % 2], dh[bi % 2]
                fw.dma("sp", hb[:, :, 0:Wb], hTv[:, :, t0:t0 + Wb], reads=[self.dep_hT], writes=[dhb])
                self.rmsnorm_block(hb, dhb, hn, dhn, sq, dsq, rstd, drstd, Wb)
                for f in range(32):
                    ps, dp = self.next_ps()
                    for k in range(8):
                        fw.op("pe", lambda e: e.matmul(ps[:, 0:Wb], lhsT=w1[:, k, f * 128:(f + 1) * 128],
                                                       rhs=hn[:, k, 0:Wb], start=(k == 0), stop=(k == 7)),
                              reads=[dhn, dw1], writes=[dp])
                    r, dr = rl[f % 2], drl[f % 2]
                    fw.op("act", lambda e: e.activation(out=r[:, 0:Wb], in_=ps[:, 0:Wb], func=AF.Relu),
                          reads=[dp], writes=[dr])
                    eng = "dve" if f % 2 == 0 else "pool"
                    fw.op(eng, lambda e: e.tensor_tensor(out=act[:, f, 0:Wb], in0=r[:, 0:Wb], in1=r[:, 0:Wb], op=ALU.mult),
                          reads=[dr], writes=[dact])
                for dtile in range(8):
                    ps, dp = self.next_ps()
                    for f in range(32):
                        fw.op("pe", lambda e: e.matmul(ps[:, 0:Wb], lhsT=w2[:, f, dtile * 128:(dtile + 1) * 128],
                                                       rhs=act[:, f, 0:Wb], start=(f == 0), stop=(f == 31)),
                              reads=[dact, dw2], writes=[dp])
                    fw.op("dve", lambda e: e.tensor_tensor(out=hb[:, dtile, 0:Wb], in0=ps[:, 0:Wb], in1=hb[:, dtile, 0:Wb],
                                                           op=ALU.add), reads=[dp, dhb], writes=[dhb])
                fw.dma("pool", hTv[:, :, t0:t0 + Wb], hb[:, :, 0:Wb], reads=[dhb], writes=[self.dep_hT])
        fw.barrier()

    def phase_final(self, out):
        fw = self.fw
        I = self.I
        with ExitStack() as es:
            self.ensure_eps(es)
            fnw = self.sb(es, "pf_w", [128, 8])
            dfnw = Dep()
            fw.dma("sp", fnw[:], I["final_norm_w"].rearrange("(kt p) -> p kt", p=128), writes=[dfnw], slow=True)
            h = [self.sb(es, "pf_h%d" % i, [128, 8, 512]) for i in range(2)]
            dh = [Dep(), Dep()]
            sq = self.sb(es, "pf_sq", [128, 8, 512], BF16)
            dsq = Dep()
            rstd = self.sb(es, "pf_rstd", [128, 512])
            drstd = Dep()
            o = [self.sb(es, "pf_o%d" % i, [128, D]) for i in range(2)]
            do = [Dep(), Dep()]
            dout = Dep()
            hTv = self.hT.rearrange("(kt p) t -> p kt t", p=128)
            no = 0
            for bi in range(8):
                t0 = NMETA + bi * 512
                W = 512
                hb, dhb = h[bi % 2], dh[bi % 2]
                fw.dma("sp", hb[:, :, 0:W], hTv[:, :, t0:t0 + W], reads=[self.dep_hT], writes=[dhb])
                for kt in range(8):
                    fw.op("act", lambda e: e.activation(out=sq[:, kt, 0:W], in_=hb[:, kt, 0:W], func=AF.Square),
                          reads=[dhb], writes=[dsq])
                ps, dp = self.next_ps()
                for kt in range(8):
                    fw.op("pe", lambda e: e.matmul(ps[:, 0:W], lhsT=self.ones_bf[:, :], rhs=sq[:, kt, 0:W],
                                                   start=(kt == 0), stop=(kt == 7)),
                          reads=[dsq, self.d_const], writes=[dp])
                fw.op("act", lambda e: e.activation(out=rstd[:, 0:W], in_=ps[:, 0:W], func=AF.Sqrt,
                                                    bias=self.eps_t[:, 0:1], scale=1.0 / D),
                      reads=[dp, self.d_const], writes=[drstd])
                fw.op("dve", lambda e: e.reciprocal(out=rstd[:, 0:W], in_=rstd[:, 0:W]), reads=[drstd], writes=[drstd])
                for kt in range(8):
                    fw.op("dve", lambda e: e.scalar_tensor_tensor(out=hb[:, kt, 0:W], in0=hb[:, kt, 0:W],
                                                                  scalar=fnw[:, kt:kt + 1], in1=rstd[:, 0:W],
                                                                  op0=ALU.mult, op1=ALU.mult),
                          reads=[dhb, drstd, dfnw], writes=[dhb])
                for tt in range(4):
                    ob, dob = o[no % 2], do[no % 2]
                    no += 1
                    for kt in range(8):
                        ps, dp = self.next_ps()
                        fw.op("pe", lambda e: e.transpose(out=ps[:, 0:128], in_=hb[:, kt, tt * 128:(tt + 1) * 128],
                                                          identity=self.ident[:, :]),
                              reads=[dhb, self.d_const], writes=[dp])
                        if kt % 2 == 0:
                            fw.op("act", lambda e: e.copy(out=ob[:, kt * 128:(kt + 1) * 128], in_=ps[:, 0:128]),
                                  reads=[dp], writes=[dob])
                        else:
                            fw.op("dve", lambda e: e.tensor_copy(out=ob[:, kt * 128:(kt + 1) * 128], in_=ps[:, 0:128]),
                                  reads=[dp], writes=[dob])
                    r0 = bi * 512 + tt * 128
                    fw.dma("pool", out[r0:r0 + 128, :], ob[:, :], reads=[dob], writes=[dout])
        fw.barrier()


WEIGHT_SHAPES = [
    ("meta_tokens", (16, 1024)), ("final_norm_w", (1024,)), ("mix_norm_w", (2, 1024)), ("w_in", (2, 1024, 5896)),
    ("s5_lambda_re", (2, 2, 16, 64)), ("s5_lambda_im", (2, 2, 16, 64)), ("s5_log_step", (2, 2, 16)),
    ("s5_b_re", (2, 16, 64, 16)), ("s5_b_im", (2, 16, 64, 16)), ("s5_c_re", (2, 16, 16, 64)),
    ("s5_c_im", (2, 16, 16, 64)), ("s5_d", (2, 256)), ("s5_glu_w", (2, 256, 512)), ("s5_glu_b", (2, 512)),
    ("ssd_conv_w", (2, 5, 1024)), ("ssd_conv_b", (2, 1024)), ("ssd_a_log", (2, 2, 8)), ("ssd_dt_bias", (2, 2, 8)),
    ("ssd_d", (2, 8)), ("ssd_norm_w", (2, 512)), ("rwkv_mu_rkv", (2, 3, 256)), ("rwkv_mu_wag", (2, 3, 256)),
    ("rwkv_w0", (2, 2, 256)), ("rwkv_w1", (2, 2, 256, 64)), ("rwkv_w2", (2, 2, 64, 256)), ("rwkv_a0", (2, 2, 256)),
    ("rwkv_a1", (2, 2, 256, 64)), ("rwkv_a2", (2, 2, 64, 256)), ("rwkv_g1", (2, 256, 128)), ("rwkv_g2", (2, 128, 256)),
    ("rwkv_k_k", (2, 256)), ("rwkv_k_a", (2, 256)), ("rwkv_r_k", (2, 4, 64)), ("rwkv_ln_w", (2, 256)),
    ("rwkv_ln_b", (2, 256)), ("proj_a", (2, 256, 1024)), ("proj_b", (2, 512, 1024)), ("proj_c", (2, 256, 1024)),
    ("w_out", (2, 1024, 1024)), ("mlp_norm_w", (2, 1024)), ("mlp_w1", (2, 1024, 4096)), ("mlp_w2", (2, 4096, 1024)),
]


def host_consts():
    return {"c_ident": np.eye(128, dtype=np.float32),
            "c_iota": np.ascontiguousarray(np.broadcast_to(np.arange(512, dtype=np.float32), (128, 512))),
            "c_triu": np.triu(np.ones((128, 128), np.float32)),
            "c_blk": np.kron(np.eye(2, dtype=np.float32), np.ones((64, 64), np.float32)),
            "c_trilT_s": np.tril(np.ones((128, 128), np.float32), -1),
            "c_padm": np.ascontiguousarray(np.broadcast_to((np.arange(128) < 16).astype(np.float32)[:, None], (128, 8))),
            "c_mneg": np.where(np.triu(np.ones((128, 128), bool)), 0.0, -30000.0).astype(np.float32),
            "c_tril_s": np.triu(np.ones((128, 128), np.float32), 1),
            "c_tril_i": np.triu(np.ones((128, 128), np.float32), 0)}


def run(inputs, cfg, ncores=8):
    b = Builder(cfg)
    nc = b.build()
    consts = host_consts()
    in_maps = []
    for c in range(ncores):
        m = {"x": np.ascontiguousarray(inputs["x"][c], dtype=np.float32)}
        for name, _ in WEIGHT_SHAPES:
            m[name] = np.ascontiguousarray(inputs[name], dtype=np.float32)
        m.update(consts)
        in_maps.append(m)
    res = run_bass_kernel_spmd(nc, in_maps, core_ids=list(range(ncores)))
    return res, b


def kernel(**inputs):
    res, _ = run(inputs, {})
    return np.stack([np.asarray(res.results[c]["out"]) for c in range(8)], axis=0).astype(np.float32)


PI = float(np.pi)
S5W = 256


def _mix_s5(self, li):
    fw = self.fw
    I = self.I
    nc = self.nc
    with ExitStack() as es:
        lr = self.sb(es, "s5_lr", [128, 16])
        lim = self.sb(es, "s5_li", [128, 16])
        dpar = Dep()
        fw.dma("sp", lr[:], I["s5_lambda_re"][li].rearrange("d (q gp) n -> (gp n) (d q)", gp=2), writes=[dpar], slow=True)
        fw.dma("sp", lim[:], I["s5_lambda_im"][li].rearrange("d (q gp) n -> (gp n) (d q)", gp=2), writes=[dpar], slow=True)
        stepb = self.sb(es, "s5_stepb", [128, 2, 8, 2])
        fw.dma("sp", stepb[:], I["s5_log_step"][li].rearrange("d (q gp) -> d q gp", gp=2).partition_broadcast(128),
               writes=[dpar], slow=True)
        step = self.sb(es, "s5_step", [128, 16])
        fw.op("act", lambda e: e.activation(out=step[0:64, :].rearrange("p (d q) -> p d q", d=2), in_=stepb[0:64, :, :, 0],
                                            func=AF.Exp), reads=[dpar], writes=[dpar])
        fw.op("act", lambda e: e.activation(out=step[64:128, :].rearrange("p (d q) -> p d q", d=2),
                                            in_=stepb[64:128, :, :, 1], func=AF.Exp), reads=[dpar], writes=[dpar])
        th = self.sb(es, "s5_th", [128, 16])
        rho = self.sb(es, "s5_rho", [128, 16])
        fw.op("dve", lambda e: e.tensor_tensor(out=th[:], in0=lim[:], in1=step[:], op=ALU.mult), reads=[dpar], writes=[dpar])
        fw.op("dve", lambda e: e.tensor_tensor(out=rho[:], in0=lr[:], in1=step[:], op=ALU.mult), reads=[dpar], writes=[dpar])
        fw.op("act", lambda e: e.activation(out=rho[:], in_=rho[:], func=AF.Exp), reads=[dpar], writes=[dpar])

        NT = S5W + 1
        tc = self.sb(es, "s5_tc", [128, 16, NT])
        ts = self.sb(es, "s5_ts", [128, 16, NT])
        dtab = Dep()
        with ExitStack() as es2:
            iot = self.sb(es2, "s5_iota", [128, NT])
            fw.dma("sp", iot[:], I["c_iota"][:, 0:NT], writes=[dtab])
            ph = self.sb(es2, "s5_ph", [128, 16, NT])
            ki = self.sb(es2, "s5_ki", [128, 16, NT], mybir.dt.int32)
            kf = self.sb(es2, "s5_kf", [128, 16, NT])
            for j in range(16):
                fw.op("dve", lambda e: e.tensor_scalar(out=ph[:, j, :], in0=iot[:], scalar1=th[:, j:j + 1], scalar2=None,
                                                       op0=ALU.mult), reads=[dpar, dtab], writes=[dtab])
            fw.op("dve", lambda e: e.tensor_scalar(out=ki[:], in0=ph[:], scalar1=1.0 / (2 * PI), scalar2=None, op0=ALU.mult),
                  reads=[dtab], writes=[dtab])
            fw.op("dve", lambda e: e.tensor_copy(out=kf[:], in_=ki[:]), reads=[dtab], writes=[dtab])
            fw.op("dve", lambda e: e.scalar_tensor_tensor(out=ph[:], in0=kf[:], scalar=-2 * PI, in1=ph[:], op0=ALU.mult,
                                                          op1=ALU.add), reads=[dtab], writes=[dtab])

            def wrap(t):
                fw.op("dve", lambda e: e.tensor_scalar(out=kf[:], in0=t[:], scalar1=PI, scalar2=-2 * PI, op0=ALU.is_gt,
                                                       op1=ALU.mult), reads=[dtab], writes=[dtab])
                fw.op("dve", lambda e: e.tensor_tensor(out=t[:], in0=t[:], in1=kf[:], op=ALU.add), reads=[dtab], writes=[dtab])
                fw.op("dve", lambda e: e.tensor_scalar(out=kf[:], in0=t[:], scalar1=-PI, scalar2=2 * PI, op0=ALU.is_lt,
                                                       op1=ALU.mult), reads=[dtab], writes=[dtab])
                fw.op("dve", lambda e: e.tensor_tensor(out=t[:], in0=t[:], in1=kf[:], op=ALU.add), reads=[dtab], writes=[dtab])

            wrap(ph)
            fw.op("act", lambda e: e.activation(out=ts[:], in_=ph[:], func=AF.Sin), reads=[dtab], writes=[dtab])
            fw.op("dve", lambda e: e.tensor_scalar(out=ph[:], in0=ph[:], scalar1=PI / 2, scalar2=None, op0=ALU.add),
                  reads=[dtab], writes=[dtab])
            wrap(ph)
            fw.op("act", lambda e: e.activation(out=tc[:], in_=ph[:], func=AF.Sin), reads=[dtab], writes=[dtab])
            fw.barrier()
        nsW = self.sb(es, "s5_nsW", [128, 16])
        fw.op("dve", lambda e: e.tensor_scalar(out=nsW[:], in0=ts[:, :, S5W], scalar1=-1.0, scalar2=None, op0=ALU.mult),
              reads=[dtab], writes=[dpar])
        nsB = self.sb(es, "s5_nsB", [128, 16])
        abr = self.sb(es, "s5_abr", [128, 16])
        abi = self.sb(es, "s5_abi", [128, 16])
        fw.op("dve", lambda e: e.tensor_tensor(out=abr[:], in0=rho[:], in1=tc[:, :, 1], op=ALU.mult), reads=[dpar, dtab], writes=[dpar])
        fw.op("dve", lambda e: e.tensor_tensor(out=abi[:], in0=rho[:], in1=ts[:, :, 1], op=ALU.mult), reads=[dpar, dtab], writes=[dpar])
        den = self.sb(es, "s5_den", [128, 16])
        t1 = self.sb(es, "s5_t1", [128, 16])
        t2 = self.sb(es, "s5_t2", [128, 16])
        cor = self.sb(es, "s5_cor", [128, 16])
        coi = self.sb(es, "s5_coi", [128, 16])
        V = lambda fn: fw.op("dve", fn, reads=[dpar], writes=[dpar])
        V(lambda e: e.tensor_tensor(out=den[:], in0=lr[:], in1=lr[:], op=ALU.mult))
        V(lambda e: e.tensor_tensor(out=t1[:], in0=lim[:], in1=lim[:], op=ALU.mult))
        V(lambda e: e.tensor_tensor(out=den[:], in0=den[:], in1=t1[:], op=ALU.add))
        V(lambda e: e.reciprocal(out=den[:], in_=den[:]))
        V(lambda e: e.tensor_scalar(out=abr[:], in0=abr[:], scalar1=-1.0, scalar2=None, op0=ALU.add))
        V(lambda e: e.tensor_tensor(out=t1[:], in0=abr[:], in1=lr[:], op=ALU.mult))
        V(lambda e: e.tensor_tensor(out=t2[:], in0=abi[:], in1=lim[:], op=ALU.mult))
        V(lambda e: e.tensor_tensor(out=t1[:], in0=t1[:], in1=t2[:], op=ALU.add))
        V(lambda e: e.tensor_tensor(out=cor[:], in0=t1[:], in1=den[:], op=ALU.mult))
        V(lambda e: e.tensor_tensor(out=t1[:], in0=abi[:], in1=lr[:], op=ALU.mult))
        V(lambda e: e.tensor_tensor(out=t2[:], in0=abr[:], in1=lim[:], op=ALU.mult))
        V(lambda e: e.tensor_tensor(out=t1[:], in0=t1[:], in1=t2[:], op=ALU.subtract))
        V(lambda e: e.tensor_tensor(out=coi[:], in0=t1[:], in1=den[:], op=ALU.mult))

        LB = self.sb(es, "s5_LB", [128, 2, 8, 2, 128], BF16)
        LC = self.sb(es, "s5_LC", [128, 8, 2, 128], BF16)
        dLB = Dep()
        fw.op("dve", lambda e: e.memset(LC[:], 0.0), writes=[dLB])
        with ExitStack() as es2:
            Xr = self.sb(es2, "s5_Xr", [128, 8, 128])
            Xi = self.sb(es2, "s5_Xi", [128, 8, 128])
            dX = Dep()
            fw.op("dve", lambda e: e.memset(Xr[:], 0.0), writes=[dX])
            fw.op("dve", lambda e: e.memset(Xi[:], 0.0), writes=[dX])
            for (X, nm) in ((Xr, "s5_b_re"), (Xi, "s5_b_im")):
                for q in range(8):
                    r = q % 4
                    fw.dma("sp", X[0:64, q, 32 * r:32 * r + 16], I[nm][li, 2 * q], writes=[dX])
                    fw.dma("sp", X[64:128, q, 32 * r + 16:32 * r + 32], I[nm][li, 2 * q + 1], writes=[dX])
            Xc = self.sb(es2, "s5_Xc", [128, 2, 8, 2, 128])
            tmpx = self.sb(es2, "s5_tmpx", [128, 8, 128])
            for d in range(2):
                cr = cor[:, d * 8:(d + 1) * 8].unsqueeze(2).to_broadcast([128, 8, 128])
                ci = coi[:, d * 8:(d + 1) * 8].unsqueeze(2).to_broadcast([128, 8, 128])
                fw.op("dve", lambda e: e.tensor_tensor(out=Xc[:, d, :, 0, :], in0=Xr[:], in1=cr, op=ALU.mult), reads=[dX, dpar], writes=[dX])
                fw.op("dve", lambda e: e.tensor_tensor(out=tmpx[:], in0=Xi[:], in1=ci, op=ALU.mult), reads=[dX, dpar], writes=[dX])
                fw.op("dve", lambda e: e.tensor_tensor(out=Xc[:, d, :, 0, :], in0=Xc[:, d, :, 0, :], in1=tmpx[:], op=ALU.subtract), reads=[dX], writes=[dX])
                fw.op("dve", lambda e: e.tensor_tensor(out=Xc[:, d, :, 1, :], in0=Xi[:], in1=cr, op=ALU.mult), reads=[dX, dpar], writes=[dX])
                fw.op("dve", lambda e: e.tensor_tensor(out=tmpx[:], in0=Xr[:], in1=ci, op=ALU.mult), reads=[dX, dpar], writes=[dX])
                fw.op("dve", lambda e: e.tensor_tensor(out=Xc[:, d, :, 1, :], in0=Xc[:, d, :, 1, :], in1=tmpx[:], op=ALU.add), reads=[dX], writes=[dX])
            for d in range(2):
                for q in range(8):
                    for ri in range(2):
                        r = q % 4
                        ps, dp = self.next_ps()
                        fw.op("pe", lambda e: e.transpose(out=ps[:, 0:128], in_=Xc[:, d, q, ri, :],
                                                          identity=self.ident[:, :]), reads=[dX, self.d_const], writes=[dp])
                        fw.op("act", lambda e: e.copy(out=LB[:, d, q, ri, :], in_=ps[:, 0:128]),
                              reads=[dp], writes=[dLB])
            Yr = self.sb(es2, "s5_Yr", [32, 8, 128])
            Yi = self.sb(es2, "s5_Yi", [32, 8, 128])
            dY = Dep()
            fw.op("dve", lambda e: e.memset(Yr[:], 0.0), writes=[dY])
            fw.op("dve", lambda e: e.memset(Yi[:], 0.0), writes=[dY])
            for (Y, nm) in ((Yr, "s5_c_re"), (Yi, "s5_c_im")):
                src = I[nm][li].rearrange("(q gp) h n -> gp h q n", gp=2)
                fw.dma("sp", Y[0:16, :, 0:64], src[0], writes=[dY])
                fw.dma("sp", Y[16:32, :, 64:128], src[1], writes=[dY])
            for q in range(8):
                for ri, Y in enumerate((Yr, Yi)):
                    ps, dp = self.next_ps()
                    fw.op("pe", lambda e: e.transpose(out=ps[:, 0:32], in_=Y[:, q, :], identity=self.ident[0:32, 0:32]),
                          reads=[dY, self.d_const], writes=[dp])
                    if ri == 0:
                        fw.op("act", lambda e: e.copy(out=LC[:, q, 0, 32 * (q % 4):32 * (q % 4) + 32], in_=ps[:, 0:32]), reads=[dp], writes=[dLB])
                    else:
                        fw.op("act", lambda e: e.mul(out=LC[:, q, 1, 32 * (q % 4):32 * (q % 4) + 32], in_=ps[:, 0:32], mul=-1.0), reads=[dp], writes=[dLB])
            fw.barrier()

        ubf = self.sb(es, "s5_ubf", [128, 2, L], BF16)
        urv = self.sb(es, "s5_urv", [128, 2, L], BF16)
        yacc = self.sb(es, "s5_yacc", [128, 2, L])
        du = Dep()
        dyacc = Dep()
        with ExitStack() as es2:
            uf = self.sb(es2, "s5_uf", [128, 2, L])
            fw.dma("sp", uf[:], self.projT[OFF_U:OFF_U + 256, :].rearrange("(kt p) t -> p kt t", p=128),
                   reads=[self.dep_proj], writes=[du])
            for kt in range(2):
                fw.op("dve", lambda e: e.tensor_copy(out=ubf[:, kt, :], in_=uf[:, kt, :]), reads=[du], writes=[du])
                fw.op("pool", lambda e: e.tensor_copy(out=urv[:, kt, ::-1], in_=uf[:, kt, :]), reads=[du], writes=[du])
            fw.barrier()

        blocks = [(i * S5W, S5W) for i in range(L // S5W)]
        if L % S5W:
            blocks.append((L - L % S5W, L % S5W))
        NB = 3
        tmp = [[self.sb(es, "s5_w%d_%d" % (i, k), [128, S5W]) for k in range(6)] for i in range(NB)]
        dtmp = [[Dep() for k in range(6)] for i in range(NB)]
        hb = [[self.sb(es, "s5_h%d_%d" % (i, k), [128, S5W], BF16) for k in range(2)] for i in range(NB)]
        dhb = [[Dep() for k in range(2)] for i in range(NB)]
        init = [[self.sb(es, "s5_in%d_%d" % (i, k), [128, 1]) for k in range(3)] for i in range(2)]
        dinit = [Dep(), Dep()]
        it = 0
        for d in range(2):
            usrc = ubf if d == 0 else urv
            for q in range(8):
                j = d * 8 + q
                r = q % 4
                kt = q // 4
                rho_b = rho[:, j:j + 1]
                prev = None
                for bi, (t0, W) in enumerate(blocks):
                    T, dT = tmp[it % NB], dtmp[it % NB]
                    H, dH = hb[it % NB], dhb[it % NB]
                    it += 1
                    pre, dpre = self.next_ps()
                    pim, dpim = self.next_ps()
                    fw.op("pe", lambda e: e.matmul(pre[:, 0:W], lhsT=LB[:, d, q, 0, :],
                                                   rhs=usrc[:, kt, t0:t0 + W], start=True, stop=True),
                          reads=[dLB, du], writes=[dpre])
                    fw.op("pe", lambda e: e.matmul(pim[:, 0:W], lhsT=LB[:, d, q, 1, :],
                                                   rhs=usrc[:, kt, t0:t0 + W], start=True, stop=True),
                          reads=[dLB, du], writes=[dpim])
                    c_, s_ = tc[:, j, 0:W], ts[:, j, 0:W]
                    fw.op("dve", lambda e: e.tensor_tensor(out=T[0][:, 0:W], in0=pre[:, 0:W], in1=c_, op=ALU.mult), reads=[dpre, dtab], writes=[dT[0]])
                    fw.op("dve", lambda e: e.tensor_tensor(out=T[1][:, 0:W], in0=pim[:, 0:W], in1=s_, op=ALU.mult), reads=[dpim, dtab], writes=[dT[1]])
                    fw.op("dve", lambda e: e.tensor_tensor(out=T[2][:, 0:W], in0=pim[:, 0:W], in1=c_, op=ALU.mult), reads=[dpim, dtab], writes=[dT[2]])
                    fw.op("dve", lambda e: e.tensor_tensor(out=T[3][:, 0:W], in0=pre[:, 0:W], in1=s_, op=ALU.mult), reads=[dpre, dtab], writes=[dT[3]])
                    fw.op("pool", lambda e: e.tensor_tensor(out=T[0][:, 0:W], in0=T[0][:, 0:W], in1=T[1][:, 0:W], op=ALU.add), reads=[dT[0], dT[1]], writes=[dT[0]])
                    fw.op("pool", lambda e: e.tensor_tensor(out=T[2][:, 0:W], in0=T[2][:, 0:W], in1=T[3][:, 0:W], op=ALU.subtract), reads=[dT[2], dT[3]], writes=[dT[2]])
                    ini, dini = init[bi % 2], dinit[bi % 2]
                    if bi == 0:
                        i_re, i_im = 0.0, 0.0
                        rd = []
                    else:
                        pT, pdT, pW, pini, pdini = prev
                        cW, sW = tc[:, j, pW:pW + 1], ts[:, j, pW:pW + 1]
                        fw.op("dve", lambda e: e.tensor_scalar(out=ini[2][:], in0=pT[4][:, pW - 1:pW], scalar1=cW, scalar2=None, op0=ALU.mult), reads=[pdT[4], dtab], writes=[dini])
                        fw.op("dve", lambda e: e.scalar_tensor_tensor(out=ini[2][:], in0=pT[5][:, pW - 1:pW], scalar=sW, in1=ini[2][:], op0=ALU.mult, op1=ALU.subtract), reads=[pdT[5], dini, dtab], writes=[dini])
                        fw.op("dve", lambda e: e.tensor_scalar(out=ini[0][:], in0=ini[2][:], scalar1=-1.0, scalar2=None, op0=ALU.mult), reads=[dini], writes=[dini])
                        fw.op("dve", lambda e: e.tensor_scalar(out=ini[2][:], in0=pT[4][:, pW - 1:pW], scalar1=sW, scalar2=None, op0=ALU.mult), reads=[pdT[4], dtab], writes=[dini])
                        fw.op("dve", lambda e: e.scalar_tensor_tensor(out=ini[1][:], in0=pT[5][:, pW - 1:pW], scalar=cW, in1=ini[2][:], op0=ALU.mult, op1=ALU.add), reads=[pdT[5], dini, dtab], writes=[dini])
                        i_re, i_im = ini[0][:, 0:1], ini[1][:, 0:1]
                        rd = [dini]
                    fw.op("dve", lambda e: e.tensor_tensor_scan(out=T[4][:, 0:W], data0=rho_b.to_broadcast([128, W]), data1=T[0][:, 0:W], initial=i_re, op0=ALU.mult, op1=ALU.add),
                          reads=[dT[0], dpar] + rd, writes=[dT[4]])
                    fw.op("dve", lambda e: e.tensor_tensor_scan(out=T[5][:, 0:W], data0=rho_b.to_broadcast([128, W]), data1=T[2][:, 0:W], initial=i_im, op0=ALU.mult, op1=ALU.add),
                          reads=[dT[2], dpar] + rd, writes=[dT[5]])
                    prev = (T, dT, W, ini, dini)
                    fw.op("pool", lambda e: e.tensor_tensor(out=T[0][:, 0:W], in0=T[4][:, 0:W], in1=c_, op=ALU.mult), reads=[dT[4], dtab], writes=[dT[0]])
                    fw.op("pool", lambda e: e.tensor_tensor(out=T[1][:, 0:W], in0=T[5][:, 0:W], in1=s_, op=ALU.mult), reads=[dT[5], dtab], writes=[dT[1]])
                    fw.op("pool", lambda e: e.tensor_tensor(out=H[0][:, 0:W], in0=T[0][:, 0:W], in1=T[1][:, 0:W], op=ALU.subtract), reads=[dT[0], dT[1]], writes=[dH[0]])
                    fw.op("dve", lambda e: e.tensor_tensor(out=T[2][:, 0:W], in0=T[4][:, 0:W], in1=s_, op=ALU.mult), reads=[dT[4], dtab], writes=[dT[2]])
                    fw.op("dve", lambda e: e.tensor_tensor(out=T[3][:, 0:W], in0=T[5][:, 0:W], in1=c_, op=ALU.mult), reads=[dT[5], dtab], writes=[dT[3]])
                    fw.op("pool", lambda e: e.tensor_tensor(out=H[1][:, 0:W], in0=T[2][:, 0:W], in1=T[3][:, 0:W], op=ALU.add), reads=[dT[2], dT[3]], writes=[dH[1]])
                    py, dpy = self.next_ps()
                    fw.op("pe", lambda e: e.matmul(py[:, 0:W], lhsT=LC[:, q, 0, :], rhs=H[0][:, 0:W], start=True, stop=False), reads=[dLB, dH[0]], writes=[dpy])
                    fw.op("pe", lambda e: e.matmul(py[:, 0:W], lhsT=LC[:, q, 1, :], rhs=H[1][:, 0:W], start=False, stop=True), reads=[dLB, dH[1]], writes=[dpy])
                    if d == 0 and r == 0:
                        fw.op("act", lambda e: e.copy(out=yacc[:, kt, t0:t0 + W], in_=py[:, 0:W]), reads=[dpy], writes=[dyacc])
                    elif d == 0:
                        ya = yacc[:, kt, t0:t0 + W]
                        fw.op("dve", lambda e: e.tensor_tensor(out=ya, in0=py[:, 0:W], in1=ya, op=ALU.add), reads=[dpy, dyacc], writes=[dyacc])
                    else:
                        lo = L - (t0 + W)
                        ya = yacc[:, kt, lo:lo + W]
                        fw.op("dve", lambda e: e.tensor_tensor(out=ya[:, ::-1], in0=py[:, 0:W], in1=ya[:, ::-1], op=ALU.add), reads=[dpy, dyacc], writes=[dyacc])
        fw.barrier()
        self._s5_post(li, es, yacc, dyacc)


def _s5_post(self, li, es_outer, yacc, dyacc):
    fw = self.fw
    I = self.I
    with ExitStack() as es:
        gw, dgw = self.load_weight_bf(es, "s5_gw", I["s5_glu_w"][li], 256, 512)
        dsk = self.sb(es, "s5_dsk", [128, 2])
        gb = self.sb(es, "s5_gb", [128, 4])
        dpp = Dep()
        fw.dma("sp", dsk[:], I["s5_d"][li].rearrange("(kt p) -> p kt", p=128), writes=[dpp], slow=True)
        fw.dma("sp", gb[:], I["s5_glu_b"][li].rearrange("(kt p) -> p kt", p=128), writes=[dpp], slow=True)
        W = 512
        uf = [self.sb(es, "s5p_u%d" % i, [128, 2, W]) for i in range(2)]
        duf = [Dep(), Dep()]
        t1 = self.sb(es, "s5p_t1", [128, 2, W])
        t2 = self.sb(es, "s5p_t2", [128, 2, W])
        dt1 = Dep()
        gl = [self.sb(es, "s5p_gl%d" % i, [128, 2, W], BF16) for i in range(2)]
        dgl = [Dep(), Dep()]
        sg = [self.sb(es, "s5p_sg%d" % i, [128, W]) for i in range(2)]
        dsg = [Dep(), Dep()]
        o = [self.sb(es, "s5p_o%d" % i, [128, 2, W]) for i in range(2)]
        do = [Dep(), Dep()]
        uv = self.projT[OFF_U:OFF_U + 256, :].rearrange("(kt p) t -> p kt t", p=128)
        yv = self.yaT.rearrange("(kt p) t -> p kt t", p=128)
        for bi, (t0, Wb) in enumerate(BLOCKS):
            u, du = uf[bi % 2], duf[bi % 2]
            g, dg = gl[bi % 2], dgl[bi % 2]
            ob, dob = o[bi % 2], do[bi % 2]
            fw.dma("sp", u[:, :, 0:Wb], uv[:, :, t0:t0 + Wb], reads=[self.dep_proj], writes=[du])
            for kt in range(2):
                fw.op("dve", lambda e: e.scalar_tensor_tensor(out=t1[:, kt, 0:Wb], in0=u[:, kt, 0:Wb], scalar=dsk[:, kt:kt + 1], in1=yacc[:, kt, t0:t0 + Wb], op0=ALU.mult, op1=ALU.add),
                      reads=[du, dpp, dyacc], writes=[dt1])
                fw.op("pool", lambda e: e.tensor_tensor(out=t2[:, kt, 0:Wb], in0=t1[:, kt, 0:Wb], in1=t1[:, kt, 0:Wb], op=ALU.mult), reads=[dt1], writes=[dt1])
                fw.op("dve", lambda e: e.tensor_scalar(out=t2[:, kt, 0:Wb], in0=t2[:, kt, 0:Wb], scalar1=0.044715, scalar2=1.0, op0=ALU.mult, op1=ALU.add), reads=[dt1], writes=[dt1])
                fw.op("pool", lambda e: e.tensor_tensor(out=t2[:, kt, 0:Wb], in0=t2[:, kt, 0:Wb], in1=t1[:, kt, 0:Wb], op=ALU.mult), reads=[dt1], writes=[dt1])
                fw.op("act", lambda e: e.activation(out=t2[:, kt, 0:Wb], in_=t2[:, kt, 0:Wb], func=AF.Sigmoid, scale=1.5957691216), reads=[dt1], writes=[dt1])
                fw.op("dve", lambda e: e.tensor_tensor(out=g[:, kt, 0:Wb], in0=t2[:, kt, 0:Wb], in1=t1[:, kt, 0:Wb], op=ALU.mult), reads=[dt1], writes=[dg])
            for c in range(2):
                plo, dplo = self.next_ps()
                phi, dphi = self.next_ps()
                for k in range(2):
                    fw.op("pe", lambda e: e.matmul(plo[:, 0:Wb], lhsT=gw[:, k, c * 128:(c + 1) * 128], rhs=g[:, k, 0:Wb], start=(k == 0), stop=(k == 1)), reads=[dgw, dg], writes=[dplo])
                for k in range(2):
                    fw.op("pe", lambda e: e.matmul(phi[:, 0:Wb], lhsT=gw[:, k, 256 + c * 128:256 + (c + 1) * 128], rhs=g[:, k, 0:Wb], start=(k == 0), stop=(k == 1)), reads=[dgw, dg], writes=[dphi])
                s, ds = sg[c], dsg[c]
                fw.op("act", lambda e: e.activation(out=s[:, 0:Wb], in_=phi[:, 0:Wb], func=AF.Sigmoid, bias=gb[:, 2 + c:3 + c]), reads=[dphi, dpp], writes=[ds])
                fw.op("dve", lambda e: e.scalar_tensor_tensor(out=ob[:, c, 0:Wb], in0=plo[:, 0:Wb], scalar=gb[:, c:c + 1], in1=s[:, 0:Wb], op0=ALU.add, op1=ALU.mult), reads=[dplo, ds, dpp], writes=[dob])
            fw.dma("pool", yv[:, :, t0:t0 + Wb], ob[:, :, 0:Wb], reads=[dob], writes=[self.dep_ya])
    fw.barrier()


Builder.mix_s5 = _mix_s5
Builder._s5_post = _s5_post


LP = 33 * 128
NCH = 33


def _mix_ssd(self, li):
    fw = self.fw
    I = self.I
    xcT = self.scratch_once("xcT", (1024, L))
    d_xc = self.dep_once("xcT")
    xbv = self.projT[OFF_XBC:OFF_XBC + 1024, :].rearrange("(j p) t -> p j t", p=128)
    xcv = xcT.rearrange("(j p) t -> p j t", p=128)
    with ExitStack() as es:
        cw = self.sb(es, "sd_cw", [128, 5, 8])
        cb = self.sb(es, "sd_cb", [128, 8])
        dcw = Dep()
        for k in range(5):
            fw.dma("sp", cw[:, k, :], I["ssd_conv_w"][li, k].rearrange("(j p) -> p j", p=128), writes=[dcw], slow=True)
        fw.dma("sp", cb[:], I["ssd_conv_b"][li].rearrange("(j p) -> p j", p=128), writes=[dcw], slow=True)
        xp = [self.sb(es, "sd_xp%d" % i, [128, L + 4]) for i in range(2)]
        dxp = [Dep(), Dep()]
        acc = [self.sb(es, "sd_acc%d" % i, [128, L]) for i in range(2)]
        dacc = [Dep(), Dep()]
        for i in range(2):
            fw.op("dve", lambda e: e.memset(xp[i][:, 0:2], 0.0), writes=[dxp[i]])
            fw.op("dve", lambda e: e.memset(xp[i][:, L + 2:L + 4], 0.0), writes=[dxp[i]])
        for j in range(8):
            x_, dx_ = xp[j % 2], dxp[j % 2]
            a_, da_ = acc[j % 2], dacc[j % 2]
            fw.dma("sp", x_[:, 2:L + 2], xbv[:, j, :], reads=[self.dep_proj], writes=[dx_])
            eng = "dve"
            fw.op(eng, lambda e: e.tensor_scalar(out=a_[:], in0=x_[:, 0:L], scalar1=cw[:, 0, j:j + 1], scalar2=cb[:, j:j + 1], op0=ALU.mult, op1=ALU.add),
                  reads=[dx_, dcw], writes=[da_])
            for k in range(1, 5):
                fw.op(eng, lambda e: e.scalar_tensor_tensor(out=a_[:], in0=x_[:, k:k + L], scalar=cw[:, k, j:j + 1], in1=a_[:], op0=ALU.mult, op1=ALU.add),
                      reads=[dx_, dcw, da_], writes=[da_])
            fw.op("act", lambda e: e.activation(out=a_[:], in_=a_[:], func=AF.Silu), reads=[da_], writes=[da_])
            fw.dma("pool", xcv[:, j, :], a_[:], reads=[da_], writes=[d_xc])
    fw.barrier()

    with ExitStack() as es:
        triu = self.sb(es, "sd_triu", [128, 128])
        mneg = self.sb(es, "sd_mneg", [128, 128])
        onesf = self.sb(es, "sd_onesf", [128, 128])
        negones = self.sb(es, "sd_negones", [128, 128])
        identb = self.sb(es, "sd_identb", [128, 128], BF16)
        dc = Dep()
        fw.dma("sp", triu[:], I["c_triu"][:, :], writes=[dc])
        fw.dma("sp", mneg[:], I["c_mneg"][:, :], writes=[dc])
        padm = self.sb(es, "sd_padm", [128, 8])
        fw.dma("sp", padm[:], I["c_padm"][:, :], writes=[dc])
        fw.op("dve", lambda e: e.memset(onesf[:], 1.0), writes=[dc])
        fw.op("dve", lambda e: e.memset(negones[:], -1.0), writes=[dc])
        fw.op("dve", lambda e: e.tensor_copy(out=identb[:], in_=self.ident[:]), reads=[self.d_const], writes=[dc])
        dtb = self.sb(es, "sd_dtb", [8, 2])
        nea = self.sb(es, "sd_nea", [8, 2])
        dpp = Dep()
        fw.dma("sp", dtb[:], I["ssd_dt_bias"][li].rearrange("d h -> h d"), writes=[dpp], slow=True)
        fw.dma("sp", nea[:], I["ssd_a_log"][li].rearrange("d h -> h d"), writes=[dpp], slow=True)
        fw.op("act", lambda e: e.activation(out=nea[:], in_=nea[:], func=AF.Exp), reads=[dpp], writes=[dpp])
        fw.op("dve", lambda e: e.tensor_scalar(out=nea[:], in0=nea[:], scalar1=-1.0, scalar2=None, op0=ALU.mult), reads=[dpp], writes=[dpp])
        dtok = self.sb(es, "sd_dtok", [128, NCH, 16])
        ddtok = Dep()
        xs = self.sb(es, "sd_xs", [128, 4, LP], BF16)
        Bm = self.sb(es, "sd_B", [128, 2, LP], BF16)
        Cm = self.sb(es, "sd_C", [128, 2, LP], BF16)
        dws = Dep()
        yacc = self.sb(es, "sd_yacc", [128, 4, L])
        dyacc = Dep()
        ST = self.sb(es, "sd_ST", [128, 8, 64])
        STb = self.sb(es, "sd_STb", [128, 8, 64], BF16)
        dST = [Dep() for _ in range(8)]
        dSTb = [Dep() for _ in range(8)]
        dbias = self.sb(es, "sd_dbias", [128, 2, 8])
        nea_bc = self.sb(es, "sd_neabc", [128, 2, 8])
        fw.dma("sp", dbias[:], I["ssd_dt_bias"][li].partition_broadcast(128), writes=[dpp], slow=True)
        fw.dma("sp", nea_bc[:], I["ssd_a_log"][li].partition_broadcast(128), writes=[dpp], slow=True)
        fw.op("act", lambda e: e.activation(out=nea_bc[:], in_=nea_bc[:], func=AF.Exp), reads=[dpp], writes=[dpp])
        fw.op("dve", lambda e: e.tensor_scalar(out=nea_bc[:], in0=nea_bc[:], scalar1=-1.0, scalar2=None, op0=ALU.mult), reads=[dpp], writes=[dpp])

        for d in range(2):
          with ExitStack() as es3:
            stg = [self.sb(es3, "sd_stg%d" % i, [128, L]) for i in range(2)]
            dstg = [Dep(), Dep()]
            raw = self.sb(es3, "sd_raw", [8, L])
            rawd = self.sb(es3, "sd_rawd", [8, LP])
            draw = Dep()
            for j in range(8):
                s_, ds_ = stg[j % 2], dstg[j % 2]
                fw.dma("sp", s_[:], xcv[:, j, :], reads=[d_xc], writes=[ds_])
                dst = xs[:, j, :] if j < 4 else (Bm[:, j - 4, :] if j < 6 else Cm[:, j - 6, :])
                eng = "dve" if j % 2 == 0 else "pool"
                fw.op(eng, lambda e: e.memset(dst[:, L:LP], 0.0), writes=[dws])
                if d == 0:
                    fw.op(eng, lambda e: e.tensor_copy(out=dst[:, 0:L], in_=s_[:]), reads=[ds_], writes=[dws])
                else:
                    fw.op(eng, lambda e: e.tensor_copy(out=dst[:, 0:L][:, ::-1], in_=s_[:]), reads=[ds_], writes=[dws])
            fw.dma("sp", raw[:], self.projT[OFF_DT:OFF_DT + 8, :], reads=[self.dep_proj], writes=[draw])
            fw.op("dve", lambda e: e.memset(rawd[:, L:LP], 0.0), writes=[draw])
            if d == 0:
                fw.op("dve", lambda e: e.tensor_copy(out=rawd[:, 0:L], in_=raw[:]), reads=[draw], writes=[draw])
            else:
                fw.op("dve", lambda e: e.tensor_copy(out=rawd[:, 0:L][:, ::-1], in_=raw[:]), reads=[draw], writes=[draw])
            for c in range(NCH):
                ps, dp = self.next_ps()
                fw.op("pe", lambda e: e.transpose(out=ps[:, 0:8], in_=rawd[:, c * 128:(c + 1) * 128], identity=self.ident[0:8, 0:8]), reads=[draw, self.d_const], writes=[dp])
                fw.op("dve", lambda e: e.tensor_tensor(out=dtok[:, c, 0:8], in0=ps[:, 0:8], in1=dbias[:, d, :], op=ALU.add), reads=[dp, dpp], writes=[ddtok])
            fw.op("act", lambda e: e.activation(out=dtok[:, :, 0:8], in_=dtok[:, :, 0:8], func=AF.Exp), reads=[ddtok], writes=[ddtok])
            fw.op("act", lambda e: e.activation(out=dtok[:, :, 0:8], in_=dtok[:, :, 0:8], func=AF.Ln, bias=self.one_t[:, 0:1]), reads=[ddtok, self.d_const], writes=[ddtok])
            fw.op("dve", lambda e: e.tensor_tensor(out=dtok[:, NCH - 1, 0:8], in0=dtok[:, NCH - 1, 0:8], in1=padm[:, :], op=ALU.mult), reads=[ddtok, dc], writes=[ddtok])
            fw.op("dve", lambda e: e.tensor_tensor(out=dtok[:, :, 8:16], in0=dtok[:, :, 0:8], in1=nea_bc[:, d, :].unsqueeze(1).to_broadcast([128, NCH, 8]), op=ALU.mult), reads=[ddtok, dpp], writes=[ddtok])
            fw.barrier()
          with ExitStack() as es4:
            NBF = 2
            xtk = [self.sb(es4, "sd_xtk%d" % i, [128, 512]) for i in range(NBF)]
            dxtk = [Dep() for _ in range(NBF)]
            btok = [self.sb(es4, "sd_btok%d" % i, [128, 256], BF16) for i in range(NBF)]
            dbtok = [Dep() for _ in range(NBF)]
            cbt = [self.sb(es4, "sd_cbt%d" % i, [128, 2, 128]) for i in range(NBF)]
            dcbt = [Dep() for _ in range(NBF)]
            sm = [self.sb(es4, "sd_sm%d" % i, [128, 4, 8]) for i in range(NBF)]
            dsm = [Dep() for _ in range(NBF)]
            NH = 16
            atri = [self.sb(es4, "sd_atri%d" % i, [128, 128]) for i in range(NH)]
            datri = [Dep() for _ in range(NH)]
            DT = [self.sb(es4, "sd_DT%d" % i, [128, 128]) for i in range(NH)]
            dDT = [Dep() for _ in range(NH)]
            EE = [self.sb(es4, "sd_EE%d" % i, [128, 128]) for i in range(NH)]
            dEE = [Dep() for _ in range(NH)]
            MT = [self.sb(es4, "sd_MT%d" % i, [128, 128], BF16) for i in range(NH)]
            dMT = [Dep() for _ in range(NH)]
            CE = [self.sb(es4, "sd_CE%d" % i, [128, 128], BF16) for i in range(NH)]
            dCE = [Dep() for _ in range(NH)]
            xdt = [self.sb(es4, "sd_xdt%d" % i, [128, 2, 64], BF16) for i in range(NH)]
            dxdt = [Dep() for _ in range(NH)]
            for j in range(8):
                fw.op("dve", lambda e: e.memset(ST[:, j, :], 0.0), writes=[dST[j]])
                fw.op("pool", lambda e: e.memset(STb[:, j, :], 0.0), writes=[dSTb[j]])
            ih = 0
            for c in range(NCH):
                t0 = c * 128
                Wv = min(128, L - t0)
                k_ = c % NBF
                px, dpx = self.next_ps()
                for j in range(4):
                    fw.op("pe", lambda e: e.matmul(px[:, j * 128:(j + 1) * 128], lhsT=xs[:, j, t0:t0 + 128], rhs=identb[:, :], start=True, stop=True), reads=[dws, dc], writes=[dpx])
                xtok, dxtok = xtk[k_], dxtk[k_]
                fw.op("act", lambda e: e.copy(out=xtok[:, :], in_=px[:, 0:512]), reads=[dpx], writes=[dxtok])
                pb, dpb = self.next_ps()
                for g in range(2):
                    fw.op("pe", lambda e: e.matmul(pb[:, g * 128:(g + 1) * 128], lhsT=Bm[:, g, t0:t0 + 128], rhs=identb[:, :], start=True, stop=True), reads=[dws, dc], writes=[dpb])
                fw.op("act", lambda e: e.copy(out=btok[k_][:, :], in_=pb[:, 0:256]), reads=[dpb], writes=[dbtok[k_]])
                pc, dpc = self.next_ps()
                fw.op("pe", lambda e: e.matmul(pc[:, 0:8], lhsT=triu[:, :], rhs=dtok[:, c, 8:16], start=True, stop=True), reads=[dc, ddtok], writes=[dpc])
                fw.op("pe", lambda e: e.matmul(pc[:, 8:16], lhsT=onesf[:, :], rhs=dtok[:, c, 8:16], start=True, stop=True), reads=[dc, ddtok], writes=[dpc])
                S_, dS_ = sm[k_], dsm[k_]
                fw.op("act", lambda e: e.copy(out=S_[:, 0, :], in_=pc[:, 0:8]), reads=[dpc], writes=[dS_])
                fw.op("dve", lambda e: e.tensor_tensor(out=S_[:, 1, :], in0=pc[:, 8:16], in1=S_[:, 0, :], op=ALU.subtract), reads=[dpc, dS_], writes=[dS_])
                fw.op("act", lambda e: e.activation(out=S_[:, 1, :], in_=S_[:, 1, :], func=AF.Exp), reads=[dS_], writes=[dS_])
                fw.op("dve", lambda e: e.tensor_tensor(out=S_[:, 2, :], in0=S_[:, 1, :], in1=dtok[:, c, 0:8], op=ALU.mult), reads=[dS_, ddtok], writes=[dS_])
                fw.op("act", lambda e: e.activation(out=S_[:, 3, :], in_=pc[:, 8:16], func=AF.Exp), reads=[dpc], writes=[dS_])
                for g in range(2):
                    pcb, dpcb = self.next_ps()
                    fw.op("pe", lambda e: e.matmul(pcb[:, 0:128], lhsT=Bm[:, g, t0:t0 + 128], rhs=Cm[:, g, t0:t0 + 128], start=True, stop=True), reads=[dws], writes=[dpcb])
                    fw.op("act", lambda e: e.copy(out=cbt[k_][:, g, :], in_=pcb[:, 0:128]), reads=[dpcb], writes=[dcbt[k_]])
                hb0 = (c % 2) * 8
                pDs = {}
                for jp in range(4):
                    pD, dpD = self.next_ps()
                    for jj in range(2):
                        j = jp * 2 + jj
                        h_ = hb0 + j
                        o = jj * 256
                        pDs[j] = (pD, dpD, o)
                        fw.op("dve" if jj == 0 else "pool", lambda e: e.tensor_scalar(out=atri[h_][:], in0=triu[:], scalar1=dtok[:, c, 8 + j:9 + j], scalar2=None, op0=ALU.mult), reads=[dc, ddtok], writes=[datri[h_]])
                        fw.op("pe", lambda e: e.matmul(pD[:, o:o + 128], lhsT=onesf[:, :], rhs=atri[h_][:, :], start=True, stop=False), reads=[dc, datri[h_]], writes=[dpD])
                        fw.op("pe", lambda e: e.matmul(pD[:, o:o + 128], lhsT=atri[h_][:, :], rhs=negones[:, :], start=False, stop=False), reads=[dc, datri[h_]], writes=[dpD])
                        fw.op("pe", lambda e: e.matmul(pD[:, o:o + 128], lhsT=self.ident[:, :], rhs=mneg[:, :], start=False, stop=True), reads=[dc, self.d_const], writes=[dpD])
                        fw.op("pe", lambda e: e.matmul(pD[:, o + 128:o + 256], lhsT=onesf[:, :], rhs=atri[h_][:, :], start=True, stop=True), reads=[dc, datri[h_]], writes=[dpD])
                for j in range(8):
                    g = j // 4
                    h_ = hb0 + j
                    pD, dpD, o = pDs[j]
                    fw.op("act", lambda e: e.activation(out=DT[h_][:], in_=pD[:, o:o + 128], func=AF.Exp), reads=[dpD], writes=[dDT[h_]])
                    fw.op("act", lambda e: e.activation(out=EE[h_][:], in_=pD[:, o + 128:o + 256], func=AF.Exp), reads=[dpD], writes=[dEE[h_]])
                    fw.op("dve", lambda e: e.tensor_tensor(out=MT[h_][:], in0=cbt[k_][:, g, :], in1=DT[h_][:], op=ALU.mult), reads=[dcbt[k_], dDT[h_]], writes=[dMT[h_]])
                    fw.op("pool", lambda e: e.tensor_tensor(out=CE[h_][:], in0=Cm[:, g, t0:t0 + 128], in1=EE[h_][:], op=ALU.mult), reads=[dws, dEE[h_]], writes=[dCE[h_]])
                    fw.op("dve", lambda e: e.tensor_scalar(out=xdt[h_][:, 0, :], in0=xtok[:, j * 64:(j + 1) * 64], scalar1=dtok[:, c, j:j + 1], scalar2=None, op0=ALU.mult), reads=[dxtok, ddtok], writes=[dxdt[h_]])
                    fw.op("pool", lambda e: e.tensor_scalar(out=xdt[h_][:, 1, :], in0=xtok[:, j * 64:(j + 1) * 64], scalar1=S_[:, 2, j:j + 1], scalar2=None, op0=ALU.mult), reads=[dxtok, dS_], writes=[dxdt[h_]])
                for jp in range(4):
                    py, dpy = self.next_ps()
                    for jj in range(2):
                        j = jp * 2 + jj
                        g = j // 4
                        h_ = hb0 + j
                        fw.op("pe", lambda e: e.matmul(py[jj * 64:(jj + 1) * 64, 0:128], lhsT=xdt[h_][:, 0, :], rhs=MT[h_][:, :], start=True, stop=False), reads=[dxdt[h_], dMT[h_]], writes=[dpy])
                        fw.op("pe", lambda e: e.matmul(py[jj * 64:(jj + 1) * 64, 0:128], lhsT=STb[:, j, :], rhs=CE[h_][:, :], start=False, stop=True), reads=[dSTb[j], dCE[h_]], writes=[dpy])
                    for jj in range(2):
                        j = jp * 2 + jj
                        g = j // 4
                        h_ = hb0 + j
                        fw.op("pe", lambda e: e.matmul(py[:, 128 + jj * 64:192 + jj * 64], lhsT=btok[k_][:, g * 128:(g + 1) * 128], rhs=xdt[h_][:, 1, :], start=True, stop=True), reads=[dbtok[k_], dxdt[h_]], writes=[dpy])
                    if d == 0:
                        fw.op("act", lambda e: e.copy(out=yacc[:, jp, t0:t0 + Wv], in_=py[:, 0:Wv]), reads=[dpy], writes=[dyacc])
                    else:
                        lo = L - (t0 + Wv)
                        ya = yacc[:, jp, lo:lo + Wv]
                        fw.op("dve", lambda e: e.tensor_tensor(out=ya[:, ::-1], in0=py[:, 0:Wv], in1=ya[:, ::-1], op=ALU.add), reads=[dpy, dyacc], writes=[dyacc])
                    for jj in range(2):
                        j = jp * 2 + jj
                        fw.op("dve", lambda e: e.scalar_tensor_tensor(out=ST[:, j, :], in0=ST[:, j, :], scalar=S_[:, 3, j:j + 1], in1=py[:, 128 + jj * 64:192 + jj * 64], op0=ALU.mult, op1=ALU.add), reads=[dST[j], dS_, dpy], writes=[dST[j]])
                        fw.op("act", lambda e: e.copy(out=STb[:, j, :], in_=ST[:, j, :]), reads=[dST[j]], writes=[dSTb[j]])
            fw.barrier()
        fw.barrier()
        if "dbg_yacc" in self.cfg.get("dump", ()):
            dbg = self.scratch("dbg_yacc", (512, L))
            fw.dma("sp", dbg.rearrange("(j p) t -> p j t", p=128), yacc[:, :, :], reads=[dyacc], writes=[Dep()])
            dbg2 = self.scratch("dbg_dtok", (128, NCH * 16))
            fw.dma("sp", dbg2[:, :], dtok[:, :, :].rearrange("p c k -> p (c k)"), reads=[ddtok], writes=[Dep()])
        with ExitStack() as es2:
            self.ensure_eps(es2)
            dsk = self.sb(es2, "sd_dsk", [128, 4])
            nw = self.sb(es2, "sd_nw", [128, 4])
            dq = Dep()
            for j in range(8):
                fw.dma("sp", dsk[64 * (j % 2):64 * (j % 2) + 64, j // 2:j // 2 + 1], I["ssd_d"][li, j:j + 1].partition_broadcast(64), writes=[dq], slow=True)
            fw.dma("sp", nw[:], I["ssd_norm_w"][li].rearrange("(j p) -> p j", p=128), writes=[dq], slow=True)
            W = 512
            xb = [self.sb(es2, "sd4_x%d" % i, [128, 4, W]) for i in range(2)]
            zb = [self.sb(es2, "sd4_z%d" % i, [128, 4, W]) for i in range(2)]
            dxb = [Dep(), Dep()]
            yb = self.sb(es2, "sd4_y", [128, 4, W])
            sq = self.sb(es2, "sd4_sq", [128, 4, W], BF16)
            dyb = Dep()
            rstd = self.sb(es2, "sd4_r", [128, W])
            ob = [self.sb(es2, "sd4_o%d" % i, [128, 4, W]) for i in range(2)]
            dob = [Dep(), Dep()]
            zv = self.projT[OFF_Z:OFF_Z + 512, :].rearrange("(j p) t -> p j t", p=128)
            yv = self.ybT.rearrange("(j p) t -> p j t", p=128)
            for bi, (t0, Wb) in enumerate(BLOCKS):
                x_, z_, dxz = xb[bi % 2], zb[bi % 2], dxb[bi % 2]
                o_, do_ = ob[bi % 2], dob[bi % 2]
                fw.dma("sp", x_[:, :, 0:Wb], xcv[:, 0:4, t0:t0 + Wb], reads=[d_xc], writes=[dxz])
                fw.dma("sp", z_[:, :, 0:Wb], zv[:, :, t0:t0 + Wb], reads=[self.dep_proj], writes=[dxz])
                fw.op("act", lambda e: e.activation(out=z_[:, :, 0:Wb], in_=z_[:, :, 0:Wb], func=AF.Silu), reads=[dxz], writes=[dxz])
                for j in range(4):
                    fw.op("dve", lambda e: e.scalar_tensor_tensor(out=yb[:, j, 0:Wb], in0=x_[:, j, 0:Wb], scalar=dsk[:, j:j + 1], in1=yacc[:, j, t0:t0 + Wb], op0=ALU.mult, op1=ALU.add), reads=[dxz, dq, dyacc], writes=[dyb])
                    fw.op("pool", lambda e: e.tensor_tensor(out=yb[:, j, 0:Wb], in0=yb[:, j, 0:Wb], in1=z_[:, j, 0:Wb], op=ALU.mult), reads=[dyb, dxz], writes=[dyb])
                    fw.op("act", lambda e: e.activation(out=sq[:, j, 0:Wb], in_=yb[:, j, 0:Wb], func=AF.Square), reads=[dyb], writes=[dyb])
                ps, dp = self.next_ps()
                for j in range(4):
                    fw.op("pe", lambda e: e.matmul(ps[:, 0:Wb], lhsT=self.ones_bf[:, :], rhs=sq[:, j, 0:Wb], start=(j == 0), stop=(j == 3)), reads=[dyb, self.d_const], writes=[dp])
                fw.op("act", lambda e: e.activation(out=rstd[:, 0:Wb], in_=ps[:, 0:Wb], func=AF.Sqrt, bias=self.eps_t[:, 0:1], scale=1.0 / 512), reads=[dp, self.d_const], writes=[dyb])
                fw.op("dve", lambda e: e.reciprocal(out=rstd[:, 0:Wb], in_=rstd[:, 0:Wb]), reads=[dyb], writes=[dyb])
                for j in range(4):
                    fw.op("dve", lambda e: e.scalar_tensor_tensor(out=o_[:, j, 0:Wb], in0=yb[:, j, 0:Wb], scalar=nw[:, j:j + 1], in1=rstd[:, 0:Wb], op0=ALU.mult, op1=ALU.mult), reads=[dyb, dq], writes=[do_])
                fw.dma("pool", yv[:, :, t0:t0 + Wb], o_[:, :, 0:Wb], reads=[do_], writes=[self.dep_yb])
    fw.barrier()


def _scratch_once(self, name, shape):
    if not hasattr(self, "_sc"):
        self._sc = {}
        self._scd = {}
    if name not in self._sc:
        self._sc[name] = self.scratch(name, shape)
        self._scd[name] = Dep()
    return self._sc[name]


def _dep_once(self, name):
    return self._scd[name]


Builder.mix_ssd = _mix_ssd
Builder.scratch_once = _scratch_once
Builder.dep_once = _dep_once


RW_ARR = ("r", "v", "kkn", "g", "bonus", "lw0", "kd0", "b0", "lw1", "kd1", "b1")


def _mix_rwkv(self, li):
    fw = self.fw
    I = self.I
    SC = {n: self.scratch_once("rw_" + n, (256, L)) for n in RW_ARR}
    dSC = {n: self.dep_once("rw_" + n) for n in RW_ARR}
    scv = {n: SC[n].rearrange("(kt p) t -> p kt t", p=128) for n in RW_ARR}

    def vec2(es_, name, ap1d, dep):
        t = self.sb(es_, name, [128, 2])
        fw.dma("sp", t[:], ap1d.rearrange("(kt p) -> p kt", p=128), writes=[dep], slow=True)
        return t

    with ExitStack() as es:
        dpar = Dep()
        mu = [vec2(es, "rw_mu%d" % a, I["rwkv_mu_rkv"][li, a], dpar) for a in range(3)]
        muw = [vec2(es, "rw_muw%d" % a, I["rwkv_mu_wag"][li, a], dpar) for a in range(3)]
        w0 = [vec2(es, "rw_w0%d" % d, I["rwkv_w0"][li, d], dpar) for d in range(2)]
        a0 = [vec2(es, "rw_a0%d" % d, I["rwkv_a0"][li, d], dpar) for d in range(2)]
        k_k = vec2(es, "rw_kk", I["rwkv_k_k"][li], dpar)
        k_a = vec2(es, "rw_ka", I["rwkv_k_a"][li], dpar)
        r_k = vec2(es, "rw_rk", I["rwkv_r_k"][li].rearrange("h n -> (h n)"), dpar)
        tiny = self.sb(es, "rw_tiny", [128, 1])
        fw.op("dve", lambda e: e.memset(tiny[:], 1e-12), writes=[dpar])
        blk = self.sb(es, "rw_blk", [128, 128], BF16)
        with ExitStack() as es2:
            blkf = self.sb(es2, "rw_blkf", [128, 128])
            dblk = Dep()
            fw.dma("sp", blkf[:], I["c_blk"][:, :], writes=[dblk])
            fw.op("dve", lambda e: e.tensor_copy(out=blk[:], in_=blkf[:]), reads=[dblk], writes=[dpar])
            fw.barrier()
        w1 = [self.load_weight_bf(es, "rw_w1%d" % d, I["rwkv_w1"][li, d], 256, 64) for d in range(2)]
        a1 = [self.load_weight_bf(es, "rw_a1%d" % d, I["rwkv_a1"][li, d], 256, 64) for d in range(2)]
        g1 = self.load_weight_bf(es, "rw_g1", I["rwkv_g1"][li], 256, 128)
        g2 = self.load_weight_bf(es, "rw_g2", I["rwkv_g2"][li], 128, 256)

        def load64(name, ap):
            t = self.sb(es, name, [64, 256], BF16)
            dd = Dep()
            with ExitStack() as es2:
                tf = self.sb(es2, name + "f", [64, 256])
                fw.dma("sp", tf[:], ap, writes=[dd])
                fw.op("dve", lambda e: e.tensor_copy(out=t[:], in_=tf[:]), reads=[dd], writes=[dd])
                fw.barrier()
            return t, dd
        w2 = [load64("rw_w2%d" % d, I["rwkv_w2"][li, d]) for d in range(2)]
        a2 = [load64("rw_a2%d" % d, I["rwkv_a2"][li, d]) for d in range(2)]

        W = 512
        X = self.sb(es, "rw_X", [128, 4, 2, W + 2])
        dX = Dep()
        Q = self.sb(es, "rw_Q", [128, 3, 2, W])
        dQ = Dep()
        T1 = self.sb(es, "rw_T1", [128, 2, W])
        dT1 = Dep()
        XW = self.sb(es, "rw_XW", [128, 3, 2, W], BF16)
        dXW = Dep()
        Hh = self.sb(es, "rw_Hh", [128, W], BF16)
        dHh = Dep()
        AS = self.sb(es, "rw_AS", [128, 2, W])
        dAS = Dep()
        KK = self.sb(es, "rw_KK", [128, 2, W])
        dKK = Dep()
        SQ = self.sb(es, "rw_SQ", [128, 2, W], BF16)
        dSQ = Dep()
        RS = self.sb(es, "rw_RS", [128, W])
        dRS = Dep()
        O = {n: self.sb(es, "rw_O_" + n, [128, 2, W]) for n in ("g", "bonus", "lw", "kd", "b")}
        dO = {n: Dep() for n in O}
        rkv_src = [self.projT[OFF_RKVX + a * 256:OFF_RKVX + (a + 1) * 256, :].rearrange("(kt p) t -> p kt t", p=128) for a in range(4)]
        for bi, (t0, Wb) in enumerate(BLOCKS):
            lo = max(t0 - 1, 0)
            hi = min(t0 + Wb + 1, L)
            c0 = lo - (t0 - 1)
            if t0 == 0:
                fw.op("dve", lambda e: e.memset(X[:, :, :, 0:1], 0.0), writes=[dX])
            if t0 + Wb == L:
                fw.op("dve", lambda e: e.memset(X[:, :, :, Wb + 1:Wb + 2], 0.0), writes=[dX])
            for a in range(4):
                fw.dma("sp", X[:, a, :, c0:c0 + (hi - lo)], rkv_src[a][:, :, lo:hi], reads=[self.dep_proj], writes=[dX])
            for a in range(4):
                for kt in range(2):
                    ctr, lf, rt = X[:, a, kt, 1:Wb + 1], X[:, a, kt, 0:Wb], X[:, a, kt, 2:Wb + 2]
                    fw.op("pool", lambda e: e.tensor_tensor(out=T1[:, kt, 0:Wb], in0=lf, in1=rt, op=ALU.add), reads=[dX], writes=[dT1])
                    fw.op("dve", lambda e: e.scalar_tensor_tensor(out=T1[:, kt, 0:Wb], in0=T1[:, kt, 0:Wb], scalar=0.5, in1=ctr, op0=ALU.mult, op1=ALU.subtract), reads=[dT1, dX], writes=[dT1])
                    if a < 3:
                        fw.op("dve", lambda e: e.scalar_tensor_tensor(out=Q[:, a, kt, 0:Wb], in0=T1[:, kt, 0:Wb], scalar=mu[a][:, kt:kt + 1], in1=ctr, op0=ALU.mult, op1=ALU.add), reads=[dT1, dX, dpar], writes=[dQ])
                    else:
                        for i3 in range(3):
                            fw.op("dve", lambda e: e.scalar_tensor_tensor(out=XW[:, i3, kt, 0:Wb], in0=T1[:, kt, 0:Wb], scalar=muw[i3][:, kt:kt + 1], in1=ctr, op0=ALU.mult, op1=ALU.add), reads=[dT1, dX, dpar], writes=[dXW])
            fw.dma("pool", scv["r"][:, :, t0:t0 + Wb], Q[:, 0, :, 0:Wb], reads=[dQ], writes=[dSC["r"]])
            fw.dma("pool", scv["v"][:, :, t0:t0 + Wb], Q[:, 2, :, 0:Wb], reads=[dQ], writes=[dSC["v"]])
            ps, dp = self.next_ps()
            for kt in range(2):
                fw.op("pe", lambda e: e.matmul(ps[:, 0:Wb], lhsT=g1[0][:, kt, :], rhs=XW[:, 2, kt, 0:Wb], start=(kt == 0), stop=(kt == 1)), reads=[g1[1], dXW], writes=[dp])
            fw.op("act", lambda e: e.activation(out=Hh[:, 0:Wb], in_=ps[:, 0:Wb], func=AF.Sigmoid), reads=[dp], writes=[dHh])
            for ct in range(2):
                ps, dp = self.next_ps()
                fw.op("pe", lambda e: e.matmul(ps[:, 0:Wb], lhsT=g2[0][:, 0, ct * 128:(ct + 1) * 128], rhs=Hh[:, 0:Wb], start=True, stop=True), reads=[g2[1], dHh], writes=[dp])
                fw.op("act", lambda e: e.copy(out=O["g"][:, ct, 0:Wb], in_=ps[:, 0:Wb]), reads=[dp], writes=[dO["g"]])
            fw.dma("pool", scv["g"][:, :, t0:t0 + Wb], O["g"][:, :, 0:Wb], reads=[dO["g"]], writes=[dSC["g"]])
            for kt in range(2):
                fw.op("dve", lambda e: e.tensor_scalar(out=KK[:, kt, 0:Wb], in0=Q[:, 1, kt, 0:Wb], scalar1=k_k[:, kt:kt + 1], scalar2=None, op0=ALU.mult), reads=[dQ, dpar], writes=[dKK])
                fw.op("act", lambda e: e.activation(out=SQ[:, kt, 0:Wb], in_=KK[:, kt, 0:Wb], func=AF.Square), reads=[dKK], writes=[dSQ])
                ps, dp = self.next_ps()
                fw.op("pe", lambda e: e.matmul(ps[:, 0:Wb], lhsT=blk[:, :], rhs=SQ[:, kt, 0:Wb], start=True, stop=True), reads=[dSQ, dpar], writes=[dp])
                fw.op("act", lambda e: e.activation(out=RS[:, 0:Wb], in_=ps[:, 0:Wb], func=AF.Sqrt, bias=tiny[:, 0:1]), reads=[dp, dpar], writes=[dRS])
                fw.op("dve", lambda e: e.reciprocal(out=RS[:, 0:Wb], in_=RS[:, 0:Wb]), reads=[dRS], writes=[dRS])
                fw.op("dve", lambda e: e.tensor_tensor(out=KK[:, kt, 0:Wb], in0=KK[:, kt, 0:Wb], in1=RS[:, 0:Wb], op=ALU.mult), reads=[dKK, dRS], writes=[dKK])
            fw.dma("pool", scv["kkn"][:, :, t0:t0 + Wb], KK[:, :, 0:Wb], reads=[dKK], writes=[dSC["kkn"]])
            for kt in range(2):
                fw.op("pool", lambda e: e.tensor_tensor(out=T1[:, kt, 0:Wb], in0=Q[:, 0, kt, 0:Wb], in1=Q[:, 1, kt, 0:Wb], op=ALU.mult), reads=[dQ, dT1], writes=[dT1])
                fw.op("dve", lambda e: e.tensor_scalar(out=SQ[:, kt, 0:Wb], in0=T1[:, kt, 0:Wb], scalar1=r_k[:, kt:kt + 1], scalar2=None, op0=ALU.mult), reads=[dT1, dpar, dSQ], writes=[dSQ])
                ps, dp = self.next_ps()
                fw.op("pe", lambda e: e.matmul(ps[:, 0:Wb], lhsT=blk[:, :], rhs=SQ[:, kt, 0:Wb], start=True, stop=True), reads=[dSQ, dpar], writes=[dp])
                fw.op("dve", lambda e: e.tensor_tensor(out=O["bonus"][:, kt, 0:Wb], in0=ps[:, 0:Wb], in1=Q[:, 2, kt, 0:Wb], op=ALU.mult), reads=[dp, dQ], writes=[dO["bonus"]])
            fw.dma("pool", scv["bonus"][:, :, t0:t0 + Wb], O["bonus"][:, :, 0:Wb], reads=[dO["bonus"]], writes=[dSC["bonus"]])
            for d in range(2):
                ps, dp = self.next_ps()
                for kt in range(2):
                    fw.op("pe", lambda e: e.matmul(ps[0:64, 0:Wb], lhsT=w1[d][0][:, kt, :], rhs=XW[:, 0, kt, 0:Wb], start=(kt == 0), stop=(kt == 1)), reads=[w1[d][1], dXW], writes=[dp])
                fw.op("act", lambda e: e.activation(out=Hh[0:64, 0:Wb], in_=ps[0:64, 0:Wb], func=AF.Tanh), reads=[dp], writes=[dHh])
                for ct in range(2):
                    ps, dp = self.next_ps()
                    fw.op("pe", lambda e: e.matmul(ps[:, 0:Wb], lhsT=w2[d][0][:, ct * 128:(ct + 1) * 128], rhs=Hh[0:64, 0:Wb], start=True, stop=True), reads=[w2[d][1], dHh], writes=[dp])
                    fw.op("act", lambda e: e.activation(out=O["lw"][:, ct, 0:Wb], in_=ps[:, 0:Wb], func=AF.Sigmoid, bias=w0[d][:, ct:ct + 1]), reads=[dp, dpar], writes=[dO["lw"]])
                    fw.op("dve", lambda e: e.tensor_scalar(out=O["lw"][:, ct, 0:Wb], in0=O["lw"][:, ct, 0:Wb], scalar1=-0.6065306597126334, scalar2=None, op0=ALU.mult), reads=[dO["lw"]], writes=[dO["lw"]])
                fw.dma("pool", scv["lw%d" % d][:, :, t0:t0 + Wb], O["lw"][:, :, 0:Wb], reads=[dO["lw"]], writes=[dSC["lw%d" % d]])
                ps, dp = self.next_ps()
                for kt in range(2):
                    fw.op("pe", lambda e: e.matmul(ps[0:64, 0:Wb], lhsT=a1[d][0][:, kt, :], rhs=XW[:, 1, kt, 0:Wb], start=(kt == 0), stop=(kt == 1)), reads=[a1[d][1], dXW], writes=[dp])
                fw.op("act", lambda e: e.copy(out=Hh[0:64, 0:Wb], in_=ps[0:64, 0:Wb]), reads=[dp], writes=[dHh])
                for ct in range(2):
                    ps, dp = self.next_ps()
                    fw.op("pe", lambda e: e.matmul(ps[:, 0:Wb], lhsT=a2[d][0][:, ct * 128:(ct + 1) * 128], rhs=Hh[0:64, 0:Wb], start=True, stop=True), reads=[a2[d][1], dHh], writes=[dp])
                    fw.op("act", lambda e: e.activation(out=AS[:, ct, 0:Wb], in_=ps[:, 0:Wb], func=AF.Sigmoid, bias=a0[d][:, ct:ct + 1]), reads=[dp, dpar], writes=[dAS])
                for kt in range(2):
                    fw.op("dve", lambda e: e.tensor_scalar(out=O["kd"][:, kt, 0:Wb], in0=AS[:, kt, 0:Wb], scalar1=-1.0, scalar2=None, op0=ALU.add), reads=[dAS], writes=[dO["kd"]])
                    fw.op("dve", lambda e: e.tensor_scalar(out=O["kd"][:, kt, 0:Wb], in0=O["kd"][:, kt, 0:Wb], scalar1=k_a[:, kt:kt + 1], scalar2=1.0, op0=ALU.mult, op1=ALU.add), reads=[dO["kd"], dpar], writes=[dO["kd"]])
                    fw.op("pool", lambda e: e.tensor_tensor(out=O["kd"][:, kt, 0:Wb], in0=O["kd"][:, kt, 0:Wb], in1=Q[:, 1, kt, 0:Wb], op=ALU.mult), reads=[dO["kd"], dQ], writes=[dO["kd"]])
                    fw.op("pool", lambda e: e.tensor_tensor(out=O["b"][:, kt, 0:Wb], in0=KK[:, kt, 0:Wb], in1=AS[:, kt, 0:Wb], op=ALU.mult), reads=[dKK, dAS], writes=[dO["b"]])
                fw.dma("pool", scv["kd%d" % d][:, :, t0:t0 + Wb], O["kd"][:, :, 0:Wb], reads=[dO["kd"]], writes=[dSC["kd%d" % d]])
                fw.dma("pool", scv["b%d" % d][:, :, t0:t0 + Wb], O["b"][:, :, 0:Wb], reads=[dO["b"]], writes=[dSC["b%d" % d]])
    fw.barrier()
    self._rwkv_scan(li, SC, dSC, scv)


Builder.mix_rwkv = _mix_rwkv


def _rwkv_scan(self, li, SC, dSC, scv):
    fw = self.fw
    I = self.I
    names = ("rt", "at", "kt", "bt", "kh", "bh", "vb")
    AD = {}
    dAD = {}
    for d in range(2):
        for kt in range(2):
            for n in names:
                AD[(d, kt, n)] = self.scratch_once("rwA_%d_%d_%s" % (d, kt, n), (128, LP)) if False else None
    if not hasattr(self, "_rwA"):
        self._rwA = {}
        for d in range(2):
            for kt in range(2):
                self._rwA[(d, kt)] = self.nc.dram_tensor("rwA_%d_%d" % (d, kt), [128, 7, LP], BF16, kind="Internal").ap()
        self._rwA_dep = {k: Dep() for k in self._rwA}
    with ExitStack() as es:
        yacc = self.sb(es, "rs_yacc", [128, 2, L])
        dyacc = Dep()
        tril_s = self.sb(es, "rs_tril_s", [128, 128])
        tril_i = self.sb(es, "rs_tril_i", [128, 128])
        trilT_s = self.sb(es, "rs_trilT_s", [128, 128])
        identb = self.sb(es, "rs_identb", [128, 128], BF16)
        dc = Dep()
        fw.dma("sp", tril_s[:], I["c_tril_s"][:, :], writes=[dc])
        fw.dma("sp", tril_i[:], I["c_tril_i"][:, :], writes=[dc])
        fw.dma("sp", trilT_s[:], I["c_trilT_s"][:, :], writes=[dc])
        fw.op("dve", lambda e: e.tensor_copy(out=identb[:], in_=self.ident[:]), reads=[self.d_const], writes=[dc])
        ones1 = self.sb(es, "rs_ones", [128, 128])
        fw.op("dve", lambda e: e.memset(ones1[:], 1.0), writes=[dc])
        etot = self.sb(es, "rs_etot", [128, 2, NCH])
        detot = Dep()
        for d in range(2):
            for kt in range(2):
                with ExitStack() as es3:
                    A = self.sb(es3, "rs_A", [128, 7, LP], BF16)
                    dA = Dep()
                    AI = {n: i for i, n in enumerate(names)}
                    stg = [self.sb(es3, "rs_stg%d" % i, [128, L]) for i in range(2)]
                    dstg = [Dep(), Dep()]
                    cs = self.sb(es3, "rs_cs", [128, LP])
                    lwr = self.sb(es3, "rs_lwr", [128, LP])
                    E1 = self.sb(es3, "rs_E1", [128, LP])
                    E2 = self.sb(es3, "rs_E2", [128, LP])
                    dcs, dlw, dE1, dE2 = Dep(), Dep(), Dep(), Dep()
                    fw.op("pool", lambda e: e.memset(A[:, :, L:LP], 0.0), writes=[dA])

                    def ld(i, name):
                        fw.dma("sp", stg[i][:], scv[name][:, kt, :], reads=[dSC[name]], writes=[dstg[i]])
                        return stg[i][:, :] if d == 0 else stg[i][:, ::-1]

                    sv = ld(0, "lw%d" % d)
                    fw.op("dve", lambda e: e.memset(lwr[:, L:LP], 0.0), writes=[dlw])
                    fw.op("dve", lambda e: e.tensor_copy(out=lwr[:, 0:L], in_=sv), reads=[dstg[0]], writes=[dlw])
                    for c in range(NCH):
                        fw.op("dve", lambda e: e.tensor_tensor_scan(out=cs[:, c * 128:(c + 1) * 128], data0=ones1[:, :], data1=lwr[:, c * 128:(c + 1) * 128], initial=0.0, op0=ALU.mult, op1=ALU.add),
                              reads=[dlw, dc], writes=[dcs])
                    fw.op("act", lambda e: e.activation(out=etot[:, kt, :], in_=cs[:, 127::128], func=AF.Exp), reads=[dcs], writes=[detot])
                    fw.op("act", lambda e: e.activation(out=E1[:], in_=cs[:], func=AF.Exp), reads=[dcs], writes=[dE1])
                    sv = ld(1, "r")
                    fw.op("dve", lambda e: e.tensor_tensor(out=A[:, AI["rt"], 0:L], in0=sv, in1=E1[:, 0:L], op=ALU.mult), reads=[dstg[1], dE1], writes=[dA])
                    fw.op("pool", lambda e: e.tensor_tensor(out=lwr[:], in0=cs[:], in1=lwr[:], op=ALU.subtract), reads=[dcs, dlw], writes=[dlw])
                    fw.op("act", lambda e: e.activation(out=E1[:], in_=lwr[:], func=AF.Exp), reads=[dlw, dE1], writes=[dE1])
                    sv = ld(0, "kkn")
                    fw.op("dve", lambda e: e.scalar_tensor_tensor(out=A[:, AI["at"], 0:L], in0=sv, scalar=-1.0, in1=E1[:, 0:L], op0=ALU.mult, op1=ALU.mult), reads=[dstg[0], dE1], writes=[dA])
                    for c in range(NCH):
                        fw.op("dve", lambda e: e.tensor_scalar(out=lwr[:, c * 128:(c + 1) * 128], in0=cs[:, c * 128:(c + 1) * 128], scalar1=cs[:, c * 128 + 127:c * 128 + 128], scalar2=-1.0, op0=ALU.subtract, op1=ALU.mult),
                              reads=[dcs, dlw], writes=[dlw])
                    fw.op("act", lambda e: e.activation(out=E1[:], in_=lwr[:], func=AF.Exp), reads=[dlw, dE1], writes=[dE1])
                    fw.op("act", lambda e: e.activation(out=E2[:], in_=cs[:], func=AF.Exp, scale=-1.0), reads=[dcs], writes=[dE2])
                    sv = ld(1, "kd%d" % d)
                    fw.op("dve", lambda e: e.tensor_tensor(out=A[:, AI["kt"], 0:L], in0=sv, in1=E2[:, 0:L], op=ALU.mult), reads=[dstg[1], dE2], writes=[dA])
                    fw.op("pool", lambda e: e.tensor_tensor(out=A[:, AI["kh"], 0:L], in0=sv, in1=E1[:, 0:L], op=ALU.mult), reads=[dstg[1], dE1], writes=[dA])
                    sv = ld(0, "b%d" % d)
                    fw.op("dve", lambda e: e.tensor_tensor(out=A[:, AI["bt"], 0:L], in0=sv, in1=E2[:, 0:L], op=ALU.mult), reads=[dstg[0], dE2], writes=[dA])
                    fw.op("pool", lambda e: e.tensor_tensor(out=A[:, AI["bh"], 0:L], in0=sv, in1=E1[:, 0:L], op=ALU.mult), reads=[dstg[0], dE1], writes=[dA])
                    sv = ld(1, "v")
                    fw.op("dve", lambda e: e.tensor_copy(out=A[:, AI["vb"], 0:L], in_=sv), reads=[dstg[1]], writes=[dA])
                    fw.dma("pool", self._rwA[(d, kt)][:, :, :], A[:, :, :], reads=[dA], writes=[self._rwA_dep[(d, kt)]])
                    fw.barrier()
            with ExitStack() as es4:
                AA = [self.sb(es4, "rs_AA%d" % kt, [128, 7, LP], BF16) for kt in range(2)]
                dAA = [Dep(), Dep()]
                for kt in range(2):
                    fw.dma("sp", AA[kt][:, :, :], self._rwA[(d, kt)][:, :, :], reads=[self._rwA_dep[(d, kt)]], writes=[dAA[kt]])
                AI = {n: i for i, n in enumerate(names)}
                S0 = self.sb(es4, "rs_S0", [128, 2, 64])
                S0b = self.sb(es4, "rs_S0b", [128, 2, 64], BF16)
                dS0 = [[Dep(), Dep()], [Dep(), Dep()]]
                dS0b = [[Dep(), Dep()], [Dep(), Dep()]]
                fw.op("dve", lambda e: e.memset(S0[:], 0.0), writes=[x for y in dS0 for x in y])
                fw.op("dve", lambda e: e.memset(S0b[:], 0.0), writes=[x for y in dS0b for x in y])
                NHB = 2
                chains = [(kt, hh) for kt in range(2) for hh in range(2)]

                def mk(nm, shape, dt=F32):
                    return ({ch: [self.sb(es4, "rs_%s_%d%d_%d" % (nm, ch[0], ch[1], i), shape, dt) for i in range(NHB)] for ch in chains},
                            {ch: [Dep() for i in range(NHB)] for ch in chains})
                TK, dTK = mk("TK", [128, 192], BF16)
                Pm, dPm = mk("P", [128, 128])
                PTm, dPTm = mk("PT", [128, 128])
                XT, dXT = mk("XT", [128, 128])
                XTb, dXTb = mk("XTb", [128, 128], BF16)
                Ak, dAk = mk("Ak", [128, 3, 128], BF16)
                P1b, dP1b = mk("P1b", [128, 64], BF16)
                Zb, dZb = mk("Zb", [128, 64], BF16)
                for c in range(NCH):
                    t0 = c * 128
                    Wv = min(128, L - t0)
                    C = slice(t0, t0 + 128)
                    b_ = c % NHB
                    for ch in chains:
                        kt, hh = ch
                        R = slice(64 * hh, 64 * hh + 64)
                        Aq = lambda n: AA[kt][R, AI[n], C]
                        tk, dtk = TK[ch][b_], dTK[ch][b_]
                        ptk, dptk = self.next_ps()
                        for i3, n in enumerate(("vb", "kh", "bh")):
                            fw.op("pe", lambda e: e.matmul(ptk[:, i3 * 64:(i3 + 1) * 64], lhsT=Aq(n), rhs=identb[R, 64 * hh:64 * hh + 64], start=True, stop=True), reads=[dAA[kt], dc], writes=[dptk])
                        fw.op("pe", lambda e: e.matmul(ptk[:, 256:384], lhsT=Aq("bt"), rhs=Aq("rt"), start=True, stop=True), reads=[dAA[kt]], writes=[dptk])
                        pa, dpa = self.next_ps()
                        for i5, (l_, r_) in enumerate((("bt", "at"), ("at", "bt"), ("kt", "at"), ("kt", "rt"))):
                            fw.op("pe", lambda e: e.matmul(pa[:, i5 * 128:(i5 + 1) * 128], lhsT=Aq(l_), rhs=Aq(r_), start=True, stop=True), reads=[dAA[kt]], writes=[dpa])
                        fw.op("act", lambda e: e.copy(out=tk[:, :], in_=ptk[:, 0:192]), reads=[dptk], writes=[dtk])
                        P_, dP_ = Pm[ch][b_], dPm[ch][b_]
                        PT_, dPT_ = PTm[ch][b_], dPTm[ch][b_]
                        X_, dX_ = XT[ch][b_], dXT[ch][b_]
                        ak, dak = Ak[ch][b_], dAk[ch][b_]
                        fw.op("dve", lambda e: e.tensor_tensor(out=PT_[:], in0=pa[:, 0:128], in1=tril_s[:], op=ALU.mult), reads=[dpa, dc], writes=[dPT_])
                        fw.op("dve", lambda e: e.tensor_tensor(out=P_[:], in0=pa[:, 128:256], in1=trilT_s[:], op=ALU.mult), reads=[dpa, dc], writes=[dP_])
                        fw.op("dve", lambda e: e.tensor_tensor(out=ak[:, 0, :], in0=pa[:, 256:384], in1=tril_s[:], op=ALU.mult), reads=[dpa, dc], writes=[dak])
                        fw.op("dve", lambda e: e.tensor_tensor(out=ak[:, 1, :], in0=pa[:, 384:512], in1=tril_i[:], op=ALU.mult), reads=[dpa, dc], writes=[dak])
                        fw.op("dve", lambda e: e.tensor_tensor(out=ak[:, 2, :], in0=ptk[:, 256:384], in1=tril_i[:], op=ALU.mult), reads=[dptk, dc], writes=[dak])
                        fw.op("pool", lambda e: e.tensor_tensor(out=X_[:], in0=PT_[:], in1=self.ident[:], op=ALU.add), reads=[dPT_, self.d_const], writes=[dX_])
                    for s_ in range(6):
                        pns = {}
                        for ch in chains:
                            P_, dP_ = Pm[ch][b_], dPm[ch][b_]
                            PT_, dPT_ = PTm[ch][b_], dPTm[ch][b_]
                            pn, dpn = self.next_ps()
                            pns[ch] = (pn, dpn)
                            fw.op("pe", lambda e: e.matmul(pn[:, 0:128], lhsT=PT_[:, :], rhs=P_[:, :], start=True, stop=True), reads=[dPT_, dP_], writes=[dpn])
                            if s_ < 5:
                                fw.op("pe", lambda e: e.matmul(pn[:, 128:256], lhsT=P_[:, :], rhs=PT_[:, :], start=True, stop=True), reads=[dPT_, dP_], writes=[dpn])
                        for ch in chains:
                            P_, dP_ = Pm[ch][b_], dPm[ch][b_]
                            PT_, dPT_ = PTm[ch][b_], dPTm[ch][b_]
                            pn, dpn = pns[ch]
                            fw.op("act", lambda e: e.copy(out=P_[:], in_=pn[:, 0:128]), reads=[dpn], writes=[dP_])
                            if s_ < 5:
                                fw.op("pool" if False else "act", lambda e: e.copy(out=PT_[:], in_=pn[:, 128:256]), reads=[dpn], writes=[dPT_])
                        pxs = {}
                        for ch in chains:
                            P_, dP_ = Pm[ch][b_], dPm[ch][b_]
                            X_, dX_ = XT[ch][b_], dXT[ch][b_]
                            px_, dpx_ = self.next_ps()
                            pxs[ch] = (px_, dpx_)
                            fw.op("pe", lambda e: e.matmul(px_[:, 0:128], lhsT=P_[:, :], rhs=X_[:, :], start=True, stop=True), reads=[dP_, dX_], writes=[dpx_])
                        for ch in chains:
                            X_, dX_ = XT[ch][b_], dXT[ch][b_]
                            px_, dpx_ = pxs[ch]
                            fw.op("dve", lambda e: e.tensor_tensor(out=X_[:], in0=px_[:, 0:128], in1=X_[:], op=ALU.add), reads=[dpx_, dX_], writes=[dX_])
                    pps = {}
                    for ch in chains:
                        kt, hh = ch
                        R = slice(64 * hh, 64 * hh + 64)
                        X_, dX_ = XT[ch][b_], dXT[ch][b_]
                        xb_, dxb_ = XTb[ch][b_], dXTb[ch][b_]
                        tk, dtk = TK[ch][b_], dTK[ch][b_]
                        ak, dak = Ak[ch][b_], dAk[ch][b_]
                        fw.op("act", lambda e: e.copy(out=xb_[:], in_=X_[:]), reads=[dX_], writes=[dxb_])
                        pp, dpp = self.next_ps()
                        pps[ch] = (pp, dpp)
                        fw.op("pe", lambda e: e.matmul(pp[:, 0:64], lhsT=AA[kt][R, AI["at"], C], rhs=S0b[R, kt, :], start=True, stop=False), reads=[dAA[kt], dS0b[kt][hh]], writes=[dpp])
                        fw.op("pe", lambda e: e.matmul(pp[:, 0:64], lhsT=ak[:, 0, :], rhs=tk[:, 0:64], start=False, stop=True), reads=[dak, dtk], writes=[dpp])
                    for ch in chains:
                        pp, dpp = pps[ch]
                        p1, dp1 = P1b[ch][b_], dP1b[ch][b_]
                        fw.op("act", lambda e: e.copy(out=p1[:], in_=pp[:, 0:64]), reads=[dpp], writes=[dp1])
                    pzs = {}
                    for ch in chains:
                        xb_, dxb_ = XTb[ch][b_], dXTb[ch][b_]
                        p1, dp1 = P1b[ch][b_], dP1b[ch][b_]
                        pz, dpz = self.next_ps()
                        pzs[ch] = (pz, dpz)
                        fw.op("pe", lambda e: e.matmul(pz[:, 0:64], lhsT=xb_[:, :], rhs=p1[:, :], start=True, stop=True), reads=[dxb_, dp1], writes=[dpz])
                    for ch in chains:
                        pz, dpz = pzs[ch]
                        z_, dz_ = Zb[ch][b_], dZb[ch][b_]
                        fw.op("dve", lambda e: e.tensor_copy(out=z_[:], in_=pz[:, 0:64]), reads=[dpz], writes=[dz_])
                    for ch in chains:
                        kt, hh = ch
                        R = slice(64 * hh, 64 * hh + 64)
                        tk, dtk = TK[ch][b_], dTK[ch][b_]
                        ak, dak = Ak[ch][b_], dAk[ch][b_]
                        z_, dz_ = Zb[ch][b_], dZb[ch][b_]
                        py, dpy = self.next_ps()
                        fw.op("pe", lambda e: e.matmul(py[R, 0:128], lhsT=S0b[R, kt, :], rhs=AA[kt][R, AI["rt"], C], start=True, stop=False), reads=[dAA[kt], dS0b[kt][hh]], writes=[dpy])
                        fw.op("pe", lambda e: e.matmul(py[R, 0:128], lhsT=tk[:, 0:64], rhs=ak[:, 1, :], start=False, stop=False), reads=[dak, dtk], writes=[dpy])
                        fw.op("pe", lambda e: e.matmul(py[R, 0:128], lhsT=z_[:, :], rhs=ak[:, 2, :], start=False, stop=True), reads=[dak, dz_], writes=[dpy])
                        pS, dpS = py, dpy
                        fw.op("pe", lambda e: e.matmul(pS[R, 256:320], lhsT=tk[:, 64:128], rhs=tk[:, 0:64], start=True, stop=False), reads=[dtk], writes=[dpS])
                        fw.op("pe", lambda e: e.matmul(pS[R, 256:320], lhsT=tk[:, 128:192], rhs=z_[:, :], start=False, stop=True), reads=[dtk, dz_], writes=[dpS])
                        if d == 0:
                            fw.op("act", lambda e: e.copy(out=yacc[R, kt, t0:t0 + Wv], in_=py[R, 0:Wv]), reads=[dpy], writes=[dyacc])
                        else:
                            lo = L - (t0 + Wv)
                            ya = yacc[R, kt, lo:lo + Wv]
                            fw.op("dve", lambda e: e.tensor_tensor(out=ya[:, ::-1], in0=py[R, 0:Wv], in1=ya[:, ::-1], op=ALU.add), reads=[dpy, dyacc], writes=[dyacc])
                        fw.op("dve", lambda e: e.scalar_tensor_tensor(out=S0[R, kt, :], in0=S0[R, kt, :], scalar=etot[R, kt, c:c + 1], in1=pS[R, 256:320], op0=ALU.mult, op1=ALU.add), reads=[dS0[kt][hh], detot, dpS], writes=[dS0[kt][hh]])
                        fw.op("act", lambda e: e.copy(out=S0b[R, kt, :], in_=S0[R, kt, :]), reads=[dS0[kt][hh]], writes=[dS0b[kt][hh]])
                fw.barrier()
        fw.barrier()
        with ExitStack() as es5:
            dq = Dep()
            def vec2(name, ap1d):
                t = self.sb(es5, name, [128, 2])
                fw.dma("sp", t[:], ap1d.rearrange("(kt p) -> p kt", p=128), writes=[dq], slow=True)
                return t
            lnw = vec2("r3_lnw", I["rwkv_ln_w"][li])
            lnb = vec2("r3_lnb", I["rwkv_ln_b"][li])
            epsl = self.sb(es5, "r3_eps", [128, 1])
            fw.op("dve", lambda e: e.memset(epsl[:], 64e-5), writes=[dq])
            blk = self.sb(es5, "r3_blk", [128, 128], BF16)
            blkf = self.sb(es5, "r3_blkf", [128, 128])
            fw.dma("sp", blkf[:], I["c_blk"][:, :], writes=[dq])
            fw.op("dve", lambda e: e.tensor_copy(out=blk[:], in_=blkf[:]), reads=[dq], writes=[dq])
            W = 512
            yb = self.sb(es5, "r3_yb", [128, W], BF16)
            sq = self.sb(es5, "r3_sq", [128, W], BF16)
            mean = self.sb(es5, "r3_mean", [128, W])
            var = self.sb(es5, "r3_var", [128, W])
            yc = [self.sb(es5, "r3_yc%d" % i, [128, 2, W]) for i in range(2)]
            bg = [self.sb(es5, "r3_bg%d" % i, [128, 2, 2, W]) for i in range(2)]
            dt_ = Dep()
            dyc = [Dep(), Dep()]
            dbg = [Dep(), Dep()]
            yv = self.ycT.rearrange("(kt p) t -> p kt t", p=128)
            for bi, (t0, Wb) in enumerate(BLOCKS):
                y_, dy_ = yc[bi % 2], dyc[bi % 2]
                b_, db_ = bg[bi % 2], dbg[bi % 2]
                fw.dma("sp", b_[:, 0, :, 0:Wb], scv["bonus"][:, :, t0:t0 + Wb], reads=[dSC["bonus"]], writes=[db_])
                fw.dma("sp", b_[:, 1, :, 0:Wb], scv["g"][:, :, t0:t0 + Wb], reads=[dSC["g"]], writes=[db_])
                for kt in range(2):
                    ysl = yacc[:, kt, t0:t0 + Wb]
                    fw.op("act", lambda e: e.copy(out=yb[:, 0:Wb], in_=ysl), reads=[dyacc], writes=[dt_])
                    fw.op("pool", lambda e: e.tensor_tensor(out=sq[:, 0:Wb], in0=ysl, in1=ysl, op=ALU.mult), reads=[dyacc], writes=[dt_])
                    pm, dpm = self.next_ps()
                    pq, dpq = self.next_ps()
                    fw.op("pe", lambda e: e.matmul(pm[:, 0:Wb], lhsT=blk[:, :], rhs=yb[:, 0:Wb], start=True, stop=True), reads=[dq, dt_], writes=[dpm])
                    fw.op("pe", lambda e: e.matmul(pq[:, 0:Wb], lhsT=blk[:, :], rhs=sq[:, 0:Wb], start=True, stop=True), reads=[dq, dt_], writes=[dpq])
                    fw.op("act", lambda e: e.mul(out=mean[:, 0:Wb], in_=pm[:, 0:Wb], mul=1.0 / 64), reads=[dpm], writes=[dt_])
                    fw.op("dve", lambda e: e.tensor_tensor(out=var[:, 0:Wb], in0=mean[:, 0:Wb], in1=mean[:, 0:Wb], op=ALU.mult), reads=[dt_], writes=[dt_])
                    fw.op("dve", lambda e: e.scalar_tensor_tensor(out=var[:, 0:Wb], in0=pq[:, 0:Wb], scalar=1.0 / 64, in1=var[:, 0:Wb], op0=ALU.mult, op1=ALU.subtract), reads=[dpq, dt_], writes=[dt_])
                    fw.op("act", lambda e: e.activation(out=var[:, 0:Wb], in_=var[:, 0:Wb], func=AF.Sqrt, bias=epsl[:, 0:1]), reads=[dt_, dq], writes=[dt_])
                    fw.op("dve", lambda e: e.reciprocal(out=var[:, 0:Wb], in_=var[:, 0:Wb]), reads=[dt_], writes=[dt_])
                    fw.op("pool", lambda e: e.tensor_tensor(out=y_[:, kt, 0:Wb], in0=ysl, in1=mean[:, 0:Wb], op=ALU.subtract), reads=[dyacc, dt_], writes=[dy_])
                    fw.op("dve", lambda e: e.tensor_tensor(out=y_[:, kt, 0:Wb], in0=y_[:, kt, 0:Wb], in1=var[:, 0:Wb], op=ALU.mult), reads=[dy_, dt_], writes=[dy_])
                    fw.op("dve", lambda e: e.tensor_scalar(out=y_[:, kt, 0:Wb], in0=y_[:, kt, 0:Wb], scalar1=lnw[:, kt:kt + 1], scalar2=lnb[:, kt:kt + 1], op0=ALU.mult, op1=ALU.add), reads=[dy_, dq], writes=[dy_])
                    fw.op("pool", lambda e: e.tensor_tensor(out=y_[:, kt, 0:Wb], in0=y_[:, kt, 0:Wb], in1=b_[:, 0, kt, 0:Wb], op=ALU.add), reads=[dy_, db_], writes=[dy_])
                    fw.op("dve", lambda e: e.tensor_tensor(out=y_[:, kt, 0:Wb], in0=y_[:, kt, 0:Wb], in1=b_[:, 1, kt, 0:Wb], op=ALU.mult), reads=[dy_, db_], writes=[dy_])
                fw.dma("pool", yv[:, :, t0:t0 + Wb], y_[:, :, 0:Wb], reads=[dy_], writes=[self.dep_yc])
    fw.barrier()


Builder._rwkv_scan = _rwkv_scan
```

```python
import numpy as np
from contextlib import ExitStack
import concourse.bass as bass
import concourse.mybir as mybir
from concourse.bass_utils import run_bass_kernel_spmd

F32 = mybir.dt.float32
BF16 = mybir.dt.bfloat16
F32R = mybir.dt.float32r


def R32(ap):
    return ap.bitcast(F32R)
AF = mybir.ActivationFunctionType
ALU = mybir.AluOpType

D = 1024
SEQ = 4096
NMETA = 16
L = SEQ + NMETA
DEPTH = 2
DFF = 4096
NIN = 5896
EPS = 1e-6
NDS = 48

OFF_U, OFF_Z, OFF_XBC, OFF_DT, OFF_RKVX, OFF_G = 0, 256, 768, 1792, 1800, 2824

BLOCKS = [(i * 512, 512) for i in range(8)] + [(4096, 16)]


class Dep:
    __slots__ = ("w", "r", "x")

    def __init__(self, excl=False):
        self.w = None
        self.r = []
        self.x = excl


class FW:
    def __init__(self, nc, es):
        self.nc = nc
        self.engs = dict(pe=nc.tensor, act=nc.scalar, dve=nc.vector, pool=nc.gpsimd, sp=nc.sync)
        self.sem = {k: es.enter_context(nc.semaphore("s_" + k)) for k in self.engs}
        self.cnt = {k: 0 for k in self.engs}
        self.seen = {k: {} for k in self.engs}
        self.dsem = [es.enter_context(nc.semaphore("d%d" % i)) for i in range(NDS)]
        self.dval = [0] * NDS
        self.dnext = 0
        self.dnext2 = 0
        self.nins = 0

    def _wait(self, eng, ev):
        key, val = ev
        if self.seen[eng].get(key, 0) >= val:
            return
        self.seen[eng][key] = val
        sem = self.sem[key[1]] if key[0] == "e" else self.dsem[key[1]]
        self.engs[eng].wait_ge(sem, val)

    def _deps(self, eng, reads, writes):
        me = ("e", eng)
        for d in reads:
            if d.w is not None:
                self._wait(eng, d.w)
        for d in writes:
            if d.w is not None and (d.w[0] != me or eng == "pool"):
                self._wait(eng, d.w)
            for r in d.r:
                self._wait(eng, r)

    def _post(self, ev, reads, writes):
        for d in writes:
            d.w = ev
            d.r = []
        for d in reads:
            d.r = [r for r in d.r if r[0] != ev[0]] + [ev]

    def op(self, eng, fn, reads=(), writes=()):
        xs = [d for d in reads if d.x]
        if xs:
            writes = list(writes) + [d for d in xs if d not in writes]
            reads = [d for d in reads if not d.x]
        self._deps(eng, reads, writes)
        ins = fn(self.engs[eng])
        self.cnt[eng] += 1
        self.nins += 1
        ins.then_inc(self.sem[eng], 1)
        self._post((("e", eng), self.cnt[eng]), reads, writes)

    def dma(self, q, out, in_, reads=(), writes=(), slow=False):
        self._deps(q, reads, writes)
        half = NDS // 2
        if q == "pool":
            i = half + self.dnext2
            self.dnext2 = (self.dnext2 + 1) % half
        else:
            i = self.dnext
            self.dnext = (self.dnext + 1) % half
        if self.dval[i] > 0:
            self._wait(q, (("d", i), self.dval[i]))
        self.dval[i] += 16
        self.nins += 1
        if slow:
            self.engs[q].dma_start(out=out, in_=in_, allow_slow_non_contiguous=True).then_inc(self.dsem[i], 16)
        else:
            self.engs[q].dma_start(out=out, in_=in_).then_inc(self.dsem[i], 16)
        self._post((("d", i), self.dval[i]), reads, writes)

    def barrier(self):
        for e in self.engs:
            for e2 in self.engs:
                if self.cnt[e2] > 0:
                    self._wait(e, (("e", e2), self.cnt[e2]))
            for i in range(NDS):
                if self.dval[i] > 0:
                    self._wait(e, (("d", i), self.dval[i]))


def col_tiles(lo, hi):
    out = []
    c = lo
    while c < hi:
        m = min(128, hi - c)
        out.append((c, m))
        c += m
    return out


IN_TILES = (col_tiles(OFF_U, OFF_Z) + col_tiles(OFF_Z, OFF_XBC) + col_tiles(OFF_XBC, OFF_DT)
            + col_tiles(OFF_DT, OFF_RKVX) + col_tiles(OFF_RKVX, OFF_G) + col_tiles(OFF_G, NIN))


class Builder:
    def __init__(self, cfg):
        self.cfg = cfg
        nc = self.nc = bass.Bass("TRN2", target_bir_lowering=False)
        self.I = {}
        self.es = ExitStack()

    def inp(self, name, shape):
        t = self.nc.dram_tensor(name, list(shape), F32, kind="ExternalInput").ap()
        self.I[name] = t
        return t

    def scratch(self, name, shape, dt=F32):
        kind = "ExternalOutput" if name in self.cfg.get("dump", ()) else "Internal"
        return self.nc.dram_tensor(name, list(shape), dt, kind=kind).ap()

    def sb(self, es, name, shape, dt=F32):
        self.uid = getattr(self, "uid", 0) + 1
        return es.enter_context(self.nc.sbuf_tensor("%s_%d" % (name, self.uid), list(shape), dt))

    def build(self):
        nc = self.nc
        cfg = self.cfg
        with self.es as es:
            fw = self.fw = FW(nc, es)
            I = self.I
            x = self.inp("x", (SEQ, D))
            for name, shape in WEIGHT_SHAPES:
                self.inp(name, shape)
            self.inp("c_ident", (128, 128))
            self.inp("c_iota", (128, 512))
            self.inp("c_triu", (128, 128))
            self.inp("c_padm", (128, 8))
            self.inp("c_blk", (128, 128))
            self.inp("c_trilT_s", (128, 128))
            self.inp("c_mneg", (128, 128))
            self.inp("c_tril_s", (128, 128))
            self.inp("c_tril_i", (128, 128))
            out = nc.dram_tensor("out", [SEQ, D], F32, kind="ExternalOutput").ap()
            self.hT = self.scratch("hT", (D, L))
            self.projT = self.scratch("projT", (NIN, L))
            self.yaT = self.scratch("yaT", (256, L))
            self.ybT = self.scratch("ybT", (512, L))
            self.ycT = self.scratch("ycT", (256, L))
            self.dep_hT = Dep()
            self.dep_proj = Dep()
            self.dep_ya, self.dep_yb, self.dep_yc = Dep(), Dep(), Dep()

            self.ident = self.sb(es, "ident", [128, 128])
            self.ones_bf = self.sb(es, "ones_bf", [128, 128], BF16)
            self.d_const = Dep()
            fw.dma("sp", self.ident[:], I["c_ident"][:, :], writes=[self.d_const])
            fw.op("dve", lambda e: e.memset(self.ones_bf[:], 1.0), writes=[self.d_const])
            self.one_t = self.sb(es, "one_t", [128, 1])
            fw.op("dve", lambda e: e.memset(self.one_t[:], 1.0), writes=[self.d_const])
            self.ps = [es.enter_context(nc.psum_tensor("ps%d" % i, [128, 512], F32)) for i in range(8)]
            self.dps = [Dep(True) for _ in range(8)]
            self.psn = 0

            only = cfg.get("only")
            if only is not None:
                for ph in only:
                    if ph == "p0":
                        self.phase0(x)
                    elif ph == "p1":
                        self.phase1(0)
                    elif ph == "3a":
                        self.phase3a(0)
                    elif ph == "3b":
                        self.phase3b(0)
                    elif ph == "pf":
                        self.phase_final(out)
                fw.barrier()
                return nc
            self.phase0(x)
            nlayers = cfg.get("layers", DEPTH)
            for li in range(cfg.get("li0", 0), cfg.get("li0", 0) + nlayers):
                self.phase1(li)
                if cfg.get("fake_mix", False):
                    self.fake_mix()
                else:
                    self.mixers(li)
                if cfg.get("stop_after_mix", False):
                    break
                self.phase3a(li)
                self.phase3b(li)
            if not cfg.get("stop_after_mix", False):
                self.phase_final(out)
            fw.barrier()
        return nc

    def next_ps(self):
        i = self.psn
        self.psn = (i + 1) % 8
        return self.ps[i], self.dps[i]

    def phase0(self, x):
        fw = self.fw
        I = self.I
        with ExitStack() as es:
            xin = [self.sb(es, "p0_x%d" % i, [128, D]) for i in range(2)]
            dxin = [Dep(), Dep()]
            ho = [self.sb(es, "p0_h%d" % i, [128, 8, 128]) for i in range(2)]
            dho = [Dep(), Dep()]
            ntile = (L + 127) // 128
            for ti in range(ntile):
                t0 = ti * 128
                w = min(128, L - t0)
                xi, dx = xin[ti % 2], dxin[ti % 2]
                if ti == 0:
                    fw.dma("sp", xi[0:NMETA, :], I["meta_tokens"][:, :], writes=[dx])
                    fw.dma("sp", xi[NMETA:128, :], x[0:128 - NMETA, :], writes=[dx])
                else:
                    fw.dma("sp", xi[0:w, :], x[t0 - NMETA:t0 - NMETA + w, :], writes=[dx])
                h, dh = ho[ti % 2], dho[ti % 2]
                for kt in range(8):
                    ps, dp = self.next_ps()
                    fw.op("pe", lambda e: e.transpose(out=ps[:, 0:w], in_=xi[0:w, kt * 128:(kt + 1) * 128],
                                                      identity=self.ident[0:w, 0:w]),
                          reads=[dx, self.d_const], writes=[dp])
                    eng = "act" if kt % 2 == 0 else "dve"
                    if eng == "act":
                        fw.op("act", lambda e: e.copy(out=h[:, kt, 0:w], in_=ps[:, 0:w]), reads=[dp], writes=[dh])
                    else:
                        fw.op("dve", lambda e: e.tensor_copy(out=h[:, kt, 0:w], in_=ps[:, 0:w]), reads=[dp], writes=[dh])
                fw.dma("pool", self.hT.rearrange("(kt p) t -> p kt t", p=128)[:, :, t0:t0 + w], h[:, :, 0:w],
                       reads=[dh], writes=[self.dep_hT])
        fw.barrier()

    def load_weight_bf(self, es, name, w_ap, K, N, scale_ap=None, chunk=512):
        fw = self.fw
        kt_n = K // 128
        wbf = self.sb(es, name, [128, kt_n, N], BF16)
        dw = Dep()
        for kt in range(kt_n):
            fw.dma("pool", wbf[:, kt, :], w_ap[kt * 128:(kt + 1) * 128, :], writes=[dw])
        if scale_ap is not None:
            sc = self.sb(es, name + "_sc", [128, kt_n])
            dsc = Dep()
            fw.dma("sp", sc[:], scale_ap.rearrange("(kt p) -> p kt", p=128), writes=[dsc], slow=True)
            self.wcol = (sc, dsc)
        return wbf, dw

    def rmsnorm_block(self, h, dh, hn, dhn, sq, dsq, rstd, drstd, W):
        fw = self.fw
        for kt in range(8):
            fw.op("act", lambda e: e.activation(out=sq[:, kt, 0:W], in_=h[:, kt, 0:W], func=AF.Square),
                  reads=[dh], writes=[dsq])
        ps, dp = self.next_ps()
        for kt in range(8):
            fw.op("pe", lambda e: e.matmul(ps[:, 0:W], lhsT=self.ones_bf[:, :], rhs=sq[:, kt, 0:W],
                                           start=(kt == 0), stop=(kt == 7)),
                  reads=[dsq, self.d_const], writes=[dp])
        fw.op("act", lambda e: e.activation(out=rstd[:, 0:W], in_=ps[:, 0:W], func=AF.Sqrt, bias=self.eps_t[:, 0:1],
                                            scale=1.0 / D),
              reads=[dp, self.d_const], writes=[drstd])
        fw.op("dve", lambda e: e.reciprocal(out=rstd[:, 0:W], in_=rstd[:, 0:W]), reads=[drstd], writes=[drstd])
        sc, dsc = self.wcol
        for kt in range(8):
            fw.op("dve", lambda e: e.scalar_tensor_tensor(out=hn[:, kt, 0:W], in0=h[:, kt, 0:W], scalar=sc[:, kt:kt + 1],
                                                          in1=rstd[:, 0:W], op0=ALU.mult, op1=ALU.mult),
                  reads=[dh, drstd, dsc], writes=[dhn])

    def ensure_eps(self, es):
        self.eps_t = self.sb(es, "eps_t", [128, 1])
        self.fw.op("dve", lambda e: e.memset(self.eps_t[:], EPS), writes=[self.d_const])

    def phase1(self, li):
        fw = self.fw
        I = self.I
        with ExitStack() as es:
            self.ensure_eps(es)
            wbf, dw = self.load_weight_bf(es, "p1_w", I["w_in"][li], D, NIN, scale_ap=I["mix_norm_w"][li])
            h = [self.sb(es, "p1_h%d" % i, [128, 8, 512]) for i in range(2)]
            dh = [Dep(), Dep()]
            sq = self.sb(es, "p1_sq", [128, 8, 512], BF16)
            dsq = Dep()
            rstd = self.sb(es, "p1_rstd", [128, 512])
            drstd = Dep()
            hn = [self.sb(es, "p1_hn%d" % i, [128, 8, 512], BF16) for i in range(2)]
            dhn = [Dep(), Dep()]
            stg = [self.sb(es, "p1_o%d" % i, [128, 8, 512]) for i in range(2)]
            dstg = [Dep() for _ in range(2)]
            hTv = self.hT.rearrange("(kt p) t -> p kt t", p=128)
            batches = [(0, 6), (768, 8), (1792, None), (1800, 8)] + [(OFF_G + i * 1024, 8) for i in range(3)]
            ns = 0
            nb = 0
            def prologue(bi):
                t0_, W_ = BLOCKS[bi]
                fw.dma("sp", h[bi % 2][:, :, 0:W_], hTv[:, :, t0_:t0_ + W_], reads=[self.dep_hT], writes=[dh[bi % 2]])
                self.rmsnorm_block(h[bi % 2], dh[bi % 2], hn[bi % 2], dhn[bi % 2], sq, dsq, rstd, drstd, W_)

            prologue(0)
            for bi, (t0, W) in enumerate(BLOCKS):
                hnb, dhnb = hn[bi % 2], dhn[bi % 2]
                for bti, (r0, nt) in enumerate(batches):
                    if bti == 4 and bi + 1 < len(BLOCKS):
                        prologue(bi + 1)
                    s_, ds_ = stg[nb % 2], dstg[nb % 2]
                    nb += 1
                    tiles = [(r0, 8)] if nt is None else [(r0 + i * 128, 128) for i in range(nt)]
                    for ti, (c0, M) in enumerate(tiles):
                        ps, dp = self.next_ps()
                        for kt in range(8):
                            fw.op("pe", lambda e: e.matmul(ps[0:M, 0:W], lhsT=wbf[:, kt, c0:c0 + M], rhs=hnb[:, kt, 0:W],
                                                           start=(kt == 0), stop=(kt == 7)),
                                  reads=[dhnb, dw], writes=[dp])
                        if ns % 2 == 0:
                            fw.op("act", lambda e: e.copy(out=s_[0:M, ti, 0:W], in_=ps[0:M, 0:W]), reads=[dp], writes=[ds_])
                        else:
                            fw.op("dve", lambda e: e.tensor_copy(out=s_[0:M, ti, 0:W], in_=ps[0:M, 0:W]), reads=[dp], writes=[ds_])
                        ns += 1
                    if nt is None:
                        fw.dma("pool", self.projT[r0:r0 + 8, t0:t0 + W], s_[0:8, 0, 0:W], reads=[ds_], writes=[self.dep_proj])
                    else:
                        fw.dma("pool", self.projT[r0:r0 + nt * 128, t0:t0 + W].rearrange("(k p) t -> p k t", p=128),
                               s_[:, 0:nt, 0:W], reads=[ds_], writes=[self.dep_proj])
        fw.barrier()

    def fake_mix(self):
        fw = self.fw
        with ExitStack() as es:
            t = self.sb(es, "fm_t", [128, L])
            dt_ = Dep()
            for (dst, ddst, src0, n) in ((self.yaT, self.dep_ya, OFF_U, 2), (self.ybT, self.dep_yb, OFF_Z, 4),
                                         (self.ycT, self.dep_yc, OFF_RKVX, 2)):
                for j in range(n):
                    fw.dma("sp", t[:, :], self.projT[src0 + j * 128:src0 + (j + 1) * 128, :], reads=[self.dep_proj],
                           writes=[dt_])
                    fw.dma("sp", dst[j * 128:(j + 1) * 128, :], t[:, :], reads=[dt_], writes=[ddst])
        fw.barrier()

    def mixers(self, li):
        which = self.cfg.get("mix", ("s5", "ssd", "rwkv"))
        if "s5" in which:
            self.mix_s5(li)
        if "ssd" in which:
            self.mix_ssd(li)
        if "rwkv" in which:
            self.mix_rwkv(li)

    def phase3a(self, li):
        fw = self.fw
        I = self.I
        W = 512
        with ExitStack() as es:
            pa, dpa = self.load_weight_bf(es, "p3_pa", I["proj_a"][li], 256, D)
            pb, dpb = self.load_weight_bf(es, "p3_pb", I["proj_b"][li], 512, D)
            pc, dpc = self.load_weight_bf(es, "p3_pc", I["proj_c"][li], 256, D)
            wo, dwo = self.load_weight_bf(es, "p3_wo", I["w_out"][li], D, D)
            ystg = [self.sb(es, "p3_ys%d" % i, [128, 8, W]) for i in range(2)]
            dystg = [Dep(), Dep()]
            ybf = [self.sb(es, "p3_yb%d" % i, [128, 8, W], BF16) for i in range(2)]
            dybf = [Dep(), Dep()]
            g = [self.sb(es, "p3_g%d" % i, [128, 3, W]) for i in range(2)]
            dg = [Dep(), Dep()]
            mrg2 = [self.sb(es, "p3_m%d" % i, [128, 8, W], BF16) for i in range(2)]
            dmrg2 = [Dep(), Dep()]
            tmp = [self.sb(es, "p3_t%d" % i, [128, W]) for i in range(2)]
            dtmp = [Dep(), Dep()]
            h = [self.sb(es, "p3_h%d" % i, [128, 8, W]) for i in range(2)]
            dh = [Dep(), Dep()]
            hTv = self.hT.rearrange("(kt p) t -> p kt t", p=128)
            gv = self.projT[OFF_G:NIN, :].rearrange("(b kt p) t -> p b kt t", p=128, b=3)
            ng = 0

            def wout(pb_, dtile):
                t0_, Wp = BLOCKS[pb_]
                hp, dhp = h[pb_ % 2], dh[pb_ % 2]
                mp, dmp = mrg2[pb_ % 2], dmrg2[pb_ % 2]
                ps, dp = self.next_ps()
                for k in range(8):
                    fw.op("pe", lambda e: e.matmul(ps[:, 0:Wp], lhsT=wo[:, k, dtile * 128:(dtile + 1) * 128],
                                                   rhs=mp[:, k, 0:Wp], start=(k == 0), stop=(k == 7)),
                          reads=[dmp, dwo], writes=[dp])
                fw.op("dve", lambda e: e.tensor_tensor(out=hp[:, dtile, 0:Wp], in0=ps[:, 0:Wp], in1=hp[:, dtile, 0:Wp],
                                                       op=ALU.add), reads=[dp, dhp], writes=[dhp])

            def store(pb_):
                t0_, Wp = BLOCKS[pb_]
                fw.dma("pool", hTv[:, :, t0_:t0_ + Wp], h[pb_ % 2][:, :, 0:Wp], reads=[dh[pb_ % 2]], writes=[self.dep_hT])

            for bi, (t0, Wb) in enumerate(BLOCKS):
                ys, dys = ystg[bi % 2], dystg[bi % 2]
                yb, dyb = ybf[bi % 2], dybf[bi % 2]
                hb, dhb = h[bi % 2], dh[bi % 2]
                fw.dma("sp", ys[:, 0:2, 0:Wb], self.yaT.rearrange("(kt p) t -> p kt t", p=128)[:, :, t0:t0 + Wb],
                       reads=[self.dep_ya], writes=[dys])
                fw.dma("sp", ys[:, 2:6, 0:Wb], self.ybT.rearrange("(kt p) t -> p kt t", p=128)[:, :, t0:t0 + Wb],
                       reads=[self.dep_yb], writes=[dys])
                fw.dma("sp", ys[:, 6:8, 0:Wb], self.ycT.rearrange("(kt p) t -> p kt t", p=128)[:, :, t0:t0 + Wb],
                       reads=[self.dep_yc], writes=[dys])
                fw.dma("sp", hb[:, :, 0:Wb], hTv[:, :, t0:t0 + Wb], reads=[self.dep_hT], writes=[dhb])
                for kt in range(8):
                    eng = "dve" if kt % 2 == 0 else "pool"
                    fw.op(eng, lambda e: e.tensor_copy(out=yb[:, kt, 0:Wb], in_=ys[:, kt, 0:Wb]), reads=[dys], writes=[dyb])
                mrg, dmrg = mrg2[bi % 2], dmrg2[bi % 2]
                for dtile in range(8):
                    gg, dgg = g[ng % 2], dg[ng % 2]
                    ng += 1
                    fw.dma("sp", gg[:, :, 0:Wb], gv[:, :, dtile, t0:t0 + Wb], reads=[self.dep_proj], writes=[dgg])
                    fw.op("act", lambda e: e.activation(out=gg[:, :, 0:Wb], in_=gg[:, :, 0:Wb], func=AF.Sigmoid),
                          reads=[dgg], writes=[dgg])
                    tm, dtm = tmp[dtile % 2], dtmp[dtile % 2]
                    for br, (wt, dwt, k0, nk) in enumerate(((pa, dpa, 0, 2), (pb, dpb, 2, 4), (pc, dpc, 6, 2))):
                        ps, dp = self.next_ps()
                        for k in range(nk):
                            fw.op("pe", lambda e: e.matmul(ps[:, 0:Wb], lhsT=wt[:, k, dtile * 128:(dtile + 1) * 128],
                                                           rhs=yb[:, k0 + k, 0:Wb], start=(k == 0), stop=(k == nk - 1)),
                                  reads=[dyb, dwt], writes=[dp])
                        if br == 0:
                            fw.op("dve", lambda e: e.tensor_tensor(out=tm[:, 0:Wb], in0=ps[:, 0:Wb], in1=gg[:, 0, 0:Wb],
                                                                   op=ALU.mult), reads=[dp, dgg], writes=[dtm])
                        else:
                            fw.op("dve", lambda e: e.tensor_tensor(out=gg[:, br, 0:Wb], in0=ps[:, 0:Wb],
                                                                   in1=gg[:, br, 0:Wb], op=ALU.mult),
                                  reads=[dp, dgg], writes=[dgg])
                            if br == 1:
                                fw.op("dve", lambda e: e.tensor_tensor(out=tm[:, 0:Wb], in0=tm[:, 0:Wb],
                                                                       in1=gg[:, 1, 0:Wb], op=ALU.add),
                                      reads=[dtm, dgg], writes=[dtm])
                            else:
                                fw.op("dve", lambda e: e.tensor_tensor(out=mrg[:, dtile, 0:Wb], in0=tm[:, 0:Wb],
                                                                       in1=gg[:, 2, 0:Wb], op=ALU.add),
                                      reads=[dtm, dgg], writes=[dmrg])
                    if bi > 0:
                        wout(bi - 1, dtile)
                if bi > 0:
                    store(bi - 1)
            for dtile in range(8):
                wout(len(BLOCKS) - 1, dtile)
            store(len(BLOCKS) - 1)
        fw.barrier()

    def phase3b(self, li):
        fw = self.fw
        I = self.I
        W = 384
        blocks = [(s, min(W, L - s)) for s in range(0, L, W)]
        with ExitStack() as es:
            self.ensure_eps(es)
            w1, dw1 = self.load_weight_bf(es, "p4_w1", I["mlp_w1"][li], D, DFF, scale_ap=I["mlp_norm_w"][li])
            w2, dw2 = self.load_weight_bf(es, "p4_w2", I["mlp_w2"][li], DFF, D)
            h = [self.sb(es, "p4_h%d" % i, [128, 8, W]) for i in range(2)]
            dh = [Dep(), Dep()]
            sq = self.sb(es, "p4_sq", [128, 8, W], BF16)
            dsq = Dep()
            rstd = self.sb(es, "p4_rstd", [128, W])
            drstd = Dep()
            hn2 = [self.sb(es, "p4_hn%d" % i, [128, 8, W], BF16) for i in range(2)]
            dhn2 = [Dep(), Dep()]
            act = self.sb(es, "p4_act", [128, 32, W], BF16)
            dact = Dep()
            rl = [self.sb(es, "p4_rl%d" % i, [128, W]) for i in range(2)]
            drl = [Dep(), Dep()]
            hTv = self.hT.rearrange("(kt p) t -> p kt t", p=128)
            def prologue(bi):
                t0_, W_ = blocks[bi]
                fw.dma("sp", h[bi % 2][:, :, 0:W_], hTv[:, :, t0_:t0_ + W_], reads=[self.dep_hT], writes=[dh[bi % 2]])
                self.rmsnorm_block(h[bi % 2], dh[bi % 2], hn2[bi % 2], dhn2[bi % 2], sq, dsq, rstd, drstd, W_)

            prologue(0)
            for bi, (t0, Wb) in enumerate(blocks):
                hb, dhb = h[bi % 2], dh[bi % 2]
                hn, dhn = hn2[bi % 2], dhn2[bi % 2]
                for f in range(32):
                    ps, dp = self.next_ps()
                    for k in range(8):
                        fw.op("pe", lambda e: e.matmul(ps[:, 0:Wb], lhsT=w1[:, k, f * 128:(f + 1) * 128],
                                                       rhs=hn[:, k, 0:Wb], start=(k == 0), stop=(k == 7)),
                              reads=[dhn, dw1], writes=[dp])
                    r, dr = rl[f % 2], drl[f % 2]
                    fw.op("act", lambda e: e.activation(out=r[:, 0:Wb], in_=ps[:, 0:Wb], func=AF.Relu),
                          reads=[dp], writes=[dr])
                    eng = "dve" if f % 2 == 0 else "pool"
                    fw.op(eng, lambda e: e.tensor_tensor(out=act[:, f, 0:Wb], in0=r[:, 0:Wb], in1=r[:, 0:Wb], op=ALU.mult),
                          reads=[dr], writes=[dact])
                if bi + 1 < len(blocks):
                    prologue(bi + 1)
                for dtile in range(8):
                    ps, dp = self.next_ps()
                    for f in range(32):
                        fw.op("pe", lambda e: e.matmul(ps[:, 0:Wb], lhsT=w2[:, f, dtile * 128:(dtile + 1) * 128],
                                                       rhs=act[:, f, 0:Wb], start=(f == 0), stop=(f == 31)),
                              reads=[dact, dw2], writes=[dp])
                    fw.op("dve", lambda e: e.tensor_tensor(out=hb[:, dtile, 0:Wb], in0=ps[:, 0:Wb], in1=hb[:, dtile, 0:Wb],
                                                           op=ALU.add), reads=[dp, dhb], writes=[dhb])
                fw.dma("pool", hTv[:, :, t0:t0 + Wb], hb[:, :, 0:Wb], reads=[dhb], writes=[self.dep_hT])
        fw.barrier()

    def phase_final(self, out):
        fw = self.fw
        I = self.I
        with ExitStack() as es:
            self.ensure_eps(es)
            fnw = self.sb(es, "pf_w", [128, 8])
            dfnw = Dep()
            fw.dma("sp", fnw[:], I["final_norm_w"].rearrange("(kt p) -> p kt", p=128), writes=[dfnw], slow=True)
            h = [self.sb(es, "pf_h%d" % i, [128, 8, 512]) for i in range(2)]
            dh = [Dep(), Dep()]
            sq = self.sb(es, "pf_sq", [128, 8, 512], BF16)
            dsq = Dep()
            rstd = self.sb(es, "pf_rstd", [128, 512])
            drstd = Dep()
            o = [self.sb(es, "pf_o%d" % i, [128, D]) for i in range(2)]
            do = [Dep(), Dep()]
            dout = Dep()
            hTv = self.hT.rearrange("(kt p) t -> p kt t", p=128)
            no = 0
            for bi in range(8):
                t0 = NMETA + bi * 512
                W = 512
                hb, dhb = h[bi % 2], dh[bi % 2]
                fw.dma("sp", hb[:, :, 0:W], hTv[:, :, t0:t0 + W], reads=[self.dep_hT], writes=[dhb])
                for kt in range(8):
                    fw.op("act", lambda e: e.activation(out=sq[:, kt, 0:W], in_=hb[:, kt, 0:W], func=AF.Square),
                          reads=[dhb], writes=[dsq])
                ps, dp = self.next_ps()
                for kt in range(8):
                    fw.op("pe", lambda e: e.matmul(ps[:, 0:W], lhsT=self.ones_bf[:, :], rhs=sq[:, kt, 0:W],
                                                   start=(kt == 0), stop=(kt == 7)),
                          reads=[dsq, self.d_const], writes=[dp])
                fw.op("act", lambda e: e.activation(out=rstd[:, 0:W], in_=ps[:, 0:W], func=AF.Sqrt,
                                                    bias=self.eps_t[:, 0:1], scale=1.0 / D),
                      reads=[dp, self.d_const], writes=[drstd])
                fw.op("dve", lambda e: e.reciprocal(out=rstd[:, 0:W], in_=rstd[:, 0:W]), reads=[drstd], writes=[drstd])
                for kt in range(8):
                    fw.op("dve", lambda e: e.scalar_tensor_tensor(out=hb[:, kt, 0:W], in0=hb[:, kt, 0:W],
                                                                  scalar=fnw[:, kt:kt + 1], in1=rstd[:, 0:W],
                                                                  op0=ALU.mult, op1=ALU.mult),
                          reads=[dhb, drstd, dfnw], writes=[dhb])
                for tt in range(4):
                    ob, dob = o[no % 2], do[no % 2]
                    no += 1
                    for kt in range(8):
                        ps, dp = self.next_ps()
                        fw.op("pe", lambda e: e.transpose(out=ps[:, 0:128], in_=hb[:, kt, tt * 128:(tt + 1) * 128],
                                                          identity=self.ident[:, :]),
                              reads=[dhb, self.d_const], writes=[dp])
                        if kt % 2 == 0:
                            fw.op("act", lambda e: e.copy(out=ob[:, kt * 128:(kt + 1) * 128], in_=ps[:, 0:128]),
                                  reads=[dp], writes=[dob])
                        else:
                            fw.op("dve", lambda e: e.tensor_copy(out=ob[:, kt * 128:(kt + 1) * 128], in_=ps[:, 0:128]),
                                  reads=[dp], writes=[dob])
                    r0 = bi * 512 + tt * 128
                    fw.dma("pool", out[r0:r0 + 128, :], ob[:, :], reads=[dob], writes=[dout])
        fw.barrier()


WEIGHT_SHAPES = [
    ("meta_tokens", (16, 1024)), ("final_norm_w", (1024,)), ("mix_norm_w", (2, 1024)), ("w_in", (2, 1024, 5896)),
    ("s5_lambda_re", (2, 2, 16, 64)), ("s5_lambda_im", (2, 2, 16, 64)), ("s5_log_step", (2, 2, 16)),
    ("s5_b_re", (2, 16, 64, 16)), ("s5_b_im", (2, 16, 64, 16)), ("s5_c_re", (2, 16, 16, 64)),
    ("s5_c_im", (2, 16, 16, 64)), ("s5_d", (2, 256)), ("s5_glu_w", (2, 256, 512)), ("s5_glu_b", (2, 512)),
    ("ssd_conv_w", (2, 5, 1024)), ("ssd_conv_b", (2, 1024)), ("ssd_a_log", (2, 2, 8)), ("ssd_dt_bias", (2, 2, 8)),
    ("ssd_d", (2, 8)), ("ssd_norm_w", (2, 512)), ("rwkv_mu_rkv", (2, 3, 256)), ("rwkv_mu_wag", (2, 3, 256)),
    ("rwkv_w0", (2, 2, 256)), ("rwkv_w1", (2, 2, 256, 64)), ("rwkv_w2", (2, 2, 64, 256)), ("rwkv_a0", (2, 2, 256)),
    ("rwkv_a1", (2, 2, 256, 64)), ("rwkv_a2", (2, 2, 64, 256)), ("rwkv_g1", (2, 256, 128)), ("rwkv_g2", (2, 128, 256)),
    ("rwkv_k_k", (2, 256)), ("rwkv_k_a", (2, 256)), ("rwkv_r_k", (2, 4, 64)), ("rwkv_ln_w", (2, 256)),
    ("rwkv_ln_b", (2, 256)), ("proj_a", (2, 256, 1024)), ("proj_b", (2, 512, 1024)), ("proj_c", (2, 256, 1024)),
    ("w_out", (2, 1024, 1024)), ("mlp_norm_w", (2, 1024)), ("mlp_w1", (2, 1024, 4096)), ("mlp_w2", (2, 4096, 1024)),
]


def host_consts():
    return {"c_ident": np.eye(128, dtype=np.float32),
            "c_iota": np.ascontiguousarray(np.broadcast_to(np.arange(512, dtype=np.float32), (128, 512))),
            "c_triu": np.triu(np.ones((128, 128), np.float32)),
            "c_blk": np.kron(np.eye(2, dtype=np.float32), np.ones((64, 64), np.float32)),
            "c_trilT_s": np.tril(np.ones((128, 128), np.float32), -1),
            "c_padm": np.ascontiguousarray(np.broadcast_to((np.arange(128) < 16).astype(np.float32)[:, None], (128, 8))),
            "c_mneg": np.where(np.triu(np.ones((128, 128), bool)), 0.0, -30000.0).astype(np.float32),
            "c_tril_s": np.triu(np.ones((128, 128), np.float32), 1),
            "c_tril_i": np.triu(np.ones((128, 128), np.float32), 0)}


def run(inputs, cfg, ncores=8):
    b = Builder(cfg)
    nc = b.build()
    consts = host_consts()
    in_maps = []
    for c in range(ncores):
        m = {"x": np.ascontiguousarray(inputs["x"][c], dtype=np.float32)}
        for name, _ in WEIGHT_SHAPES:
            m[name] = np.ascontiguousarray(inputs[name], dtype=np.float32)
        m.update(consts)
        in_maps.append(m)
    res = run_bass_kernel_spmd(nc, in_maps, core_ids=list(range(ncores)))
    return res, b


def kernel(**inputs):
    res, _ = run(inputs, {})
    return np.stack([np.asarray(res.results[c]["out"]) for c in range(8)], axis=0).astype(np.float32)


PI = float(np.pi)
S5W = 256


def _mix_s5(self, li):
    fw = self.fw
    I = self.I
    nc = self.nc
    with ExitStack() as es:
        lr = self.sb(es, "s5_lr", [128, 16])
        lim = self.sb(es, "s5_li", [128, 16])
        dpar = Dep()
        fw.dma("sp", lr[:], I["s5_lambda_re"][li].rearrange("d (q gp) n -> (gp n) (d q)", gp=2), writes=[dpar], slow=True)
        fw.dma("sp", lim[:], I["s5_lambda_im"][li].rearrange("d (q gp) n -> (gp n) (d q)", gp=2), writes=[dpar], slow=True)
        stepb = self.sb(es, "s5_stepb", [128, 2, 8, 2])
        fw.dma("sp", stepb[:], I["s5_log_step"][li].rearrange("d (q gp) -> d q gp", gp=2).partition_broadcast(128),
               writes=[dpar], slow=True)
        step = self.sb(es, "s5_step", [128, 16])
        fw.op("act", lambda e: e.activation(out=step[0:64, :].rearrange("p (d q) -> p d q", d=2), in_=stepb[0:64, :, :, 0],
                                            func=AF.Exp), reads=[dpar], writes=[dpar])
        fw.op("act", lambda e: e.activation(out=step[64:128, :].rearrange("p (d q) -> p d q", d=2),
                                            in_=stepb[64:128, :, :, 1], func=AF.Exp), reads=[dpar], writes=[dpar])
        th = self.sb(es, "s5_th", [128, 16])
        rho = self.sb(es, "s5_rho", [128, 16])
        fw.op("dve", lambda e: e.tensor_tensor(out=th[:], in0=lim[:], in1=step[:], op=ALU.mult), reads=[dpar], writes=[dpar])
        fw.op("dve", lambda e: e.tensor_tensor(out=rho[:], in0=lr[:], in1=step[:], op=ALU.mult), reads=[dpar], writes=[dpar])
        fw.op("act", lambda e: e.activation(out=rho[:], in_=rho[:], func=AF.Exp), reads=[dpar], writes=[dpar])

        NT = S5W + 1
        tc = self.sb(es, "s5_tc", [128, 16, NT])
        ts = self.sb(es, "s5_ts", [128, 16, NT])
        dtab = Dep()
        with ExitStack() as es2:
            iot = self.sb(es2, "s5_iota", [128, NT])
            fw.dma("sp", iot[:], I["c_iota"][:, 0:NT], writes=[dtab])
            ph = self.sb(es2, "s5_ph", [128, 16, NT])
            ki = self.sb(es2, "s5_ki", [128, 16, NT], mybir.dt.int32)
            kf = self.sb(es2, "s5_kf", [128, 16, NT])
            for j in range(16):
                fw.op("dve", lambda e: e.tensor_scalar(out=ph[:, j, :], in0=iot[:], scalar1=th[:, j:j + 1], scalar2=None,
                                                       op0=ALU.mult), reads=[dpar, dtab], writes=[dtab])
            fw.op("dve", lambda e: e.tensor_scalar(out=ki[:], in0=ph[:], scalar1=1.0 / (2 * PI), scalar2=None, op0=ALU.mult),
                  reads=[dtab], writes=[dtab])
            fw.op("dve", lambda e: e.tensor_copy(out=kf[:], in_=ki[:]), reads=[dtab], writes=[dtab])
            fw.op("dve", lambda e: e.scalar_tensor_tensor(out=ph[:], in0=kf[:], scalar=-2 * PI, in1=ph[:], op0=ALU.mult,
                                                          op1=ALU.add), reads=[dtab], writes=[dtab])

            def wrap(t):
                fw.op("dve", lambda e: e.tensor_scalar(out=kf[:], in0=t[:], scalar1=PI, scalar2=-2 * PI, op0=ALU.is_gt,
                                                       op1=ALU.mult), reads=[dtab], writes=[dtab])
                fw.op("dve", lambda e: e.tensor_tensor(out=t[:], in0=t[:], in1=kf[:], op=ALU.add), reads=[dtab], writes=[dtab])
                fw.op("dve", lambda e: e.tensor_scalar(out=kf[:], in0=t[:], scalar1=-PI, scalar2=2 * PI, op0=ALU.is_lt,
                                                       op1=ALU.mult), reads=[dtab], writes=[dtab])
                fw.op("dve", lambda e: e.tensor_tensor(out=t[:], in0=t[:], in1=kf[:], op=ALU.add), reads=[dtab], writes=[dtab])

            wrap(ph)
            fw.op("act", lambda e: e.activation(out=ts[:], in_=ph[:], func=AF.Sin), reads=[dtab], writes=[dtab])
            fw.op("dve", lambda e: e.tensor_scalar(out=ph[:], in0=ph[:], scalar1=PI / 2, scalar2=None, op0=ALU.add),
                  reads=[dtab], writes=[dtab])
            wrap(ph)
            fw.op("act", lambda e: e.activation(out=tc[:], in_=ph[:], func=AF.Sin), reads=[dtab], writes=[dtab])
            fw.barrier()
        nsW = self.sb(es, "s5_nsW", [128, 16])
        fw.op("dve", lambda e: e.tensor_scalar(out=nsW[:], in0=ts[:, :, S5W], scalar1=-1.0, scalar2=None, op0=ALU.mult),
              reads=[dtab], writes=[dpar])
        nsB = self.sb(es, "s5_nsB", [128, 16])
        abr = self.sb(es, "s5_abr", [128, 16])
        abi = self.sb(es, "s5_abi", [128, 16])
        fw.op("dve", lambda e: e.tensor_tensor(out=abr[:], in0=rho[:], in1=tc[:, :, 1], op=ALU.mult), reads=[dpar, dtab], writes=[dpar])
        fw.op("dve", lambda e: e.tensor_tensor(out=abi[:], in0=rho[:], in1=ts[:, :, 1], op=ALU.mult), reads=[dpar, dtab], writes=[dpar])
        den = self.sb(es, "s5_den", [128, 16])
        t1 = self.sb(es, "s5_t1", [128, 16])
        t2 = self.sb(es, "s5_t2", [128, 16])
        cor = self.sb(es, "s5_cor", [128, 16])
        coi = self.sb(es, "s5_coi", [128, 16])
        V = lambda fn: fw.op("dve", fn, reads=[dpar], writes=[dpar])
        V(lambda e: e.tensor_tensor(out=den[:], in0=lr[:], in1=lr[:], op=ALU.mult))
        V(lambda e: e.tensor_tensor(out=t1[:], in0=lim[:], in1=lim[:], op=ALU.mult))
        V(lambda e: e.tensor_tensor(out=den[:], in0=den[:], in1=t1[:], op=ALU.add))
        V(lambda e: e.reciprocal(out=den[:], in_=den[:]))
        V(lambda e: e.tensor_scalar(out=abr[:], in0=abr[:], scalar1=-1.0, scalar2=None, op0=ALU.add))
        V(lambda e: e.tensor_tensor(out=t1[:], in0=abr[:], in1=lr[:], op=ALU.mult))
        V(lambda e: e.tensor_tensor(out=t2[:], in0=abi[:], in1=lim[:], op=ALU.mult))
        V(lambda e: e.tensor_tensor(out=t1[:], in0=t1[:], in1=t2[:], op=ALU.add))
        V(lambda e: e.tensor_tensor(out=cor[:], in0=t1[:], in1=den[:], op=ALU.mult))
        V(lambda e: e.tensor_tensor(out=t1[:], in0=abi[:], in1=lr[:], op=ALU.mult))
        V(lambda e: e.tensor_tensor(out=t2[:], in0=abr[:], in1=lim[:], op=ALU.mult))
        V(lambda e: e.tensor_tensor(out=t1[:], in0=t1[:], in1=t2[:], op=ALU.subtract))
        V(lambda e: e.tensor_tensor(out=coi[:], in0=t1[:], in1=den[:], op=ALU.mult))

        LB = self.sb(es, "s5_LB", [128, 2, 8, 2, 128], BF16)
        LC = self.sb(es, "s5_LC", [128, 8, 2, 128], BF16)
        dLB = Dep()
        fw.op("dve", lambda e: e.memset(LC[:], 0.0), writes=[dLB])
        with ExitStack() as es2:
            Xr = self.sb(es2, "s5_Xr", [128, 8, 128])
            Xi = self.sb(es2, "s5_Xi", [128, 8, 128])
            dX = Dep()
            fw.op("dve", lambda e: e.memset(Xr[:], 0.0), writes=[dX])
            fw.op("dve", lambda e: e.memset(Xi[:], 0.0), writes=[dX])
            for (X, nm) in ((Xr, "s5_b_re"), (Xi, "s5_b_im")):
                for q in range(8):
                    r = q % 4
                    fw.dma("sp", X[0:64, q, 32 * r:32 * r + 16], I[nm][li, 2 * q], writes=[dX])
                    fw.dma("sp", X[64:128, q, 32 * r + 16:32 * r + 32], I[nm][li, 2 * q + 1], writes=[dX])
            Xc = self.sb(es2, "s5_Xc", [128, 2, 8, 2, 128])
            tmpx = self.sb(es2, "s5_tmpx", [128, 8, 128])
            for d in range(2):
                cr = cor[:, d * 8:(d + 1) * 8].unsqueeze(2).to_broadcast([128, 8, 128])
                ci = coi[:, d * 8:(d + 1) * 8].unsqueeze(2).to_broadcast([128, 8, 128])
                fw.op("dve", lambda e: e.tensor_tensor(out=Xc[:, d, :, 0, :], in0=Xr[:], in1=cr, op=ALU.mult), reads=[dX, dpar], writes=[dX])
                fw.op("dve", lambda e: e.tensor_tensor(out=tmpx[:], in0=Xi[:], in1=ci, op=ALU.mult), reads=[dX, dpar], writes=[dX])
                fw.op("dve", lambda e: e.tensor_tensor(out=Xc[:, d, :, 0, :], in0=Xc[:, d, :, 0, :], in1=tmpx[:], op=ALU.subtract), reads=[dX], writes=[dX])
                fw.op("dve", lambda e: e.tensor_tensor(out=Xc[:, d, :, 1, :], in0=Xi[:], in1=cr, op=ALU.mult), reads=[dX, dpar], writes=[dX])
                fw.op("dve", lambda e: e.tensor_tensor(out=tmpx[:], in0=Xr[:], in1=ci, op=ALU.mult), reads=[dX, dpar], writes=[dX])
                fw.op("dve", lambda e: e.tensor_tensor(out=Xc[:, d, :, 1, :], in0=Xc[:, d, :, 1, :], in1=tmpx[:], op=ALU.add), reads=[dX], writes=[dX])
            for d in range(2):
                for q in range(8):
                    for ri in range(2):
                        r = q % 4
                        ps, dp = self.next_ps()
                        fw.op("pe", lambda e: e.transpose(out=ps[:, 0:128], in_=Xc[:, d, q, ri, :],
                                                          identity=self.ident[:, :]), reads=[dX, self.d_const], writes=[dp])
                        fw.op("act", lambda e: e.copy(out=LB[:, d, q, ri, :], in_=ps[:, 0:128]),
                              reads=[dp], writes=[dLB])
            Yr = self.sb(es2, "s5_Yr", [32, 8, 128])
            Yi = self.sb(es2, "s5_Yi", [32, 8, 128])
            dY = Dep()
            fw.op("dve", lambda e: e.memset(Yr[:], 0.0), writes=[dY])
            fw.op("dve", lambda e: e.memset(Yi[:], 0.0), writes=[dY])
            for (Y, nm) in ((Yr, "s5_c_re"), (Yi, "s5_c_im")):
                src = I[nm][li].rearrange("(q gp) h n -> gp h q n", gp=2)
                fw.dma("sp", Y[0:16, :, 0:64], src[0], writes=[dY])
                fw.dma("sp", Y[16:32, :, 64:128], src[1], writes=[dY])
            for q in range(8):
                for ri, Y in enumerate((Yr, Yi)):
                    ps, dp = self.next_ps()
                    fw.op("pe", lambda e: e.transpose(out=ps[:, 0:32], in_=Y[:, q, :], identity=self.ident[0:32, 0:32]),
                          reads=[dY, self.d_const], writes=[dp])
                    if ri == 0:
                        fw.op("act", lambda e: e.copy(out=LC[:, q, 0, 32 * (q % 4):32 * (q % 4) + 32], in_=ps[:, 0:32]), reads=[dp], writes=[dLB])
                    else:
                        fw.op("act", lambda e: e.mul(out=LC[:, q, 1, 32 * (q % 4):32 * (q % 4) + 32], in_=ps[:, 0:32], mul=-1.0), reads=[dp], writes=[dLB])
            fw.barrier()

        ubf = self.sb(es, "s5_ubf", [128, 2, L], BF16)
        urv = self.sb(es, "s5_urv", [128, 2, L], BF16)
        yacc = self.sb(es, "s5_yacc", [128, 2, L])
        du = Dep()
        dyacc = Dep()
        with ExitStack() as es2:
            uf = self.sb(es2, "s5_uf", [128, 2, L])
            fw.dma("sp", uf[:], self.projT[OFF_U:OFF_U + 256, :].rearrange("(kt p) t -> p kt t", p=128),
                   reads=[self.dep_proj], writes=[du])
            for kt in range(2):
                fw.op("dve", lambda e: e.tensor_copy(out=ubf[:, kt, :], in_=uf[:, kt, :]), reads=[du], writes=[du])
                fw.op("pool", lambda e: e.tensor_copy(out=urv[:, kt, ::-1], in_=uf[:, kt, :]), reads=[du], writes=[du])
            fw.barrier()

        blocks = [(i * S5W, S5W) for i in range(L // S5W)]
        if L % S5W:
            blocks.append((L - L % S5W, L % S5W))
        NB = 3
        tmp = [[self.sb(es, "s5_w%d_%d" % (i, k), [128, S5W]) for k in range(6)] for i in range(NB)]
        dtmp = [[Dep() for k in range(6)] for i in range(NB)]
        hb = [[self.sb(es, "s5_h%d_%d" % (i, k), [128, S5W], BF16) for k in range(2)] for i in range(NB)]
        dhb = [[Dep() for k in range(2)] for i in range(NB)]
        init = [[self.sb(es, "s5_in%d_%d" % (i, k), [128, 1]) for k in range(3)] for i in range(2)]
        dinit = [Dep(), Dep()]
        it = 0
        for d in range(2):
            usrc = ubf if d == 0 else urv
            for q in range(8):
                j = d * 8 + q
                r = q % 4
                kt = q // 4
                rho_b = rho[:, j:j + 1]
                prev = None
                for bi, (t0, W) in enumerate(blocks):
                    T, dT = tmp[it % NB], dtmp[it % NB]
                    H, dH = hb[it % NB], dhb[it % NB]
                    it += 1
                    pre, dpre = self.next_ps()
                    pim, dpim = self.next_ps()
                    fw.op("pe", lambda e: e.matmul(pre[:, 0:W], lhsT=LB[:, d, q, 0, :],
                                                   rhs=usrc[:, kt, t0:t0 + W], start=True, stop=True),
                          reads=[dLB, du], writes=[dpre])
                    fw.op("pe", lambda e: e.matmul(pim[:, 0:W], lhsT=LB[:, d, q, 1, :],
                                                   rhs=usrc[:, kt, t0:t0 + W], start=True, stop=True),
                          reads=[dLB, du], writes=[dpim])
                    c_, s_ = tc[:, j, 0:W], ts[:, j, 0:W]
                    fw.op("dve", lambda e: e.tensor_tensor(out=T[0][:, 0:W], in0=pre[:, 0:W], in1=c_, op=ALU.mult), reads=[dpre, dtab], writes=[dT[0]])
                    fw.op("dve", lambda e: e.tensor_tensor(out=T[1][:, 0:W], in0=pim[:, 0:W], in1=s_, op=ALU.mult), reads=[dpim, dtab], writes=[dT[1]])
                    fw.op("dve", lambda e: e.tensor_tensor(out=T[2][:, 0:W], in0=pim[:, 0:W], in1=c_, op=ALU.mult), reads=[dpim, dtab], writes=[dT[2]])
                    fw.op("dve", lambda e: e.tensor_tensor(out=T[3][:, 0:W], in0=pre[:, 0:W], in1=s_, op=ALU.mult), reads=[dpre, dtab], writes=[dT[3]])
                    fw.op("pool", lambda e: e.tensor_tensor(out=T[0][:, 0:W], in0=T[0][:, 0:W], in1=T[1][:, 0:W], op=ALU.add), reads=[dT[0], dT[1]], writes=[dT[0]])
                    fw.op("pool", lambda e: e.tensor_tensor(out=T[2][:, 0:W], in0=T[2][:, 0:W], in1=T[3][:, 0:W], op=ALU.subtract), reads=[dT[2], dT[3]], writes=[dT[2]])
                    ini, dini = init[bi % 2], dinit[bi % 2]
                    if bi == 0:
                        i_re, i_im = 0.0, 0.0
                        rd = []
                    else:
                        pT, pdT, pW, pini, pdini = prev
                        cW, sW = tc[:, j, pW:pW + 1], ts[:, j, pW:pW + 1]
                        fw.op("dve", lambda e: e.tensor_scalar(out=ini[2][:], in0=pT[4][:, pW - 1:pW], scalar1=cW, scalar2=None, op0=ALU.mult), reads=[pdT[4], dtab], writes=[dini])
                        fw.op("dve", lambda e: e.scalar_tensor_tensor(out=ini[2][:], in0=pT[5][:, pW - 1:pW], scalar=sW, in1=ini[2][:], op0=ALU.mult, op1=ALU.subtract), reads=[pdT[5], dini, dtab], writes=[dini])
                        fw.op("dve", lambda e: e.tensor_scalar(out=ini[0][:], in0=ini[2][:], scalar1=-1.0, scalar2=None, op0=ALU.mult), reads=[dini], writes=[dini])
                        fw.op("dve", lambda e: e.tensor_scalar(out=ini[2][:], in0=pT[4][:, pW - 1:pW], scalar1=sW, scalar2=None, op0=ALU.mult), reads=[pdT[4], dtab], writes=[dini])
                        fw.op("dve", lambda e: e.scalar_tensor_tensor(out=ini[1][:], in0=pT[5][:, pW - 1:pW], scalar=cW, in1=ini[2][:], op0=ALU.mult, op1=ALU.add), reads=[pdT[5], dini, dtab], writes=[dini])
                        i_re, i_im = ini[0][:, 0:1], ini[1][:, 0:1]
                        rd = [dini]
                    fw.op("dve", lambda e: e.tensor_tensor_scan(out=T[4][:, 0:W], data0=rho_b.to_broadcast([128, W]), data1=T[0][:, 0:W], initial=i_re, op0=ALU.mult, op1=ALU.add),
                          reads=[dT[0], dpar] + rd, writes=[dT[4]])
                    fw.op("dve", lambda e: e.tensor_tensor_scan(out=T[5][:, 0:W], data0=rho_b.to_broadcast([128, W]), data1=T[2][:, 0:W], initial=i_im, op0=ALU.mult, op1=ALU.add),
                          reads=[dT[2], dpar] + rd, writes=[dT[5]])
                    prev = (T, dT, W, ini, dini)
                    fw.op("pool", lambda e: e.tensor_tensor(out=T[0][:, 0:W], in0=T[4][:, 0:W], in1=c_, op=ALU.mult), reads=[dT[4], dtab], writes=[dT[0]])
                    fw.op("pool", lambda e: e.tensor_tensor(out=T[1][:, 0:W], in0=T[5][:, 0:W], in1=s_, op=ALU.mult), reads=[dT[5], dtab], writes=[dT[1]])
                    fw.op("pool", lambda e: e.tensor_tensor(out=H[0][:, 0:W], in0=T[0][:, 0:W], in1=T[1][:, 0:W], op=ALU.subtract), reads=[dT[0], dT[1]], writes=[dH[0]])
                    fw.op("dve", lambda e: e.tensor_tensor(out=T[2][:, 0:W], in0=T[4][:, 0:W], in1=s_, op=ALU.mult), reads=[dT[4], dtab], writes=[dT[2]])
                    fw.op("dve", lambda e: e.tensor_tensor(out=T[3][:, 0:W], in0=T[5][:, 0:W], in1=c_, op=ALU.mult), reads=[dT[5], dtab], writes=[dT[3]])
                    fw.op("pool", lambda e: e.tensor_tensor(out=H[1][:, 0:W], in0=T[2][:, 0:W], in1=T[3][:, 0:W], op=ALU.add), reads=[dT[2], dT[3]], writes=[dH[1]])
                    py, dpy = self.next_ps()
                    fw.op("pe", lambda e: e.matmul(py[:, 0:W], lhsT=LC[:, q, 0, :], rhs=H[0][:, 0:W], start=True, stop=False), reads=[dLB, dH[0]], writes=[dpy])
                    fw.op("pe", lambda e: e.matmul(py[:, 0:W], lhsT=LC[:, q, 1, :], rhs=H[1][:, 0:W], start=False, stop=True), reads=[dLB, dH[1]], writes=[dpy])
                    if d == 0 and r == 0:
                        fw.op("act", lambda e: e.copy(out=yacc[:, kt, t0:t0 + W], in_=py[:, 0:W]), reads=[dpy], writes=[dyacc])
                    elif d == 0:
                        ya = yacc[:, kt, t0:t0 + W]
                        fw.op("dve", lambda e: e.tensor_tensor(out=ya, in0=py[:, 0:W], in1=ya, op=ALU.add), reads=[dpy, dyacc], writes=[dyacc])
                    else:
                        lo = L - (t0 + W)
                        ya = yacc[:, kt, lo:lo + W]
                        fw.op("dve", lambda e: e.tensor_tensor(out=ya[:, ::-1], in0=py[:, 0:W], in1=ya[:, ::-1], op=ALU.add), reads=[dpy, dyacc], writes=[dyacc])
        fw.barrier()
        self._s5_post(li, es, yacc, dyacc)


def _s5_post(self, li, es_outer, yacc, dyacc):
    fw = self.fw
    I = self.I
    with ExitStack() as es:
        gw, dgw = self.load_weight_bf(es, "s5_gw", I["s5_glu_w"][li], 256, 512)
        dsk = self.sb(es, "s5_dsk", [128, 2])
        gb = self.sb(es, "s5_gb", [128, 4])
        dpp = Dep()
        fw.dma("sp", dsk[:], I["s5_d"][li].rearrange("(kt p) -> p kt", p=128), writes=[dpp], slow=True)
        fw.dma("sp", gb[:], I["s5_glu_b"][li].rearrange("(kt p) -> p kt", p=128), writes=[dpp], slow=True)
        W = 512
        uf = [self.sb(es, "s5p_u%d" % i, [128, 2, W]) for i in range(2)]
        duf = [Dep(), Dep()]
        t1 = self.sb(es, "s5p_t1", [128, 2, W])
        t2 = self.sb(es, "s5p_t2", [128, 2, W])
        dt1 = Dep()
        gl = [self.sb(es, "s5p_gl%d" % i, [128, 2, W], BF16) for i in range(2)]
        dgl = [Dep(), Dep()]
        sg = [self.sb(es, "s5p_sg%d" % i, [128, W]) for i in range(2)]
        dsg = [Dep(), Dep()]
        o = [self.sb(es, "s5p_o%d" % i, [128, 2, W]) for i in range(2)]
        do = [Dep(), Dep()]
        uv = self.projT[OFF_U:OFF_U + 256, :].rearrange("(kt p) t -> p kt t", p=128)
        yv = self.yaT.rearrange("(kt p) t -> p kt t", p=128)
        for bi, (t0, Wb) in enumerate(BLOCKS):
            u, du = uf[bi % 2], duf[bi % 2]
            g, dg = gl[bi % 2], dgl[bi % 2]
            ob, dob = o[bi % 2], do[bi % 2]
            fw.dma("sp", u[:, :, 0:Wb], uv[:, :, t0:t0 + Wb], reads=[self.dep_proj], writes=[du])
            for kt in range(2):
                fw.op("dve", lambda e: e.scalar_tensor_tensor(out=t1[:, kt, 0:Wb], in0=u[:, kt, 0:Wb], scalar=dsk[:, kt:kt + 1], in1=yacc[:, kt, t0:t0 + Wb], op0=ALU.mult, op1=ALU.add),
                      reads=[du, dpp, dyacc], writes=[dt1])
                fw.op("pool", lambda e: e.tensor_tensor(out=t2[:, kt, 0:Wb], in0=t1[:, kt, 0:Wb], in1=t1[:, kt, 0:Wb], op=ALU.mult), reads=[dt1], writes=[dt1])
                fw.op("dve", lambda e: e.tensor_scalar(out=t2[:, kt, 0:Wb], in0=t2[:, kt, 0:Wb], scalar1=0.044715, scalar2=1.0, op0=ALU.mult, op1=ALU.add), reads=[dt1], writes=[dt1])
                fw.op("pool", lambda e: e.tensor_tensor(out=t2[:, kt, 0:Wb], in0=t2[:, kt, 0:Wb], in1=t1[:, kt, 0:Wb], op=ALU.mult), reads=[dt1], writes=[dt1])
                fw.op("act", lambda e: e.activation(out=t2[:, kt, 0:Wb], in_=t2[:, kt, 0:Wb], func=AF.Sigmoid, scale=1.5957691216), reads=[dt1], writes=[dt1])
                fw.op("dve", lambda e: e.tensor_tensor(out=g[:, kt, 0:Wb], in0=t2[:, kt, 0:Wb], in1=t1[:, kt, 0:Wb], op=ALU.mult), reads=[dt1], writes=[dg])
            for c in range(2):
                plo, dplo = self.next_ps()
                phi, dphi = self.next_ps()
                for k in range(2):
                    fw.op("pe", lambda e: e.matmul(plo[:, 0:Wb], lhsT=gw[:, k, c * 128:(c + 1) * 128], rhs=g[:, k, 0:Wb], start=(k == 0), stop=(k == 1)), reads=[dgw, dg], writes=[dplo])
                for k in range(2):
                    fw.op("pe", lambda e: e.matmul(phi[:, 0:Wb], lhsT=gw[:, k, 256 + c * 128:256 + (c + 1) * 128], rhs=g[:, k, 0:Wb], start=(k == 0), stop=(k == 1)), reads=[dgw, dg], writes=[dphi])
                s, ds = sg[c], dsg[c]
                fw.op("act", lambda e: e.activation(out=s[:, 0:Wb], in_=phi[:, 0:Wb], func=AF.Sigmoid, bias=gb[:, 2 + c:3 + c]), reads=[dphi, dpp], writes=[ds])
                fw.op("dve", lambda e: e.scalar_tensor_tensor(out=ob[:, c, 0:Wb], in0=plo[:, 0:Wb], scalar=gb[:, c:c + 1], in1=s[:, 0:Wb], op0=ALU.add, op1=ALU.mult), reads=[dplo, ds, dpp], writes=[dob])
            fw.dma("pool", yv[:, :, t0:t0 + Wb], ob[:, :, 0:Wb], reads=[dob], writes=[self.dep_ya])
    fw.barrier()


Builder.mix_s5 = _mix_s5
Builder._s5_post = _s5_post


LP = 33 * 128
NCH = 33


def _mix_ssd(self, li):
    fw = self.fw
    I = self.I
    xcT = self.scratch_once("xcT", (1024, L))
    d_xc = self.dep_once("xcT")
    xbv = self.projT[OFF_XBC:OFF_XBC + 1024, :].rearrange("(j p) t -> p j t", p=128)
    xcv = xcT.rearrange("(j p) t -> p j t", p=128)
    with ExitStack() as es:
        cw = self.sb(es, "sd_cw", [128, 5, 8])
        cb = self.sb(es, "sd_cb", [128, 8])
        dcw = Dep()
        for k in range(5):
            fw.dma("sp", cw[:, k, :], I["ssd_conv_w"][li, k].rearrange("(j p) -> p j", p=128), writes=[dcw], slow=True)
        fw.dma("sp", cb[:], I["ssd_conv_b"][li].rearrange("(j p) -> p j", p=128), writes=[dcw], slow=True)
        xp = [self.sb(es, "sd_xp%d" % i, [128, L + 4]) for i in range(2)]
        dxp = [Dep(), Dep()]
        acc = [self.sb(es, "sd_acc%d" % i, [128, L]) for i in range(2)]
        dacc = [Dep(), Dep()]
        for i in range(2):
            fw.op("dve", lambda e: e.memset(xp[i][:, 0:2], 0.0), writes=[dxp[i]])
            fw.op("dve", lambda e: e.memset(xp[i][:, L + 2:L + 4], 0.0), writes=[dxp[i]])
        for j in range(8):
            x_, dx_ = xp[j % 2], dxp[j % 2]
            a_, da_ = acc[j % 2], dacc[j % 2]
            fw.dma("sp", x_[:, 2:L + 2], xbv[:, j, :], reads=[self.dep_proj], writes=[dx_])
            eng = "dve"
            fw.op(eng, lambda e: e.tensor_scalar(out=a_[:], in0=x_[:, 0:L], scalar1=cw[:, 0, j:j + 1], scalar2=cb[:, j:j + 1], op0=ALU.mult, op1=ALU.add),
                  reads=[dx_, dcw], writes=[da_])
            for k in range(1, 5):
                fw.op(eng, lambda e: e.scalar_tensor_tensor(out=a_[:], in0=x_[:, k:k + L], scalar=cw[:, k, j:j + 1], in1=a_[:], op0=ALU.mult, op1=ALU.add),
                      reads=[dx_, dcw, da_], writes=[da_])
            fw.op("act", lambda e: e.activation(out=a_[:], in_=a_[:], func=AF.Silu), reads=[da_], writes=[da_])
            fw.dma("pool", xcv[:, j, :], a_[:], reads=[da_], writes=[d_xc])
    fw.barrier()

    with ExitStack() as es:
        triu = self.sb(es, "sd_triu", [128, 128])
        mneg = self.sb(es, "sd_mneg", [128, 128])
        onesf = self.sb(es, "sd_onesf", [128, 128])
        negones = self.sb(es, "sd_negones", [128, 128])
        identb = self.sb(es, "sd_identb", [128, 128], BF16)
        dc = Dep()
        fw.dma("sp", triu[:], I["c_triu"][:, :], writes=[dc])
        fw.dma("sp", mneg[:], I["c_mneg"][:, :], writes=[dc])
        padm = self.sb(es, "sd_padm", [128, 8])
        fw.dma("sp", padm[:], I["c_padm"][:, :], writes=[dc])
        fw.op("dve", lambda e: e.memset(onesf[:], 1.0), writes=[dc])
        fw.op("dve", lambda e: e.memset(negones[:], -1.0), writes=[dc])
        fw.op("dve", lambda e: e.tensor_copy(out=identb[:], in_=self.ident[:]), reads=[self.d_const], writes=[dc])
        dtb = self.sb(es, "sd_dtb", [8, 2])
        nea = self.sb(es, "sd_nea", [8, 2])
        dpp = Dep()
        fw.dma("sp", dtb[:], I["ssd_dt_bias"][li].rearrange("d h -> h d"), writes=[dpp], slow=True)
        fw.dma("sp", nea[:], I["ssd_a_log"][li].rearrange("d h -> h d"), writes=[dpp], slow=True)
        fw.op("act", lambda e: e.activation(out=nea[:], in_=nea[:], func=AF.Exp), reads=[dpp], writes=[dpp])
        fw.op("dve", lambda e: e.tensor_scalar(out=nea[:], in0=nea[:], scalar1=-1.0, scalar2=None, op0=ALU.mult), reads=[dpp], writes=[dpp])
        dtok = self.sb(es, "sd_dtok", [128, NCH, 16])
        ddtok = Dep()
        xs = self.sb(es, "sd_xs", [128, 4, LP], BF16)
        Bm = self.sb(es, "sd_B", [128, 2, LP], BF16)
        Cm = self.sb(es, "sd_C", [128, 2, LP], BF16)
        dws = Dep()
        yacc = self.sb(es, "sd_yacc", [128, 4, L])
        dyacc = Dep()
        ST = self.sb(es, "sd_ST", [128, 8, 64])
        STb = self.sb(es, "sd_STb", [128, 8, 64], BF16)
        dST = [Dep() for _ in range(8)]
        dSTb = [Dep() for _ in range(8)]
        dbias = self.sb(es, "sd_dbias", [128, 2, 8])
        nea_bc = self.sb(es, "sd_neabc", [128, 2, 8])
        fw.dma("sp", dbias[:], I["ssd_dt_bias"][li].partition_broadcast(128), writes=[dpp], slow=True)
        fw.dma("sp", nea_bc[:], I["ssd_a_log"][li].partition_broadcast(128), writes=[dpp], slow=True)
        fw.op("act", lambda e: e.activation(out=nea_bc[:], in_=nea_bc[:], func=AF.Exp), reads=[dpp], writes=[dpp])
        fw.op("dve", lambda e: e.tensor_scalar(out=nea_bc[:], in0=nea_bc[:], scalar1=-1.0, scalar2=None, op0=ALU.mult), reads=[dpp], writes=[dpp])

        for d in range(2):
          with ExitStack() as es3:
            stg = [self.sb(es3, "sd_stg%d" % i, [128, L]) for i in range(2)]
            dstg = [Dep(), Dep()]
            raw = self.sb(es3, "sd_raw", [8, L])
            rawd = self.sb(es3, "sd_rawd", [8, LP])
            draw = Dep()
            for j in range(8):
                s_, ds_ = stg[j % 2], dstg[j % 2]
                fw.dma("sp", s_[:], xcv[:, j, :], reads=[d_xc], writes=[ds_])
                dst = xs[:, j, :] if j < 4 else (Bm[:, j - 4, :] if j < 6 else Cm[:, j - 6, :])
                eng = "dve" if j % 2 == 0 else "pool"
                fw.op(eng, lambda e: e.memset(dst[:, L:LP], 0.0), writes=[dws])
                if d == 0:
                    fw.op(eng, lambda e: e.tensor_copy(out=dst[:, 0:L], in_=s_[:]), reads=[ds_], writes=[dws])
                else:
                    fw.op(eng, lambda e: e.tensor_copy(out=dst[:, 0:L][:, ::-1], in_=s_[:]), reads=[ds_], writes=[dws])
            fw.dma("sp", raw[:], self.projT[OFF_DT:OFF_DT + 8, :], reads=[self.dep_proj], writes=[draw])
            fw.op("dve", lambda e: e.memset(rawd[:, L:LP], 0.0), writes=[draw])
            if d == 0:
                fw.op("dve", lambda e: e.tensor_copy(out=rawd[:, 0:L], in_=raw[:]), reads=[draw], writes=[draw])
            else:
                fw.op("dve", lambda e: e.tensor_copy(out=rawd[:, 0:L][:, ::-1], in_=raw[:]), reads=[draw], writes=[draw])
            for c in range(NCH):
                ps, dp = self.next_ps()
                fw.op("pe", lambda e: e.transpose(out=ps[:, 0:8], in_=rawd[:, c * 128:(c + 1) * 128], identity=self.ident[0:8, 0:8]), reads=[draw, self.d_const], writes=[dp])
                fw.op("dve", lambda e: e.tensor_tensor(out=dtok[:, c, 0:8], in0=ps[:, 0:8], in1=dbias[:, d, :], op=ALU.add), reads=[dp, dpp], writes=[ddtok])
            fw.op("act", lambda e: e.activation(out=dtok[:, :, 0:8], in_=dtok[:, :, 0:8], func=AF.Exp), reads=[ddtok], writes=[ddtok])
            fw.op("act", lambda e: e.activation(out=dtok[:, :, 0:8], in_=dtok[:, :, 0:8], func=AF.Ln, bias=self.one_t[:, 0:1]), reads=[ddtok, self.d_const], writes=[ddtok])
            fw.op("dve", lambda e: e.tensor_tensor(out=dtok[:, NCH - 1, 0:8], in0=dtok[:, NCH - 1, 0:8], in1=padm[:, :], op=ALU.mult), reads=[ddtok, dc], writes=[ddtok])
            fw.op("dve", lambda e: e.tensor_tensor(out=dtok[:, :, 8:16], in0=dtok[:, :, 0:8], in1=nea_bc[:, d, :].unsqueeze(1).to_broadcast([128, NCH, 8]), op=ALU.mult), reads=[ddtok, dpp], writes=[ddtok])
            fw.barrier()
          with ExitStack() as es4:
            NBF = 2
            xtk = [self.sb(es4, "sd_xtk%d" % i, [128, 512]) for i in range(NBF)]
            dxtk = [Dep() for _ in range(NBF)]
            btok = [self.sb(es4, "sd_btok%d" % i, [128, 256], BF16) for i in range(NBF)]
            dbtok = [Dep() for _ in range(NBF)]
            cbt = [self.sb(es4, "sd_cbt%d" % i, [128, 2, 128]) for i in range(NBF)]
            dcbt = [Dep() for _ in range(NBF)]
            sm = [self.sb(es4, "sd_sm%d" % i, [128, 4, 8]) for i in range(NBF)]
            dsm = [Dep() for _ in range(NBF)]
            NH = 16
            atri = [self.sb(es4, "sd_atri%d" % i, [128, 128]) for i in range(NH)]
            datri = [Dep() for _ in range(NH)]
            DT = [self.sb(es4, "sd_DT%d" % i, [128, 128]) for i in range(NH)]
            dDT = [Dep() for _ in range(NH)]
            EE = [self.sb(es4, "sd_EE%d" % i, [128, 128]) for i in range(NH)]
            dEE = [Dep() for _ in range(NH)]
            MT = [self.sb(es4, "sd_MT%d" % i, [128, 128], BF16) for i in range(NH)]
            dMT = [Dep() for _ in range(NH)]
            CE = [self.sb(es4, "sd_CE%d" % i, [128, 128], BF16) for i in range(NH)]
            dCE = [Dep() for _ in range(NH)]
            xdt = [self.sb(es4, "sd_xdt%d" % i, [128, 2, 64], BF16) for i in range(NH)]
            dxdt = [Dep() for _ in range(NH)]
            for j in range(8):
                fw.op("dve", lambda e: e.memset(ST[:, j, :], 0.0), writes=[dST[j]])
                fw.op("pool", lambda e: e.memset(STb[:, j, :], 0.0), writes=[dSTb[j]])
            ih = 0
            for c in range(NCH):
                t0 = c * 128
                Wv = min(128, L - t0)
                k_ = c % NBF
                px, dpx = self.next_ps()
                for j in range(4):
                    fw.op("pe", lambda e: e.matmul(px[:, j * 128:(j + 1) * 128], lhsT=xs[:, j, t0:t0 + 128], rhs=identb[:, :], start=True, stop=True), reads=[dws, dc], writes=[dpx])
                xtok, dxtok = xtk[k_], dxtk[k_]
                fw.op("act", lambda e: e.copy(out=xtok[:, :], in_=px[:, 0:512]), reads=[dpx], writes=[dxtok])
                pb, dpb = self.next_ps()
                for g in range(2):
                    fw.op("pe", lambda e: e.matmul(pb[:, g * 128:(g + 1) * 128], lhsT=Bm[:, g, t0:t0 + 128], rhs=identb[:, :], start=True, stop=True), reads=[dws, dc], writes=[dpb])
                fw.op("act", lambda e: e.copy(out=btok[k_][:, :], in_=pb[:, 0:256]), reads=[dpb], writes=[dbtok[k_]])
                pc, dpc = self.next_ps()
                fw.op("pe", lambda e: e.matmul(pc[:, 0:8], lhsT=triu[:, :], rhs=dtok[:, c, 8:16], start=True, stop=True), reads=[dc, ddtok], writes=[dpc])
                fw.op("pe", lambda e: e.matmul(pc[:, 8:16], lhsT=onesf[:, :], rhs=dtok[:, c, 8:16], start=True, stop=True), reads=[dc, ddtok], writes=[dpc])
                S_, dS_ = sm[k_], dsm[k_]
                fw.op("act", lambda e: e.copy(out=S_[:, 0, :], in_=pc[:, 0:8]), reads=[dpc], writes=[dS_])
                fw.op("dve", lambda e: e.tensor_tensor(out=S_[:, 1, :], in0=pc[:, 8:16], in1=S_[:, 0, :], op=ALU.subtract), reads=[dpc, dS_], writes=[dS_])
                fw.op("act", lambda e: e.activation(out=S_[:, 1, :], in_=S_[:, 1, :], func=AF.Exp), reads=[dS_], writes=[dS_])
                fw.op("dve", lambda e: e.tensor_tensor(out=S_[:, 2, :], in0=S_[:, 1, :], in1=dtok[:, c, 0:8], op=ALU.mult), reads=[dS_, ddtok], writes=[dS_])
                fw.op("act", lambda e: e.activation(out=S_[:, 3, :], in_=pc[:, 8:16], func=AF.Exp), reads=[dpc], writes=[dS_])
                for g in range(2):
                    pcb, dpcb = self.next_ps()
                    fw.op("pe", lambda e: e.matmul(pcb[:, 0:128], lhsT=Bm[:, g, t0:t0 + 128], rhs=Cm[:, g, t0:t0 + 128], start=True, stop=True), reads=[dws], writes=[dpcb])
                    fw.op("act", lambda e: e.copy(out=cbt[k_][:, g, :], in_=pcb[:, 0:128]), reads=[dpcb], writes=[dcbt[k_]])
                hb0 = (c % 2) * 8
                pDs = {}
                for jp in range(4):
                    pD, dpD = self.next_ps()
                    for jj in range(2):
                        j = jp * 2 + jj
                        h_ = hb0 + j
                        o = jj * 256
                        pDs[j] = (pD, dpD, o)
                        fw.op("dve" if jj == 0 else "pool", lambda e: e.tensor_scalar(out=atri[h_][:], in0=triu[:], scalar1=dtok[:, c, 8 + j:9 + j], scalar2=None, op0=ALU.mult), reads=[dc, ddtok], writes=[datri[h_]])
                        fw.op("pe", lambda e: e.matmul(pD[:, o:o + 128], lhsT=onesf[:, :], rhs=atri[h_][:, :], start=True, stop=False), reads=[dc, datri[h_]], writes=[dpD])
                        fw.op("pe", lambda e: e.matmul(pD[:, o:o + 128], lhsT=atri[h_][:, :], rhs=negones[:, :], start=False, stop=False), reads=[dc, datri[h_]], writes=[dpD])
                        fw.op("pe", lambda e: e.matmul(pD[:, o:o + 128], lhsT=self.ident[:, :], rhs=mneg[:, :], start=False, stop=True), reads=[dc, self.d_const], writes=[dpD])
                        fw.op("pe", lambda e: e.matmul(pD[:, o + 128:o + 256], lhsT=onesf[:, :], rhs=atri[h_][:, :], start=True, stop=True), reads=[dc, datri[h_]], writes=[dpD])
                for j in range(8):
                    g = j // 4
                    h_ = hb0 + j
                    pD, dpD, o = pDs[j]
                    fw.op("act", lambda e: e.activation(out=DT[h_][:], in_=pD[:, o:o + 128], func=AF.Exp), reads=[dpD], writes=[dDT[h_]])
                    fw.op("act", lambda e: e.activation(out=EE[h_][:], in_=pD[:, o + 128:o + 256], func=AF.Exp), reads=[dpD], writes=[dEE[h_]])
                    fw.op("dve", lambda e: e.tensor_tensor(out=MT[h_][:], in0=cbt[k_][:, g, :], in1=DT[h_][:], op=ALU.mult), reads=[dcbt[k_], dDT[h_]], writes=[dMT[h_]])
                    fw.op("pool", lambda e: e.tensor_tensor(out=CE[h_][:], in0=Cm[:, g, t0:t0 + 128], in1=EE[h_][:], op=ALU.mult), reads=[dws, dEE[h_]], writes=[dCE[h_]])
                    fw.op("dve", lambda e: e.tensor_scalar(out=xdt[h_][:, 0, :], in0=xtok[:, j * 64:(j + 1) * 64], scalar1=dtok[:, c, j:j + 1], scalar2=None, op0=ALU.mult), reads=[dxtok, ddtok], writes=[dxdt[h_]])
                    fw.op("pool", lambda e: e.tensor_scalar(out=xdt[h_][:, 1, :], in0=xtok[:, j * 64:(j + 1) * 64], scalar1=S_[:, 2, j:j + 1], scalar2=None, op0=ALU.mult), reads=[dxtok, dS_], writes=[dxdt[h_]])
                for jp in range(4):
                    py, dpy = self.next_ps()
                    for jj in range(2):
                        j = jp * 2 + jj
                        g = j // 4
                        h_ = hb0 + j
                        fw.op("pe", lambda e: e.matmul(py[jj * 64:(jj + 1) * 64, 0:128], lhsT=xdt[h_][:, 0, :], rhs=MT[h_][:, :], start=True, stop=False), reads=[dxdt[h_], dMT[h_]], writes=[dpy])
                        fw.op("pe", lambda e: e.matmul(py[jj * 64:(jj + 1) * 64, 0:128], lhsT=STb[:, j, :], rhs=CE[h_][:, :], start=False, stop=True), reads=[dSTb[j], dCE[h_]], writes=[dpy])
                    for jj in range(2):
                        j = jp * 2 + jj
                        g = j // 4
                        h_ = hb0 + j
                        fw.op("pe", lambda e: e.matmul(py[:, 128 + jj * 64:192 + jj * 64], lhsT=btok[k_][:, g * 128:(g + 1) * 128], rhs=xdt[h_][:, 1, :], start=True, stop=True), reads=[dbtok[k_], dxdt[h_]], writes=[dpy])
                    if d == 0:
                        fw.op("act", lambda e: e.copy(out=yacc[:, jp, t0:t0 + Wv], in_=py[:, 0:Wv]), reads=[dpy], writes=[dyacc])
                    else:
                        lo = L - (t0 + Wv)
                        ya = yacc[:, jp, lo:lo + Wv]
                        fw.op("dve", lambda e: e.tensor_tensor(out=ya[:, ::-1], in0=py[:, 0:Wv], in1=ya[:, ::-1], op=ALU.add), reads=[dpy, dyacc], writes=[dyacc])
                    for jj in range(2):
                        j = jp * 2 + jj
                        fw.op("dve", lambda e: e.scalar_tensor_tensor(out=ST[:, j, :], in0=ST[:, j, :], scalar=S_[:, 3, j:j + 1], in1=py[:, 128 + jj * 64:192 + jj * 64], op0=ALU.mult, op1=ALU.add), reads=[dST[j], dS_, dpy], writes=[dST[j]])
                        fw.op("act", lambda e: e.copy(out=STb[:, j, :], in_=ST[:, j, :]), reads=[dST[j]], writes=[dSTb[j]])
            fw.barrier()
        fw.barrier()
        if "dbg_yacc" in self.cfg.get("dump", ()):
            dbg = self.scratch("dbg_yacc", (512, L))
            fw.dma("sp", dbg.rearrange("(j p) t -> p j t", p=128), yacc[:, :, :], reads=[dyacc], writes=[Dep()])
            dbg2 = self.scratch("dbg_dtok", (128, NCH * 16))
            fw.dma("sp", dbg2[:, :], dtok[:, :, :].rearrange("p c k -> p (c k)"), reads=[ddtok], writes=[Dep()])
        with ExitStack() as es2:
            self.ensure_eps(es2)
            dsk = self.sb(es2, "sd_dsk", [128, 4])
            nw = self.sb(es2, "sd_nw", [128, 4])
            dq = Dep()
            for j in range(8):
                fw.dma("sp", dsk[64 * (j % 2):64 * (j % 2) + 64, j // 2:j // 2 + 1], I["ssd_d"][li, j:j + 1].partition_broadcast(64), writes=[dq], slow=True)
            fw.dma("sp", nw[:], I["ssd_norm_w"][li].rearrange("(j p) -> p j", p=128), writes=[dq], slow=True)
            W = 512
            xb = [self.sb(es2, "sd4_x%d" % i, [128, 4, W]) for i in range(2)]
            zb = [self.sb(es2, "sd4_z%d" % i, [128, 4, W]) for i in range(2)]
            dxb = [Dep(), Dep()]
            yb = self.sb(es2, "sd4_y", [128, 4, W])
            sq = self.sb(es2, "sd4_sq", [128, 4, W], BF16)
            dyb = Dep()
            rstd = self.sb(es2, "sd4_r", [128, W])
            ob = [self.sb(es2, "sd4_o%d" % i, [128, 4, W]) for i in range(2)]
            dob = [Dep(), Dep()]
            zv = self.projT[OFF_Z:OFF_Z + 512, :].rearrange("(j p) t -> p j t", p=128)
            yv = self.ybT.rearrange("(j p) t -> p j t", p=128)
            for bi, (t0, Wb) in enumerate(BLOCKS):
                x_, z_, dxz = xb[bi % 2], zb[bi % 2], dxb[bi % 2]
                o_, do_ = ob[bi % 2], dob[bi % 2]
                fw.dma("sp", x_[:, :, 0:Wb], xcv[:, 0:4, t0:t0 + Wb], reads=[d_xc], writes=[dxz])
                fw.dma("sp", z_[:, :, 0:Wb], zv[:, :, t0:t0 + Wb], reads=[self.dep_proj], writes=[dxz])
                fw.op("act", lambda e: e.activation(out=z_[:, :, 0:Wb], in_=z_[:, :, 0:Wb], func=AF.Silu), reads=[dxz], writes=[dxz])
                for j in range(4):
                    fw.op("dve", lambda e: e.scalar_tensor_tensor(out=yb[:, j, 0:Wb], in0=x_[:, j, 0:Wb], scalar=dsk[:, j:j + 1], in1=yacc[:, j, t0:t0 + Wb], op0=ALU.mult, op1=ALU.add), reads=[dxz, dq, dyacc], writes=[dyb])
                    fw.op("pool", lambda e: e.tensor_tensor(out=yb[:, j, 0:Wb], in0=yb[:, j, 0:Wb], in1=z_[:, j, 0:Wb], op=ALU.mult), reads=[dyb, dxz], writes=[dyb])
                    fw.op("act", lambda e: e.activation(out=sq[:, j, 0:Wb], in_=yb[:, j, 0:Wb], func=AF.Square), reads=[dyb], writes=[dyb])
                ps, dp = self.next_ps()
                for j in range(4):
                    fw.op("pe", lambda e: e.matmul(ps[:, 0:Wb], lhsT=self.ones_bf[:, :], rhs=sq[:, j, 0:Wb], start=(j == 0), stop=(j == 3)), reads=[dyb, self.d_const], writes=[dp])
                fw.op("act", lambda e: e.activation(out=rstd[:, 0:Wb], in_=ps[:, 0:Wb], func=AF.Sqrt, bias=self.eps_t[:, 0:1], scale=1.0 / 512), reads=[dp, self.d_const], writes=[dyb])
                fw.op("dve", lambda e: e.reciprocal(out=rstd[:, 0:Wb], in_=rstd[:, 0:Wb]), reads=[dyb], writes=[dyb])
                for j in range(4):
                    fw.op("dve", lambda e: e.scalar_tensor_tensor(out=o_[:, j, 0:Wb], in0=yb[:, j, 0:Wb], scalar=nw[:, j:j + 1], in1=rstd[:, 0:Wb], op0=ALU.mult, op1=ALU.mult), reads=[dyb, dq], writes=[do_])
                fw.dma("pool", yv[:, :, t0:t0 + Wb], o_[:, :, 0:Wb], reads=[do_], writes=[self.dep_yb])
    fw.barrier()


def _scratch_once(self, name, shape):
    if not hasattr(self, "_sc"):
        self._sc = {}
        self._scd = {}
    if name not in self._sc:
        self._sc[name] = self.scratch(name, shape)
        self._scd[name] = Dep()
    return self._sc[name]


def _dep_once(self, name):
    return self._scd[name]


Builder.mix_ssd = _mix_ssd
Builder.scratch_once = _scratch_once
Builder.dep_once = _dep_once


RW_ARR = ("r", "v", "kkn", "g", "bonus", "lw0", "kd0", "b0", "lw1", "kd1", "b1")


def _mix_rwkv(self, li):
    fw = self.fw
    I = self.I
    SC = {n: self.scratch_once("rw_" + n, (256, L)) for n in RW_ARR}
    dSC = {n: self.dep_once("rw_" + n) for n in RW_ARR}
    scv = {n: SC[n].rearrange("(kt p) t -> p kt t", p=128) for n in RW_ARR}

    def vec2(es_, name, ap1d, dep):
        t = self.sb(es_, name, [128, 2])
        fw.dma("sp", t[:], ap1d.rearrange("(kt p) -> p kt", p=128), writes=[dep], slow=True)
        return t

    with ExitStack() as es:
        dpar = Dep()
        mu = [vec2(es, "rw_mu%d" % a, I["rwkv_mu_rkv"][li, a], dpar) for a in range(3)]
        muw = [vec2(es, "rw_muw%d" % a, I["rwkv_mu_wag"][li, a], dpar) for a in range(3)]
        w0 = [vec2(es, "rw_w0%d" % d, I["rwkv_w0"][li, d], dpar) for d in range(2)]
        a0 = [vec2(es, "rw_a0%d" % d, I["rwkv_a0"][li, d], dpar) for d in range(2)]
        k_k = vec2(es, "rw_kk", I["rwkv_k_k"][li], dpar)
        k_a = vec2(es, "rw_ka", I["rwkv_k_a"][li], dpar)
        r_k = vec2(es, "rw_rk", I["rwkv_r_k"][li].rearrange("h n -> (h n)"), dpar)
        tiny = self.sb(es, "rw_tiny", [128, 1])
        fw.op("dve", lambda e: e.memset(tiny[:], 1e-12), writes=[dpar])
        blk = self.sb(es, "rw_blk", [128, 128], BF16)
        with ExitStack() as es2:
            blkf = self.sb(es2, "rw_blkf", [128, 128])
            dblk = Dep()
            fw.dma("sp", blkf[:], I["c_blk"][:, :], writes=[dblk])
            fw.op("dve", lambda e: e.tensor_copy(out=blk[:], in_=blkf[:]), reads=[dblk], writes=[dpar])
            fw.barrier()
        w1 = [self.load_weight_bf(es, "rw_w1%d" % d, I["rwkv_w1"][li, d], 256, 64) for d in range(2)]
        a1 = [self.load_weight_bf(es, "rw_a1%d" % d, I["rwkv_a1"][li, d], 256, 64) for d in range(2)]
        g1 = self.load_weight_bf(es, "rw_g1", I["rwkv_g1"][li], 256, 128)
        g2 = self.load_weight_bf(es, "rw_g2", I["rwkv_g2"][li], 128, 256)

        def load64(name, ap):
            t = self.sb(es, name, [64, 256], BF16)
            dd = Dep()
            with ExitStack() as es2:
                tf = self.sb(es2, name + "f", [64, 256])
                fw.dma("sp", tf[:], ap, writes=[dd])
                fw.op("dve", lambda e: e.tensor_copy(out=t[:], in_=tf[:]), reads=[dd], writes=[dd])
                fw.barrier()
            return t, dd
        w2 = [load64("rw_w2%d" % d, I["rwkv_w2"][li, d]) for d in range(2)]
        a2 = [load64("rw_a2%d" % d, I["rwkv_a2"][li, d]) for d in range(2)]

        W = 512
        X = self.sb(es, "rw_X", [128, 4, 2, W + 2])
        dX = Dep()
        Q = self.sb(es, "rw_Q", [128, 3, 2, W])
        dQ = Dep()
        T1 = self.sb(es, "rw_T1", [128, 2, W])
        dT1 = Dep()
        XW = self.sb(es, "rw_XW", [128, 3, 2, W], BF16)
        dXW = Dep()
        Hh = self.sb(es, "rw_Hh", [128, W], BF16)
        dHh = Dep()
        AS = self.sb(es, "rw_AS", [128, 2, W])
        dAS = Dep()
        KK = self.sb(es, "rw_KK", [128, 2, W])
        dKK = Dep()
        SQ = self.sb(es, "rw_SQ", [128, 2, W], BF16)
        dSQ = Dep()
        RS = self.sb(es, "rw_RS", [128, W])
        dRS = Dep()
        O = {n: self.sb(es, "rw_O_" + n, [128, 2, W]) for n in ("g", "bonus", "lw", "kd", "b")}
        dO = {n: Dep() for n in O}
        rkv_src = [self.projT[OFF_RKVX + a * 256:OFF_RKVX + (a + 1) * 256, :].rearrange("(kt p) t -> p kt t", p=128) for a in range(4)]
        for bi, (t0, Wb) in enumerate(BLOCKS):
            lo = max(t0 - 1, 0)
            hi = min(t0 + Wb + 1, L)
            c0 = lo - (t0 - 1)
            if t0 == 0:
                fw.op("dve", lambda e: e.memset(X[:, :, :, 0:1], 0.0), writes=[dX])
            if t0 + Wb == L:
                fw.op("dve", lambda e: e.memset(X[:, :, :, Wb + 1:Wb + 2], 0.0), writes=[dX])
            for a in range(4):
                fw.dma("sp", X[:, a, :, c0:c0 + (hi - lo)], rkv_src[a][:, :, lo:hi], reads=[self.dep_proj], writes=[dX])
            for a in range(4):
                for kt in range(2):
                    ctr, lf, rt = X[:, a, kt, 1:Wb + 1], X[:, a, kt, 0:Wb], X[:, a, kt, 2:Wb + 2]
                    fw.op("pool", lambda e: e.tensor_tensor(out=T1[:, kt, 0:Wb], in0=lf, in1=rt, op=ALU.add), reads=[dX], writes=[dT1])
                    fw.op("dve", lambda e: e.scalar_tensor_tensor(out=T1[:, kt, 0:Wb], in0=T1[:, kt, 0:Wb], scalar=0.5, in1=ctr, op0=ALU.mult, op1=ALU.subtract), reads=[dT1, dX], writes=[dT1])
                    if a < 3:
                        fw.op("dve", lambda e: e.scalar_tensor_tensor(out=Q[:, a, kt, 0:Wb], in0=T1[:, kt, 0:Wb], scalar=mu[a][:, kt:kt + 1], in1=ctr, op0=ALU.mult, op1=ALU.add), reads=[dT1, dX, dpar], writes=[dQ])
                    else:
                        for i3 in range(3):
                            fw.op("dve", lambda e: e.scalar_tensor_tensor(out=XW[:, i3, kt, 0:Wb], in0=T1[:, kt, 0:Wb], scalar=muw[i3][:, kt:kt + 1], in1=ctr, op0=ALU.mult, op1=ALU.add), reads=[dT1, dX, dpar], writes=[dXW])
            fw.dma("pool", scv["r"][:, :, t0:t0 + Wb], Q[:, 0, :, 0:Wb], reads=[dQ], writes=[dSC["r"]])
            fw.dma("pool", scv["v"][:, :, t0:t0 + Wb], Q[:, 2, :, 0:Wb], reads=[dQ], writes=[dSC["v"]])
            ps, dp = self.next_ps()
            for kt in range(2):
                fw.op("pe", lambda e: e.matmul(ps[:, 0:Wb], lhsT=g1[0][:, kt, :], rhs=XW[:, 2, kt, 0:Wb], start=(kt == 0), stop=(kt == 1)), reads=[g1[1], dXW], writes=[dp])
            fw.op("act", lambda e: e.activation(out=Hh[:, 0:Wb], in_=ps[:, 0:Wb], func=AF.Sigmoid), reads=[dp], writes=[dHh])
            for ct in range(2):
                ps, dp = self.next_ps()
                fw.op("pe", lambda e: e.matmul(ps[:, 0:Wb], lhsT=g2[0][:, 0, ct * 128:(ct + 1) * 128], rhs=Hh[:, 0:Wb], start=True, stop=True), reads=[g2[1], dHh], writes=[dp])
                fw.op("act", lambda e: e.copy(out=O["g"][:, ct, 0:Wb], in_=ps[:, 0:Wb]), reads=[dp], writes=[dO["g"]])
            fw.dma("pool", scv["g"][:, :, t0:t0 + Wb], O["g"][:, :, 0:Wb], reads=[dO["g"]], writes=[dSC["g"]])
            for kt in range(2):
                fw.op("dve", lambda e: e.tensor_scalar(out=KK[:, kt, 0:Wb], in0=Q[:, 1, kt, 0:Wb], scalar1=k_k[:, kt:kt + 1], scalar2=None, op0=ALU.mult), reads=[dQ, dpar], writes=[dKK])
                fw.op("act", lambda e: e.activation(out=SQ[:, kt, 0:Wb], in_=KK[:, kt, 0:Wb], func=AF.Square), reads=[dKK], writes=[dSQ])
                ps, dp = self.next_ps()
                fw.op("pe", lambda e: e.matmul(ps[:, 0:Wb], lhsT=blk[:, :], rhs=SQ[:, kt, 0:Wb], start=True, stop=True), reads=[dSQ, dpar], writes=[dp])
                fw.op("act", lambda e: e.activation(out=RS[:, 0:Wb], in_=ps[:, 0:Wb], func=AF.Sqrt, bias=tiny[:, 0:1]), reads=[dp, dpar], writes=[dRS])
                fw.op("dve", lambda e: e.reciprocal(out=RS[:, 0:Wb], in_=RS[:, 0:Wb]), reads=[dRS], writes=[dRS])
                fw.op("dve", lambda e: e.tensor_tensor(out=KK[:, kt, 0:Wb], in0=KK[:, kt, 0:Wb], in1=RS[:, 0:Wb], op=ALU.mult), reads=[dKK, dRS], writes=[dKK])
            fw.dma("pool", scv["kkn"][:, :, t0:t0 + Wb], KK[:, :, 0:Wb], reads=[dKK], writes=[dSC["kkn"]])
            for kt in range(2):
                fw.op("pool", lambda e: e.tensor_tensor(out=T1[:, kt, 0:Wb], in0=Q[:, 0, kt, 0:Wb], in1=Q[:, 1, kt, 0:Wb], op=ALU.mult), reads=[dQ, dT1], writes=[dT1])
                fw.op("dve", lambda e: e.tensor_scalar(out=SQ[:, kt, 0:Wb], in0=T1[:, kt, 0:Wb], scalar1=r_k[:, kt:kt + 1], scalar2=None, op0=ALU.mult), reads=[dT1, dpar, dSQ], writes=[dSQ])
                ps, dp = self.next_ps()
                fw.op("pe", lambda e: e.matmul(ps[:, 0:Wb], lhsT=blk[:, :], rhs=SQ[:, kt, 0:Wb], start=True, stop=True), reads=[dSQ, dpar], writes=[dp])
                fw.op("dve", lambda e: e.tensor_tensor(out=O["bonus"][:, kt, 0:Wb], in0=ps[:, 0:Wb], in1=Q[:, 2, kt, 0:Wb], op=ALU.mult), reads=[dp, dQ], writes=[dO["bonus"]])
            fw.dma("pool", scv["bonus"][:, :, t0:t0 + Wb], O["bonus"][:, :, 0:Wb], reads=[dO["bonus"]], writes=[dSC["bonus"]])
            for d in range(2):
                ps, dp = self.next_ps()
                for kt in range(2):
                    fw.op("pe", lambda e: e.matmul(ps[0:64, 0:Wb], lhsT=w1[d][0][:, kt, :], rhs=XW[:, 0, kt, 0:Wb], start=(kt == 0), stop=(kt == 1)), reads=[w1[d][1], dXW], writes=[dp])
                fw.op("act", lambda e: e.activation(out=Hh[0:64, 0:Wb], in_=ps[0:64, 0:Wb], func=AF.Tanh), reads=[dp], writes=[dHh])
                for ct in range(2):
                    ps, dp = self.next_ps()
                    fw.op("pe", lambda e: e.matmul(ps[:, 0:Wb], lhsT=w2[d][0][:, ct * 128:(ct + 1) * 128], rhs=Hh[0:64, 0:Wb], start=True, stop=True), reads=[w2[d][1], dHh], writes=[dp])
                    fw.op("act", lambda e: e.activation(out=O["lw"][:, ct, 0:Wb], in_=ps[:, 0:Wb], func=AF.Sigmoid, bias=w0[d][:, ct:ct + 1]), reads=[dp, dpar], writes=[dO["lw"]])
                    fw.op("dve", lambda e: e.tensor_scalar(out=O["lw"][:, ct, 0:Wb], in0=O["lw"][:, ct, 0:Wb], scalar1=-0.6065306597126334, scalar2=None, op0=ALU.mult), reads=[dO["lw"]], writes=[dO["lw"]])
                fw.dma("pool", scv["lw%d" % d][:, :, t0:t0 + Wb], O["lw"][:, :, 0:Wb], reads=[dO["lw"]], writes=[dSC["lw%d" % d]])
                ps, dp = self.next_ps()
                for kt in range(2):
                    fw.op("pe", lambda e: e.matmul(ps[0:64, 0:Wb], lhsT=a1[d][0][:, kt, :], rhs=XW[:, 1, kt, 0:Wb], start=(kt == 0), stop=(kt == 1)), reads=[a1[d][1], dXW], writes=[dp])
                fw.op("act", lambda e: e.copy(out=Hh[0:64, 0:Wb], in_=ps[0:64, 0:Wb]), reads=[dp], writes=[dHh])
                for ct in range(2):
                    ps, dp = self.next_ps()
                    fw.op("pe", lambda e: e.matmul(ps[:, 0:Wb], lhsT=a2[d][0][:, ct * 128:(ct + 1) * 128], rhs=Hh[0:64, 0:Wb], start=True, stop=True), reads=[a2[d][1], dHh], writes=[dp])
                    fw.op("act", lambda e: e.activation(out=AS[:, ct, 0:Wb], in_=ps[:, 0:Wb], func=AF.Sigmoid, bias=a0[d][:, ct:ct + 1]), reads=[dp, dpar], writes=[dAS])
                for kt in range(2):
                    fw.op("dve", lambda e: e.tensor_scalar(out=O["kd"][:, kt, 0:Wb], in0=AS[:, kt, 0:Wb], scalar1=-1.0, scalar2=None, op0=ALU.add), reads=[dAS], writes=[dO["kd"]])
                    fw.op("dve", lambda e: e.tensor_scalar(out=O["kd"][:, kt, 0:Wb], in0=O["kd"][:, kt, 0:Wb], scalar1=k_a[:, kt:kt + 1], scalar2=1.0, op0=ALU.mult, op1=ALU.add), reads=[dO["kd"], dpar], writes=[dO["kd"]])
                    fw.op("pool", lambda e: e.tensor_tensor(out=O["kd"][:, kt, 0:Wb], in0=O["kd"][:, kt, 0:Wb], in1=Q[:, 1, kt, 0:Wb], op=ALU.mult), reads=[dO["kd"], dQ], writes=[dO["kd"]])
                    fw.op("pool", lambda e: e.tensor_tensor(out=O["b"][:, kt, 0:Wb], in0=KK[:, kt, 0:Wb], in1=AS[:, kt, 0:Wb], op=ALU.mult), reads=[dKK, dAS], writes=[dO["b"]])
                fw.dma("pool", scv["kd%d" % d][:, :, t0:t0 + Wb], O["kd"][:, :, 0:Wb], reads=[dO["kd"]], writes=[dSC["kd%d" % d]])
                fw.dma("pool", scv["b%d" % d][:, :, t0:t0 + Wb], O["b"][:, :, 0:Wb], reads=[dO["b"]], writes=[dSC["b%d" % d]])
    fw.barrier()
    self._rwkv_scan(li, SC, dSC, scv)


Builder.mix_rwkv = _mix_rwkv


def _rwkv_scan(self, li, SC, dSC, scv):
    fw = self.fw
    I = self.I
    names = ("rt", "at", "kt", "bt", "kh", "bh", "vb")
    AD = {}
    dAD = {}
    for d in range(2):
        for kt in range(2):
            for n in names:
                AD[(d, kt, n)] = self.scratch_once("rwA_%d_%d_%s" % (d, kt, n), (128, LP)) if False else None
    if not hasattr(self, "_rwA"):
        self._rwA = {}
        for d in range(2):
            for kt in range(2):
                self._rwA[(d, kt)] = self.nc.dram_tensor("rwA_%d_%d" % (d, kt), [128, 7, LP], BF16, kind="Internal").ap()
        self._rwA_dep = {k: Dep() for k in self._rwA}
    with ExitStack() as es:
        yacc = self.sb(es, "rs_yacc", [128, 2, L])
        dyacc = Dep()
        tril_s = self.sb(es, "rs_tril_s", [128, 128])
        tril_i = self.sb(es, "rs_tril_i", [128, 128])
        trilT_s = self.sb(es, "rs_trilT_s", [128, 128])
        identb = self.sb(es, "rs_identb", [128, 128], BF16)
        dc = Dep()
        fw.dma("sp", tril_s[:], I["c_tril_s"][:, :], writes=[dc])
        fw.dma("sp", tril_i[:], I["c_tril_i"][:, :], writes=[dc])
        fw.dma("sp", trilT_s[:], I["c_trilT_s"][:, :], writes=[dc])
        fw.op("dve", lambda e: e.tensor_copy(out=identb[:], in_=self.ident[:]), reads=[self.d_const], writes=[dc])
        ones1 = self.sb(es, "rs_ones", [128, 128])
        fw.op("dve", lambda e: e.memset(ones1[:], 1.0), writes=[dc])
        etot = self.sb(es, "rs_etot", [128, 2, NCH])
        detot = Dep()
        for d in range(2):
            for kt in range(2):
                with ExitStack() as es3:
                    A = self.sb(es3, "rs_A", [128, 7, LP], BF16)
                    dA = Dep()
                    AI = {n: i for i, n in enumerate(names)}
                    stg = [self.sb(es3, "rs_stg%d" % i, [128, L]) for i in range(2)]
                    dstg = [Dep(), Dep()]
                    cs = self.sb(es3, "rs_cs", [128, LP])
                    lwr = self.sb(es3, "rs_lwr", [128, LP])
                    E1 = self.sb(es3, "rs_E1", [128, LP])
                    E2 = self.sb(es3, "rs_E2", [128, LP])
                    dcs, dlw, dE1, dE2 = Dep(), Dep(), Dep(), Dep()
                    fw.op("pool", lambda e: e.memset(A[:, :, L:LP], 0.0), writes=[dA])

                    def ld(i, name):
                        fw.dma("sp", stg[i][:], scv[name][:, kt, :], reads=[dSC[name]], writes=[dstg[i]])
                        return stg[i][:, :] if d == 0 else stg[i][:, ::-1]

                    sv = ld(0, "lw%d" % d)
                    fw.op("dve", lambda e: e.memset(lwr[:, L:LP], 0.0), writes=[dlw])
                    fw.op("dve", lambda e: e.tensor_copy(out=lwr[:, 0:L], in_=sv), reads=[dstg[0]], writes=[dlw])
                    for c in range(NCH):
                        fw.op("dve", lambda e: e.tensor_tensor_scan(out=cs[:, c * 128:(c + 1) * 128], data0=ones1[:, :], data1=lwr[:, c * 128:(c + 1) * 128], initial=0.0, op0=ALU.mult, op1=ALU.add),
                              reads=[dlw, dc], writes=[dcs])
                    fw.op("act", lambda e: e.activation(out=etot[:, kt, :], in_=cs[:, 127::128], func=AF.Exp), reads=[dcs], writes=[detot])
                    fw.op("act", lambda e: e.activation(out=E1[:], in_=cs[:], func=AF.Exp), reads=[dcs], writes=[dE1])
                    sv = ld(1, "r")
                    fw.op("dve", lambda e: e.tensor_tensor(out=A[:, AI["rt"], 0:L], in0=sv, in1=E1[:, 0:L], op=ALU.mult), reads=[dstg[1], dE1], writes=[dA])
                    fw.op("pool", lambda e: e.tensor_tensor(out=lwr[:], in0=cs[:], in1=lwr[:], op=ALU.subtract), reads=[dcs, dlw], writes=[dlw])
                    fw.op("act", lambda e: e.activation(out=E1[:], in_=lwr[:], func=AF.Exp), reads=[dlw, dE1], writes=[dE1])
                    sv = ld(0, "kkn")
                    fw.op("dve", lambda e: e.scalar_tensor_tensor(out=A[:, AI["at"], 0:L], in0=sv, scalar=-1.0, in1=E1[:, 0:L], op0=ALU.mult, op1=ALU.mult), reads=[dstg[0], dE1], writes=[dA])
                    for c in range(NCH):
                        fw.op("dve", lambda e: e.tensor_scalar(out=lwr[:, c * 128:(c + 1) * 128], in0=cs[:, c * 128:(c + 1) * 128], scalar1=cs[:, c * 128 + 127:c * 128 + 128], scalar2=-1.0, op0=ALU.subtract, op1=ALU.mult),
                              reads=[dcs, dlw], writes=[dlw])
                    fw.op("act", lambda e: e.activation(out=E1[:], in_=lwr[:], func=AF.Exp), reads=[dlw, dE1], writes=[dE1])
                    fw.op("act", lambda e: e.activation(out=E2[:], in_=cs[:], func=AF.Exp, scale=-1.0), reads=[dcs], writes=[dE2])
                    sv = ld(1, "kd%d" % d)
                    fw.op("dve", lambda e: e.tensor_tensor(out=A[:, AI["kt"], 0:L], in0=sv, in1=E2[:, 0:L], op=ALU.mult), reads=[dstg[1], dE2], writes=[dA])
                    fw.op("pool", lambda e: e.tensor_tensor(out=A[:, AI["kh"], 0:L], in0=sv, in1=E1[:, 0:L], op=ALU.mult), reads=[dstg[1], dE1], writes=[dA])
                    sv = ld(0, "b%d" % d)
                    fw.op("dve", lambda e: e.tensor_tensor(out=A[:, AI["bt"], 0:L], in0=sv, in1=E2[:, 0:L], op=ALU.mult), reads=[dstg[0], dE2], writes=[dA])
                    fw.op("pool", lambda e: e.tensor_tensor(out=A[:, AI["bh"], 0:L], in0=sv, in1=E1[:, 0:L], op=ALU.mult), reads=[dstg[0], dE1], writes=[dA])
                    sv = ld(1, "v")
                    fw.op("dve", lambda e: e.tensor_copy(out=A[:, AI["vb"], 0:L], in_=sv), reads=[dstg[1]], writes=[dA])
                    fw.dma("pool", self._rwA[(d, kt)][:, :, :], A[:, :, :], reads=[dA], writes=[self._rwA_dep[(d, kt)]])
                    fw.barrier()
            with ExitStack() as es4:
                AA = [self.sb(es4, "rs_AA%d" % kt, [128, 7, LP], BF16) for kt in range(2)]
                dAA = [Dep(), Dep()]
                for kt in range(2):
                    fw.dma("sp", AA[kt][:, :, :], self._rwA[(d, kt)][:, :, :], reads=[self._rwA_dep[(d, kt)]], writes=[dAA[kt]])
                AI = {n: i for i, n in enumerate(names)}
                S0 = self.sb(es4, "rs_S0", [128, 2, 64])
                S0b = self.sb(es4, "rs_S0b", [128, 2, 64], BF16)
                dS0 = [[Dep(), Dep()], [Dep(), Dep()]]
                dS0b = [[Dep(), Dep()], [Dep(), Dep()]]
                fw.op("dve", lambda e: e.memset(S0[:], 0.0), writes=[x for y in dS0 for x in y])
                fw.op("dve", lambda e: e.memset(S0b[:], 0.0), writes=[x for y in dS0b for x in y])
                NHB = 2
                chains = [(kt, hh) for kt in range(2) for hh in range(2)]

                def mk(nm, shape, dt=F32):
                    return ({ch: [self.sb(es4, "rs_%s_%d%d_%d" % (nm, ch[0], ch[1], i), shape, dt) for i in range(NHB)] for ch in chains},
                            {ch: [Dep() for i in range(NHB)] for ch in chains})
                TK, dTK = mk("TK", [128, 192], BF16)
                Pm, dPm = mk("P", [128, 128])
                PTm, dPTm = mk("PT", [128, 128])
                XT, dXT = mk("XT", [128, 128])
                XTb, dXTb = mk("XTb", [128, 128], BF16)
                Ak, dAk = mk("Ak", [128, 3, 128], BF16)
                P1b, dP1b = mk("P1b", [128, 64], BF16)
                Zb, dZb = mk("Zb", [128, 64], BF16)
                for c in range(NCH):
                    t0 = c * 128
                    Wv = min(128, L - t0)
                    C = slice(t0, t0 + 128)
                    b_ = c % NHB
                    for ch in chains:
                        kt, hh = ch
                        R = slice(64 * hh, 64 * hh + 64)
                        Aq = lambda n: AA[kt][R, AI[n], C]
                        tk, dtk = TK[ch][b_], dTK[ch][b_]
                        ptk, dptk = self.next_ps()
                        for i3, n in enumerate(("vb", "kh", "bh")):
                            fw.op("pe", lambda e: e.matmul(ptk[:, i3 * 64:(i3 + 1) * 64], lhsT=Aq(n), rhs=identb[R, 64 * hh:64 * hh + 64], start=True, stop=True), reads=[dAA[kt], dc], writes=[dptk])
                        fw.op("pe", lambda e: e.matmul(ptk[:, 256:384], lhsT=Aq("bt"), rhs=Aq("rt"), start=True, stop=True), reads=[dAA[kt]], writes=[dptk])
                        pa, dpa = self.next_ps()
                        for i5, (l_, r_) in enumerate((("bt", "at"), ("at", "bt"), ("kt", "at"), ("kt", "rt"))):
                            fw.op("pe", lambda e: e.matmul(pa[:, i5 * 128:(i5 + 1) * 128], lhsT=Aq(l_), rhs=Aq(r_), start=True, stop=True), reads=[dAA[kt]], writes=[dpa])
                        fw.op("act", lambda e: e.copy(out=tk[:, :], in_=ptk[:, 0:192]), reads=[dptk], writes=[dtk])
                        P_, dP_ = Pm[ch][b_], dPm[ch][b_]
                        PT_, dPT_ = PTm[ch][b_], dPTm[ch][b_]
                        X_, dX_ = XT[ch][b_], dXT[ch][b_]
                        ak, dak = Ak[ch][b_], dAk[ch][b_]
                        fw.op("dve", lambda e: e.tensor_tensor(out=PT_[:], in0=pa[:, 0:128], in1=tril_s[:], op=ALU.mult), reads=[dpa, dc], writes=[dPT_])
                        fw.op("dve", lambda e: e.tensor_tensor(out=P_[:], in0=pa[:, 128:256], in1=trilT_s[:], op=ALU.mult), reads=[dpa, dc], writes=[dP_])
                        fw.op("dve", lambda e: e.tensor_tensor(out=ak[:, 0, :], in0=pa[:, 256:384], in1=tril_s[:], op=ALU.mult), reads=[dpa, dc], writes=[dak])
                        fw.op("dve", lambda e: e.tensor_tensor(out=ak[:, 1, :], in0=pa[:, 384:512], in1=tril_i[:], op=ALU.mult), reads=[dpa, dc], writes=[dak])
                        fw.op("dve", lambda e: e.tensor_tensor(out=ak[:, 2, :], in0=ptk[:, 256:384], in1=tril_i[:], op=ALU.mult), reads=[dptk, dc], writes=[dak])
                        fw.op("pool", lambda e: e.tensor_tensor(out=X_[:], in0=PT_[:], in1=self.ident[:], op=ALU.add), reads=[dPT_, self.d_const], writes=[dX_])
                    for s_ in range(6):
                        pns = {}
                        for ch in chains:
                            P_, dP_ = Pm[ch][b_], dPm[ch][b_]
                            PT_, dPT_ = PTm[ch][b_], dPTm[ch][b_]
                            pn, dpn = self.next_ps()
                            pns[ch] = (pn, dpn)
                            fw.op("pe", lambda e: e.matmul(pn[:, 0:128], lhsT=PT_[:, :], rhs=P_[:, :], start=True, stop=True), reads=[dPT_, dP_], writes=[dpn])
                            if s_ < 5:
                                fw.op("pe", lambda e: e.matmul(pn[:, 128:256], lhsT=P_[:, :], rhs=PT_[:, :], start=True, stop=True), reads=[dPT_, dP_], writes=[dpn])
                        for ch in chains:
                            P_, dP_ = Pm[ch][b_], dPm[ch][b_]
                            PT_, dPT_ = PTm[ch][b_], dPTm[ch][b_]
                            pn, dpn = pns[ch]
                            fw.op("act", lambda e: e.copy(out=P_[:], in_=pn[:, 0:128]), reads=[dpn], writes=[dP_])
                            if s_ < 5:
                                fw.op("pool" if False else "act", lambda e: e.copy(out=PT_[:], in_=pn[:, 128:256]), reads=[dpn], writes=[dPT_])
                        pxs = {}
                        for ch in chains:
                            P_, dP_ = Pm[ch][b_], dPm[ch][b_]
                            X_, dX_ = XT[ch][b_], dXT[ch][b_]
                            px_, dpx_ = self.next_ps()
                            pxs[ch] = (px_, dpx_)
                            fw.op("pe", lambda e: e.matmul(px_[:, 0:128], lhsT=P_[:, :], rhs=X_[:, :], start=True, stop=True), reads=[dP_, dX_], writes=[dpx_])
                        for ch in chains:
                            X_, dX_ = XT[ch][b_], dXT[ch][b_]
                            px_, dpx_ = pxs[ch]
                            fw.op("dve", lambda e: e.tensor_tensor(out=X_[:], in0=px_[:, 0:128], in1=X_[:], op=ALU.add), reads=[dpx_, dX_], writes=[dX_])
                    pps = {}
                    for ch in chains:
                        kt, hh = ch
                        R = slice(64 * hh, 64 * hh + 64)
                        X_, dX_ = XT[ch][b_], dXT[ch][b_]
                        xb_, dxb_ = XTb[ch][b_], dXTb[ch][b_]
                        tk, dtk = TK[ch][b_], dTK[ch][b_]
                        ak, dak = Ak[ch][b_], dAk[ch][b_]
                        fw.op("act", lambda e: e.copy(out=xb_[:], in_=X_[:]), reads=[dX_], writes=[dxb_])
                        pp, dpp = self.next_ps()
                        pps[ch] = (pp, dpp)
                        fw.op("pe", lambda e: e.matmul(pp[:, 0:64], lhsT=AA[kt][R, AI["at"], C], rhs=S0b[R, kt, :], start=True, stop=False), reads=[dAA[kt], dS0b[kt][hh]], writes=[dpp])
                        fw.op("pe", lambda e: e.matmul(pp[:, 0:64], lhsT=ak[:, 0, :], rhs=tk[:, 0:64], start=False, stop=True), reads=[dak, dtk], writes=[dpp])
                    for ch in chains:
                        pp, dpp = pps[ch]
                        p1, dp1 = P1b[ch][b_], dP1b[ch][b_]
                        fw.op("act", lambda e: e.copy(out=p1[:], in_=pp[:, 0:64]), reads=[dpp], writes=[dp1])
                    pzs = {}
                    for ch in chains:
                        xb_, dxb_ = XTb[ch][b_], dXTb[ch][b_]
                        p1, dp1 = P1b[ch][b_], dP1b[ch][b_]
                        pz, dpz = self.next_ps()
                        pzs[ch] = (pz, dpz)
                        fw.op("pe", lambda e: e.matmul(pz[:, 0:64], lhsT=xb_[:, :], rhs=p1[:, :], start=True, stop=True), reads=[dxb_, dp1], writes=[dpz])
                    for ch in chains:
                        pz, dpz = pzs[ch]
                        z_, dz_ = Zb[ch][b_], dZb[ch][b_]
                        fw.op("dve", lambda e: e.tensor_copy(out=z_[:], in_=pz[:, 0:64]), reads=[dpz], writes=[dz_])
                    for ch in chains:
                        kt, hh = ch
                        R = slice(64 * hh, 64 * hh + 64)
                        tk, dtk = TK[ch][b_], dTK[ch][b_]
                        ak, dak = Ak[ch][b_], dAk[ch][b_]
                        z_, dz_ = Zb[ch][b_], dZb[ch][b_]
                        py, dpy = self.next_ps()
                        fw.op("pe", lambda e: e.matmul(py[R, 0:128], lhsT=S0b[R, kt, :], rhs=AA[kt][R, AI["rt"], C], start=True, stop=False), reads=[dAA[kt], dS0b[kt][hh]], writes=[dpy])
                        fw.op("pe", lambda e: e.matmul(py[R, 0:128], lhsT=tk[:, 0:64], rhs=ak[:, 1, :], start=False, stop=False), reads=[dak, dtk], writes=[dpy])
                        fw.op("pe", lambda e: e.matmul(py[R, 0:128], lhsT=z_[:, :], rhs=ak[:, 2, :], start=False, stop=True), reads=[dak, dz_], writes=[dpy])
                        pS, dpS = py, dpy
                        fw.op("pe", lambda e: e.matmul(pS[R, 256:320], lhsT=tk[:, 64:128], rhs=tk[:, 0:64], start=True, stop=False), reads=[dtk], writes=[dpS])
                        fw.op("pe", lambda e: e.matmul(pS[R, 256:320], lhsT=tk[:, 128:192], rhs=z_[:, :], start=False, stop=True), reads=[dtk, dz_], writes=[dpS])
                        if d == 0:
                            fw.op("act", lambda e: e.copy(out=yacc[R, kt, t0:t0 + Wv], in_=py[R, 0:Wv]), reads=[dpy], writes=[dyacc])
                        else:
                            lo = L - (t0 + Wv)
                            ya = yacc[R, kt, lo:lo + Wv]
                            fw.op("dve", lambda e: e.tensor_tensor(out=ya[:, ::-1], in0=py[R, 0:Wv], in1=ya[:, ::-1], op=ALU.add), reads=[dpy, dyacc], writes=[dyacc])
                        fw.op("dve", lambda e: e.scalar_tensor_tensor(out=S0[R, kt, :], in0=S0[R, kt, :], scalar=etot[R, kt, c:c + 1], in1=pS[R, 256:320], op0=ALU.mult, op1=ALU.add), reads=[dS0[kt][hh], detot, dpS], writes=[dS0[kt][hh]])
                        fw.op("act", lambda e: e.copy(out=S0b[R, kt, :], in_=S0[R, kt, :]), reads=[dS0[kt][hh]], writes=[dS0b[kt][hh]])
                fw.barrier()
        fw.barrier()
        with ExitStack() as es5:
            dq = Dep()
            def vec2(name, ap1d):
                t = self.sb(es5, name, [128, 2])
                fw.dma("sp", t[:], ap1d.rearrange("(kt p) -> p kt", p=128), writes=[dq], slow=True)
                return t
            lnw = vec2("r3_lnw", I["rwkv_ln_w"][li])
            lnb = vec2("r3_lnb", I["rwkv_ln_b"][li])
            epsl = self.sb(es5, "r3_eps", [128, 1])
            fw.op("dve", lambda e: e.memset(epsl[:], 64e-5), writes=[dq])
            blk = self.sb(es5, "r3_blk", [128, 128], BF16)
            blkf = self.sb(es5, "r3_blkf", [128, 128])
            fw.dma("sp", blkf[:], I["c_blk"][:, :], writes=[dq])
            fw.op("dve", lambda e: e.tensor_copy(out=blk[:], in_=blkf[:]), reads=[dq], writes=[dq])
            W = 512
            yb = self.sb(es5, "r3_yb", [128, W], BF16)
            sq = self.sb(es5, "r3_sq", [128, W], BF16)
            mean = self.sb(es5, "r3_mean", [128, W])
            var = self.sb(es5, "r3_var", [128, W])
            yc = [self.sb(es5, "r3_yc%d" % i, [128, 2, W]) for i in range(2)]
            bg = [self.sb(es5, "r3_bg%d" % i, [128, 2, 2, W]) for i in range(2)]
            dt_ = Dep()
            dyc = [Dep(), Dep()]
            dbg = [Dep(), Dep()]
            yv = self.ycT.rearrange("(kt p) t -> p kt t", p=128)
            for bi, (t0, Wb) in enumerate(BLOCKS):
                y_, dy_ = yc[bi % 2], dyc[bi % 2]
                b_, db_ = bg[bi % 2], dbg[bi % 2]
                fw.dma("sp", b_[:, 0, :, 0:Wb], scv["bonus"][:, :, t0:t0 + Wb], reads=[dSC["bonus"]], writes=[db_])
                fw.dma("sp", b_[:, 1, :, 0:Wb], scv["g"][:, :, t0:t0 + Wb], reads=[dSC["g"]], writes=[db_])
                for kt in range(2):
                    ysl = yacc[:, kt, t0:t0 + Wb]
                    fw.op("act", lambda e: e.copy(out=yb[:, 0:Wb], in_=ysl), reads=[dyacc], writes=[dt_])
                    fw.op("pool", lambda e: e.tensor_tensor(out=sq[:, 0:Wb], in0=ysl, in1=ysl, op=ALU.mult), reads=[dyacc], writes=[dt_])
                    pm, dpm = self.next_ps()
                    pq, dpq = self.next_ps()
                    fw.op("pe", lambda e: e.matmul(pm[:, 0:Wb], lhsT=blk[:, :], rhs=yb[:, 0:Wb], start=True, stop=True), reads=[dq, dt_], writes=[dpm])
                    fw.op("pe", lambda e: e.matmul(pq[:, 0:Wb], lhsT=blk[:, :], rhs=sq[:, 0:Wb], start=True, stop=True), reads=[dq, dt_], writes=[dpq])
                    fw.op("act", lambda e: e.mul(out=mean[:, 0:Wb], in_=pm[:, 0:Wb], mul=1.0 / 64), reads=[dpm], writes=[dt_])
                    fw.op("dve", lambda e: e.tensor_tensor(out=var[:, 0:Wb], in0=mean[:, 0:Wb], in1=mean[:, 0:Wb], op=ALU.mult), reads=[dt_], writes=[dt_])
                    fw.op("dve", lambda e: e.scalar_tensor_tensor(out=var[:, 0:Wb], in0=pq[:, 0:Wb], scalar=1.0 / 64, in1=var[:, 0:Wb], op0=ALU.mult, op1=ALU.subtract), reads=[dpq, dt_], writes=[dt_])
                    fw.op("act", lambda e: e.activation(out=var[:, 0:Wb], in_=var[:, 0:Wb], func=AF.Sqrt, bias=epsl[:, 0:1]), reads=[dt_, dq], writes=[dt_])
                    fw.op("dve", lambda e: e.reciprocal(out=var[:, 0:Wb], in_=var[:, 0:Wb]), reads=[dt_], writes=[dt_])
                    fw.op("pool", lambda e: e.tensor_tensor(out=y_[:, kt, 0:Wb], in0=ysl, in1=mean[:, 0:Wb], op=ALU.subtract), reads=[dyacc, dt_], writes=[dy_])
                    fw.op("dve", lambda e: e.tensor_tensor(out=y_[:, kt, 0:Wb], in0=y_[:, kt, 0:Wb], in1=var[:, 0:Wb], op=ALU.mult), reads=[dy_, dt_], writes=[dy_])
                    fw.op("dve", lambda e: e.tensor_scalar(out=y_[:, kt, 0:Wb], in0=y_[:, kt, 0:Wb], scalar1=lnw[:, kt:kt + 1], scalar2=lnb[:, kt:kt + 1], op0=ALU.mult, op1=ALU.add), reads=[dy_, dq], writes=[dy_])
                    fw.op("pool", lambda e: e.tensor_tensor(out=y_[:, kt, 0:Wb], in0=y_[:, kt, 0:Wb], in1=b_[:, 0, kt, 0:Wb], op=ALU.add), reads=[dy_, db_], writes=[dy_])
                    fw.op("dve", lambda e: e.tensor_tensor(out=y_[:, kt, 0:Wb], in0=y_[:, kt, 0:Wb], in1=b_[:, 1, kt, 0:Wb], op=ALU.mult), reads=[dy_, db_], writes=[dy_])
                fw.dma("pool", yv[:, :, t0:t0 + Wb], y_[:, :, 0:Wb], reads=[dy_], writes=[self.dep_yc])
    fw.barrier()


Builder._rwkv_scan = _rwkv_scan
```

```python
import numpy as np
from contextlib import ExitStack
import concourse.bass as bass
import concourse.mybir as mybir
from concourse.bass_utils import run_bass_kernel_spmd

F32 = mybir.dt.float32
BF16 = mybir.dt.bfloat16
F32R = mybir.dt.float32r


def R32(ap):
    return ap.bitcast(F32R)
AF = mybir.ActivationFunctionType
ALU = mybir.AluOpType

D = 1024
SEQ = 4096
NMETA = 16
L = SEQ + NMETA
DEPTH = 2
DFF = 4096
NIN = 5896
EPS = 1e-6
NDS = 48

OFF_U, OFF_Z, OFF_XBC, OFF_DT, OFF_RKVX, OFF_G = 0, 256, 768, 1792, 1800, 2824

BLOCKS = [(i * 512, 512) for i in range(8)] + [(4096, 16)]


class Dep:
    __slots__ = ("w", "r", "x")

    def __init__(self, excl=False):
        self.w = None
        self.r = []
        self.x = excl


class FW:
    def __init__(self, nc, es):
        self.nc = nc
        self.engs = dict(pe=nc.tensor, act=nc.scalar, dve=nc.vector, pool=nc.gpsimd, sp=nc.sync)
        self.sem = {k: es.enter_context(nc.semaphore("s_" + k)) for k in self.engs}
        self.cnt = {k: 0 for k in self.engs}
        self.seen = {k: {} for k in self.engs}
        self.dsem = [es.enter_context(nc.semaphore("d%d" % i)) for i in range(NDS)]
        self.dval = [0] * NDS
        self.dnext = 0
        self.dnext2 = 0
        self.nins = 0

    def _wait(self, eng, ev):
        key, val = ev
        if self.seen[eng].get(key, 0) >= val:
            return
        self.seen[eng][key] = val
        sem = self.sem[key[1]] if key[0] == "e" else self.dsem[key[1]]
        self.engs[eng].wait_ge(sem, val)

    def _deps(self, eng, reads, writes):
        me = ("e", eng)
        for d in reads:
            if d.w is not None:
                self._wait(eng, d.w)
        for d in writes:
            if d.w is not None and (d.w[0] != me or eng == "pool"):
                self._wait(eng, d.w)
            for r in d.r:
                self._wait(eng, r)

    def _post(self, ev, reads, writes):
        for d in writes:
            d.w = ev
            d.r = []
        for d in reads:
            d.r = [r for r in d.r if r[0] != ev[0]] + [ev]

    def op(self, eng, fn, reads=(), writes=()):
        xs = [d for d in reads if d.x]
        if xs:
            writes = list(writes) + [d for d in xs if d not in writes]
            reads = [d for d in reads if not d.x]
        self._deps(eng, reads, writes)
        ins = fn(self.engs[eng])
        self.cnt[eng] += 1
        self.nins += 1
        ins.then_inc(self.sem[eng], 1)
        self._post((("e", eng), self.cnt[eng]), reads, writes)

    def dma(self, q, out, in_, reads=(), writes=(), slow=False):
        self._deps(q, reads, writes)
        half = NDS // 2
        if q == "pool":
            i = half + self.dnext2
            self.dnext2 = (self.dnext2 + 1) % half
        else:
            i = self.dnext
            self.dnext = (self.dnext + 1) % half
        if self.dval[i] > 0:
            self._wait(q, (("d", i), self.dval[i]))
        self.dval[i] += 16
        self.nins += 1
        if slow:
            self.engs[q].dma_start(out=out, in_=in_, allow_slow_non_contiguous=True).then_inc(self.dsem[i], 16)
        else:
            self.engs[q].dma_start(out=out, in_=in_).then_inc(self.dsem[i], 16)
        self._post((("d", i), self.dval[i]), reads, writes)

    def barrier(self):
        for e in self.engs:
            for e2 in self.engs:
                if self.cnt[e2] > 0:
                    self._wait(e, (("e", e2), self.cnt[e2]))
            for i in range(NDS):
                if self.dval[i] > 0:
                    self._wait(e, (("d", i), self.dval[i]))


def col_tiles(lo, hi):
    out = []
    c = lo
    while c < hi:
        m = min(128, hi - c)
        out.append((c, m))
        c += m
    return out


IN_TILES = (col_tiles(OFF_U, OFF_Z) + col_tiles(OFF_Z, OFF_XBC) + col_tiles(OFF_XBC, OFF_DT)
            + col_tiles(OFF_DT, OFF_RKVX) + col_tiles(OFF_RKVX, OFF_G) + col_tiles(OFF_G, NIN))


class Builder:
    def __init__(self, cfg):
        self.cfg = cfg
        nc = self.nc = bass.Bass("TRN2", target_bir_lowering=False)
        self.I = {}
        self.es = ExitStack()

    def inp(self, name, shape):
        t = self.nc.dram_tensor(name, list(shape), F32, kind="ExternalInput").ap()
        self.I[name] = t
        return t

    def scratch(self, name, shape, dt=F32):
        kind = "ExternalOutput" if name in self.cfg.get("dump", ()) else "Internal"
        return self.nc.dram_tensor(name, list(shape), dt, kind=kind).ap()

    def sb(self, es, name, shape, dt=F32):
        self.uid = getattr(self, "uid", 0) + 1
        return es.enter_context(self.nc.sbuf_tensor("%s_%d" % (name, self.uid), list(shape), dt))

    def build(self):
        nc = self.nc
        cfg = self.cfg
        with self.es as es:
            fw = self.fw = FW(nc, es)
            I = self.I
            x = self.inp("x", (SEQ, D))
            for name, shape in WEIGHT_SHAPES:
                self.inp(name, shape)
            self.inp("c_ident", (128, 128))
            self.inp("c_iota", (128, 1024))
            self.inp("c_triu", (128, 128))
            self.inp("c_padm", (128, 8))
            self.inp("c_blk", (128, 128))
            self.inp("c_trilT_s", (128, 128))
            self.inp("c_mneg", (128, 128))
            self.inp("c_tril_s", (128, 128))
            self.inp("c_tril_i", (128, 128))
            out = nc.dram_tensor("out", [SEQ, D], F32, kind="ExternalOutput").ap()
            self.hT = self.scratch("hT", (D, L))
            self.projT = self.scratch("projT", (NIN, L))
            self.yaT = self.scratch("yaT", (256, L))
            self.ybT = self.scratch("ybT", (512, L))
            self.ycT = self.scratch("ycT", (256, L))
            self.dep_hT = Dep()
            self.dep_proj = Dep()
            self.dep_ya, self.dep_yb, self.dep_yc = Dep(), Dep(), Dep()

            self.ident = self.sb(es, "ident", [128, 128])
            self.ones_bf = self.sb(es, "ones_bf", [128, 128], BF16)
            self.d_const = Dep()
            fw.dma("sp", self.ident[:], I["c_ident"][:, :], writes=[self.d_const])
            fw.op("dve", lambda e: e.memset(self.ones_bf[:], 1.0), writes=[self.d_const])
            self.one_t = self.sb(es, "one_t", [128, 1])
            fw.op("dve", lambda e: e.memset(self.one_t[:], 1.0), writes=[self.d_const])
            self.ps = [es.enter_context(nc.psum_tensor("ps%d" % i, [128, 512], F32)) for i in range(8)]
            self.dps = [Dep(True) for _ in range(8)]
            self.psn = 0

            only = cfg.get("only")
            if only is not None:
                for ph in only:
                    if ph == "p0":
                        self.phase0(x)
                    elif ph == "p1":
                        self.phase1(0)
                    elif ph == "3a":
                        self.phase3a(0)
                    elif ph == "3b":
                        self.phase3b(0)
                    elif ph == "pf":
                        self.phase_final(out)
                fw.barrier()
                return nc
            self.phase0(x)
            nlayers = cfg.get("layers", DEPTH)
            for li in range(cfg.get("li0", 0), cfg.get("li0", 0) + nlayers):
                self.phase1(li)
                if cfg.get("fake_mix", False):
                    self.fake_mix()
                else:
                    self.mixers(li)
                if cfg.get("stop_after_mix", False):
                    break
                self.phase3a(li)
                self.phase3b(li)
            if not cfg.get("stop_after_mix", False):
                self.phase_final(out)
            fw.barrier()
        return nc

    def next_ps(self):
        i = self.psn
        self.psn = (i + 1) % 8
        return self.ps[i], self.dps[i]

    def phase0(self, x):
        fw = self.fw
        I = self.I
        with ExitStack() as es:
            xin = [self.sb(es, "p0_x%d" % i, [128, D]) for i in range(2)]
            dxin = [Dep(), Dep()]
            ho = [self.sb(es, "p0_h%d" % i, [128, 8, 128]) for i in range(2)]
            dho = [Dep(), Dep()]
            ntile = (L + 127) // 128
            for ti in range(ntile):
                t0 = ti * 128
                w = min(128, L - t0)
                xi, dx = xin[ti % 2], dxin[ti % 2]
                if ti == 0:
                    fw.dma("sp", xi[0:NMETA, :], I["meta_tokens"][:, :], writes=[dx])
                    fw.dma("sp", xi[NMETA:128, :], x[0:128 - NMETA, :], writes=[dx])
                else:
                    fw.dma("sp", xi[0:w, :], x[t0 - NMETA:t0 - NMETA + w, :], writes=[dx])
                h, dh = ho[ti % 2], dho[ti % 2]
                for kt in range(8):
                    ps, dp = self.next_ps()
                    fw.op("pe", lambda e: e.transpose(out=ps[:, 0:w], in_=xi[0:w, kt * 128:(kt + 1) * 128],
                                                      identity=self.ident[0:w, 0:w]),
                          reads=[dx, self.d_const], writes=[dp])
                    eng = "act" if kt % 2 == 0 else "dve"
                    if eng == "act":
                        fw.op("act", lambda e: e.copy(out=h[:, kt, 0:w], in_=ps[:, 0:w]), reads=[dp], writes=[dh])
                    else:
                        fw.op("dve", lambda e: e.tensor_copy(out=h[:, kt, 0:w], in_=ps[:, 0:w]), reads=[dp], writes=[dh])
                fw.dma("pool", self.hT.rearrange("(kt p) t -> p kt t", p=128)[:, :, t0:t0 + w], h[:, :, 0:w],
                       reads=[dh], writes=[self.dep_hT])
        fw.barrier()

    def load_weight_bf(self, es, name, w_ap, K, N, scale_ap=None, chunk=512):
        fw = self.fw
        kt_n = K // 128
        wbf = self.sb(es, name, [128, kt_n, N], BF16)
        dw = Dep()
        for kt in range(kt_n):
            fw.dma("pool", wbf[:, kt, :], w_ap[kt * 128:(kt + 1) * 128, :], writes=[dw])
        if scale_ap is not None:
            sc = self.sb(es, name + "_sc", [128, kt_n])
            dsc = Dep()
            fw.dma("sp", sc[:], scale_ap.rearrange("(kt p) -> p kt", p=128), writes=[dsc], slow=True)
            self.wcol = (sc, dsc)
        return wbf, dw

    def rmsnorm_block(self, h, dh, hn, dhn, sq, dsq, rstd, drstd, W):
        fw = self.fw
        for kt in range(8):
            fw.op("act", lambda e: e.activation(out=sq[:, kt, 0:W], in_=h[:, kt, 0:W], func=AF.Square),
                  reads=[dh], writes=[dsq])
        ps, dp = self.next_ps()
        for kt in range(8):
            fw.op("pe", lambda e: e.matmul(ps[:, 0:W], lhsT=self.ones_bf[:, :], rhs=sq[:, kt, 0:W],
                                           start=(kt == 0), stop=(kt == 7)),
                  reads=[dsq, self.d_const], writes=[dp])
        fw.op("act", lambda e: e.activation(out=rstd[:, 0:W], in_=ps[:, 0:W], func=AF.Sqrt, bias=self.eps_t[:, 0:1],
                                            scale=1.0 / D),
              reads=[dp, self.d_const], writes=[drstd])
        fw.op("dve", lambda e: e.reciprocal(out=rstd[:, 0:W], in_=rstd[:, 0:W]), reads=[drstd], writes=[drstd])
        sc, dsc = self.wcol
        for kt in range(8):
            fw.op("dve", lambda e: e.scalar_tensor_tensor(out=hn[:, kt, 0:W], in0=h[:, kt, 0:W], scalar=sc[:, kt:kt + 1],
                                                          in1=rstd[:, 0:W], op0=ALU.mult, op1=ALU.mult),
                  reads=[dh, drstd, dsc], writes=[dhn])

    def ensure_eps(self, es):
        self.eps_t = self.sb(es, "eps_t", [128, 1])
        self.fw.op("dve", lambda e: e.memset(self.eps_t[:], EPS), writes=[self.d_const])

    def phase1(self, li):
        fw = self.fw
        I = self.I
        with ExitStack() as es:
            self.ensure_eps(es)
            wbf, dw = self.load_weight_bf(es, "p1_w", I["w_in"][li], D, NIN, scale_ap=I["mix_norm_w"][li])
            h = [self.sb(es, "p1_h%d" % i, [128, 8, 512]) for i in range(2)]
            dh = [Dep(), Dep()]
            sq = self.sb(es, "p1_sq", [128, 8, 512], BF16)
            dsq = Dep()
            rstd = self.sb(es, "p1_rstd", [128, 512])
            drstd = Dep()
            hn = [self.sb(es, "p1_hn%d" % i, [128, 8, 512], BF16) for i in range(2)]
            dhn = [Dep(), Dep()]
            stg = [self.sb(es, "p1_o%d" % i, [128, 8, 512]) for i in range(2)]
            dstg = [Dep() for _ in range(2)]
            hTv = self.hT.rearrange("(kt p) t -> p kt t", p=128)
            batches = [(0, 6), (768, 8), (1792, None), (1800, 8)] + [(OFF_G + i * 1024, 8) for i in range(3)]
            ns = 0
            nb = 0
            def prologue(bi):
                t0_, W_ = BLOCKS[bi]
                fw.dma("sp", h[bi % 2][:, :, 0:W_], hTv[:, :, t0_:t0_ + W_], reads=[self.dep_hT], writes=[dh[bi % 2]])
                self.rmsnorm_block(h[bi % 2], dh[bi % 2], hn[bi % 2], dhn[bi % 2], sq, dsq, rstd, drstd, W_)

            prologue(0)
            for bi, (t0, W) in enumerate(BLOCKS):
                hnb, dhnb = hn[bi % 2], dhn[bi % 2]
                for bti, (r0, nt) in enumerate(batches):
                    if bti == 4 and bi + 1 < len(BLOCKS):
                        prologue(bi + 1)
                    s_, ds_ = stg[nb % 2], dstg[nb % 2]
                    nb += 1
                    tiles = [(r0, 8)] if nt is None else [(r0 + i * 128, 128) for i in range(nt)]
                    for ti, (c0, M) in enumerate(tiles):
                        ps, dp = self.next_ps()
                        for kt in range(8):
                            fw.op("pe", lambda e: e.matmul(ps[0:M, 0:W], lhsT=wbf[:, kt, c0:c0 + M], rhs=hnb[:, kt, 0:W],
                                                           start=(kt == 0), stop=(kt == 7)),
                                  reads=[dhnb, dw], writes=[dp])
                        if ns % 2 == 0:
                            fw.op("act", lambda e: e.copy(out=s_[0:M, ti, 0:W], in_=ps[0:M, 0:W]), reads=[dp], writes=[ds_])
                        else:
                            fw.op("dve", lambda e: e.tensor_copy(out=s_[0:M, ti, 0:W], in_=ps[0:M, 0:W]), reads=[dp], writes=[ds_])
                        ns += 1
                    if nt is None:
                        fw.dma("pool", self.projT[r0:r0 + 8, t0:t0 + W], s_[0:8, 0, 0:W], reads=[ds_], writes=[self.dep_proj])
                    else:
                        fw.dma("pool", self.projT[r0:r0 + nt * 128, t0:t0 + W].rearrange("(k p) t -> p k t", p=128),
                               s_[:, 0:nt, 0:W], reads=[ds_], writes=[self.dep_proj])
        fw.barrier()

    def fake_mix(self):
        fw = self.fw
        with ExitStack() as es:
            t = self.sb(es, "fm_t", [128, L])
            dt_ = Dep()
            for (dst, ddst, src0, n) in ((self.yaT, self.dep_ya, OFF_U, 2), (self.ybT, self.dep_yb, OFF_Z, 4),
                                         (self.ycT, self.dep_yc, OFF_RKVX, 2)):
                for j in range(n):
                    fw.dma("sp", t[:, :], self.projT[src0 + j * 128:src0 + (j + 1) * 128, :], reads=[self.dep_proj],
                           writes=[dt_])
                    fw.dma("sp", dst[j * 128:(j + 1) * 128, :], t[:, :], reads=[dt_], writes=[ddst])
        fw.barrier()

    def mixers(self, li):
        which = self.cfg.get("mix", ("s5", "ssd", "rwkv"))
        if "s5" in which:
            self.mix_s5(li)
        if "ssd" in which:
            self.mix_ssd(li)
        if "rwkv" in which:
            self.mix_rwkv(li)

    def phase3a(self, li):
        fw = self.fw
        I = self.I
        W = 512
        with ExitStack() as es:
            pa, dpa = self.load_weight_bf(es, "p3_pa", I["proj_a"][li], 256, D)
            pb, dpb = self.load_weight_bf(es, "p3_pb", I["proj_b"][li], 512, D)
            pc, dpc = self.load_weight_bf(es, "p3_pc", I["proj_c"][li], 256, D)
            wo, dwo = self.load_weight_bf(es, "p3_wo", I["w_out"][li], D, D)
            ystg = [self.sb(es, "p3_ys%d" % i, [128, 8, W]) for i in range(2)]
            dystg = [Dep(), Dep()]
            ybf = [self.sb(es, "p3_yb%d" % i, [128, 8, W], BF16) for i in range(2)]
            dybf = [Dep(), Dep()]
            g = [self.sb(es, "p3_g%d" % i, [128, 3, W]) for i in range(2)]
            dg = [Dep(), Dep()]
            mrg2 = [self.sb(es, "p3_m%d" % i, [128, 8, W], BF16) for i in range(2)]
            dmrg2 = [Dep(), Dep()]
            tmp = [self.sb(es, "p3_t%d" % i, [128, W]) for i in range(2)]
            dtmp = [Dep(), Dep()]
            h = [self.sb(es, "p3_h%d" % i, [128, 8, W]) for i in range(2)]
            dh = [Dep(), Dep()]
            hTv = self.hT.rearrange("(kt p) t -> p kt t", p=128)
            gv = self.projT[OFF_G:NIN, :].rearrange("(b kt p) t -> p b kt t", p=128, b=3)
            ng = 0

            def wout(pb_, dtile):
                t0_, Wp = BLOCKS[pb_]
                hp, dhp = h[pb_ % 2], dh[pb_ % 2]
                mp, dmp = mrg2[pb_ % 2], dmrg2[pb_ % 2]
                ps, dp = self.next_ps()
                for k in range(8):
                    fw.op("pe", lambda e: e.matmul(ps[:, 0:Wp], lhsT=wo[:, k, dtile * 128:(dtile + 1) * 128],
                                                   rhs=mp[:, k, 0:Wp], start=(k == 0), stop=(k == 7)),
                          reads=[dmp, dwo], writes=[dp])
                fw.op("dve", lambda e: e.tensor_tensor(out=hp[:, dtile, 0:Wp], in0=ps[:, 0:Wp], in1=hp[:, dtile, 0:Wp],
                                                       op=ALU.add), reads=[dp, dhp], writes=[dhp])

            def store(pb_):
                t0_, Wp = BLOCKS[pb_]
                fw.dma("pool", hTv[:, :, t0_:t0_ + Wp], h[pb_ % 2][:, :, 0:Wp], reads=[dh[pb_ % 2]], writes=[self.dep_hT])

            for bi, (t0, Wb) in enumerate(BLOCKS):
                ys, dys = ystg[bi % 2], dystg[bi % 2]
                yb, dyb = ybf[bi % 2], dybf[bi % 2]
                hb, dhb = h[bi % 2], dh[bi % 2]
                fw.dma("sp", ys[:, 0:2, 0:Wb], self.yaT.rearrange("(kt p) t -> p kt t", p=128)[:, :, t0:t0 + Wb],
                       reads=[self.dep_ya], writes=[dys])
                fw.dma("sp", ys[:, 2:6, 0:Wb], self.ybT.rearrange("(kt p) t -> p kt t", p=128)[:, :, t0:t0 + Wb],
                       reads=[self.dep_yb], writes=[dys])
                fw.dma("sp", ys[:, 6:8, 0:Wb], self.ycT.rearrange("(kt p) t -> p kt t", p=128)[:, :, t0:t0 + Wb],
                       reads=[self.dep_yc], writes=[dys])
                fw.dma("sp", hb[:, :, 0:Wb], hTv[:, :, t0:t0 + Wb], reads=[self.dep_hT], writes=[dhb])
                for kt in range(8):
                    eng = "dve" if kt % 2 == 0 else "pool"
                    fw.op(eng, lambda e: e.tensor_copy(out=yb[:, kt, 0:Wb], in_=ys[:, kt, 0:Wb]), reads=[dys], writes=[dyb])
                mrg, dmrg = mrg2[bi % 2], dmrg2[bi % 2]
                for dtile in range(8):
                    gg, dgg = g[ng % 2], dg[ng % 2]
                    ng += 1
                    fw.dma("sp", gg[:, :, 0:Wb], gv[:, :, dtile, t0:t0 + Wb], reads=[self.dep_proj], writes=[dgg])
                    fw.op("act", lambda e: e.activation(out=gg[:, :, 0:Wb], in_=gg[:, :, 0:Wb], func=AF.Sigmoid),
                          reads=[dgg], writes=[dgg])
                    tm, dtm = tmp[dtile % 2], dtmp[dtile % 2]
                    for br, (wt, dwt, k0, nk) in enumerate(((pa, dpa, 0, 2), (pb, dpb, 2, 4), (pc, dpc, 6, 2))):
                        ps, dp = self.next_ps()
                        for k in range(nk):
                            fw.op("pe", lambda e: e.matmul(ps[:, 0:Wb], lhsT=wt[:, k, dtile * 128:(dtile + 1) * 128],
                                                           rhs=yb[:, k0 + k, 0:Wb], start=(k == 0), stop=(k == nk - 1)),
                                  reads=[dyb, dwt], writes=[dp])
                        if br == 0:
                            fw.op("dve", lambda e: e.tensor_tensor(out=tm[:, 0:Wb], in0=ps[:, 0:Wb], in1=gg[:, 0, 0:Wb],
                                                                   op=ALU.mult), reads=[dp, dgg], writes=[dtm])
                        else:
                            fw.op("dve", lambda e: e.tensor_tensor(out=gg[:, br, 0:Wb], in0=ps[:, 0:Wb],
                                                                   in1=gg[:, br, 0:Wb], op=ALU.mult),
                                  reads=[dp, dgg], writes=[dgg])
                            if br == 1:
                                fw.op("dve", lambda e: e.tensor_tensor(out=tm[:, 0:Wb], in0=tm[:, 0:Wb],
                                                                       in1=gg[:, 1, 0:Wb], op=ALU.add),
                                      reads=[dtm, dgg], writes=[dtm])
                            else:
                                fw.op("dve", lambda e: e.tensor_tensor(out=mrg[:, dtile, 0:Wb], in0=tm[:, 0:Wb],
                                                                       in1=gg[:, 2, 0:Wb], op=ALU.add),
                                      reads=[dtm, dgg], writes=[dmrg])
                    if bi > 0:
                        wout(bi - 1, dtile)
                if bi > 0:
                    store(bi - 1)
            for dtile in range(8):
                wout(len(BLOCKS) - 1, dtile)
            store(len(BLOCKS) - 1)
        fw.barrier()

    def phase3b(self, li):
        fw = self.fw
        I = self.I
        W = 384
        blocks = [(s, min(W, L - s)) for s in range(0, L, W)]
        with ExitStack() as es:
            self.ensure_eps(es)
            w1, dw1 = self.load_weight_bf(es, "p4_w1", I["mlp_w1"][li], D, DFF, scale_ap=I["mlp_norm_w"][li])
            w2, dw2 = self.load_weight_bf(es, "p4_w2", I["mlp_w2"][li], DFF, D)
            h = [self.sb(es, "p4_h%d" % i, [128, 8, W]) for i in range(2)]
            dh = [Dep(), Dep()]
            sq = self.sb(es, "p4_sq", [128, 8, W], BF16)
            dsq = Dep()
            rstd = self.sb(es, "p4_rstd", [128, W])
            drstd = Dep()
            hn2 = [self.sb(es, "p4_hn%d" % i, [128, 8, W], BF16) for i in range(2)]
            dhn2 = [Dep(), Dep()]
            act = self.sb(es, "p4_act", [128, 32, W], BF16)
            dact = Dep()
            rl = [self.sb(es, "p4_rl%d" % i, [128, W]) for i in range(2)]
            drl = [Dep(), Dep()]
            hTv = self.hT.rearrange("(kt p) t -> p kt t", p=128)
            def prologue(bi):
                t0_, W_ = blocks[bi]
                fw.dma("sp", h[bi % 2][:, :, 0:W_], hTv[:, :, t0_:t0_ + W_], reads=[self.dep_hT], writes=[dh[bi % 2]])
                self.rmsnorm_block(h[bi % 2], dh[bi % 2], hn2[bi % 2], dhn2[bi % 2], sq, dsq, rstd, drstd, W_)

            prologue(0)
            for bi, (t0, Wb) in enumerate(blocks):
                hb, dhb = h[bi % 2], dh[bi % 2]
                hn, dhn = hn2[bi % 2], dhn2[bi % 2]
                for f in range(32):
                    ps, dp = self.next_ps()
                    for k in range(8):
                        fw.op("pe", lambda e: e.matmul(ps[:, 0:Wb], lhsT=w1[:, k, f * 128:(f + 1) * 128],
                                                       rhs=hn[:, k, 0:Wb], start=(k == 0), stop=(k == 7)),
                              reads=[dhn, dw1], writes=[dp])
                    r, dr = rl[f % 2], drl[f % 2]
                    fw.op("act", lambda e: e.activation(out=r[:, 0:Wb], in_=ps[:, 0:Wb], func=AF.Relu),
                          reads=[dp], writes=[dr])
                    eng = "dve" if f % 2 == 0 else "pool"
                    fw.op(eng, lambda e: e.tensor_tensor(out=act[:, f, 0:Wb], in0=r[:, 0:Wb], in1=r[:, 0:Wb], op=ALU.mult),
                          reads=[dr], writes=[dact])
                if bi + 1 < len(blocks):
                    prologue(bi + 1)
                for dtile in range(8):
                    ps, dp = self.next_ps()
                    for f in range(32):
                        fw.op("pe", lambda e: e.matmul(ps[:, 0:Wb], lhsT=w2[:, f, dtile * 128:(dtile + 1) * 128],
                                                       rhs=act[:, f, 0:Wb], start=(f == 0), stop=(f == 31)),
                              reads=[dact, dw2], writes=[dp])
                    fw.op("dve", lambda e: e.tensor_tensor(out=hb[:, dtile, 0:Wb], in0=ps[:, 0:Wb], in1=hb[:, dtile, 0:Wb],
                                                           op=ALU.add), reads=[dp, dhb], writes=[dhb])
                fw.dma("pool", hTv[:, :, t0:t0 + Wb], hb[:, :, 0:Wb], reads=[dhb], writes=[self.dep_hT])
        fw.barrier()

    def phase_final(self, out):
        fw = self.fw
        I = self.I
        with ExitStack() as es:
            self.ensure_eps(es)
            fnw = self.sb(es, "pf_w", [128, 8])
            dfnw = Dep()
            fw.dma("sp", fnw[:], I["final_norm_w"].rearrange("(kt p) -> p kt", p=128), writes=[dfnw], slow=True)
            h = [self.sb(es, "pf_h%d" % i, [128, 8, 512]) for i in range(2)]
            dh = [Dep(), Dep()]
            sq = self.sb(es, "pf_sq", [128, 8, 512], BF16)
            dsq = Dep()
            rstd = self.sb(es, "pf_rstd", [128, 512])
            drstd = Dep()
            o = [self.sb(es, "pf_o%d" % i, [128, D]) for i in range(2)]
            do = [Dep(), Dep()]
            dout = Dep()
            hTv = self.hT.rearrange("(kt p) t -> p kt t", p=128)
            no = 0
            for bi in range(8):
                t0 = NMETA + bi * 512
                W = 512
                hb, dhb = h[bi % 2], dh[bi % 2]
                fw.dma("sp", hb[:, :, 0:W], hTv[:, :, t0:t0 + W], reads=[self.dep_hT], writes=[dhb])
                for kt in range(8):
                    fw.op("act", lambda e: e.activation(out=sq[:, kt, 0:W], in_=hb[:, kt, 0:W], func=AF.Square),
                          reads=[dhb], writes=[dsq])
                ps, dp = self.next_ps()
                for kt in range(8):
                    fw.op("pe", lambda e: e.matmul(ps[:, 0:W], lhsT=self.ones_bf[:, :], rhs=sq[:, kt, 0:W],
                                                   start=(kt == 0), stop=(kt == 7)),
                          reads=[dsq, self.d_const], writes=[dp])
                fw.op("act", lambda e: e.activation(out=rstd[:, 0:W], in_=ps[:, 0:W], func=AF.Sqrt,
                                                    bias=self.eps_t[:, 0:1], scale=1.0 / D),
                      reads=[dp, self.d_const], writes=[drstd])
                fw.op("dve", lambda e: e.reciprocal(out=rstd[:, 0:W], in_=rstd[:, 0:W]), reads=[drstd], writes=[drstd])
                for kt in range(8):
                    fw.op("dve", lambda e: e.scalar_tensor_tensor(out=hb[:, kt, 0:W], in0=hb[:, kt, 0:W],
                                                                  scalar=fnw[:, kt:kt + 1], in1=rstd[:, 0:W],
                                                                  op0=ALU.mult, op1=ALU.mult),
                          reads=[dhb, drstd, dfnw], writes=[dhb])
                for tt in range(4):
                    ob, dob = o[no % 2], do[no % 2]
                    no += 1
                    for kt in range(8):
                        ps, dp = self.next_ps()
                        fw.op("pe", lambda e: e.transpose(out=ps[:, 0:128], in_=hb[:, kt, tt * 128:(tt + 1) * 128],
                                                          identity=self.ident[:, :]),
                              reads=[dhb, self.d_const], writes=[dp])
                        if kt % 2 == 0:
                            fw.op("act", lambda e: e.copy(out=ob[:, kt * 128:(kt + 1) * 128], in_=ps[:, 0:128]),
                                  reads=[dp], writes=[dob])
                        else:
                            fw.op("dve", lambda e: e.tensor_copy(out=ob[:, kt * 128:(kt + 1) * 128], in_=ps[:, 0:128]),
                                  reads=[dp], writes=[dob])
                    r0 = bi * 512 + tt * 128
                    fw.dma("pool", out[r0:r0 + 128, :], ob[:, :], reads=[dob], writes=[dout])
        fw.barrier()


WEIGHT_SHAPES = [
    ("meta_tokens", (16, 1024)), ("final_norm_w", (1024,)), ("mix_norm_w", (2, 1024)), ("w_in", (2, 1024, 5896)),
    ("s5_lambda_re", (2, 2, 16, 64)), ("s5_lambda_im", (2, 2, 16, 64)), ("s5_log_step", (2, 2, 16)),
    ("s5_b_re", (2, 16, 64, 16)), ("s5_b_im", (2, 16, 64, 16)), ("s5_c_re", (2, 16, 16, 64)),
    ("s5_c_im", (2, 16, 16, 64)), ("s5_d", (2, 256)), ("s5_glu_w", (2, 256, 512)), ("s5_glu_b", (2, 512)),
    ("ssd_conv_w", (2, 5, 1024)), ("ssd_conv_b", (2, 1024)), ("ssd_a_log", (2, 2, 8)), ("ssd_dt_bias", (2, 2, 8)),
    ("ssd_d", (2, 8)), ("ssd_norm_w", (2, 512)), ("rwkv_mu_rkv", (2, 3, 256)), ("rwkv_mu_wag", (2, 3, 256)),
    ("rwkv_w0", (2, 2, 256)), ("rwkv_w1", (2, 2, 256, 64)), ("rwkv_w2", (2, 2, 64, 256)), ("rwkv_a0", (2, 2, 256)),
    ("rwkv_a1", (2, 2, 256, 64)), ("rwkv_a2", (2, 2, 64, 256)), ("rwkv_g1", (2, 256, 128)), ("rwkv_g2", (2, 128, 256)),
    ("rwkv_k_k", (2, 256)), ("rwkv_k_a", (2, 256)), ("rwkv_r_k", (2, 4, 64)), ("rwkv_ln_w", (2, 256)),
    ("rwkv_ln_b", (2, 256)), ("proj_a", (2, 256, 1024)), ("proj_b", (2, 512, 1024)), ("proj_c", (2, 256, 1024)),
    ("w_out", (2, 1024, 1024)), ("mlp_norm_w", (2, 1024)), ("mlp_w1", (2, 1024, 4096)), ("mlp_w2", (2, 4096, 1024)),
]


def host_consts():
    return {"c_ident": np.eye(128, dtype=np.float32),
            "c_iota": np.ascontiguousarray(np.broadcast_to(np.arange(1024, dtype=np.float32), (128, 1024))),
            "c_triu": np.triu(np.ones((128, 128), np.float32)),
            "c_blk": np.kron(np.eye(2, dtype=np.float32), np.ones((64, 64), np.float32)),
            "c_trilT_s": np.tril(np.ones((128, 128), np.float32), -1),
            "c_padm": np.ascontiguousarray(np.broadcast_to((np.arange(128) < 16).astype(np.float32)[:, None], (128, 8))),
            "c_mneg": np.where(np.triu(np.ones((128, 128), bool)), 0.0, -30000.0).astype(np.float32),
            "c_tril_s": np.triu(np.ones((128, 128), np.float32), 1),
            "c_tril_i": np.triu(np.ones((128, 128), np.float32), 0)}


def run(inputs, cfg, ncores=8):
    b = Builder(cfg)
    nc = b.build()
    consts = host_consts()
    in_maps = []
    for c in range(ncores):
        m = {"x": np.ascontiguousarray(inputs["x"][c], dtype=np.float32)}
        for name, _ in WEIGHT_SHAPES:
            m[name] = np.ascontiguousarray(inputs[name], dtype=np.float32)
        m.update(consts)
        in_maps.append(m)
    res = run_bass_kernel_spmd(nc, in_maps, core_ids=list(range(ncores)))
    return res, b


def kernel(**inputs):
    res, _ = run(inputs, {})
    return np.stack([np.asarray(res.results[c]["out"]) for c in range(8)], axis=0).astype(np.float32)


PI = float(np.pi)
S5W = 512


def _mix_s5(self, li):
    fw = self.fw
    I = self.I
    nc = self.nc
    with ExitStack() as es:
        lr = self.sb(es, "s5_lr", [128, 16])
        lim = self.sb(es, "s5_li", [128, 16])
        dpar = Dep()
        fw.dma("sp", lr[:], I["s5_lambda_re"][li].rearrange("d (q gp) n -> (gp n) (d q)", gp=2), writes=[dpar], slow=True)
        fw.dma("sp", lim[:], I["s5_lambda_im"][li].rearrange("d (q gp) n -> (gp n) (d q)", gp=2), writes=[dpar], slow=True)
        stepb = self.sb(es, "s5_stepb", [128, 2, 8, 2])
        fw.dma("sp", stepb[:], I["s5_log_step"][li].rearrange("d (q gp) -> d q gp", gp=2).partition_broadcast(128),
               writes=[dpar], slow=True)
        step = self.sb(es, "s5_step", [128, 16])
        fw.op("act", lambda e: e.activation(out=step[0:64, :].rearrange("p (d q) -> p d q", d=2), in_=stepb[0:64, :, :, 0],
                                            func=AF.Exp), reads=[dpar], writes=[dpar])
        fw.op("act", lambda e: e.activation(out=step[64:128, :].rearrange("p (d q) -> p d q", d=2),
                                            in_=stepb[64:128, :, :, 1], func=AF.Exp), reads=[dpar], writes=[dpar])
        th = self.sb(es, "s5_th", [128, 16])
        rho = self.sb(es, "s5_rho", [128, 16])
        fw.op("dve", lambda e: e.tensor_tensor(out=th[:], in0=lim[:], in1=step[:], op=ALU.mult), reads=[dpar], writes=[dpar])
        fw.op("dve", lambda e: e.tensor_tensor(out=rho[:], in0=lr[:], in1=step[:], op=ALU.mult), reads=[dpar], writes=[dpar])
        fw.op("act", lambda e: e.activation(out=rho[:], in_=rho[:], func=AF.Exp), reads=[dpar], writes=[dpar])

        NT = S5W + 1
        tc = self.sb(es, "s5_tc", [128, 16, NT])
        ts = self.sb(es, "s5_ts", [128, 16, NT])
        dtab = Dep()
        with ExitStack() as es2:
            iot = self.sb(es2, "s5_iota", [128, NT])
            fw.dma("sp", iot[:], I["c_iota"][:, 0:NT], writes=[dtab])
            ph = self.sb(es2, "s5_ph", [128, 16, NT])
            ki = self.sb(es2, "s5_ki", [128, 16, NT], mybir.dt.int32)
            kf = self.sb(es2, "s5_kf", [128, 16, NT])
            for j in range(16):
                fw.op("dve", lambda e: e.tensor_scalar(out=ph[:, j, :], in0=iot[:], scalar1=th[:, j:j + 1], scalar2=None,
                                                       op0=ALU.mult), reads=[dpar, dtab], writes=[dtab])
            fw.op("dve", lambda e: e.tensor_scalar(out=ki[:], in0=ph[:], scalar1=1.0 / (2 * PI), scalar2=None, op0=ALU.mult),
                  reads=[dtab], writes=[dtab])
            fw.op("dve", lambda e: e.tensor_copy(out=kf[:], in_=ki[:]), reads=[dtab], writes=[dtab])
            fw.op("dve", lambda e: e.scalar_tensor_tensor(out=ph[:], in0=kf[:], scalar=-2 * PI, in1=ph[:], op0=ALU.mult,
                                                          op1=ALU.add), reads=[dtab], writes=[dtab])

            def wrap(t):
                fw.op("dve", lambda e: e.tensor_scalar(out=kf[:], in0=t[:], scalar1=PI, scalar2=-2 * PI, op0=ALU.is_gt,
                                                       op1=ALU.mult), reads=[dtab], writes=[dtab])
                fw.op("dve", lambda e: e.tensor_tensor(out=t[:], in0=t[:], in1=kf[:], op=ALU.add), reads=[dtab], writes=[dtab])
                fw.op("dve", lambda e: e.tensor_scalar(out=kf[:], in0=t[:], scalar1=-PI, scalar2=2 * PI, op0=ALU.is_lt,
                                                       op1=ALU.mult), reads=[dtab], writes=[dtab])
                fw.op("dve", lambda e: e.tensor_tensor(out=t[:], in0=t[:], in1=kf[:], op=ALU.add), reads=[dtab], writes=[dtab])

            wrap(ph)
            fw.op("act", lambda e: e.activation(out=ts[:], in_=ph[:], func=AF.Sin), reads=[dtab], writes=[dtab])
            fw.op("dve", lambda e: e.tensor_scalar(out=ph[:], in0=ph[:], scalar1=PI / 2, scalar2=None, op0=ALU.add),
                  reads=[dtab], writes=[dtab])
            wrap(ph)
            fw.op("act", lambda e: e.activation(out=tc[:], in_=ph[:], func=AF.Sin), reads=[dtab], writes=[dtab])
            fw.barrier()
        nsW = self.sb(es, "s5_nsW", [128, 16])
        fw.op("dve", lambda e: e.tensor_scalar(out=nsW[:], in0=ts[:, :, S5W], scalar1=-1.0, scalar2=None, op0=ALU.mult),
              reads=[dtab], writes=[dpar])
        nsB = self.sb(es, "s5_nsB", [128, 16])
        abr = self.sb(es, "s5_abr", [128, 16])
        abi = self.sb(es, "s5_abi", [128, 16])
        fw.op("dve", lambda e: e.tensor_tensor(out=abr[:], in0=rho[:], in1=tc[:, :, 1], op=ALU.mult), reads=[dpar, dtab], writes=[dpar])
        fw.op("dve", lambda e: e.tensor_tensor(out=abi[:], in0=rho[:], in1=ts[:, :, 1], op=ALU.mult), reads=[dpar, dtab], writes=[dpar])
        den = self.sb(es, "s5_den", [128, 16])
        t1 = self.sb(es, "s5_t1", [128, 16])
        t2 = self.sb(es, "s5_t2", [128, 16])
        cor = self.sb(es, "s5_cor", [128, 16])
        coi = self.sb(es, "s5_coi", [128, 16])
        V = lambda fn: fw.op("dve", fn, reads=[dpar], writes=[dpar])
        V(lambda e: e.tensor_tensor(out=den[:], in0=lr[:], in1=lr[:], op=ALU.mult))
        V(lambda e: e.tensor_tensor(out=t1[:], in0=lim[:], in1=lim[:], op=ALU.mult))
        V(lambda e: e.tensor_tensor(out=den[:], in0=den[:], in1=t1[:], op=ALU.add))
        V(lambda e: e.reciprocal(out=den[:], in_=den[:]))
        V(lambda e: e.tensor_scalar(out=abr[:], in0=abr[:], scalar1=-1.0, scalar2=None, op0=ALU.add))
        V(lambda e: e.tensor_tensor(out=t1[:], in0=abr[:], in1=lr[:], op=ALU.mult))
        V(lambda e: e.tensor_tensor(out=t2[:], in0=abi[:], in1=lim[:], op=ALU.mult))
        V(lambda e: e.tensor_tensor(out=t1[:], in0=t1[:], in1=t2[:], op=ALU.add))
        V(lambda e: e.tensor_tensor(out=cor[:], in0=t1[:], in1=den[:], op=ALU.mult))
        V(lambda e: e.tensor_tensor(out=t1[:], in0=abi[:], in1=lr[:], op=ALU.mult))
        V(lambda e: e.tensor_tensor(out=t2[:], in0=abr[:], in1=lim[:], op=ALU.mult))
        V(lambda e: e.tensor_tensor(out=t1[:], in0=t1[:], in1=t2[:], op=ALU.subtract))
        V(lambda e: e.tensor_tensor(out=coi[:], in0=t1[:], in1=den[:], op=ALU.mult))

        LB = self.sb(es, "s5_LB", [128, 2, 8, 2, 128], BF16)
        LC = self.sb(es, "s5_LC", [128, 8, 2, 128], BF16)
        dLB = Dep()
        fw.op("dve", lambda e: e.memset(LC[:], 0.0), writes=[dLB])
        with ExitStack() as es2:
            Xr = self.sb(es2, "s5_Xr", [128, 8, 128])
            Xi = self.sb(es2, "s5_Xi", [128, 8, 128])
            dX = Dep()
            fw.op("dve", lambda e: e.memset(Xr[:], 0.0), writes=[dX])
            fw.op("dve", lambda e: e.memset(Xi[:], 0.0), writes=[dX])
            for (X, nm) in ((Xr, "s5_b_re"), (Xi, "s5_b_im")):
                for q in range(8):
                    r = q % 4
                    fw.dma("sp", X[0:64, q, 32 * r:32 * r + 16], I[nm][li, 2 * q], writes=[dX])
                    fw.dma("sp", X[64:128, q, 32 * r + 16:32 * r + 32], I[nm][li, 2 * q + 1], writes=[dX])
            Xc = self.sb(es2, "s5_Xc", [128, 2, 8, 2, 128])
            tmpx = self.sb(es2, "s5_tmpx", [128, 8, 128])
            for d in range(2):
                cr = cor[:, d * 8:(d + 1) * 8].unsqueeze(2).to_broadcast([128, 8, 128])
                ci = coi[:, d * 8:(d + 1) * 8].unsqueeze(2).to_broadcast([128, 8, 128])
                fw.op("dve", lambda e: e.tensor_tensor(out=Xc[:, d, :, 0, :], in0=Xr[:], in1=cr, op=ALU.mult), reads=[dX, dpar], writes=[dX])
                fw.op("dve", lambda e: e.tensor_tensor(out=tmpx[:], in0=Xi[:], in1=ci, op=ALU.mult), reads=[dX, dpar], writes=[dX])
                fw.op("dve", lambda e: e.tensor_tensor(out=Xc[:, d, :, 0, :], in0=Xc[:, d, :, 0, :], in1=tmpx[:], op=ALU.subtract), reads=[dX], writes=[dX])
                fw.op("dve", lambda e: e.tensor_tensor(out=Xc[:, d, :, 1, :], in0=Xi[:], in1=cr, op=ALU.mult), reads=[dX, dpar], writes=[dX])
                fw.op("dve", lambda e: e.tensor_tensor(out=tmpx[:], in0=Xr[:], in1=ci, op=ALU.mult), reads=[dX, dpar], writes=[dX])
                fw.op("dve", lambda e: e.tensor_tensor(out=Xc[:, d, :, 1, :], in0=Xc[:, d, :, 1, :], in1=tmpx[:], op=ALU.add), reads=[dX], writes=[dX])
            for d in range(2):
                for q in range(8):
                    for ri in range(2):
                        r = q % 4
                        ps, dp = self.next_ps()
                        fw.op("pe", lambda e: e.transpose(out=ps[:, 0:128], in_=Xc[:, d, q, ri, :],
                                                          identity=self.ident[:, :]), reads=[dX, self.d_const], writes=[dp])
                        fw.op("act", lambda e: e.copy(out=LB[:, d, q, ri, :], in_=ps[:, 0:128]),
                              reads=[dp], writes=[dLB])
            Yr = self.sb(es2, "s5_Yr", [32, 8, 128])
            Yi = self.sb(es2, "s5_Yi", [32, 8, 128])
            dY = Dep()
            fw.op("dve", lambda e: e.memset(Yr[:], 0.0), writes=[dY])
            fw.op("dve", lambda e: e.memset(Yi[:], 0.0), writes=[dY])
            for (Y, nm) in ((Yr, "s5_c_re"), (Yi, "s5_c_im")):
                src = I[nm][li].rearrange("(q gp) h n -> gp h q n", gp=2)
                fw.dma("sp", Y[0:16, :, 0:64], src[0], writes=[dY])
                fw.dma("sp", Y[16:32, :, 64:128], src[1], writes=[dY])
            for q in range(8):
                for ri, Y in enumerate((Yr, Yi)):
                    ps, dp = self.next_ps()
                    fw.op("pe", lambda e: e.transpose(out=ps[:, 0:32], in_=Y[:, q, :], identity=self.ident[0:32, 0:32]),
                          reads=[dY, self.d_const], writes=[dp])
                    if ri == 0:
                        fw.op("act", lambda e: e.copy(out=LC[:, q, 0, 32 * (q % 4):32 * (q % 4) + 32], in_=ps[:, 0:32]), reads=[dp], writes=[dLB])
                    else:
                        fw.op("act", lambda e: e.mul(out=LC[:, q, 1, 32 * (q % 4):32 * (q % 4) + 32], in_=ps[:, 0:32], mul=-1.0), reads=[dp], writes=[dLB])
            fw.barrier()

        ubf = self.sb(es, "s5_ubf", [128, 2, L], BF16)
        urv = self.sb(es, "s5_urv", [128, 2, L], BF16)
        yacc = self.sb(es, "s5_yacc", [128, 2, L])
        du = Dep()
        dyacc = Dep()
        with ExitStack() as es2:
            uf = self.sb(es2, "s5_uf", [128, 2, L])
            fw.dma("sp", uf[:], self.projT[OFF_U:OFF_U + 256, :].rearrange("(kt p) t -> p kt t", p=128),
                   reads=[self.dep_proj], writes=[du])
            for kt in range(2):
                fw.op("dve", lambda e: e.tensor_copy(out=ubf[:, kt, :], in_=uf[:, kt, :]), reads=[du], writes=[du])
                fw.op("pool", lambda e: e.tensor_copy(out=urv[:, kt, ::-1], in_=uf[:, kt, :]), reads=[du], writes=[du])
            fw.barrier()

        blocks = [(i * S5W, S5W) for i in range(L // S5W)]
        if L % S5W:
            blocks.append((L - L % S5W, L % S5W))
        NB = 2
        tmp = [[self.sb(es, "s5_w%d_%d" % (i, k), [128, S5W]) for k in range(6)] for i in range(NB)]
        dtmp = [[Dep() for k in range(6)] for i in range(NB)]
        hb = [[self.sb(es, "s5_h%d_%d" % (i, k), [128, S5W], BF16) for k in range(2)] for i in range(NB)]
        dhb = [[Dep() for k in range(2)] for i in range(NB)]
        init = [[self.sb(es, "s5_in%d_%d" % (i, k), [128, 1]) for k in range(3)] for i in range(2)]
        dinit = [Dep(), Dep()]
        it = 0
        for d in range(2):
            usrc = ubf if d == 0 else urv
            for q in range(8):
                j = d * 8 + q
                r = q % 4
                kt = q // 4
                rho_b = rho[:, j:j + 1]
                prev = None
                for bi, (t0, W) in enumerate(blocks):
                    T, dT = tmp[it % NB], dtmp[it % NB]
                    H, dH = hb[it % NB], dhb[it % NB]
                    it += 1
                    pre, dpre = self.next_ps()
                    pim, dpim = self.next_ps()
                    fw.op("pe", lambda e: e.matmul(pre[:, 0:W], lhsT=LB[:, d, q, 0, :],
                                                   rhs=usrc[:, kt, t0:t0 + W], start=True, stop=True),
                          reads=[dLB, du], writes=[dpre])
                    fw.op("pe", lambda e: e.matmul(pim[:, 0:W], lhsT=LB[:, d, q, 1, :],
                                                   rhs=usrc[:, kt, t0:t0 + W], start=True, stop=True),
                          reads=[dLB, du], writes=[dpim])
                    c_, s_ = tc[:, j, 0:W], ts[:, j, 0:W]
                    fw.op("dve", lambda e: e.tensor_tensor(out=T[0][:, 0:W], in0=pre[:, 0:W], in1=c_, op=ALU.mult), reads=[dpre, dtab], writes=[dT[0]])
                    fw.op("dve", lambda e: e.tensor_tensor(out=T[1][:, 0:W], in0=pim[:, 0:W], in1=s_, op=ALU.mult), reads=[dpim, dtab], writes=[dT[1]])
                    fw.op("dve", lambda e: e.tensor_tensor(out=T[2][:, 0:W], in0=pim[:, 0:W], in1=c_, op=ALU.mult), reads=[dpim, dtab], writes=[dT[2]])
                    fw.op("dve", lambda e: e.tensor_tensor(out=T[3][:, 0:W], in0=pre[:, 0:W], in1=s_, op=ALU.mult), reads=[dpre, dtab], writes=[dT[3]])
                    fw.op("pool", lambda e: e.tensor_tensor(out=T[0][:, 0:W], in0=T[0][:, 0:W], in1=T[1][:, 0:W], op=ALU.add), reads=[dT[0], dT[1]], writes=[dT[0]])
                    fw.op("pool", lambda e: e.tensor_tensor(out=T[2][:, 0:W], in0=T[2][:, 0:W], in1=T[3][:, 0:W], op=ALU.subtract), reads=[dT[2], dT[3]], writes=[dT[2]])
                    ini, dini = init[bi % 2], dinit[bi % 2]
                    if bi == 0:
                        i_re, i_im = 0.0, 0.0
                        rd = []
                    else:
                        pT, pdT, pW, pini, pdini = prev
                        cW, sW = tc[:, j, pW:pW + 1], ts[:, j, pW:pW + 1]
                        fw.op("dve", lambda e: e.tensor_scalar(out=ini[2][:], in0=pT[4][:, pW - 1:pW], scalar1=cW, scalar2=None, op0=ALU.mult), reads=[pdT[4], dtab], writes=[dini])
                        fw.op("dve", lambda e: e.scalar_tensor_tensor(out=ini[2][:], in0=pT[5][:, pW - 1:pW], scalar=sW, in1=ini[2][:], op0=ALU.mult, op1=ALU.subtract), reads=[pdT[5], dini, dtab], writes=[dini])
                        fw.op("dve", lambda e: e.tensor_scalar(out=ini[0][:], in0=ini[2][:], scalar1=-1.0, scalar2=None, op0=ALU.mult), reads=[dini], writes=[dini])
                        fw.op("dve", lambda e: e.tensor_scalar(out=ini[2][:], in0=pT[4][:, pW - 1:pW], scalar1=sW, scalar2=None, op0=ALU.mult), reads=[pdT[4], dtab], writes=[dini])
                        fw.op("dve", lambda e: e.scalar_tensor_tensor(out=ini[1][:], in0=pT[5][:, pW - 1:pW], scalar=cW, in1=ini[2][:], op0=ALU.mult, op1=ALU.add), reads=[pdT[5], dini, dtab], writes=[dini])
                        i_re, i_im = ini[0][:, 0:1], ini[1][:, 0:1]
                        rd = [dini]
                    fw.op("dve", lambda e: e.tensor_tensor_scan(out=T[4][:, 0:W], data0=rho_b.to_broadcast([128, W]), data1=T[0][:, 0:W], initial=i_re, op0=ALU.mult, op1=ALU.add),
                          reads=[dT[0], dpar] + rd, writes=[dT[4]])
                    fw.op("dve", lambda e: e.tensor_tensor_scan(out=T[5][:, 0:W], data0=rho_b.to_broadcast([128, W]), data1=T[2][:, 0:W], initial=i_im, op0=ALU.mult, op1=ALU.add),
                          reads=[dT[2], dpar] + rd, writes=[dT[5]])
                    prev = (T, dT, W, ini, dini)
                    fw.op("pool", lambda e: e.tensor_tensor(out=T[0][:, 0:W], in0=T[4][:, 0:W], in1=c_, op=ALU.mult), reads=[dT[4], dtab], writes=[dT[0]])
                    fw.op("pool", lambda e: e.tensor_tensor(out=T[1][:, 0:W], in0=T[5][:, 0:W], in1=s_, op=ALU.mult), reads=[dT[5], dtab], writes=[dT[1]])
                    fw.op("pool", lambda e: e.tensor_tensor(out=H[0][:, 0:W], in0=T[0][:, 0:W], in1=T[1][:, 0:W], op=ALU.subtract), reads=[dT[0], dT[1]], writes=[dH[0]])
                    fw.op("dve", lambda e: e.tensor_tensor(out=T[2][:, 0:W], in0=T[4][:, 0:W], in1=s_, op=ALU.mult), reads=[dT[4], dtab], writes=[dT[2]])
                    fw.op("dve", lambda e: e.tensor_tensor(out=T[3][:, 0:W], in0=T[5][:, 0:W], in1=c_, op=ALU.mult), reads=[dT[5], dtab], writes=[dT[3]])
                    fw.op("pool", lambda e: e.tensor_tensor(out=H[1][:, 0:W], in0=T[2][:, 0:W], in1=T[3][:, 0:W], op=ALU.add), reads=[dT[2], dT[3]], writes=[dH[1]])
                    py, dpy = self.next_ps()
                    fw.op("pe", lambda e: e.matmul(py[:, 0:W], lhsT=LC[:, q, 0, :], rhs=H[0][:, 0:W], start=True, stop=False), reads=[dLB, dH[0]], writes=[dpy])
                    fw.op("pe", lambda e: e.matmul(py[:, 0:W], lhsT=LC[:, q, 1, :], rhs=H[1][:, 0:W], start=False, stop=True), reads=[dLB, dH[1]], writes=[dpy])
                    if d == 0 and r == 0:
                        fw.op("act", lambda e: e.copy(out=yacc[:, kt, t0:t0 + W], in_=py[:, 0:W]), reads=[dpy], writes=[dyacc])
                    elif d == 0:
                        ya = yacc[:, kt, t0:t0 + W]
                        fw.op("dve", lambda e: e.tensor_tensor(out=ya, in0=py[:, 0:W], in1=ya, op=ALU.add), reads=[dpy, dyacc], writes=[dyacc])
                    else:
                        lo = L - (t0 + W)
                        ya = yacc[:, kt, lo:lo + W]
                        fw.op("dve", lambda e: e.tensor_tensor(out=ya[:, ::-1], in0=py[:, 0:W], in1=ya[:, ::-1], op=ALU.add), reads=[dpy, dyacc], writes=[dyacc])
        fw.barrier()
        self._s5_post(li, es, yacc, dyacc)


def _s5_post(self, li, es_outer, yacc, dyacc):
    fw = self.fw
    I = self.I
    with ExitStack() as es:
        gw, dgw = self.load_weight_bf(es, "s5_gw", I["s5_glu_w"][li], 256, 512)
        dsk = self.sb(es, "s5_dsk", [128, 2])
        gb = self.sb(es, "s5_gb", [128, 4])
        dpp = Dep()
        fw.dma("sp", dsk[:], I["s5_d"][li].rearrange("(kt p) -> p kt", p=128), writes=[dpp], slow=True)
        fw.dma("sp", gb[:], I["s5_glu_b"][li].rearrange("(kt p) -> p kt", p=128), writes=[dpp], slow=True)
        W = 512
        uf = [self.sb(es, "s5p_u%d" % i, [128, 2, W]) for i in range(2)]
        duf = [Dep(), Dep()]
        t1 = self.sb(es, "s5p_t1", [128, 2, W])
        t2 = self.sb(es, "s5p_t2", [128, 2, W])
        dt1 = Dep()
        gl = [self.sb(es, "s5p_gl%d" % i, [128, 2, W], BF16) for i in range(2)]
        dgl = [Dep(), Dep()]
        sg = [self.sb(es, "s5p_sg%d" % i, [128, W]) for i in range(2)]
        dsg = [Dep(), Dep()]
        o = [self.sb(es, "s5p_o%d" % i, [128, 2, W]) for i in range(2)]
        do = [Dep(), Dep()]
        uv = self.projT[OFF_U:OFF_U + 256, :].rearrange("(kt p) t -> p kt t", p=128)
        yv = self.yaT.rearrange("(kt p) t -> p kt t", p=128)
        for bi, (t0, Wb) in enumerate(BLOCKS):
            u, du = uf[bi % 2], duf[bi % 2]
            g, dg = gl[bi % 2], dgl[bi % 2]
            ob, dob = o[bi % 2], do[bi % 2]
            fw.dma("sp", u[:, :, 0:Wb], uv[:, :, t0:t0 + Wb], reads=[self.dep_proj], writes=[du])
            for kt in range(2):
                fw.op("dve", lambda e: e.scalar_tensor_tensor(out=t1[:, kt, 0:Wb], in0=u[:, kt, 0:Wb], scalar=dsk[:, kt:kt + 1], in1=yacc[:, kt, t0:t0 + Wb], op0=ALU.mult, op1=ALU.add),
                      reads=[du, dpp, dyacc], writes=[dt1])
                fw.op("pool", lambda e: e.tensor_tensor(out=t2[:, kt, 0:Wb], in0=t1[:, kt, 0:Wb], in1=t1[:, kt, 0:Wb], op=ALU.mult), reads=[dt1], writes=[dt1])
                fw.op("dve", lambda e: e.tensor_scalar(out=t2[:, kt, 0:Wb], in0=t2[:, kt, 0:Wb], scalar1=0.044715, scalar2=1.0, op0=ALU.mult, op1=ALU.add), reads=[dt1], writes=[dt1])
                fw.op("pool", lambda e: e.tensor_tensor(out=t2[:, kt, 0:Wb], in0=t2[:, kt, 0:Wb], in1=t1[:, kt, 0:Wb], op=ALU.mult), reads=[dt1], writes=[dt1])
                fw.op("act", lambda e: e.activation(out=t2[:, kt, 0:Wb], in_=t2[:, kt, 0:Wb], func=AF.Sigmoid, scale=1.5957691216), reads=[dt1], writes=[dt1])
                fw.op("dve", lambda e: e.tensor_tensor(out=g[:, kt, 0:Wb], in0=t2[:, kt, 0:Wb], in1=t1[:, kt, 0:Wb], op=ALU.mult), reads=[dt1], writes=[dg])
            for c in range(2):
                plo, dplo = self.next_ps()
                phi, dphi = self.next_ps()
                for k in range(2):
                    fw.op("pe", lambda e: e.matmul(plo[:, 0:Wb], lhsT=gw[:, k, c * 128:(c + 1) * 128], rhs=g[:, k, 0:Wb], start=(k == 0), stop=(k == 1)), reads=[dgw, dg], writes=[dplo])
                for k in range(2):
                    fw.op("pe", lambda e: e.matmul(phi[:, 0:Wb], lhsT=gw[:, k, 256 + c * 128:256 + (c + 1) * 128], rhs=g[:, k, 0:Wb], start=(k == 0), stop=(k == 1)), reads=[dgw, dg], writes=[dphi])
                s, ds = sg[c], dsg[c]
                fw.op("act", lambda e: e.activation(out=s[:, 0:Wb], in_=phi[:, 0:Wb], func=AF.Sigmoid, bias=gb[:, 2 + c:3 + c]), reads=[dphi, dpp], writes=[ds])
                fw.op("dve", lambda e: e.scalar_tensor_tensor(out=ob[:, c, 0:Wb], in0=plo[:, 0:Wb], scalar=gb[:, c:c + 1], in1=s[:, 0:Wb], op0=ALU.add, op1=ALU.mult), reads=[dplo, ds, dpp], writes=[dob])
            fw.dma("pool", yv[:, :, t0:t0 + Wb], ob[:, :, 0:Wb], reads=[dob], writes=[self.dep_ya])
    fw.barrier()


Builder.mix_s5 = _mix_s5
Builder._s5_post = _s5_post


LP = 33 * 128
NCH = 33


def _mix_ssd(self, li):
    fw = self.fw
    I = self.I
    xcT = self.scratch_once("xcT", (1024, L))
    d_xc = self.dep_once("xcT")
    xbv = self.projT[OFF_XBC:OFF_XBC + 1024, :].rearrange("(j p) t -> p j t", p=128)
    xcv = xcT.rearrange("(j p) t -> p j t", p=128)
    with ExitStack() as es:
        cw = self.sb(es, "sd_cw", [128, 5, 8])
        cb = self.sb(es, "sd_cb", [128, 8])
        dcw = Dep()
        for k in range(5):
            fw.dma("sp", cw[:, k, :], I["ssd_conv_w"][li, k].rearrange("(j p) -> p j", p=128), writes=[dcw], slow=True)
        fw.dma("sp", cb[:], I["ssd_conv_b"][li].rearrange("(j p) -> p j", p=128), writes=[dcw], slow=True)
        xp = [self.sb(es, "sd_xp%d" % i, [128, L + 4]) for i in range(2)]
        dxp = [Dep(), Dep()]
        acc = [self.sb(es, "sd_acc%d" % i, [128, L]) for i in range(2)]
        dacc = [Dep(), Dep()]
        for i in range(2):
            fw.op("dve", lambda e: e.memset(xp[i][:, 0:2], 0.0), writes=[dxp[i]])
            fw.op("dve", lambda e: e.memset(xp[i][:, L + 2:L + 4], 0.0), writes=[dxp[i]])
        for j in range(8):
            x_, dx_ = xp[j % 2], dxp[j % 2]
            a_, da_ = acc[j % 2], dacc[j % 2]
            fw.dma("sp", x_[:, 2:L + 2], xbv[:, j, :], reads=[self.dep_proj], writes=[dx_])
            eng = "dve"
            fw.op(eng, lambda e: e.tensor_scalar(out=a_[:], in0=x_[:, 0:L], scalar1=cw[:, 0, j:j + 1], scalar2=cb[:, j:j + 1], op0=ALU.mult, op1=ALU.add),
                  reads=[dx_, dcw], writes=[da_])
            for k in range(1, 5):
                fw.op(eng, lambda e: e.scalar_tensor_tensor(out=a_[:], in0=x_[:, k:k + L], scalar=cw[:, k, j:j + 1], in1=a_[:], op0=ALU.mult, op1=ALU.add),
                      reads=[dx_, dcw, da_], writes=[da_])
            fw.op("act", lambda e: e.activation(out=a_[:], in_=a_[:], func=AF.Silu), reads=[da_], writes=[da_])
            fw.dma("pool", xcv[:, j, :], a_[:], reads=[da_], writes=[d_xc])
    fw.barrier()

    with ExitStack() as es:
        triu = self.sb(es, "sd_triu", [128, 128])
        mneg = self.sb(es, "sd_mneg", [128, 128])
        onesf = self.sb(es, "sd_onesf", [128, 128])
        negones = self.sb(es, "sd_negones", [128, 128])
        identb = self.sb(es, "sd_identb", [128, 128], BF16)
        dc = Dep()
        fw.dma("sp", triu[:], I["c_triu"][:, :], writes=[dc])
        fw.dma("sp", mneg[:], I["c_mneg"][:, :], writes=[dc])
        padm = self.sb(es, "sd_padm", [128, 8])
        fw.dma("sp", padm[:], I["c_padm"][:, :], writes=[dc])
        fw.op("dve", lambda e: e.memset(onesf[:], 1.0), writes=[dc])
        fw.op("dve", lambda e: e.memset(negones[:], -1.0), writes=[dc])
        fw.op("dve", lambda e: e.tensor_copy(out=identb[:], in_=self.ident[:]), reads=[self.d_const], writes=[dc])
        dtb = self.sb(es, "sd_dtb", [8, 2])
        nea = self.sb(es, "sd_nea", [8, 2])
        dpp = Dep()
        fw.dma("sp", dtb[:], I["ssd_dt_bias"][li].rearrange("d h -> h d"), writes=[dpp], slow=True)
        fw.dma("sp", nea[:], I["ssd_a_log"][li].rearrange("d h -> h d"), writes=[dpp], slow=True)
        fw.op("act", lambda e: e.activation(out=nea[:], in_=nea[:], func=AF.Exp), reads=[dpp], writes=[dpp])
        fw.op("dve", lambda e: e.tensor_scalar(out=nea[:], in0=nea[:], scalar1=-1.0, scalar2=None, op0=ALU.mult), reads=[dpp], writes=[dpp])
        dtok = self.sb(es, "sd_dtok", [128, NCH, 16])
        ddtok = Dep()
        xs = self.sb(es, "sd_xs", [128, 4, LP], BF16)
        Bm = self.sb(es, "sd_B", [128, 2, LP], BF16)
        Cm = self.sb(es, "sd_C", [128, 2, LP], BF16)
        dws = Dep()
        yacc = self.sb(es, "sd_yacc", [128, 4, L])
        dyacc = Dep()
        ST = self.sb(es, "sd_ST", [128, 8, 64])
        STb = self.sb(es, "sd_STb", [128, 8, 64], BF16)
        dST = [Dep() for _ in range(8)]
        dSTb = [Dep() for _ in range(8)]
        dbias = self.sb(es, "sd_dbias", [128, 2, 8])
        nea_bc = self.sb(es, "sd_neabc", [128, 2, 8])
        fw.dma("sp", dbias[:], I["ssd_dt_bias"][li].partition_broadcast(128), writes=[dpp], slow=True)
        fw.dma("sp", nea_bc[:], I["ssd_a_log"][li].partition_broadcast(128), writes=[dpp], slow=True)
        fw.op("act", lambda e: e.activation(out=nea_bc[:], in_=nea_bc[:], func=AF.Exp), reads=[dpp], writes=[dpp])
        fw.op("dve", lambda e: e.tensor_scalar(out=nea_bc[:], in0=nea_bc[:], scalar1=-1.0, scalar2=None, op0=ALU.mult), reads=[dpp], writes=[dpp])

        for d in range(2):
          with ExitStack() as es3:
            stg = [self.sb(es3, "sd_stg%d" % i, [128, L]) for i in range(2)]
            dstg = [Dep(), Dep()]
            raw = self.sb(es3, "sd_raw", [8, L])
            rawd = self.sb(es3, "sd_rawd", [8, LP])
            draw = Dep()
            for j in range(8):
                s_, ds_ = stg[j % 2], dstg[j % 2]
                fw.dma("sp", s_[:], xcv[:, j, :], reads=[d_xc], writes=[ds_])
                dst = xs[:, j, :] if j < 4 else (Bm[:, j - 4, :] if j < 6 else Cm[:, j - 6, :])
                eng = "dve" if j % 2 == 0 else "pool"
                fw.op(eng, lambda e: e.memset(dst[:, L:LP], 0.0), writes=[dws])
                if d == 0:
                    fw.op(eng, lambda e: e.tensor_copy(out=dst[:, 0:L], in_=s_[:]), reads=[ds_], writes=[dws])
                else:
                    fw.op(eng, lambda e: e.tensor_copy(out=dst[:, 0:L][:, ::-1], in_=s_[:]), reads=[ds_], writes=[dws])
            fw.dma("sp", raw[:], self.projT[OFF_DT:OFF_DT + 8, :], reads=[self.dep_proj], writes=[draw])
            fw.op("dve", lambda e: e.memset(rawd[:, L:LP], 0.0), writes=[draw])
            if d == 0:
                fw.op("dve", lambda e: e.tensor_copy(out=rawd[:, 0:L], in_=raw[:]), reads=[draw], writes=[draw])
            else:
                fw.op("dve", lambda e: e.tensor_copy(out=rawd[:, 0:L][:, ::-1], in_=raw[:]), reads=[draw], writes=[draw])
            for c in range(NCH):
                ps, dp = self.next_ps()
                fw.op("pe", lambda e: e.transpose(out=ps[:, 0:8], in_=rawd[:, c * 128:(c + 1) * 128], identity=self.ident[0:8, 0:8]), reads=[draw, self.d_const], writes=[dp])
                fw.op("dve", lambda e: e.tensor_tensor(out=dtok[:, c, 0:8], in0=ps[:, 0:8], in1=dbias[:, d, :], op=ALU.add), reads=[dp, dpp], writes=[ddtok])
            fw.op("act", lambda e: e.activation(out=dtok[:, :, 0:8], in_=dtok[:, :, 0:8], func=AF.Exp), reads=[ddtok], writes=[ddtok])
            fw.op("act", lambda e: e.activation(out=dtok[:, :, 0:8], in_=dtok[:, :, 0:8], func=AF.Ln, bias=self.one_t[:, 0:1]), reads=[ddtok, self.d_const], writes=[ddtok])
            fw.op("dve", lambda e: e.tensor_tensor(out=dtok[:, NCH - 1, 0:8], in0=dtok[:, NCH - 1, 0:8], in1=padm[:, :], op=ALU.mult), reads=[ddtok, dc], writes=[ddtok])
            fw.op("dve", lambda e: e.tensor_tensor(out=dtok[:, :, 8:16], in0=dtok[:, :, 0:8], in1=nea_bc[:, d, :].unsqueeze(1).to_broadcast([128, NCH, 8]), op=ALU.mult), reads=[ddtok, dpp], writes=[ddtok])
            fw.barrier()
          with ExitStack() as es4:
            NBF = 2
            xtk = [self.sb(es4, "sd_xtk%d" % i, [128, 512]) for i in range(NBF)]
            dxtk = [Dep() for _ in range(NBF)]
            btok = [self.sb(es4, "sd_btok%d" % i, [128, 256], BF16) for i in range(NBF)]
            dbtok = [Dep() for _ in range(NBF)]
            cbt = [self.sb(es4, "sd_cbt%d" % i, [128, 2, 128]) for i in range(NBF)]
            dcbt = [Dep() for _ in range(NBF)]
            sm = [self.sb(es4, "sd_sm%d" % i, [128, 4, 8]) for i in range(NBF)]
            dsm = [Dep() for _ in range(NBF)]
            NH = 16
            atri = [self.sb(es4, "sd_atri%d" % i, [128, 128]) for i in range(NH)]
            datri = [Dep() for _ in range(NH)]
            DT = [self.sb(es4, "sd_DT%d" % i, [128, 128]) for i in range(NH)]
            dDT = [Dep() for _ in range(NH)]
            EE = [self.sb(es4, "sd_EE%d" % i, [128, 128]) for i in range(NH)]
            dEE = [Dep() for _ in range(NH)]
            MT = [self.sb(es4, "sd_MT%d" % i, [128, 128], BF16) for i in range(NH)]
            dMT = [Dep() for _ in range(NH)]
            CE = [self.sb(es4, "sd_CE%d" % i, [128, 128], BF16) for i in range(NH)]
            dCE = [Dep() for _ in range(NH)]
            xdt = [self.sb(es4, "sd_xdt%d" % i, [128, 2, 64], BF16) for i in range(NH)]
            dxdt = [Dep() for _ in range(NH)]
            for j in range(8):
                fw.op("dve", lambda e: e.memset(ST[:, j, :], 0.0), writes=[dST[j]])
                fw.op("pool", lambda e: e.memset(STb[:, j, :], 0.0), writes=[dSTb[j]])
            ih = 0
            for c in range(NCH):
                t0 = c * 128
                Wv = min(128, L - t0)
                k_ = c % NBF
                px, dpx = self.next_ps()
                for j in range(4):
                    fw.op("pe", lambda e: e.matmul(px[:, j * 128:(j + 1) * 128], lhsT=xs[:, j, t0:t0 + 128], rhs=identb[:, :], start=True, stop=True), reads=[dws, dc], writes=[dpx])
                xtok, dxtok = xtk[k_], dxtk[k_]
                fw.op("act", lambda e: e.copy(out=xtok[:, :], in_=px[:, 0:512]), reads=[dpx], writes=[dxtok])
                pb, dpb = self.next_ps()
                for g in range(2):
                    fw.op("pe", lambda e: e.matmul(pb[:, g * 128:(g + 1) * 128], lhsT=Bm[:, g, t0:t0 + 128], rhs=identb[:, :], start=True, stop=True), reads=[dws, dc], writes=[dpb])
                fw.op("act", lambda e: e.copy(out=btok[k_][:, :], in_=pb[:, 0:256]), reads=[dpb], writes=[dbtok[k_]])
                pc, dpc = self.next_ps()
                fw.op("pe", lambda e: e.matmul(pc[:, 0:8], lhsT=triu[:, :], rhs=dtok[:, c, 8:16], start=True, stop=True), reads=[dc, ddtok], writes=[dpc])
                fw.op("pe", lambda e: e.matmul(pc[:, 8:16], lhsT=onesf[:, :], rhs=dtok[:, c, 8:16], start=True, stop=True), reads=[dc, ddtok], writes=[dpc])
                S_, dS_ = sm[k_], dsm[k_]
                fw.op("act", lambda e: e.copy(out=S_[:, 0, :], in_=pc[:, 0:8]), reads=[dpc], writes=[dS_])
                fw.op("dve", lambda e: e.tensor_tensor(out=S_[:, 1, :], in0=pc[:, 8:16], in1=S_[:, 0, :], op=ALU.subtract), reads=[dpc, dS_], writes=[dS_])
                fw.op("act", lambda e: e.activation(out=S_[:, 1, :], in_=S_[:, 1, :], func=AF.Exp), reads=[dS_], writes=[dS_])
                fw.op("dve", lambda e: e.tensor_tensor(out=S_[:, 2, :], in0=S_[:, 1, :], in1=dtok[:, c, 0:8], op=ALU.mult), reads=[dS_, ddtok], writes=[dS_])
                fw.op("act", lambda e: e.activation(out=S_[:, 3, :], in_=pc[:, 8:16], func=AF.Exp), reads=[dpc], writes=[dS_])
                for g in range(2):
                    pcb, dpcb = self.next_ps()
                    fw.op("pe", lambda e: e.matmul(pcb[:, 0:128], lhsT=Bm[:, g, t0:t0 + 128], rhs=Cm[:, g, t0:t0 + 128], start=True, stop=True), reads=[dws], writes=[dpcb])
                    fw.op("act", lambda e: e.copy(out=cbt[k_][:, g, :], in_=pcb[:, 0:128]), reads=[dpcb], writes=[dcbt[k_]])
                hb0 = (c % 2) * 8
                pDs = {}
                for jp in range(4):
                    pD, dpD = self.next_ps()
                    for jj in range(2):
                        j = jp * 2 + jj
                        h_ = hb0 + j
                        o = jj * 256
                        pDs[j] = (pD, dpD, o)
                        fw.op("dve" if jj == 0 else "pool", lambda e: e.tensor_scalar(out=atri[h_][:], in0=triu[:], scalar1=dtok[:, c, 8 + j:9 + j], scalar2=None, op0=ALU.mult), reads=[dc, ddtok], writes=[datri[h_]])
                        fw.op("pe", lambda e: e.matmul(pD[:, o:o + 128], lhsT=onesf[:, :], rhs=atri[h_][:, :], start=True, stop=False), reads=[dc, datri[h_]], writes=[dpD])
                        fw.op("pe", lambda e: e.matmul(pD[:, o:o + 128], lhsT=atri[h_][:, :], rhs=negones[:, :], start=False, stop=False), reads=[dc, datri[h_]], writes=[dpD])
                        fw.op("pe", lambda e: e.matmul(pD[:, o:o + 128], lhsT=self.ident[:, :], rhs=mneg[:, :], start=False, stop=True), reads=[dc, self.d_const], writes=[dpD])
                        fw.op("pe", lambda e: e.matmul(pD[:, o + 128:o + 256], lhsT=onesf[:, :], rhs=atri[h_][:, :], start=True, stop=True), reads=[dc, datri[h_]], writes=[dpD])
                for j in range(8):
                    g = j // 4
                    h_ = hb0 + j
                    pD, dpD, o = pDs[j]
                    fw.op("act", lambda e: e.activation(out=DT[h_][:], in_=pD[:, o:o + 128], func=AF.Exp), reads=[dpD], writes=[dDT[h_]])
                    fw.op("act", lambda e: e.activation(out=EE[h_][:], in_=pD[:, o + 128:o + 256], func=AF.Exp), reads=[dpD], writes=[dEE[h_]])
                    fw.op("dve", lambda e: e.tensor_tensor(out=MT[h_][:], in0=cbt[k_][:, g, :], in1=DT[h_][:], op=ALU.mult), reads=[dcbt[k_], dDT[h_]], writes=[dMT[h_]])
                    fw.op("pool", lambda e: e.tensor_tensor(out=CE[h_][:], in0=Cm[:, g, t0:t0 + 128], in1=EE[h_][:], op=ALU.mult), reads=[dws, dEE[h_]], writes=[dCE[h_]])
                    fw.op("dve", lambda e: e.tensor_scalar(out=xdt[h_][:, 0, :], in0=xtok[:, j * 64:(j + 1) * 64], scalar1=dtok[:, c, j:j + 1], scalar2=None, op0=ALU.mult), reads=[dxtok, ddtok], writes=[dxdt[h_]])
                    fw.op("pool", lambda e: e.tensor_scalar(out=xdt[h_][:, 1, :], in0=xtok[:, j * 64:(j + 1) * 64], scalar1=S_[:, 2, j:j + 1], scalar2=None, op0=ALU.mult), reads=[dxtok, dS_], writes=[dxdt[h_]])
                for jp in range(4):
                    py, dpy = self.next_ps()
                    for jj in range(2):
                        j = jp * 2 + jj
                        g = j // 4
                        h_ = hb0 + j
                        fw.op("pe", lambda e: e.matmul(py[jj * 64:(jj + 1) * 64, 0:128], lhsT=xdt[h_][:, 0, :], rhs=MT[h_][:, :], start=True, stop=False), reads=[dxdt[h_], dMT[h_]], writes=[dpy])
                        fw.op("pe", lambda e: e.matmul(py[jj * 64:(jj + 1) * 64, 0:128], lhsT=STb[:, j, :], rhs=CE[h_][:, :], start=False, stop=True), reads=[dSTb[j], dCE[h_]], writes=[dpy])
                    for jj in range(2):
                        j = jp * 2 + jj
                        g = j // 4
                        h_ = hb0 + j
                        fw.op("pe", lambda e: e.matmul(py[:, 128 + jj * 64:192 + jj * 64], lhsT=btok[k_][:, g * 128:(g + 1) * 128], rhs=xdt[h_][:, 1, :], start=True, stop=True), reads=[dbtok[k_], dxdt[h_]], writes=[dpy])
                    if d == 0:
                        fw.op("act", lambda e: e.copy(out=yacc[:, jp, t0:t0 + Wv], in_=py[:, 0:Wv]), reads=[dpy], writes=[dyacc])
                    else:
                        lo = L - (t0 + Wv)
                        ya = yacc[:, jp, lo:lo + Wv]
                        fw.op("dve", lambda e: e.tensor_tensor(out=ya[:, ::-1], in0=py[:, 0:Wv], in1=ya[:, ::-1], op=ALU.add), reads=[dpy, dyacc], writes=[dyacc])
                    for jj in range(2):
                        j = jp * 2 + jj
                        fw.op("dve", lambda e: e.scalar_tensor_tensor(out=ST[:, j, :], in0=ST[:, j, :], scalar=S_[:, 3, j:j + 1], in1=py[:, 128 + jj * 64:192 + jj * 64], op0=ALU.mult, op1=ALU.add), reads=[dST[j], dS_, dpy], writes=[dST[j]])
                        fw.op("act", lambda e: e.copy(out=STb[:, j, :], in_=ST[:, j, :]), reads=[dST[j]], writes=[dSTb[j]])
            fw.barrier()
        fw.barrier()
        if "dbg_yacc" in self.cfg.get("dump", ()):
            dbg = self.scratch("dbg_yacc", (512, L))
            fw.dma("sp", dbg.rearrange("(j p) t -> p j t", p=128), yacc[:, :, :], reads=[dyacc], writes=[Dep()])
            dbg2 = self.scratch("dbg_dtok", (128, NCH * 16))
            fw.dma("sp", dbg2[:, :], dtok[:, :, :].rearrange("p c k -> p (c k)"), reads=[ddtok], writes=[Dep()])
        with ExitStack() as es2:
            self.ensure_eps(es2)
            dsk = self.sb(es2, "sd_dsk", [128, 4])
            nw = self.sb(es2, "sd_nw", [128, 4])
            dq = Dep()
            for j in range(8):
                fw.dma("sp", dsk[64 * (j % 2):64 * (j % 2) + 64, j // 2:j // 2 + 1], I["ssd_d"][li, j:j + 1].partition_broadcast(64), writes=[dq], slow=True)
            fw.dma("sp", nw[:], I["ssd_norm_w"][li].rearrange("(j p) -> p j", p=128), writes=[dq], slow=True)
            W = 512
            xb = [self.sb(es2, "sd4_x%d" % i, [128, 4, W]) for i in range(2)]
            zb = [self.sb(es2, "sd4_z%d" % i, [128, 4, W]) for i in range(2)]
            dxb = [Dep(), Dep()]
            yb = self.sb(es2, "sd4_y", [128, 4, W])
            sq = self.sb(es2, "sd4_sq", [128, 4, W], BF16)
            dyb = Dep()
            rstd = self.sb(es2, "sd4_r", [128, W])
            ob = [self.sb(es2, "sd4_o%d" % i, [128, 4, W]) for i in range(2)]
            dob = [Dep(), Dep()]
            zv = self.projT[OFF_Z:OFF_Z + 512, :].rearrange("(j p) t -> p j t", p=128)
            yv = self.ybT.rearrange("(j p) t -> p j t", p=128)
            for bi, (t0, Wb) in enumerate(BLOCKS):
                x_, z_, dxz = xb[bi % 2], zb[bi % 2], dxb[bi % 2]
                o_, do_ = ob[bi % 2], dob[bi % 2]
                fw.dma("sp", x_[:, :, 0:Wb], xcv[:, 0:4, t0:t0 + Wb], reads=[d_xc], writes=[dxz])
                fw.dma("sp", z_[:, :, 0:Wb], zv[:, :, t0:t0 + Wb], reads=[self.dep_proj], writes=[dxz])
                fw.op("act", lambda e: e.activation(out=z_[:, :, 0:Wb], in_=z_[:, :, 0:Wb], func=AF.Silu), reads=[dxz], writes=[dxz])
                for j in range(4):
                    fw.op("dve", lambda e: e.scalar_tensor_tensor(out=yb[:, j, 0:Wb], in0=x_[:, j, 0:Wb], scalar=dsk[:, j:j + 1], in1=yacc[:, j, t0:t0 + Wb], op0=ALU.mult, op1=ALU.add), reads=[dxz, dq, dyacc], writes=[dyb])
                    fw.op("pool", lambda e: e.tensor_tensor(out=yb[:, j, 0:Wb], in0=yb[:, j, 0:Wb], in1=z_[:, j, 0:Wb], op=ALU.mult), reads=[dyb, dxz], writes=[dyb])
                    fw.op("act", lambda e: e.activation(out=sq[:, j, 0:Wb], in_=yb[:, j, 0:Wb], func=AF.Square), reads=[dyb], writes=[dyb])
                ps, dp = self.next_ps()
                for j in range(4):
                    fw.op("pe", lambda e: e.matmul(ps[:, 0:Wb], lhsT=self.ones_bf[:, :], rhs=sq[:, j, 0:Wb], start=(j == 0), stop=(j == 3)), reads=[dyb, self.d_const], writes=[dp])
                fw.op("act", lambda e: e.activation(out=rstd[:, 0:Wb], in_=ps[:, 0:Wb], func=AF.Sqrt, bias=self.eps_t[:, 0:1], scale=1.0 / 512), reads=[dp, self.d_const], writes=[dyb])
                fw.op("dve", lambda e: e.reciprocal(out=rstd[:, 0:Wb], in_=rstd[:, 0:Wb]), reads=[dyb], writes=[dyb])
                for j in range(4):
                    fw.op("dve", lambda e: e.scalar_tensor_tensor(out=o_[:, j, 0:Wb], in0=yb[:, j, 0:Wb], scalar=nw[:, j:j + 1], in1=rstd[:, 0:Wb], op0=ALU.mult, op1=ALU.mult), reads=[dyb, dq], writes=[do_])
                fw.dma("pool", yv[:, :, t0:t0 + Wb], o_[:, :, 0:Wb], reads=[do_], writes=[self.dep_yb])
    fw.barrier()


def _scratch_once(self, name, shape):
    if not hasattr(self, "_sc"):
        self._sc = {}
        self._scd = {}
    if name not in self._sc:
        self._sc[name] = self.scratch(name, shape)
        self._scd[name] = Dep()
    return self._sc[name]


def _dep_once(self, name):
    return self._scd[name]


Builder.mix_ssd = _mix_ssd
Builder.scratch_once = _scratch_once
Builder.dep_once = _dep_once


RW_ARR = ("r", "v", "kkn", "g", "bonus", "lw0", "kd0", "b0", "lw1", "kd1", "b1")


def _mix_rwkv(self, li):
    fw = self.fw
    I = self.I
    SC = {n: self.scratch_once("rw_" + n, (256, L)) for n in RW_ARR}
    dSC = {n: self.dep_once("rw_" + n) for n in RW_ARR}
    scv = {n: SC[n].rearrange("(kt p) t -> p kt t", p=128) for n in RW_ARR}

    def vec2(es_, name, ap1d, dep):
        t = self.sb(es_, name, [128, 2])
        fw.dma("sp", t[:], ap1d.rearrange("(kt p) -> p kt", p=128), writes=[dep], slow=True)
        return t

    with ExitStack() as es:
        dpar = Dep()
        mu = [vec2(es, "rw_mu%d" % a, I["rwkv_mu_rkv"][li, a], dpar) for a in range(3)]
        muw = [vec2(es, "rw_muw%d" % a, I["rwkv_mu_wag"][li, a], dpar) for a in range(3)]
        w0 = [vec2(es, "rw_w0%d" % d, I["rwkv_w0"][li, d], dpar) for d in range(2)]
        a0 = [vec2(es, "rw_a0%d" % d, I["rwkv_a0"][li, d], dpar) for d in range(2)]
        k_k = vec2(es, "rw_kk", I["rwkv_k_k"][li], dpar)
        k_a = vec2(es, "rw_ka", I["rwkv_k_a"][li], dpar)
        r_k = vec2(es, "rw_rk", I["rwkv_r_k"][li].rearrange("h n -> (h n)"), dpar)
        tiny = self.sb(es, "rw_tiny", [128, 1])
        fw.op("dve", lambda e: e.memset(tiny[:], 1e-12), writes=[dpar])
        blk = self.sb(es, "rw_blk", [128, 128], BF16)
        with ExitStack() as es2:
            blkf = self.sb(es2, "rw_blkf", [128, 128])
            dblk = Dep()
            fw.dma("sp", blkf[:], I["c_blk"][:, :], writes=[dblk])
            fw.op("dve", lambda e: e.tensor_copy(out=blk[:], in_=blkf[:]), reads=[dblk], writes=[dpar])
            fw.barrier()
        w1 = [self.load_weight_bf(es, "rw_w1%d" % d, I["rwkv_w1"][li, d], 256, 64) for d in range(2)]
        a1 = [self.load_weight_bf(es, "rw_a1%d" % d, I["rwkv_a1"][li, d], 256, 64) for d in range(2)]
        g1 = self.load_weight_bf(es, "rw_g1", I["rwkv_g1"][li], 256, 128)
        g2 = self.load_weight_bf(es, "rw_g2", I["rwkv_g2"][li], 128, 256)

        def load64(name, ap):
            t = self.sb(es, name, [64, 256], BF16)
            dd = Dep()
            with ExitStack() as es2:
                tf = self.sb(es2, name + "f", [64, 256])
                fw.dma("sp", tf[:], ap, writes=[dd])
                fw.op("dve", lambda e: e.tensor_copy(out=t[:], in_=tf[:]), reads=[dd], writes=[dd])
                fw.barrier()
            return t, dd
        w2 = [load64("rw_w2%d" % d, I["rwkv_w2"][li, d]) for d in range(2)]
        a2 = [load64("rw_a2%d" % d, I["rwkv_a2"][li, d]) for d in range(2)]

        W = 512
        X = self.sb(es, "rw_X", [128, 4, 2, W + 2])
        dX = Dep()
        Q = self.sb(es, "rw_Q", [128, 3, 2, W])
        dQ = Dep()
        T1 = self.sb(es, "rw_T1", [128, 2, W])
        dT1 = Dep()
        XW = self.sb(es, "rw_XW", [128, 3, 2, W], BF16)
        dXW = Dep()
        Hh = self.sb(es, "rw_Hh", [128, W], BF16)
        dHh = Dep()
        AS = self.sb(es, "rw_AS", [128, 2, W])
        dAS = Dep()
        KK = self.sb(es, "rw_KK", [128, 2, W])
        dKK = Dep()
        SQ = self.sb(es, "rw_SQ", [128, 2, W], BF16)
        dSQ = Dep()
        RS = self.sb(es, "rw_RS", [128, W])
        dRS = Dep()
        O = {n: self.sb(es, "rw_O_" + n, [128, 2, W]) for n in ("g", "bonus", "lw", "kd", "b")}
        dO = {n: Dep() for n in O}
        rkv_src = [self.projT[OFF_RKVX + a * 256:OFF_RKVX + (a + 1) * 256, :].rearrange("(kt p) t -> p kt t", p=128) for a in range(4)]
        for bi, (t0, Wb) in enumerate(BLOCKS):
            lo = max(t0 - 1, 0)
            hi = min(t0 + Wb + 1, L)
            c0 = lo - (t0 - 1)
            if t0 == 0:
                fw.op("dve", lambda e: e.memset(X[:, :, :, 0:1], 0.0), writes=[dX])
            if t0 + Wb == L:
                fw.op("dve", lambda e: e.memset(X[:, :, :, Wb + 1:Wb + 2], 0.0), writes=[dX])
            for a in range(4):
                fw.dma("sp", X[:, a, :, c0:c0 + (hi - lo)], rkv_src[a][:, :, lo:hi], reads=[self.dep_proj], writes=[dX])
            for a in range(4):
                for kt in range(2):
                    ctr, lf, rt = X[:, a, kt, 1:Wb + 1], X[:, a, kt, 0:Wb], X[:, a, kt, 2:Wb + 2]
                    fw.op("pool", lambda e: e.tensor_tensor(out=T1[:, kt, 0:Wb], in0=lf, in1=rt, op=ALU.add), reads=[dX], writes=[dT1])
                    fw.op("dve", lambda e: e.scalar_tensor_tensor(out=T1[:, kt, 0:Wb], in0=T1[:, kt, 0:Wb], scalar=0.5, in1=ctr, op0=ALU.mult, op1=ALU.subtract), reads=[dT1, dX], writes=[dT1])
                    if a < 3:
                        fw.op("dve", lambda e: e.scalar_tensor_tensor(out=Q[:, a, kt, 0:Wb], in0=T1[:, kt, 0:Wb], scalar=mu[a][:, kt:kt + 1], in1=ctr, op0=ALU.mult, op1=ALU.add), reads=[dT1, dX, dpar], writes=[dQ])
                    else:
                        for i3 in range(3):
                            fw.op("dve", lambda e: e.scalar_tensor_tensor(out=XW[:, i3, kt, 0:Wb], in0=T1[:, kt, 0:Wb], scalar=muw[i3][:, kt:kt + 1], in1=ctr, op0=ALU.mult, op1=ALU.add), reads=[dT1, dX, dpar], writes=[dXW])
            fw.dma("pool", scv["r"][:, :, t0:t0 + Wb], Q[:, 0, :, 0:Wb], reads=[dQ], writes=[dSC["r"]])
            fw.dma("pool", scv["v"][:, :, t0:t0 + Wb], Q[:, 2, :, 0:Wb], reads=[dQ], writes=[dSC["v"]])
            ps, dp = self.next_ps()
            for kt in range(2):
                fw.op("pe", lambda e: e.matmul(ps[:, 0:Wb], lhsT=g1[0][:, kt, :], rhs=XW[:, 2, kt, 0:Wb], start=(kt == 0), stop=(kt == 1)), reads=[g1[1], dXW], writes=[dp])
            fw.op("act", lambda e: e.activation(out=Hh[:, 0:Wb], in_=ps[:, 0:Wb], func=AF.Sigmoid), reads=[dp], writes=[dHh])
            for ct in range(2):
                ps, dp = self.next_ps()
                fw.op("pe", lambda e: e.matmul(ps[:, 0:Wb], lhsT=g2[0][:, 0, ct * 128:(ct + 1) * 128], rhs=Hh[:, 0:Wb], start=True, stop=True), reads=[g2[1], dHh], writes=[dp])
                fw.op("act", lambda e: e.copy(out=O["g"][:, ct, 0:Wb], in_=ps[:, 0:Wb]), reads=[dp], writes=[dO["g"]])
            fw.dma("pool", scv["g"][:, :, t0:t0 + Wb], O["g"][:, :, 0:Wb], reads=[dO["g"]], writes=[dSC["g"]])
            for kt in range(2):
                fw.op("dve", lambda e: e.tensor_scalar(out=KK[:, kt, 0:Wb], in0=Q[:, 1, kt, 0:Wb], scalar1=k_k[:, kt:kt + 1], scalar2=None, op0=ALU.mult), reads=[dQ, dpar], writes=[dKK])
                fw.op("act", lambda e: e.activation(out=SQ[:, kt, 0:Wb], in_=KK[:, kt, 0:Wb], func=AF.Square), reads=[dKK], writes=[dSQ])
                ps, dp = self.next_ps()
                fw.op("pe", lambda e: e.matmul(ps[:, 0:Wb], lhsT=blk[:, :], rhs=SQ[:, kt, 0:Wb], start=True, stop=True), reads=[dSQ, dpar], writes=[dp])
                fw.op("act", lambda e: e.activation(out=RS[:, 0:Wb], in_=ps[:, 0:Wb], func=AF.Sqrt, bias=tiny[:, 0:1]), reads=[dp, dpar], writes=[dRS])
                fw.op("dve", lambda e: e.reciprocal(out=RS[:, 0:Wb], in_=RS[:, 0:Wb]), reads=[dRS], writes=[dRS])
                fw.op("dve", lambda e: e.tensor_tensor(out=KK[:, kt, 0:Wb], in0=KK[:, kt, 0:Wb], in1=RS[:, 0:Wb], op=ALU.mult), reads=[dKK, dRS], writes=[dKK])
            fw.dma("pool", scv["kkn"][:, :, t0:t0 + Wb], KK[:, :, 0:Wb], reads=[dKK], writes=[dSC["kkn"]])
            for kt in range(2):
                fw.op("pool", lambda e: e.tensor_tensor(out=T1[:, kt, 0:Wb], in0=Q[:, 0, kt, 0:Wb], in1=Q[:, 1, kt, 0:Wb], op=ALU.mult), reads=[dQ, dT1], writes=[dT1])
                fw.op("dve", lambda e: e.tensor_scalar(out=SQ[:, kt, 0:Wb], in0=T1[:, kt, 0:Wb], scalar1=r_k[:, kt:kt + 1], scalar2=None, op0=ALU.mult), reads=[dT1, dpar, dSQ], writes=[dSQ])
                ps, dp = self.next_ps()
                fw.op("pe", lambda e: e.matmul(ps[:, 0:Wb], lhsT=blk[:, :], rhs=SQ[:, kt, 0:Wb], start=True, stop=True), reads=[dSQ, dpar], writes=[dp])
                fw.op("dve", lambda e: e.tensor_tensor(out=O["bonus"][:, kt, 0:Wb], in0=ps[:, 0:Wb], in1=Q[:, 2, kt, 0:Wb], op=ALU.mult), reads=[dp, dQ], writes=[dO["bonus"]])
            fw.dma("pool", scv["bonus"][:, :, t0:t0 + Wb], O["bonus"][:, :, 0:Wb], reads=[dO["bonus"]], writes=[dSC["bonus"]])
            for d in range(2):
                ps, dp = self.next_ps()
                for kt in range(2):
                    fw.op("pe", lambda e: e.matmul(ps[0:64, 0:Wb], lhsT=w1[d][0][:, kt, :], rhs=XW[:, 0, kt, 0:Wb], start=(kt == 0), stop=(kt == 1)), reads=[w1[d][1], dXW], writes=[dp])
                fw.op("act", lambda e: e.activation(out=Hh[0:64, 0:Wb], in_=ps[0:64, 0:Wb], func=AF.Tanh), reads=[dp], writes=[dHh])
                for ct in range(2):
                    ps, dp = self.next_ps()
                    fw.op("pe", lambda e: e.matmul(ps[:, 0:Wb], lhsT=w2[d][0][:, ct * 128:(ct + 1) * 128], rhs=Hh[0:64, 0:Wb], start=True, stop=True), reads=[w2[d][1], dHh], writes=[dp])
                    fw.op("act", lambda e: e.activation(out=O["lw"][:, ct, 0:Wb], in_=ps[:, 0:Wb], func=AF.Sigmoid, bias=w0[d][:, ct:ct + 1]), reads=[dp, dpar], writes=[dO["lw"]])
                    fw.op("dve", lambda e: e.tensor_scalar(out=O["lw"][:, ct, 0:Wb], in0=O["lw"][:, ct, 0:Wb], scalar1=-0.6065306597126334, scalar2=None, op0=ALU.mult), reads=[dO["lw"]], writes=[dO["lw"]])
                fw.dma("pool", scv["lw%d" % d][:, :, t0:t0 + Wb], O["lw"][:, :, 0:Wb], reads=[dO["lw"]], writes=[dSC["lw%d" % d]])
                ps, dp = self.next_ps()
                for kt in range(2):
                    fw.op("pe", lambda e: e.matmul(ps[0:64, 0:Wb], lhsT=a1[d][0][:, kt, :], rhs=XW[:, 1, kt, 0:Wb], start=(kt == 0), stop=(kt == 1)), reads=[a1[d][1], dXW], writes=[dp])
                fw.op("act", lambda e: e.copy(out=Hh[0:64, 0:Wb], in_=ps[0:64, 0:Wb]), reads=[dp], writes=[dHh])
                for ct in range(2):
                    ps, dp = self.next_ps()
                    fw.op("pe", lambda e: e.matmul(ps[:, 0:Wb], lhsT=a2[d][0][:, ct * 128:(ct + 1) * 128], rhs=Hh[0:64, 0:Wb], start=True, stop=True), reads=[a2[d][1], dHh], writes=[dp])
                    fw.op("act", lambda e: e.activation(out=AS[:, ct, 0:Wb], in_=ps[:, 0:Wb], func=AF.Sigmoid, bias=a0[d][:, ct:ct + 1]), reads=[dp, dpar], writes=[dAS])
                for kt in range(2):
                    fw.op("dve", lambda e: e.tensor_scalar(out=O["kd"][:, kt, 0:Wb], in0=AS[:, kt, 0:Wb], scalar1=-1.0, scalar2=None, op0=ALU.add), reads=[dAS], writes=[dO["kd"]])
                    fw.op("dve", lambda e: e.tensor_scalar(out=O["kd"][:, kt, 0:Wb], in0=O["kd"][:, kt, 0:Wb], scalar1=k_a[:, kt:kt + 1], scalar2=1.0, op0=ALU.mult, op1=ALU.add), reads=[dO["kd"], dpar], writes=[dO["kd"]])
                    fw.op("pool", lambda e: e.tensor_tensor(out=O["kd"][:, kt, 0:Wb], in0=O["kd"][:, kt, 0:Wb], in1=Q[:, 1, kt, 0:Wb], op=ALU.mult), reads=[dO["kd"], dQ], writes=[dO["kd"]])
                    fw.op("pool", lambda e: e.tensor_tensor(out=O["b"][:, kt, 0:Wb], in0=KK[:, kt, 0:Wb], in1=AS[:, kt, 0:Wb], op=ALU.mult), reads=[dKK, dAS], writes=[dO["b"]])
                fw.dma("pool", scv["kd%d" % d][:, :, t0:t0 + Wb], O["kd"][:, :, 0:Wb], reads=[dO["kd"]], writes=[dSC["kd%d" % d]])
                fw.dma("pool", scv["b%d" % d][:, :, t0:t0 + Wb], O["b"][:, :, 0:Wb], reads=[dO["b"]], writes=[dSC["b%d" % d]])
    fw.barrier()
    self._rwkv_scan(li, SC, dSC, scv)


Builder.mix_rwkv = _mix_rwkv


def _rwkv_scan(self, li, SC, dSC, scv):
    fw = self.fw
    I = self.I
    names = ("rt", "at", "kt", "bt", "kh", "bh", "vb")
    AD = {}
    dAD = {}
    for d in range(2):
        for kt in range(2):
            for n in names:
                AD[(d, kt, n)] = self.scratch_once("rwA_%d_%d_%s" % (d, kt, n), (128, LP)) if False else None
    if not hasattr(self, "_rwA"):
        self._rwA = {}
        for d in range(2):
            for kt in range(2):
                self._rwA[(d, kt)] = self.nc.dram_tensor("rwA_%d_%d" % (d, kt), [128, 7, LP], BF16, kind="Internal").ap()
        self._rwA_dep = {k: Dep() for k in self._rwA}
    with ExitStack() as es:
        yacc = self.sb(es, "rs_yacc", [128, 2, L])
        dyacc = Dep()
        tril_s = self.sb(es, "rs_tril_s", [128, 128])
        tril_i = self.sb(es, "rs_tril_i", [128, 128])
        trilT_s = self.sb(es, "rs_trilT_s", [128, 128])
        identb = self.sb(es, "rs_identb", [128, 128], BF16)
        dc = Dep()
        fw.dma("sp", tril_s[:], I["c_tril_s"][:, :], writes=[dc])
        fw.dma("sp", tril_i[:], I["c_tril_i"][:, :], writes=[dc])
        fw.dma("sp", trilT_s[:], I["c_trilT_s"][:, :], writes=[dc])
        fw.op("dve", lambda e: e.tensor_copy(out=identb[:], in_=self.ident[:]), reads=[self.d_const], writes=[dc])
        ones1 = self.sb(es, "rs_ones", [128, 128])
        fw.op("dve", lambda e: e.memset(ones1[:], 1.0), writes=[dc])
        etot = self.sb(es, "rs_etot", [128, 2, NCH])
        detot = Dep()
        for d in range(2):
            for kt in range(2):
                with ExitStack() as es3:
                    A = self.sb(es3, "rs_A", [128, 7, LP], BF16)
                    dA = Dep()
                    AI = {n: i for i, n in enumerate(names)}
                    stg = [self.sb(es3, "rs_stg%d" % i, [128, L]) for i in range(2)]
                    dstg = [Dep(), Dep()]
                    cs = self.sb(es3, "rs_cs", [128, LP])
                    lwr = self.sb(es3, "rs_lwr", [128, LP])
                    E1 = self.sb(es3, "rs_E1", [128, LP])
                    E2 = self.sb(es3, "rs_E2", [128, LP])
                    dcs, dlw, dE1, dE2 = Dep(), Dep(), Dep(), Dep()
                    fw.op("pool", lambda e: e.memset(A[:, :, L:LP], 0.0), writes=[dA])

                    def ld(i, name):
                        fw.dma("sp", stg[i][:], scv[name][:, kt, :], reads=[dSC[name]], writes=[dstg[i]])
                        return stg[i][:, :] if d == 0 else stg[i][:, ::-1]

                    sv = ld(0, "lw%d" % d)
                    fw.op("dve", lambda e: e.memset(lwr[:, L:LP], 0.0), writes=[dlw])
                    fw.op("dve", lambda e: e.tensor_copy(out=lwr[:, 0:L], in_=sv), reads=[dstg[0]], writes=[dlw])
                    for c in range(NCH):
                        fw.op("dve", lambda e: e.tensor_tensor_scan(out=cs[:, c * 128:(c + 1) * 128], data0=ones1[:, :], data1=lwr[:, c * 128:(c + 1) * 128], initial=0.0, op0=ALU.mult, op1=ALU.add),
                              reads=[dlw, dc], writes=[dcs])
                    fw.op("act", lambda e: e.activation(out=etot[:, kt, :], in_=cs[:, 127::128], func=AF.Exp), reads=[dcs], writes=[detot])
                    fw.op("act", lambda e: e.activation(out=E1[:], in_=cs[:], func=AF.Exp), reads=[dcs], writes=[dE1])
                    sv = ld(1, "r")
                    fw.op("dve", lambda e: e.tensor_tensor(out=A[:, AI["rt"], 0:L], in0=sv, in1=E1[:, 0:L], op=ALU.mult), reads=[dstg[1], dE1], writes=[dA])
                    fw.op("pool", lambda e: e.tensor_tensor(out=lwr[:], in0=cs[:], in1=lwr[:], op=ALU.subtract), reads=[dcs, dlw], writes=[dlw])
                    fw.op("act", lambda e: e.activation(out=E1[:], in_=lwr[:], func=AF.Exp), reads=[dlw, dE1], writes=[dE1])
                    sv = ld(0, "kkn")
                    fw.op("dve", lambda e: e.scalar_tensor_tensor(out=A[:, AI["at"], 0:L], in0=sv, scalar=-1.0, in1=E1[:, 0:L], op0=ALU.mult, op1=ALU.mult), reads=[dstg[0], dE1], writes=[dA])
                    for c in range(NCH):
                        fw.op("dve", lambda e: e.tensor_scalar(out=lwr[:, c * 128:(c + 1) * 128], in0=cs[:, c * 128:(c + 1) * 128], scalar1=cs[:, c * 128 + 127:c * 128 + 128], scalar2=-1.0, op0=ALU.subtract, op1=ALU.mult),
                              reads=[dcs, dlw], writes=[dlw])
                    fw.op("act", lambda e: e.activation(out=E1[:], in_=lwr[:], func=AF.Exp), reads=[dlw, dE1], writes=[dE1])
                    fw.op("act", lambda e: e.activation(out=E2[:], in_=cs[:], func=AF.Exp, scale=-1.0), reads=[dcs], writes=[dE2])
                    sv = ld(1, "kd%d" % d)
                    fw.op("dve", lambda e: e.tensor_tensor(out=A[:, AI["kt"], 0:L], in0=sv, in1=E2[:, 0:L], op=ALU.mult), reads=[dstg[1], dE2], writes=[dA])
                    fw.op("pool", lambda e: e.tensor_tensor(out=A[:, AI["kh"], 0:L], in0=sv, in1=E1[:, 0:L], op=ALU.mult), reads=[dstg[1], dE1], writes=[dA])
                    sv = ld(0, "b%d" % d)
                    fw.op("dve", lambda e: e.tensor_tensor(out=A[:, AI["bt"], 0:L], in0=sv, in1=E2[:, 0:L], op=ALU.mult), reads=[dstg[0], dE2], writes=[dA])
                    fw.op("pool", lambda e: e.tensor_tensor(out=A[:, AI["bh"], 0:L], in0=sv, in1=E1[:, 0:L], op=ALU.mult), reads=[dstg[0], dE1], writes=[dA])
                    sv = ld(1, "v")
                    fw.op("dve", lambda e: e.tensor_copy(out=A[:, AI["vb"], 0:L], in_=sv), reads=[dstg[1]], writes=[dA])
                    fw.dma("pool", self._rwA[(d, kt)][:, :, :], A[:, :, :], reads=[dA], writes=[self._rwA_dep[(d, kt)]])
                    fw.barrier()
            with ExitStack() as es4:
                AA = [self.sb(es4, "rs_AA%d" % kt, [128, 7, LP], BF16) for kt in range(2)]
                dAA = [Dep(), Dep()]
                for kt in range(2):
                    fw.dma("sp", AA[kt][:, :, :], self._rwA[(d, kt)][:, :, :], reads=[self._rwA_dep[(d, kt)]], writes=[dAA[kt]])
                AI = {n: i for i, n in enumerate(names)}
                S0 = self.sb(es4, "rs_S0", [128, 2, 64])
                S0b = self.sb(es4, "rs_S0b", [128, 2, 64], BF16)
                dS0 = [[Dep(), Dep()], [Dep(), Dep()]]
                dS0b = [[Dep(), Dep()], [Dep(), Dep()]]
                fw.op("dve", lambda e: e.memset(S0[:], 0.0), writes=[x for y in dS0 for x in y])
                fw.op("dve", lambda e: e.memset(S0b[:], 0.0), writes=[x for y in dS0b for x in y])
                NHB = 2
                chains = [(kt, hh) for kt in range(2) for hh in range(2)]

                def mk(nm, shape, dt=F32):
                    return ({ch: [self.sb(es4, "rs_%s_%d%d_%d" % (nm, ch[0], ch[1], i), shape, dt) for i in range(NHB)] for ch in chains},
                            {ch: [Dep() for i in range(NHB)] for ch in chains})
                TK, dTK = mk("TK", [128, 192], BF16)
                Pm, dPm = mk("P", [128, 128])
                PTm, dPTm = mk("PT", [128, 128])
                XT, dXT = mk("XT", [128, 128])
                XTb, dXTb = mk("XTb", [128, 128], BF16)
                Ak, dAk = mk("Ak", [128, 3, 128], BF16)
                P1b, dP1b = mk("P1b", [128, 64], BF16)
                Zb, dZb = mk("Zb", [128, 64], BF16)
                for c in range(NCH):
                    t0 = c * 128
                    Wv = min(128, L - t0)
                    C = slice(t0, t0 + 128)
                    b_ = c % NHB
                    for ch in chains:
                        kt, hh = ch
                        R = slice(64 * hh, 64 * hh + 64)
                        Aq = lambda n: AA[kt][R, AI[n], C]
                        tk, dtk = TK[ch][b_], dTK[ch][b_]
                        ptk, dptk = self.next_ps()
                        for i3, n in enumerate(("vb", "kh", "bh")):
                            fw.op("pe", lambda e: e.matmul(ptk[:, i3 * 64:(i3 + 1) * 64], lhsT=Aq(n), rhs=identb[R, 64 * hh:64 * hh + 64], start=True, stop=True), reads=[dAA[kt], dc], writes=[dptk])
                        fw.op("pe", lambda e: e.matmul(ptk[:, 256:384], lhsT=Aq("bt"), rhs=Aq("rt"), start=True, stop=True), reads=[dAA[kt]], writes=[dptk])
                        pa, dpa = self.next_ps()
                        for i5, (l_, r_) in enumerate((("bt", "at"), ("at", "bt"), ("kt", "at"), ("kt", "rt"))):
                            fw.op("pe", lambda e: e.matmul(pa[:, i5 * 128:(i5 + 1) * 128], lhsT=Aq(l_), rhs=Aq(r_), start=True, stop=True), reads=[dAA[kt]], writes=[dpa])
                        fw.op("act", lambda e: e.copy(out=tk[:, :], in_=ptk[:, 0:192]), reads=[dptk], writes=[dtk])
                        P_, dP_ = Pm[ch][b_], dPm[ch][b_]
                        PT_, dPT_ = PTm[ch][b_], dPTm[ch][b_]
                        X_, dX_ = XT[ch][b_], dXT[ch][b_]
                        ak, dak = Ak[ch][b_], dAk[ch][b_]
                        fw.op("dve", lambda e: e.tensor_tensor(out=PT_[:], in0=pa[:, 0:128], in1=tril_s[:], op=ALU.mult), reads=[dpa, dc], writes=[dPT_])
                        fw.op("dve", lambda e: e.tensor_tensor(out=P_[:], in0=pa[:, 128:256], in1=trilT_s[:], op=ALU.mult), reads=[dpa, dc], writes=[dP_])
                        fw.op("dve", lambda e: e.tensor_tensor(out=ak[:, 0, :], in0=pa[:, 256:384], in1=tril_s[:], op=ALU.mult), reads=[dpa, dc], writes=[dak])
                        fw.op("dve", lambda e: e.tensor_tensor(out=ak[:, 1, :], in0=pa[:, 384:512], in1=tril_i[:], op=ALU.mult), reads=[dpa, dc], writes=[dak])
                        fw.op("dve", lambda e: e.tensor_tensor(out=ak[:, 2, :], in0=ptk[:, 256:384], in1=tril_i[:], op=ALU.mult), reads=[dptk, dc], writes=[dak])
                        fw.op("pool", lambda e: e.tensor_tensor(out=X_[:], in0=PT_[:], in1=self.ident[:], op=ALU.add), reads=[dPT_, self.d_const], writes=[dX_])
                    for s_ in range(6):
                        pns = {}
                        for ch in chains:
                            P_, dP_ = Pm[ch][b_], dPm[ch][b_]
                            PT_, dPT_ = PTm[ch][b_], dPTm[ch][b_]
                            pn, dpn = self.next_ps()
                            pns[ch] = (pn, dpn)
                            fw.op("pe", lambda e: e.matmul(pn[:, 0:128], lhsT=PT_[:, :], rhs=P_[:, :], start=True, stop=True), reads=[dPT_, dP_], writes=[dpn])
                            if s_ < 5:
                                fw.op("pe", lambda e: e.matmul(pn[:, 128:256], lhsT=P_[:, :], rhs=PT_[:, :], start=True, stop=True), reads=[dPT_, dP_], writes=[dpn])
                        for ch in chains:
                            P_, dP_ = Pm[ch][b_], dPm[ch][b_]
                            PT_, dPT_ = PTm[ch][b_], dPTm[ch][b_]
                            pn, dpn = pns[ch]
                            fw.op("act", lambda e: e.copy(out=P_[:], in_=pn[:, 0:128]), reads=[dpn], writes=[dP_])
                            if s_ < 5:
                                fw.op("pool" if False else "act", lambda e: e.copy(out=PT_[:], in_=pn[:, 128:256]), reads=[dpn], writes=[dPT_])
                        pxs = {}
                        for ch in chains:
                            P_, dP_ = Pm[ch][b_], dPm[ch][b_]
                            X_, dX_ = XT[ch][b_], dXT[ch][b_]
                            px_, dpx_ = self.next_ps()
                            pxs[ch] = (px_, dpx_)
                            fw.op("pe", lambda e: e.matmul(px_[:, 0:128], lhsT=P_[:, :], rhs=X_[:, :], start=True, stop=True), reads=[dP_, dX_], writes=[dpx_])
                        for ch in chains:
                            X_, dX_ = XT[ch][b_], dXT[ch][b_]
                            px_, dpx_ = pxs[ch]
                            fw.op("dve", lambda e: e.tensor_tensor(out=X_[:], in0=px_[:, 0:128], in1=X_[:], op=ALU.add), reads=[dpx_, dX_], writes=[dX_])
                    pps = {}
                    for ch in chains:
                        kt, hh = ch
                        R = slice(64 * hh, 64 * hh + 64)
                        X_, dX_ = XT[ch][b_], dXT[ch][b_]
                        xb_, dxb_ = XTb[ch][b_], dXTb[ch][b_]
                        tk, dtk = TK[ch][b_], dTK[ch][b_]
                        ak, dak = Ak[ch][b_], dAk[ch][b_]
                        fw.op("act", lambda e: e.copy(out=xb_[:], in_=X_[:]), reads=[dX_], writes=[dxb_])
                        pp, dpp = self.next_ps()
                        pps[ch] = (pp, dpp)
                        fw.op("pe", lambda e: e.matmul(pp[:, 0:64], lhsT=AA[kt][R, AI["at"], C], rhs=S0b[R, kt, :], start=True, stop=False), reads=[dAA[kt], dS0b[kt][hh]], writes=[dpp])
                        fw.op("pe", lambda e: e.matmul(pp[:, 0:64], lhsT=ak[:, 0, :], rhs=tk[:, 0:64], start=False, stop=True), reads=[dak, dtk], writes=[dpp])
                    for ch in chains:
                        pp, dpp = pps[ch]
                        p1, dp1 = P1b[ch][b_], dP1b[ch][b_]
                        fw.op("act", lambda e: e.copy(out=p1[:], in_=pp[:, 0:64]), reads=[dpp], writes=[dp1])
                    pzs = {}
                    for ch in chains:
                        xb_, dxb_ = XTb[ch][b_], dXTb[ch][b_]
                        p1, dp1 = P1b[ch][b_], dP1b[ch][b_]
                        pz, dpz = self.next_ps()
                        pzs[ch] = (pz, dpz)
                        fw.op("pe", lambda e: e.matmul(pz[:, 0:64], lhsT=xb_[:, :], rhs=p1[:, :], start=True, stop=True), reads=[dxb_, dp1], writes=[dpz])
                    for ch in chains:
                        pz, dpz = pzs[ch]
                        z_, dz_ = Zb[ch][b_], dZb[ch][b_]
                        fw.op("dve", lambda e: e.tensor_copy(out=z_[:], in_=pz[:, 0:64]), reads=[dpz], writes=[dz_])
                    for ch in chains:
                        kt, hh = ch
                        R = slice(64 * hh, 64 * hh + 64)
                        tk, dtk = TK[ch][b_], dTK[ch][b_]
                        ak, dak = Ak[ch][b_], dAk[ch][b_]
                        z_, dz_ = Zb[ch][b_], dZb[ch][b_]
                        py, dpy = self.next_ps()
                        fw.op("pe", lambda e: e.matmul(py[R, 0:128], lhsT=S0b[R, kt, :], rhs=AA[kt][R, AI["rt"], C], start=True, stop=False), reads=[dAA[kt], dS0b[kt][hh]], writes=[dpy])
                        fw.op("pe", lambda e: e.matmul(py[R, 0:128], lhsT=tk[:, 0:64], rhs=ak[:, 1, :], start=False, stop=False), reads=[dak, dtk], writes=[dpy])
                        fw.op("pe", lambda e: e.matmul(py[R, 0:128], lhsT=z_[:, :], rhs=ak[:, 2, :], start=False, stop=True), reads=[dak, dz_], writes=[dpy])
                        pS, dpS = py, dpy
                        fw.op("pe", lambda e: e.matmul(pS[R, 256:320], lhsT=tk[:, 64:128], rhs=tk[:, 0:64], start=True, stop=False), reads=[dtk], writes=[dpS])
                        fw.op("pe", lambda e: e.matmul(pS[R, 256:320], lhsT=tk[:, 128:192], rhs=z_[:, :], start=False, stop=True), reads=[dtk, dz_], writes=[dpS])
                        if d == 0:
                            fw.op("act", lambda e: e.copy(out=yacc[R, kt, t0:t0 + Wv], in_=py[R, 0:Wv]), reads=[dpy], writes=[dyacc])
                        else:
                            lo = L - (t0 + Wv)
                            ya = yacc[R, kt, lo:lo + Wv]
                            fw.op("dve", lambda e: e.tensor_tensor(out=ya[:, ::-1], in0=py[R, 0:Wv], in1=ya[:, ::-1], op=ALU.add), reads=[dpy, dyacc], writes=[dyacc])
                        fw.op("dve", lambda e: e.scalar_tensor_tensor(out=S0[R, kt, :], in0=S0[R, kt, :], scalar=etot[R, kt, c:c + 1], in1=pS[R, 256:320], op0=ALU.mult, op1=ALU.add), reads=[dS0[kt][hh], detot, dpS], writes=[dS0[kt][hh]])
                        fw.op("act", lambda e: e.copy(out=S0b[R, kt, :], in_=S0[R, kt, :]), reads=[dS0[kt][hh]], writes=[dS0b[kt][hh]])
                fw.barrier()
        fw.barrier()
        with ExitStack() as es5:
            dq = Dep()
            def vec2(name, ap1d):
                t = self.sb(es5, name, [128, 2])
                fw.dma("sp", t[:], ap1d.rearrange("(kt p) -> p kt", p=128), writes=[dq], slow=True)
                return t
            lnw = vec2("r3_lnw", I["rwkv_ln_w"][li])
            lnb = vec2("r3_lnb", I["rwkv_ln_b"][li])
            epsl = self.sb(es5, "r3_eps", [128, 1])
            fw.op("dve", lambda e: e.memset(epsl[:], 64e-5), writes=[dq])
            blk = self.sb(es5, "r3_blk", [128, 128], BF16)
            blkf = self.sb(es5, "r3_blkf", [128, 128])
            fw.dma("sp", blkf[:], I["c_blk"][:, :], writes=[dq])
            fw.op("dve", lambda e: e.tensor_copy(out=blk[:], in_=blkf[:]), reads=[dq], writes=[dq])
            W = 512
            yb = self.sb(es5, "r3_yb", [128, W], BF16)
            sq = self.sb(es5, "r3_sq", [128, W], BF16)
            mean = self.sb(es5, "r3_mean", [128, W])
            var = self.sb(es5, "r3_var", [128, W])
            yc = [self.sb(es5, "r3_yc%d" % i, [128, 2, W]) for i in range(2)]
            bg = [self.sb(es5, "r3_bg%d" % i, [128, 2, 2, W]) for i in range(2)]
            dt_ = Dep()
            dyc = [Dep(), Dep()]
            dbg = [Dep(), Dep()]
            yv = self.ycT.rearrange("(kt p) t -> p kt t", p=128)
            for bi, (t0, Wb) in enumerate(BLOCKS):
                y_, dy_ = yc[bi % 2], dyc[bi % 2]
                b_, db_ = bg[bi % 2], dbg[bi % 2]
                fw.dma("sp", b_[:, 0, :, 0:Wb], scv["bonus"][:, :, t0:t0 + Wb], reads=[dSC["bonus"]], writes=[db_])
                fw.dma("sp", b_[:, 1, :, 0:Wb], scv["g"][:, :, t0:t0 + Wb], reads=[dSC["g"]], writes=[db_])
                for kt in range(2):
                    ysl = yacc[:, kt, t0:t0 + Wb]
                    fw.op("act", lambda e: e.copy(out=yb[:, 0:Wb], in_=ysl), reads=[dyacc], writes=[dt_])
                    fw.op("pool", lambda e: e.tensor_tensor(out=sq[:, 0:Wb], in0=ysl, in1=ysl, op=ALU.mult), reads=[dyacc], writes=[dt_])
                    pm, dpm = self.next_ps()
                    pq, dpq = self.next_ps()
                    fw.op("pe", lambda e: e.matmul(pm[:, 0:Wb], lhsT=blk[:, :], rhs=yb[:, 0:Wb], start=True, stop=True), reads=[dq, dt_], writes=[dpm])
                    fw.op("pe", lambda e: e.matmul(pq[:, 0:Wb], lhsT=blk[:, :], rhs=sq[:, 0:Wb], start=True, stop=True), reads=[dq, dt_], writes=[dpq])
                    fw.op("act", lambda e: e.mul(out=mean[:, 0:Wb], in_=pm[:, 0:Wb], mul=1.0 / 64), reads=[dpm], writes=[dt_])
                    fw.op("dve", lambda e: e.tensor_tensor(out=var[:, 0:Wb], in0=mean[:, 0:Wb], in1=mean[:, 0:Wb], op=ALU.mult), reads=[dt_], writes=[dt_])
                    fw.op("dve", lambda e: e.scalar_tensor_tensor(out=var[:, 0:Wb], in0=pq[:, 0:Wb], scalar=1.0 / 64, in1=var[:, 0:Wb], op0=ALU.mult, op1=ALU.subtract), reads=[dpq, dt_], writes=[dt_])
                    fw.op("act", lambda e: e.activation(out=var[:, 0:Wb], in_=var[:, 0:Wb], func=AF.Sqrt, bias=epsl[:, 0:1]), reads=[dt_, dq], writes=[dt_])
                    fw.op("dve", lambda e: e.reciprocal(out=var[:, 0:Wb], in_=var[:, 0:Wb]), reads=[dt_], writes=[dt_])
                    fw.op("pool", lambda e: e.tensor_tensor(out=y_[:, kt, 0:Wb], in0=ysl, in1=mean[:, 0:Wb], op=ALU.subtract), reads=[dyacc, dt_], writes=[dy_])
                    fw.op("dve", lambda e: e.tensor_tensor(out=y_[:, kt, 0:Wb], in0=y_[:, kt, 0:Wb], in1=var[:, 0:Wb], op=ALU.mult), reads=[dy_, dt_], writes=[dy_])
                    fw.op("dve", lambda e: e.tensor_scalar(out=y_[:, kt, 0:Wb], in0=y_[:, kt, 0:Wb], scalar1=lnw[:, kt:kt + 1], scalar2=lnb[:, kt:kt + 1], op0=ALU.mult, op1=ALU.add), reads=[dy_, dq], writes=[dy_])
                    fw.op("pool", lambda e: e.tensor_tensor(out=y_[:, kt, 0:Wb], in0=y_[:, kt, 0:Wb], in1=b_[:, 0, kt, 0:Wb], op=ALU.add), reads=[dy_, db_], writes=[dy_])
                    fw.op("dve", lambda e: e.tensor_tensor(out=y_[:, kt, 0:Wb], in0=y_[:, kt, 0:Wb], in1=b_[:, 1, kt, 0:Wb], op=ALU.mult), reads=[dy_, db_], writes=[dy_])
                fw.dma("pool", yv[:, :, t0:t0 + Wb], y_[:, :, 0:Wb], reads=[dy_], writes=[self.dep_yc])
    fw.barrier()


Builder._rwkv_scan = _rwkv_scan
```
